# Optimizing a Trainium2 kernel written in Bass

```python
import math
import jax, jax.numpy as jnp
from jax import lax
import numpy as np

D_MODEL = 1024
BATCH = 1
SEQ = 16384
DEPTH = 1

CONV_DIM = 1024
CONV_WIDTH = 3
DIFF_HEADS = 8
DIFF_HEAD_DIM = 64
DIFF_V_DIM = 2 * DIFF_HEAD_DIM
Q_BLOCK = 128
N_BUCKETS = 32
MAX_DISTANCE = 128
MEM_LEN = 256
CROSS_HEADS = 4
CROSS_HEAD_DIM = D_MODEL // CROSS_HEADS
PEER_HEADS = 8
N_KEYS = 128
N_EXPERTS = N_KEYS * N_KEYS
PEER_TOPK = 16
PEER_DK = 256
PEER_DK_HALF = PEER_DK // 2
PEER_CHUNK = 128

IN_SPLITS = [CONV_DIM, CONV_DIM, CONV_DIM,
             DIFF_HEADS * 2 * DIFF_HEAD_DIM,
             DIFF_HEADS * 2 * DIFF_HEAD_DIM,
             DIFF_HEADS * DIFF_V_DIM,
             D_MODEL, D_MODEL]
IN_COLS = sum(IN_SPLITS)

kernel_name = "hybrid_conv_diffattn_peer_block"


def rmsnorm(x, g, eps=1e-6):
    x32 = x.astype(jnp.float32)
    y = x32 * lax.rsqrt(jnp.mean(x32 * x32, axis=-1, keepdims=True) + eps)
    return (y * g.astype(jnp.float32)).astype(x.dtype)


def t5_bucket(dist):
    n = jnp.maximum(dist, 0)
    max_exact = N_BUCKETS // 2
    nf = jnp.maximum(n, 1).astype(jnp.float32)
    large = max_exact + (jnp.log(nf / max_exact) / math.log(MAX_DISTANCE / max_exact)
                         * (N_BUCKETS - max_exact)).astype(jnp.int32)
    large = jnp.minimum(large, N_BUCKETS - 1)
    return jnp.where(n < max_exact, n, large)


def short_conv_mixer(b_gate, c_gate, xc, conv_w, w_out):
    u = c_gate * xc
    rhs = conv_w.reshape(CONV_WIDTH, 1, CONV_DIM).astype(u.dtype)
    conv = lax.conv_general_dilated(u, rhs, window_strides=(1,), padding=[(CONV_WIDTH - 1, 0)],
                                    dimension_numbers=('NWC', 'WIO', 'NWC'),
                                    feature_group_count=CONV_DIM)
    return (b_gate * conv) @ w_out


def diff_attention(q, k, v, rel_bias, lam, lam_init, subln_g):
    b, s = q.shape[0], q.shape[1]
    nb = s // Q_BLOCK
    scale = DIFF_HEAD_DIM ** -0.5
    qb = q.reshape(b, nb, Q_BLOCK, DIFF_HEADS, 2, DIFF_HEAD_DIM).swapaxes(0, 1)
    kpos = jnp.arange(s, dtype=jnp.int32)

    def block(args):
        qi, blk = args
        qpos = blk * Q_BLOCK + jnp.arange(Q_BLOCK, dtype=jnp.int32)
        rel = qpos[:, None] - kpos[None, :]
        bias = jnp.transpose(rel_bias[t5_bucket(rel)], (2, 0, 1)).astype(jnp.float32)
        logits = jnp.einsum('bqhcd,bkhcd->bhcqk', qi, k).astype(jnp.float32) * scale
        logits = logits + bias[None, :, None]
        logits = jnp.where(rel[None, None, None] >= 0, logits, -1e30)
        p = jax.nn.softmax(logits, axis=-1)
        a = p[:, :, 0] - lam * p[:, :, 1]
        return jnp.einsum('bhqk,bkhe->bqhe', a.astype(v.dtype), v)

    out = lax.map(block, (qb, jnp.arange(nb, dtype=jnp.int32)))
    out = out.swapaxes(0, 1).reshape(b, s, DIFF_HEADS, DIFF_V_DIM)
    out = rmsnorm(out, subln_g, eps=1e-5) * (1.0 - lam_init)
    return out.reshape(b, s, DIFF_HEADS * DIFF_V_DIM)


def cross_attention(h, m, w_cq, w_ckv, w_co):
    b, s, _ = h.shape
    q = (h @ w_cq).reshape(b, s, CROSS_HEADS, CROSS_HEAD_DIM)
    kv = (m @ w_ckv).reshape(b, m.shape[1], 2, CROSS_HEADS, CROSS_HEAD_DIM)
    k, v = kv[:, :, 0], kv[:, :, 1]
    logits = jnp.einsum('bshd,bmhd->bhsm', q, k).astype(jnp.float32) * (CROSS_HEAD_DIM ** -0.5)
    p = jax.nn.softmax(logits, axis=-1).astype(v.dtype)
    o = jnp.einsum('bhsm,bmhd->bshd', p, v).reshape(b, s, CROSS_HEADS * CROSS_HEAD_DIM)
    return o @ w_co


def peer_ffn(h, w_pq, sub_keys, peer_u, peer_v):
    b, s, d = h.shape
    hf = h.reshape(b * s // PEER_CHUNK, PEER_CHUNK, d)

    def chunk(hc):
        q = (hc @ w_pq).reshape(PEER_CHUNK, PEER_HEADS, 2, PEER_DK_HALF)
        sc = jnp.einsum('thcd,chnd->thcn', q, sub_keys).astype(jnp.float32)
        vals, idx = lax.top_k(sc, PEER_TOPK)
        cand = vals[:, :, 0, :, None] + vals[:, :, 1, None, :]
        cand_idx = idx[:, :, 0, :, None] * N_KEYS + idx[:, :, 1, None, :]
        top_s, top_c = lax.top_k(cand.reshape(PEER_CHUNK, PEER_HEADS, PEER_TOPK * PEER_TOPK), PEER_TOPK)
        experts = jnp.take_along_axis(
            cand_idx.reshape(PEER_CHUNK, PEER_HEADS, PEER_TOPK * PEER_TOPK), top_c, axis=-1)
        g = jax.nn.softmax(top_s, axis=-1)
        ue = peer_u[experts]
        ve = peer_v[experts]
        act = jax.nn.gelu(jnp.einsum('td,thkd->thk', hc, ue).astype(jnp.float32), approximate=False)
        return jnp.einsum('thk,thkd->td', (g * act).astype(ve.dtype), ve)

    return lax.map(chunk, hf).reshape(b, s, d)


def setup_inputs(seed: int = 0) -> dict:
    key = jax.random.key(seed)
    ks = jax.random.split(key, 32)
    f32 = jnp.float32
    D = D_MODEL

    def nrm(k, shape, scale):
        return jax.random.normal(k, shape, f32) * scale

    def gain(k, shape):
        return 1.0 + 0.02 * jax.random.normal(k, shape, f32)

    return {
        "x": nrm(ks[0], (BATCH, SEQ, D), 1.0),
        "mem": nrm(ks[1], (BATCH, MEM_LEN, D), 1.0),
        "norm_mix_g": gain(ks[2], (DEPTH, D)),
        "w_in": nrm(ks[3], (DEPTH, D, IN_COLS), D ** -0.5),
        "conv_w": nrm(ks[4], (DEPTH, CONV_WIDTH, CONV_DIM), CONV_WIDTH ** -0.5),
        "w_conv_out": nrm(ks[5], (DEPTH, CONV_DIM, D), CONV_DIM ** -0.5),
        "lambda_q1": nrm(ks[6], (DEPTH, DIFF_HEAD_DIM), 0.1),
        "lambda_k1": nrm(ks[7], (DEPTH, DIFF_HEAD_DIM), 0.1),
        "lambda_q2": nrm(ks[8], (DEPTH, DIFF_HEAD_DIM), 0.1),
        "lambda_k2": nrm(ks[9], (DEPTH, DIFF_HEAD_DIM), 0.1),
        "subln_g": gain(ks[10], (DEPTH, DIFF_V_DIM)),
        "w_attn_out": nrm(ks[11], (DEPTH, DIFF_HEADS * DIFF_V_DIM, D), (DIFF_HEADS * DIFF_V_DIM) ** -0.5),
        "w_mix_out": nrm(ks[12], (DEPTH, D, D), D ** -0.5),
        "rel_bias": nrm(ks[13], (N_BUCKETS, DIFF_HEADS), 0.5),
        "norm_cross_g": gain(ks[14], (DEPTH, D)),
        "norm_mem_g": gain(ks[15], (DEPTH, D)),
        "w_cq": nrm(ks[16], (DEPTH, D, CROSS_HEADS * CROSS_HEAD_DIM), D ** -0.5),
        "w_ckv": nrm(ks[17], (DEPTH, D, 2 * CROSS_HEADS * CROSS_HEAD_DIM), D ** -0.5),
        "w_co": nrm(ks[18], (DEPTH, CROSS_HEADS * CROSS_HEAD_DIM, D), (CROSS_HEADS * CROSS_HEAD_DIM) ** -0.5),
        "norm_ffn_g": gain(ks[19], (DEPTH, D)),
        "w_pq": nrm(ks[20], (DEPTH, D, PEER_HEADS * PEER_DK), D ** -0.5),
        "sub_keys": nrm(ks[21], (DEPTH, 2, PEER_HEADS, N_KEYS, PEER_DK_HALF), PEER_DK_HALF ** -0.5),
        "peer_u": nrm(ks[22], (DEPTH, N_EXPERTS, D), D ** -0.5),
        "peer_v": nrm(ks[23], (DEPTH, N_EXPERTS, D), D ** -0.5),
        "final_g": gain(ks[24], (D,)),
    }


def reference(x, mem, norm_mix_g, w_in, conv_w, w_conv_out, lambda_q1, lambda_k1, lambda_q2, lambda_k2,
              subln_g, w_attn_out, w_mix_out, rel_bias, norm_cross_g, norm_mem_g, w_cq, w_ckv, w_co,
              norm_ffn_g, w_pq, sub_keys, peer_u, peer_v, final_g):
    b, s, _ = x.shape
    split_points = list(np.cumsum(IN_SPLITS)[:-1])
    for l in range(DEPTH):
        h = rmsnorm(x, norm_mix_g[l])
        proj = h @ w_in[l]
        cb, cc, cx, q, k, v, gc, ga = jnp.split(proj, split_points, axis=-1)
        y_conv = short_conv_mixer(cb, cc, cx, conv_w[l], w_conv_out[l])
        lam_init = 0.8 - 0.6 * math.exp(-0.3 * l)
        lam = (jnp.exp(jnp.sum(lambda_q1[l] * lambda_k1[l]).astype(jnp.float32))
               - jnp.exp(jnp.sum(lambda_q2[l] * lambda_k2[l]).astype(jnp.float32)) + lam_init)
        q = q.reshape(b, s, DIFF_HEADS, 2, DIFF_HEAD_DIM)
        k = k.reshape(b, s, DIFF_HEADS, 2, DIFF_HEAD_DIM)
        v = v.reshape(b, s, DIFF_HEADS, DIFF_V_DIM)
        y_attn = diff_attention(q, k, v, rel_bias, lam, lam_init, subln_g[l]) @ w_attn_out[l]
        merged = jax.nn.sigmoid(gc) * y_conv + jax.nn.sigmoid(ga) * y_attn
        x = x + merged @ w_mix_out[l]
        hc = rmsnorm(x, norm_cross_g[l])
        m = rmsnorm(mem, norm_mem_g[l])
        x = x + cross_attention(hc, m, w_cq[l], w_ckv[l], w_co[l])
        hf = rmsnorm(x, norm_ffn_g[l])
        x = x + peer_ffn(hf, w_pq[l], sub_keys[l], peer_u[l], peer_v[l])
    return rmsnorm(x, final_g)
```

```python
import contextlib
import math

import numpy as np
import ml_dtypes
import concourse.bass as bass
import concourse.mybir as mybir
from concourse.bass_utils import run_bass_kernel_spmd

F32 = mybir.dt.float32
BF16 = mybir.dt.bfloat16
U32 = mybir.dt.uint32
I32 = mybir.dt.int32
AF = mybir.ActivationFunctionType
ALU = mybir.AluOpType
AX = mybir.AxisListType

NCORE = 8
SEQ = 16384
D = 1024
NT_ALL = SEQ // 128
NT_OWN = 16
COMPUTE = ("pe", "act", "dve", "pool")
ALL_ENG = ("pe", "act", "dve", "pool", "sp")
SEM_ROT = 30000


class Buf:
    __slots__ = ("name", "writers", "readers", "dsem", "dcount")

    def __init__(self, name):
        self.name = name
        self.writers = []
        self.readers = []
        self.dsem = None
        self.dcount = 0


class Ins:
    __slots__ = ("eng", "fn", "deps", "is_dma", "sem", "semval", "needs_inc")

    def __init__(self, eng, fn, deps, is_dma):
        self.eng = eng
        self.fn = fn
        self.deps = deps
        self.is_dma = is_dma
        self.sem = None
        self.semval = 0
        self.needs_inc = False


class Sched:
    def __init__(self, nc, same_engine_sync=True):
        self.nc = nc
        self.ins = []
        self.streams = {e: [] for e in ALL_ENG}
        self.same_engine_sync = same_engine_sync
        self.dma_bufs = []
        self.out_tokens = []
        self.last_eng = {}
        self.last_dma = {}
        self.base_deps = frozenset()

    def barrier(self):
        self.base_deps = frozenset(list(self.last_eng.values()) + list(self.last_dma.values()))

    def _deps(self, reads, writes):
        deps = set(self.base_deps)
        for b in reads:
            deps.update(b.writers)
        for b in writes:
            deps.update(b.writers)
            deps.update(b.readers)
        return deps

    def _compress(self, lst):
        last = {}
        out = []
        for i in lst:
            it = self.ins[i]
            if it.is_dma:
                last[("d", id(it.sem))] = i
            else:
                last[it.eng] = i
        return list(last.values())

    def _commit(self, me, reads, writes):
        for b in writes:
            if b.readers:
                b.writers = [me]
                b.readers = []
            else:
                b.writers.append(me)
                if len(b.writers) > 32:
                    b.writers = self._compress(b.writers)
        for b in reads:
            if b in writes:
                continue
            b.readers.append(me)
            if len(b.readers) > 32:
                b.readers = self._compress(b.readers)

    def op(self, eng, fn, reads=(), writes=()):
        deps = self._deps(reads, writes)
        me = len(self.ins)
        it = Ins(eng, fn, deps, False)
        self.ins.append(it)
        self.streams[eng].append(it)
        self._commit(me, reads, writes)
        self.last_eng[eng] = me
        return me

    def dma(self, queue, fn, sem_buf, reads=(), writes=(), is_output=False):
        deps = self._deps(reads, writes)
        me = len(self.ins)
        it = Ins(queue, fn, deps, True)
        if sem_buf.dsem is None:
            sem_buf.dsem = "pending"
            self.dma_bufs.append(sem_buf)
        sem_buf.dcount += 16
        it.sem = sem_buf
        it.semval = sem_buf.dcount
        self.ins.append(it)
        self.streams[queue].append(it)
        self._commit(me, reads, writes)
        self.last_dma[id(sem_buf)] = me
        if is_output:
            self.out_tokens.append(me)
        return me

    def emit(self, final_engine="sp"):
        nc = self.nc
        ses = self.same_engine_sync
        for it in self.ins:
            for d in it.deps:
                dd = self.ins[d]
                if dd.is_dma:
                    continue
                if dd.eng == it.eng and not it.is_dma and (dd.eng == "pe" or not ses):
                    continue
                dd.needs_inc = True
        n_sems = {}
        for e in COMPUTE:
            c = 0
            for it in self.streams[e]:
                if it.is_dma:
                    continue
                if it.needs_inc:
                    c += 1
                    it.semval = c
            n_sems[e] = max(c - 1, 0) // SEM_ROT + 1
        with contextlib.ExitStack() as st:
            eng_sems = {e: [st.enter_context(nc.semaphore(f"s_{e}{k}")) for k in range(n_sems[e])] for e in COMPUTE}
            for i, b in enumerate(self.dma_bufs):
                b.dsem = st.enter_context(nc.semaphore(f"d{i}_{b.name}"))

            def token(it):
                if it.is_dma:
                    return (it.sem.dsem, it.semval, ("d", id(it.sem)))
                k = (it.semval - 1) // SEM_ROT
                return (eng_sems[it.eng][k], it.semval - k * SEM_ROT, (it.eng, k))

            def run(e, eng):
                known = {}
                for it in self.streams[e]:
                    need = {}
                    for d in it.deps:
                        dd = self.ins[d]
                        if not dd.is_dma and dd.eng == e and not it.is_dma and (e == "pe" or not ses):
                            continue
                        sem, val, key = token(dd)
                        if known.get(key, 0) >= val:
                            continue
                        if key not in need or need[key][1] < val:
                            need[key] = (sem, val)
                    for key, (sem, val) in need.items():
                        eng.wait_ge(sem, val)
                        known[key] = val
                    h = it.fn(eng)
                    if it.is_dma:
                        h.then_inc(it.sem.dsem, 16)
                    elif it.needs_inc:
                        k = (it.semval - 1) // SEM_ROT
                        h.then_inc(eng_sems[it.eng][k], 1)
                if e == final_engine:
                    need = {}
                    for d in self.out_tokens:
                        sem, val, key = token(self.ins[d])
                        if key not in need or need[key][1] < val:
                            need[key] = (sem, val)
                    for key, (sem, val) in need.items():
                        eng.wait_ge(sem, val)

            with nc.Block() as block:
                @block.sync
                def _(eng):
                    run("sp", eng)

                @block.tensor
                def _(eng):
                    run("pe", eng)

                @block.scalar
                def _(eng):
                    run("act", eng)

                @block.vector
                def _(eng):
                    run("dve", eng)

                @block.gpsimd
                def _(eng):
                    run("pool", eng)


class TT:
    __slots__ = ("t", "b")

    def __init__(self, t, b):
        self.t = t
        self.b = b


def _t5_bucket(n):
    n = np.maximum(n, 0)
    nf = np.maximum(n, 1).astype(np.float32)
    large = 16 + (np.log(nf / np.float32(16)) / np.float32(math.log(8)) * np.float32(16)).astype(np.int32)
    large = np.minimum(large, 31)
    return np.where(n < 16, n, large)


def _band_tables(c):
    ki = np.arange(128)[:, None]
    qi = np.arange(128)[None, :]
    bk = np.zeros((9, 128, 128), np.int64)
    mk = np.zeros((9, 128, 128), np.float32)
    for bi, b in enumerate(range(-1, 8)):
        rel = 128 * (c - b) + qi - ki
        bk[bi] = _t5_bucket(rel)
        mk[bi] = np.where(rel >= 0, 0.0, -30000.0)
    return bk, mk


class Builder:
    def __init__(self, stop_after=None, upto=None):
        self.stop_after = stop_after
        self.upto = upto
        self.nc = bass.Bass("TRN2", target_bir_lowering=False)
        self.S = Sched(self.nc)
        self.n = 0

    def din(self, name, shape, dt=F32):
        if self.upto and self.upto.startswith("peeronly") and name in ("x_all", "w_in", "w_conv_out", "w_attn_out", "w_mix_out", "w_cq", "w_ckv", "w_co", "mem"):
            shape = [1, 1]
        if not hasattr(self, "in_shapes"):
            self.in_shapes = {}
        self.in_shapes[name] = (tuple(shape), dt)
        return self.nc.dram_tensor(name, list(shape), dt, kind="ExternalInput").ap()

    def dout(self, name, shape, dt=F32):
        return self.nc.dram_tensor(name, list(shape), dt, kind="ExternalOutput").ap()

    def dscr(self, name, shape, dt):
        return self.nc.dram_tensor(name, list(shape), dt, kind="Internal").ap()

    def _arena_init(self):
        rem = self.nc.sbuf_bytes_remaining
        rem = rem() if callable(rem) else rem
        size = (int(rem) - 256) // 64 * 64
        beg, end = self.nc.bump_sbuf(size)
        self.a_beg = (int(beg) + 63) // 64 * 64
        self.a_end = int(end)
        self.a_ptr = self.a_beg
        self.a_marked = set()

    def _reset_ptr(self, mark, key):
        self.a_ptr = mark
        self.a_marked.discard(key)

    def sb(self, st, name, shape, dt, at=None):
        if id(st) not in self.a_marked:
            self.a_marked.add(id(st))
            st.callback(self._reset_ptr, self.a_ptr, id(st))
        self.n += 1
        nm = f"{name}_{self.n}"
        nbytes = int(np.prod(shape[1:])) * mybir.dt.size(dt)
        nbytes = (nbytes + 63) // 64 * 64
        if at is None:
            off = self.a_ptr
            self.a_ptr += nbytes
        else:
            off = at
        assert off + nbytes <= self.a_end, (name, off, nbytes, self.a_end)
        self.a_peak = max(getattr(self, "a_peak", 0), off + nbytes - self.a_beg)
        t = self.nc.alloc_sbuf_tensor_at(nm, list(shape), dt, offset=off)
        return TT(t, Buf(nm))

    def ps(self, st, name, shape, dt):
        self.n += 1
        nm = f"{name}_{self.n}"
        return TT(st.enter_context(self.nc.psum_tensor(nm, list(shape), dt)), Buf(nm))

    def build(self):
        nc, S = self.nc, self.S
        din, dout = self.din, self.dout
        I = {}
        I["x_all"] = din("x_all", [SEQ, D])
        I["x_own"] = din("x_own", [NT_OWN * 128, D])
        I["x_halo"] = din("x_halo", [32, D])
        I["mem"] = din("mem", [256, D])
        I["w_in"] = din("w_in", [D, 8192])
        I["gcols"] = din("gcols", [128, 4, 8])
        I["final_g_rep"] = din("final_g_rep", [128, D])
        I["conv_wT"] = din("conv_wT", [128, 8, 3])
        I["w_conv_out"] = din("w_conv_out", [D, D])
        I["lam_rep"] = din("lam_rep", [128, 4, 64])
        I["subln_rep"] = din("subln_rep", [128, 128])
        I["w_attn_out"] = din("w_attn_out", [D, D])
        I["w_mix_out"] = din("w_mix_out", [D, D])
        I["rb31_rep"] = din("rb31_rep", [128, 8])
        I["gbias"] = din("gbias", [8, 128, 9, 128])
        I["mband"] = din("mband", [128, 9, 128])
        I["w_cq"] = din("w_cq", [D, D])
        I["w_ckv"] = din("w_ckv", [D, 2048])
        I["w_co"] = din("w_co", [D, D])
        I["w_pq"] = din("w_pq", [D, 2048])
        I["skT"] = din("skT", [16, 128, 128])
        I["peer_u"] = din("peer_u", [SEQ, D])
        I["peer_v"] = din("peer_v", [SEQ, D])
        I["ident_bf"] = din("ident_bf", [128, 128], BF16)
        I["ident_f"] = din("ident_f", [128, 128])
        I["iota_f"] = din("iota_f", [128, 128])
        self.I = I
        self.out = dout("out", [NT_OWN * 128, D])
        self.dumps = set(self.stop_after.split(",")) if self.stop_after else set()
        self.dbg = {}
        if "attn" in self.dumps:
            self.dbg["attn"] = dout("dbg_attn", [128, 8 * NT_OWN * 128])
        for nm in ("x1", "x2", "x3"):
            if nm in self.dumps:
                self.dbg[nm] = dout("dbg_" + nm, [NT_OWN * 128, D])
        self.UT_d = self.dscr("UT_d", [128, 128, 8, 128], BF16)
        self.Vb_d = self.dscr("Vb_d", [SEQ, D], BF16)
        self.B_UT = Buf("UT_d")
        self.B_Vb = Buf("Vb_d")
        self.KT_d = self.dscr("KT_d", [8, 128, SEQ], BF16)
        self.V_d = self.dscr("V_d", [8, 128, NT_ALL, 128], BF16)
        self.B_KT = Buf("KT_d")
        self.B_V = Buf("V_d")

        with contextlib.ExitStack() as gst:
            self.gst = gst
            self._arena_init()
            self.bank = [self.ps(gst, f"bank{i}", [128, 512], F32) for i in range(8)]
            self.setup_consts()
            if self.upto and self.upto.startswith("peeronly"):
                with contextlib.ExitStack() as rst:
                    p0 = self.a_ptr
                    self.xres = self.sb(rst, "xres", [128, NT_OWN, D], F32, at=p0 + 32 * 1024)
                    S.dma("sp", lambda e: e.dma_start(out=self.xres.t[:], in_=I["x_own"].rearrange("(m p) d -> p m d", p=128)), self.xres.b, writes=[self.xres.b])
                    self.peer_stop = self.upto.split(":")[1] if ":" in self.upto else None
                    self.phase_peer(p0)
                S.emit()
                return nc
            self.peer_stop = None
            self.phase_kv()
            S.barrier()
            self.phase_q_attn()
            S.barrier()
            if "attn" in self.dumps:
                self.dump_attn()
            with contextlib.ExitStack() as rst:
                p0 = self.a_ptr
                self.xres = self.sb(rst, "xres", [128, NT_OWN, D], F32, at=p0 + 32 * 1024)
                self.phase_mix(p0)
                S.barrier()
                if "x1" in self.dumps:
                    self.dump_res("x1")
                if self.upto != "mix":
                    self.phase_cross(p0)
                    S.barrier()
                    if "x2" in self.dumps:
                        self.dump_res("x2")
                    if self.upto != "cross":
                        self.phase_peer(p0)
            S.emit()
        return nc

    def setup_consts(self):
        S, gst, I = self.S, self.gst, self.I
        sb = lambda name, shape, dt: self.sb(gst, name, shape, dt)
        self.ident_bf = sb("ident_bf", [128, 128], BF16)
        self.ident_f = sb("ident_f", [128, 128], F32)
        self.iota_f = sb("iota_f", [128, 128], F32)
        self.gcols = sb("gcols", [128, 4, 8], F32)
        self.lam = sb("lam", [128, 4], F32)
        self.subg = sb("subg", [128, 128], F32)
        self.rb31 = sb("rb31", [128, 8], F32)
        self.mhalf = sb("mhalf", [128, 4], F32)
        self.pA = self.a_ptr
        self.bias = sb("bias", [128, 8, 9, 128], BF16)
        self.outT = sb("outT", [128, 8, NT_OWN * 128], BF16)
        S.op("pool", lambda e: e.memset(self.mhalf.t[:], -0.5), writes=[self.mhalf.b])
        cset = Buf("consts")
        for tt, src in ((self.ident_bf, I["ident_bf"]), (self.ident_f, I["ident_f"]), (self.iota_f, I["iota_f"]),
                        (self.gcols, I["gcols"]), (self.subg, I["subln_rep"]), (self.rb31, I["rb31_rep"])):
            S.dma("sp", lambda e, tt=tt, src=src: e.dma_start(out=tt.t[:], in_=src), cset, writes=[tt.b])
        with contextlib.ExitStack() as st:
            lamin = self.sb(st, "lamin", [128, 4, 64], F32)
            prod = self.sb(st, "lamprod", [128, 2, 64], F32)
            red = self.sb(st, "lamred", [128, 2], F32)
            ex = self.sb(st, "lamex", [128, 2], F32)
            mb = self.sb(st, "mband", [128, 9, 128], F32)
            gb = [self.sb(st, f"gb{i}", [128, 9, 128], F32) for i in range(2)]
            S.dma("sp", lambda e: e.dma_start(out=lamin.t[:], in_=I["lam_rep"]), cset, writes=[lamin.b])
            S.dma("sp", lambda e: e.dma_start(out=mb.t[:], in_=I["mband"]), cset, writes=[mb.b])
            S.op("dve", lambda e: e.tensor_tensor(out=prod.t[:, 0, :], in0=lamin.t[:, 0, :], in1=lamin.t[:, 1, :], op=ALU.mult), reads=[lamin.b], writes=[prod.b])
            S.op("dve", lambda e: e.tensor_tensor(out=prod.t[:, 1, :], in0=lamin.t[:, 2, :], in1=lamin.t[:, 3, :], op=ALU.mult), reads=[lamin.b], writes=[prod.b])
            S.op("dve", lambda e: e.tensor_reduce(out=red.t[:], in_=prod.t[:], axis=AX.X, op=ALU.add), reads=[prod.b], writes=[red.b])
            S.op("act", lambda e: e.activation(out=ex.t[:], in_=red.t[:], func=AF.Exp), reads=[red.b], writes=[ex.b])
            S.op("dve", lambda e: e.tensor_tensor(out=self.lam.t[:, 0:1], in0=ex.t[:, 0:1], in1=ex.t[:, 1:2], op=ALU.subtract), reads=[ex.b], writes=[self.lam.b])
            S.op("dve", lambda e: e.tensor_scalar(out=self.lam.t[:, 0:1], in0=self.lam.t[:, 0:1], scalar1=0.2, scalar2=None, op0=ALU.add), reads=[self.lam.b], writes=[self.lam.b])
            S.op("dve", lambda e: e.tensor_scalar(out=self.lam.t[:, 1:2], in0=self.lam.t[:, 0:1], scalar1=-1.0, scalar2=None, op0=ALU.mult), reads=[self.lam.b], writes=[self.lam.b])
            S.op("dve", lambda e: e.tensor_scalar(out=self.subg.t[:], in0=self.subg.t[:], scalar1=0.8, scalar2=None, op0=ALU.mult), reads=[self.subg.b], writes=[self.subg.b])
            for h in range(8):
                g = gb[h % 2]
                S.dma("sp", lambda e, g=g, h=h: e.dma_start(out=g.t[:], in_=I["gbias"][h]), g.b, writes=[g.b])
                S.op("dve", lambda e, g=g, h=h: e.scalar_tensor_tensor(out=self.bias.t[:, h, :, :], in0=g.t[:], scalar=self.rb31.t[:, h:h + 1], in1=mb.t[:], op0=ALU.subtract, op1=ALU.add),
                     reads=[g.b, self.rb31.b, mb.b], writes=[self.bias.b])
            S.barrier()

    def load_weight(self, st_tiles, dst, col0, ncols, w_ap, gsel, kcs=range(8), queue="sp"):
        S = self.S
        for kc in kcs:
            stg = st_tiles[kc % len(st_tiles)]
            S.dma(queue, lambda e, stg=stg, kc=kc: e.dma_start(out=stg.t[:, 0:ncols], in_=w_ap[kc * 128:(kc + 1) * 128, col0:col0 + ncols]), stg.b, writes=[stg.b])
            if gsel is None:
                S.op("pool", lambda e, stg=stg, kc=kc: e.tensor_copy(out=dst.t[:, kc, 0:ncols], in_=stg.t[:, 0:ncols]), reads=[stg.b], writes=[dst.b])
            else:
                S.op("pool", lambda e, stg=stg, kc=kc: e.tensor_scalar(out=dst.t[:, kc, 0:ncols], in0=stg.t[:, 0:ncols], scalar1=self.gcols.t[:, gsel, kc:kc + 1], scalar2=None, op0=ALU.mult),
                     reads=[stg.b, self.gcols.b], writes=[dst.b])

    def norm_tiles(self, x_ap_fn, tiles, xt, sqj, ss, rstd, xn, eps=1e-6):
        S = self.S
        n = len(tiles)
        for i, (tid, xts, xns) in enumerate(tiles):
            S.dma("sp", lambda e, xts=xts, tid=tid: e.dma_start(out=xts.t[:], in_=x_ap_fn(tid)), xts.b, writes=[xts.b])
            S.op("act", lambda e, xts=xts, i=i: e.activation(out=sqj.t[:], in_=xts.t[:], func=AF.Square, accum_out=ss.t[:, i:i + 1]), reads=[xts.b], writes=[sqj.b, ss.b])
        S.op("dve", lambda e: e.tensor_scalar(out=rstd.t[:, 0:n], in0=ss.t[:, 0:n], scalar1=1.0 / D, scalar2=eps, op0=ALU.mult, op1=ALU.add), reads=[ss.b], writes=[rstd.b])
        S.op("pool", lambda e: e.tensor_tensor(out=rstd.t[:, 0:n], in0=rstd.t[:, 0:n], in1=self.mhalf.t[:, 0:n], op=ALU.pow), reads=[rstd.b, self.mhalf.b], writes=[rstd.b])
        for i, (tid, xts, xns) in enumerate(tiles):
            S.op("dve", lambda e, xts=xts, xns=xns, i=i: e.tensor_scalar(out=xns.t[:], in0=xts.t[:], scalar1=rstd.t[:, i:i + 1], scalar2=None, op0=ALU.mult), reads=[xts.b, rstd.b], writes=[xns.b])

    def transpose_tiles(self, tiles_xn, hT, tbanks, cnt0, gsel=None, col0=0, npart=128):
        S = self.S
        if gsel is not None or npart != 128:
            for i, xns in enumerate(tiles_xn):
                bk = tbanks[(cnt0 + i) % len(tbanks)]
                pt = bk.t.bitcast(BF16)
                for kc in range(8):
                    S.op("pe", lambda e, pt=pt, xns=xns, kc=kc: e.transpose(out=pt[:, kc * 128:kc * 128 + npart], in_=xns.t[0:npart, kc * 128:(kc + 1) * 128], identity=self.ident_bf.t[0:npart, 0:npart]),
                         reads=[xns.b, self.ident_bf.b], writes=[bk.b])
                for kc in range(8):
                    c0 = col0 + i * npart
                    if gsel is None:
                        S.op("act", lambda e, pt=pt, kc=kc, c0=c0: e.copy(out=hT.t[:, kc, c0:c0 + npart], in_=pt[:, kc * 128:kc * 128 + npart]), reads=[bk.b], writes=[hT.b])
                    else:
                        S.op("act", lambda e, pt=pt, kc=kc, c0=c0: e.activation(out=hT.t[:, kc, c0:c0 + npart], in_=pt[:, kc * 128:kc * 128 + npart], func=AF.Copy, scale=self.gcols.t[:, gsel, kc:kc + 1]),
                             reads=[bk.b, self.gcols.b], writes=[hT.b])
            return
        for i, xns in enumerate(tiles_xn):
            bk = tbanks[(cnt0 + i) % len(tbanks)]
            pt = bk.t.bitcast(BF16)
            for kc in range(8):
                S.op("pe", lambda e, pt=pt, xns=xns, kc=kc: e.transpose(out=pt[:, kc * 128:(kc + 1) * 128], in_=xns.t[:, kc * 128:(kc + 1) * 128], identity=self.ident_bf.t[:]),
                     reads=[xns.b, self.ident_bf.b], writes=[bk.b])
            S.op("act", lambda e, pt=pt, i=i: e.copy(out=hT.t[:, :, i * 128:(i + 1) * 128], in_=pt[:, :].rearrange("p (k t) -> p k t", k=8)), reads=[bk.b], writes=[hT.b])

    def phase_kv(self):
        S, I = self.S, self.I
        with contextlib.ExitStack() as st:
            sb = lambda name, shape, dt: self.sb(st, name, shape, dt)
            wk = sb("wk", [128, 8, 1024], BF16)
            wv = sb("wv", [128, 8, 1024], BF16)
            wst = [sb(f"wst{i}", [128, 1024], F32) for i in range(2)]
            self.load_weight(wst, wk, 4096, 1024, I["w_in"], 0)
            self.load_weight(wst, wv, 5120, 1024, I["w_in"], 0)
            xt = [sb(f"xt{i}", [128, 1024], F32) for i in range(8)]
            xn = [sb(f"xn{i}", [128, 1024], BF16) for i in range(12)]
            sqj = sb("sqj", [128, 1024], BF16)
            ss = [sb(f"ss{i}", [128, 4], F32) for i in range(3)]
            rstd = [sb(f"rstd{i}", [128, 4], F32) for i in range(3)]
            hT = [sb(f"hT{i}", [128, 8, 512], BF16) for i in range(2)]
            kst = [sb(f"kst{i}", [128, 8, 512], BF16) for i in range(2)]
            vst = [sb(f"vst{i}", [128, 4, 1024], BF16) for i in range(2)]
            tb = self.bank[0:2]
            mb = self.bank[2:8]
            NB = NT_ALL // 4
            x_all = I["x_all"]

            def A(bi):
                tiles = [(bi * 4 + i, xt[(bi * 4 + i) % 8], xn[(bi * 4 + i) % 12]) for i in range(4)]
                self.norm_tiles(lambda tid: x_all[tid * 128:(tid + 1) * 128, :], tiles, xt, sqj, ss[bi % 3], rstd[bi % 3], xn)

            def Bs(bi):
                self.transpose_tiles([xn[(bi * 4 + i) % 12] for i in range(4)], hT[bi % 2], tb, bi * 4)

            cnt = [0]

            def C(bi):
                h = hT[bi % 2]
                ks, vs = kst[bi % 2], vst[bi % 2]
                for j in range(8):
                    bk = mb[cnt[0] % 6]
                    cnt[0] += 1
                    for kc in range(8):
                        S.op("pe", lambda e, bk=bk, kc=kc, j=j: e.matmul(bk.t[:], lhsT=wk.t[:, kc, j * 128:(j + 1) * 128], rhs=h.t[:, kc, :], start=(kc == 0), stop=(kc == 7)),
                             reads=[wk.b, h.b], writes=[bk.b])
                    eng = "act" if j % 2 == 0 else "dve"
                    if eng == "act":
                        S.op("act", lambda e, bk=bk, j=j: e.copy(out=ks.t[:, j, :], in_=bk.t[:]), reads=[bk.b], writes=[ks.b])
                    else:
                        S.op("dve", lambda e, bk=bk, j=j: e.tensor_copy(out=ks.t[:, j, :], in_=bk.t[:]), reads=[bk.b], writes=[ks.b])
                S.dma("sp", lambda e: e.dma_start(out=self.KT_d[:, :, bi * 512:(bi + 1) * 512].rearrange("h p t -> p h t"), in_=ks.t[:]), ks.b, reads=[ks.b], writes=[self.B_KT])
                for i in range(4):
                    for hh in range(2):
                        bk = mb[cnt[0] % 6]
                        cnt[0] += 1
                        for kc in range(8):
                            S.op("pe", lambda e, bk=bk, kc=kc, i=i, hh=hh: e.matmul(bk.t[:], lhsT=h.t[:, kc, i * 128:(i + 1) * 128], rhs=wv.t[:, kc, hh * 512:(hh + 1) * 512], start=(kc == 0), stop=(kc == 7)),
                                 reads=[wv.b, h.b], writes=[bk.b])
                        if hh == 0:
                            S.op("act", lambda e, bk=bk, i=i, hh=hh: e.copy(out=vs.t[:, i, hh * 512:(hh + 1) * 512], in_=bk.t[:]), reads=[bk.b], writes=[vs.b])
                        else:
                            S.op("dve", lambda e, bk=bk, i=i, hh=hh: e.tensor_copy(out=vs.t[:, i, hh * 512:(hh + 1) * 512], in_=bk.t[:]), reads=[bk.b], writes=[vs.b])
                    S.dma("act", lambda e, i=i: e.dma_start(out=self.V_d[:, :, bi * 4 + i, :].rearrange("h p e -> p h e"), in_=vs.t[:, i, :].rearrange("p (h e) -> p h e", h=8)),
                          vs.b, reads=[vs.b], writes=[self.B_V])

            A(0)
            A(1)
            Bs(0)
            for bi in range(NB):
                if bi + 2 < NB:
                    A(bi + 2)
                if bi + 1 < NB:
                    Bs(bi + 1)
                C(bi)

    def phase_q_attn(self):
        S, I = self.S, self.I
        with contextlib.ExitStack() as st:
            sb = lambda name, shape, dt: self.sb(st, name, shape, dt)
            QT = sb("QT", [128, 8, NT_OWN * 128], BF16)
            with contextlib.ExitStack() as st2:
                sb2 = lambda name, shape, dt: self.sb(st2, name, shape, dt)
                wq = sb2("wq", [128, 8, 1024], BF16)
                wst = [sb2(f"wst{i}", [128, 1024], F32) for i in range(2)]
                self.load_weight(wst, wq, 3072, 1024, I["w_in"], 0)
                xt = [sb2(f"xt{i}", [128, 1024], F32) for i in range(4)]
                xn = [sb2(f"xn{i}", [128, 1024], BF16) for i in range(4)]
                sqj = sb2("sqj", [128, 1024], BF16)
                ss = sb2("ss", [128, 4], F32)
                rstd = sb2("rstd", [128, 4], F32)
                hT = sb2("hT", [128, 8, 512], BF16)
                x_own = I["x_own"]
                cnt = 0
                for bi in range(4):
                    tiles = [(bi * 4 + i, xt[i], xn[i]) for i in range(4)]
                    self.norm_tiles(lambda tid: x_own[tid * 128:(tid + 1) * 128, :], tiles, xt, sqj, ss, rstd, xn)
                    self.transpose_tiles(xn, hT, self.bank[0:2], bi * 4)
                    for j in range(8):
                        bk = self.bank[2 + cnt % 6]
                        cnt += 1
                        for kc in range(8):
                            S.op("pe", lambda e, bk=bk, kc=kc, j=j: e.matmul(bk.t[:], lhsT=wq.t[:, kc, j * 128:(j + 1) * 128], rhs=hT.t[:, kc, :], start=(kc == 0), stop=(kc == 7)),
                                 reads=[wq.b, hT.b], writes=[bk.b])
                        S.op("act", lambda e, bk=bk, j=j, bi=bi: e.activation(out=QT.t[:, j, bi * 512:(bi + 1) * 512], in_=bk.t[:], func=AF.Copy, scale=0.125), reads=[bk.b], writes=[QT.b])
                S.barrier()
            KT = [sb(f"KT{i}", [128, 4096], BF16) for i in range(4)]
            V1 = [sb(f"V1{i}", [128, 32, 130], BF16) for i in range(4)]
            E = [sb(f"E{i}", [128, 512], BF16) for i in range(3)]
            tmp0 = [sb(f"tmp0{i}", [128, 128], F32) for i in range(4)]
            oc = sb("oc", [128, 128], F32)
            on = sb("on", [128, 128], BF16)
            sqj = sb("sqj2", [128, 128], F32)
            sm = sb("sm", [128, 8], F32)
            for v in V1:
                S.op("pool", lambda e, v=v: e.memset(v.t[:, :, 128:130], 1.0), writes=[v.b])
            sbanks = self.bank[0:3]
            obanks = self.bank[3:7]
            tbank = self.bank[7]
            oacc = [(obanks[j], 0, obanks[j].b) for j in range(4)]
            scnt = 0
            for h in range(8):
                for q in range(4):
                    S.dma("sp", lambda e, q=q, h=h: e.dma_start(out=KT[q].t[:], in_=self.KT_d[h, :, q * 4096:(q + 1) * 4096]), KT[q].b, reads=[self.B_KT], writes=[KT[q].b])
                    S.dma("act", lambda e, q=q, h=h: e.dma_start(out=V1[q].t[:, :, 0:128], in_=self.V_d[h, :, q * 32:(q + 1) * 32, :]), V1[q].b, reads=[self.B_V], writes=[V1[q].b])
                for g in (3, 2, 1, 0):
                    for c in range(2):
                        nk = 32 * g + 32
                        for kt in range(nk):
                            jmin = max(0, -((-(kt - 32 * g - 7)) // 8))
                            band = {}
                            for j in range(jmin, 4):
                                b = kt - (32 * g + 8 * j)
                                if -1 <= b <= 7:
                                    band[j] = b
                            sbk = sbanks[scnt % 3]
                            Et = E[scnt % 3]
                            scnt += 1
                            kq, kl = kt // 32, kt % 32
                            lhs = KT[kq].t[c * 64:(c + 1) * 64, kl * 128:(kl + 1) * 128]
                            plain = [j for j in range(jmin, 4) if j not in band]
                            for j, b in band.items():
                                col = (j - jmin) * 128
                                S.op("pe", lambda e, sbk=sbk, col=col, b=b, h=h: e.matmul(sbk.t[:, col:col + 128], lhsT=self.ident_bf.t[:], rhs=self.bias.t[:, h, b + 1, :], start=True, stop=False),
                                     reads=[self.ident_bf.b, self.bias.b], writes=[sbk.b])
                                S.op("pe", lambda e, sbk=sbk, col=col, lhs=lhs, j=j, g=g, h=h, c=c: e.matmul(sbk.t[:, col:col + 128], lhsT=lhs, rhs=QT.t[c * 64:(c + 1) * 64, h, (4 * g + j) * 128:(4 * g + j + 1) * 128], start=False, stop=True),
                                     reads=[KT[kq].b, QT.b], writes=[sbk.b])
                            if plain:
                                j0 = plain[0]
                                assert plain == list(range(j0, 4))
                                col = (j0 - jmin) * 128
                                ncol = (4 - j0) * 128
                                S.op("pe", lambda e, sbk=sbk, col=col, ncol=ncol, lhs=lhs, j0=j0, g=g, h=h, c=c: e.matmul(sbk.t[:, col:col + ncol], lhsT=lhs, rhs=QT.t[c * 64:(c + 1) * 64, h, (4 * g + j0) * 128:(4 * g + 4) * 128], start=True, stop=True),
                                     reads=[KT[kq].b, QT.b], writes=[sbk.b])
                            nact = (4 - jmin) * 128
                            S.op("act", lambda e, sbk=sbk, Et=Et, nact=nact: e.activation(out=Et.t[:, 0:nact], in_=sbk.t[:, 0:nact], func=AF.Exp), reads=[sbk.b], writes=[Et.b])
                            for j in range(jmin, 4):
                                ob, oo, obuf = oacc[j]
                                col = (j - jmin) * 128
                                last = (kt == 32 * g + 8 * j + 7)
                                S.op("pe", lambda e, ob=ob, oo=oo, Et=Et, col=col, kq=kq, kl=kl, kt=kt, last=last, j=j: e.matmul(ob.t[:, oo:oo + 130], lhsT=Et.t[:, col:col + 128], rhs=V1[kq].t[:, kl, :], start=(kt == 0), stop=last),
                                     reads=[Et.b, V1[kq].b], writes=[obuf])
                        for j in range(4):
                            ob, oo, obuf = oacc[j]
                            m = 4 * g + j
                            if c == 0:
                                S.op("dve", lambda e, ob=ob, oo=oo, j=j: e.reciprocal(out=sm.t[:, j:j + 1], in_=ob.t[:, oo + 128:oo + 129]), reads=[obuf], writes=[sm.b])
                                S.op("dve", lambda e, ob=ob, oo=oo, j=j: e.tensor_scalar(out=tmp0[j].t[:], in0=ob.t[:, oo:oo + 128], scalar1=sm.t[:, j:j + 1], scalar2=None, op0=ALU.mult), reads=[obuf, sm.b], writes=[tmp0[j].b])
                            else:
                                S.op("dve", lambda e, ob=ob, oo=oo, j=j: e.reciprocal(out=sm.t[:, 4 + j:5 + j], in_=ob.t[:, oo + 128:oo + 129]), reads=[obuf], writes=[sm.b])
                                S.op("dve", lambda e, j=j: e.tensor_scalar(out=sm.t[:, 4 + j:5 + j], in0=sm.t[:, 4 + j:5 + j], scalar1=self.lam.t[:, 1:2], scalar2=None, op0=ALU.mult), reads=[sm.b, self.lam.b], writes=[sm.b])
                                S.op("dve", lambda e, ob=ob, oo=oo, j=j: e.scalar_tensor_tensor(out=oc.t[:], in0=ob.t[:, oo:oo + 128], scalar=sm.t[:, 4 + j:5 + j], in1=tmp0[j].t[:], op0=ALU.mult, op1=ALU.add),
                                     reads=[obuf, sm.b, tmp0[j].b], writes=[oc.b])
                                if self.stop_after == "attn_raw":
                                    pass
                                S.op("dve", lambda e: e.scalar_tensor_tensor(out=sqj.t[:], in0=oc.t[:], scalar=1.0, in1=oc.t[:], op0=ALU.mult, op1=ALU.mult, accum_out=sm.t[:, 0:1]), reads=[oc.b], writes=[sqj.b, sm.b])
                                S.op("dve", lambda e: e.tensor_scalar(out=sm.t[:, 0:1], in0=sm.t[:, 0:1], scalar1=1.0 / 128, scalar2=1e-5, op0=ALU.mult, op1=ALU.add), reads=[sm.b], writes=[sm.b])
                                S.op("pool", lambda e: e.tensor_tensor(out=sm.t[:, 0:1], in0=sm.t[:, 0:1], in1=self.mhalf.t[:, 0:1], op=ALU.pow), reads=[sm.b, self.mhalf.b], writes=[sm.b])
                                S.op("dve", lambda e: e.scalar_tensor_tensor(out=on.t[:], in0=oc.t[:], scalar=sm.t[:, 0:1], in1=self.subg.t[:], op0=ALU.mult, op1=ALU.mult),
                                     reads=[oc.b, sm.b, self.subg.b], writes=[on.b])
                                pt = tbank.t.bitcast(BF16)
                                S.op("pe", lambda e, pt=pt: e.transpose(out=pt[:, 0:128], in_=on.t[:], identity=self.ident_bf.t[:]), reads=[on.b, self.ident_bf.b], writes=[tbank.b])
                                S.op("act", lambda e, pt=pt, h=h, m=m: e.copy(out=self.outT.t[:, h, m * 128:(m + 1) * 128], in_=pt[:, 0:128]), reads=[tbank.b], writes=[self.outT.b])

    def dump_attn(self):
        S = self.S
        with contextlib.ExitStack() as st:
            f = self.sb(st, "dumpf", [128, 8, NT_OWN * 128], F32)
            S.op("dve", lambda e: e.tensor_copy(out=f.t[:], in_=self.outT.t[:]), reads=[self.outT.b], writes=[f.b])
            S.dma("sp", lambda e: e.dma_start(out=self.dbg["attn"], in_=f.t[:].rearrange("p h t -> p (h t)")), f.b, reads=[f.b], is_output=True)


def make_in_maps(inp):
    f32 = np.float32
    x = np.ascontiguousarray(inp["x"][0], dtype=f32)
    xt = x.reshape(16, 8, 128, D)
    common = {
        "x_all": x,
        "mem": np.ascontiguousarray(inp["mem"][0], dtype=f32),
        "w_in": np.ascontiguousarray(inp["w_in"][0], dtype=f32),
        "gcols": np.ascontiguousarray(np.stack([inp["norm_mix_g"][0], inp["norm_cross_g"][0], inp["norm_mem_g"][0], inp["norm_ffn_g"][0]], 0).reshape(4, 8, 128).transpose(2, 0, 1), dtype=f32),
        "final_g_rep": np.ascontiguousarray(np.broadcast_to(inp["final_g"][None, :], (128, D)), dtype=f32),
        "conv_wT": np.ascontiguousarray(inp["conv_w"][0].reshape(3, 8, 128).transpose(2, 1, 0), dtype=f32),
        "w_conv_out": np.ascontiguousarray(inp["w_conv_out"][0], dtype=f32),
        "lam_rep": np.ascontiguousarray(np.broadcast_to(np.stack([inp["lambda_q1"][0], inp["lambda_k1"][0], inp["lambda_q2"][0], inp["lambda_k2"][0]], 0)[None], (128, 4, 64)), dtype=f32),
        "subln_rep": np.ascontiguousarray(np.broadcast_to(inp["subln_g"][0][None, :], (128, 128)), dtype=f32),
        "w_attn_out": np.ascontiguousarray(inp["w_attn_out"][0], dtype=f32),
        "w_mix_out": np.ascontiguousarray(inp["w_mix_out"][0], dtype=f32),
        "rb31_rep": np.ascontiguousarray(np.broadcast_to(inp["rel_bias"][31][None, :], (128, 8)), dtype=f32),
        "w_cq": np.ascontiguousarray(inp["w_cq"][0], dtype=f32),
        "w_ckv": np.ascontiguousarray(inp["w_ckv"][0], dtype=f32),
        "w_co": np.ascontiguousarray(inp["w_co"][0], dtype=f32),
        "w_pq": np.ascontiguousarray(inp["w_pq"][0], dtype=f32),
        "skT": np.ascontiguousarray(inp["sub_keys"][0].transpose(1, 0, 3, 2).reshape(16, 128, 128), dtype=f32),
        "peer_u": np.ascontiguousarray(inp["peer_u"][0], dtype=f32),
        "peer_v": np.ascontiguousarray(inp["peer_v"][0], dtype=f32),
        "ident_bf": np.eye(128, dtype=f32).astype(ml_dtypes.bfloat16),
        "ident_f": np.eye(128, dtype=f32),
        "iota_f": np.ascontiguousarray(np.broadcast_to(np.arange(128, dtype=f32)[None, :], (128, 128))),
    }
    rel_bias = np.asarray(inp["rel_bias"], dtype=f32)
    maps = []
    for c in range(NCORE):
        m = dict(common)
        m["x_own"] = np.ascontiguousarray(xt[:, c].reshape(NT_OWN * 128, D))
        halo = np.zeros((16, 2, D), f32)
        for mm in range(16):
            t0 = (8 * mm + c) * 128
            if t0 >= 2:
                halo[mm] = x[t0 - 2:t0]
        m["x_halo"] = halo.reshape(32, D)
        bk, mk = _band_tables(c)
        gb = rel_bias[bk]
        m["gbias"] = np.ascontiguousarray(gb.transpose(3, 1, 0, 2), dtype=f32)
        m["mband"] = np.ascontiguousarray(mk.transpose(1, 0, 2), dtype=f32)
        maps.append(m)
    return maps


_CACHE = {}


def kernel(**inputs):
    stop = inputs.pop("_stop_after", None)
    upto = inputs.pop("_upto", None)
    key = (stop, upto)
    if key not in _CACHE:
        b = Builder(stop_after=stop, upto=upto)
        _CACHE[key] = (b.build(), b.in_shapes)
    nc, in_shapes = _CACHE[key]
    maps = make_in_maps(inputs)
    if upto and upto.startswith("peeronly"):
        m = maps[3]
        for nm, (shp, dt) in in_shapes.items():
            if tuple(m[nm].shape) != tuple(shp):
                m[nm] = np.zeros(shp, np.float32)
        res = run_bass_kernel_spmd(nc, [m], core_ids=[0])
        _CACHE["last_res"] = res
        return res
    res = run_bass_kernel_spmd(nc, maps, core_ids=list(range(NCORE)))
    if stop:
        _CACHE["last_res"] = res
    name = "out"
    out = np.zeros((16, 8, 128, D), np.float32)
    for c in range(NCORE):
        out[:, c] = np.asarray(res.results[c][name], dtype=np.float32).reshape(16, 128, D)
    return out.reshape(1, SEQ, D)


def _bcast_ap(t, offset, dims):
    base = t[:]
    return bass.AP(t, offset, [list(base.ap[0])] + [list(d) for d in dims])


def _norm_res(self, tiles, sqj, ss, rstd, eps=1e-6):
    S = self.S
    n = len(tiles)
    for i, (src, sbuf, xns) in enumerate(tiles):
        S.op("act", lambda e, src=src, i=i: e.activation(out=sqj.t[:], in_=src, func=AF.Square, accum_out=ss.t[:, i:i + 1]), reads=[sbuf], writes=[sqj.b, ss.b])
    S.op("dve", lambda e: e.tensor_scalar(out=rstd.t[:, 0:n], in0=ss.t[:, 0:n], scalar1=1.0 / D, scalar2=eps, op0=ALU.mult, op1=ALU.add), reads=[ss.b], writes=[rstd.b])
    S.op("pool", lambda e: e.tensor_tensor(out=rstd.t[:, 0:n], in0=rstd.t[:, 0:n], in1=self.mhalf.t[:, 0:n], op=ALU.pow), reads=[rstd.b, self.mhalf.b], writes=[rstd.b])
    for i, (src, sbuf, xns) in enumerate(tiles):
        S.op("dve", lambda e, src=src, xns=xns, i=i: e.tensor_scalar(out=xns.t[:], in0=src, scalar1=rstd.t[:, i:i + 1], scalar2=None, op0=ALU.mult), reads=[sbuf, rstd.b], writes=[xns.b])


def _wslab(self, dst, w_ap, col0, ncols, queue="pool"):
    self.S.dma(queue, lambda e: e.dma_start(out=dst.t[:, :, 0:ncols], in_=w_ap[:, col0:col0 + ncols].rearrange("(kc p) c -> p kc c", p=128)), dst.b, writes=[dst.b])


def _phase_mix(self, p0):
    S, I = self.S, self.I
    K = 1024
    with contextlib.ExitStack() as st:
        self.a_ptr = p0
        sb = lambda name, shape, dt, at=None: self.sb(st, name, shape, dt, at=at)
        mergedT = sb("mergedT", [128, 8, 2048], BF16)
        hT = sb("hT", [128, 8, 2048], BF16)
        hTh = sb("hTh", [128, 8, 32], BF16)
        zT = sb("zT", [128, 8, 2048], BF16)
        with contextlib.ExitStack() as st2:
            sb2 = lambda name, shape, dt: self.sb(st2, name, shape, dt)
            xt = [sb2(f"xt{i}", [128, 1024], F32) for i in range(4)]
            xn = [sb2(f"xn{i}", [128, 1024], BF16) for i in range(4)]
            sqj = sb2("sqj", [128, 1024], BF16)
            ss = sb2("ss", [128, 4], F32)
            rstd = sb2("rstd", [128, 4], F32)
            x_own = I["x_own"]
            for bi in range(4):
                tiles = [(bi * 4 + i, xt[i], xn[i]) for i in range(4)]
                self.norm_tiles(lambda tid: x_own[tid * 128:(tid + 1) * 128, :], tiles, xt, sqj, ss, rstd, xn)
                self.transpose_tiles(xn, hT, self.bank[0:2], bi * 4, gsel=0, col0=bi * 512)
            hx, hn = xt[0], xn[0]
            S.dma("sp", lambda e: e.dma_start(out=hx.t[0:32, :], in_=I["x_halo"]), hx.b, writes=[hx.b])
            S.op("act", lambda e: e.activation(out=sqj.t[0:32, :], in_=hx.t[0:32, :], func=AF.Square, accum_out=ss.t[0:32, 0:1]), reads=[hx.b], writes=[sqj.b, ss.b])
            S.op("dve", lambda e: e.tensor_scalar(out=rstd.t[0:32, 0:1], in0=ss.t[0:32, 0:1], scalar1=1.0 / D, scalar2=1e-6, op0=ALU.mult, op1=ALU.add), reads=[ss.b], writes=[rstd.b])
            S.op("pool", lambda e: e.tensor_tensor(out=rstd.t[0:32, 0:1], in0=rstd.t[0:32, 0:1], in1=self.mhalf.t[0:32, 0:1], op=ALU.pow), reads=[rstd.b, self.mhalf.b], writes=[rstd.b])
            S.op("dve", lambda e: e.tensor_scalar(out=hn.t[0:32, :], in0=hx.t[0:32, :], scalar1=rstd.t[0:32, 0:1], scalar2=None, op0=ALU.mult), reads=[hx.b, rstd.b], writes=[hn.b])
            self.transpose_tiles([hn], hTh, self.bank[0:2], 0, gsel=0, col0=0, npart=32)
        S.barrier()
        with contextlib.ExitStack() as st2:
            sb2 = lambda name, shape, dt: self.sb(st2, name, shape, dt)
            wc3 = [[sb2(f"wc3_{i}_{k}", [128, 8, 128], BF16) for k in range(3)] for i in range(2)]
            cwT = sb2("cwT", [128, 8, 3], F32)
            S.dma("sp", lambda e: e.dma_start(out=cwT.t[:], in_=I["conv_wT"]), cwT.b, writes=[cwT.b])
            U2 = sb2("U2", [128, 16, 130], F32)
            ycv = sb2("ycv", [128, 16, 128], F32)
            cbs = sb2("cbs", [128, 2048], F32)
            ccs = [sb2(f"ccs{i}", [128, 512], F32) for i in range(2)]
            cch_ = sb2("cch", [128, 32], F32)
            cnt = 0
            for cch in range(8):
                w3 = wc3[cch % 2]
                for k in range(3):
                    _wslab(self, w3[k], I["w_in"], k * 1024 + cch * 128, 128)
                for tb in range(4):
                    pb = [self.bank[(cnt + k) % 8] for k in range(3)]
                    cnt += 3
                    for k in range(3):
                        for kc in range(8):
                            S.op("pe", lambda e, k=k, kc=kc, tb=tb, w3=w3, pb=pb: e.matmul(pb[k].t[:], lhsT=w3[k].t[:, kc, :], rhs=hT.t[:, kc, tb * 512:(tb + 1) * 512], start=(kc == 0), stop=(kc == 7)),
                                 reads=[w3[k].b, hT.b], writes=[pb[k].b])
                    S.op("act", lambda e, tb=tb, pb=pb: e.copy(out=cbs.t[:, tb * 512:(tb + 1) * 512], in_=pb[0].t[:]), reads=[pb[0].b], writes=[cbs.b])
                    cs = ccs[tb % 2]
                    S.op("act", lambda e, cs=cs, pb=pb: e.copy(out=cs.t[:], in_=pb[1].t[:]), reads=[pb[1].b], writes=[cs.b])
                    S.op("dve", lambda e, cs=cs, pb=pb, tb=tb: e.tensor_tensor(out=U2.t[:, tb * 4:(tb + 1) * 4, 2:130], in0=pb[2].t[:].rearrange("p (m t) -> p m t", m=4), in1=cs.t[:].rearrange("p (m t) -> p m t", m=4), op=ALU.mult),
                         reads=[pb[2].b, cs.b], writes=[U2.b])
                pb = [self.bank[(cnt + k) % 8] for k in range(2)]
                cnt += 2
                for k in range(2):
                    for kc in range(8):
                        S.op("pe", lambda e, k=k, kc=kc, w3=w3, pb=pb: e.matmul(pb[k].t[:, 0:32], lhsT=w3[k + 1].t[:, kc, :], rhs=hTh.t[:, kc, :], start=(kc == 0), stop=(kc == 7)),
                             reads=[w3[k + 1].b, hTh.b], writes=[pb[k].b])
                S.op("act", lambda e, pb=pb: e.copy(out=cch_.t[:], in_=pb[0].t[:, 0:32]), reads=[pb[0].b], writes=[cch_.b])
                S.op("dve", lambda e, pb=pb: e.tensor_tensor(out=U2.t[:, :, 0:2], in0=pb[1].t[:, 0:32].rearrange("p (m t) -> p m t", m=16), in1=cch_.t[:].rearrange("p (m t) -> p m t", m=16), op=ALU.mult),
                     reads=[pb[1].b, cch_.b], writes=[U2.b])
                S.op("dve", lambda e, cch=cch: e.tensor_scalar(out=ycv.t[:], in0=U2.t[:, :, 2:130], scalar1=cwT.t[:, cch, 2:3], scalar2=None, op0=ALU.mult), reads=[U2.b, cwT.b], writes=[ycv.b])
                S.op("dve", lambda e, cch=cch: e.scalar_tensor_tensor(out=ycv.t[:], in0=U2.t[:, :, 1:129], scalar=cwT.t[:, cch, 1:2], in1=ycv.t[:], op0=ALU.mult, op1=ALU.add), reads=[U2.b, cwT.b, ycv.b], writes=[ycv.b])
                S.op("dve", lambda e, cch=cch: e.scalar_tensor_tensor(out=ycv.t[:], in0=U2.t[:, :, 0:128], scalar=cwT.t[:, cch, 0:1], in1=ycv.t[:], op0=ALU.mult, op1=ALU.add), reads=[U2.b, cwT.b, ycv.b], writes=[ycv.b])
                S.op("pool", lambda e, cch=cch: e.tensor_tensor(out=zT.t[:, cch, :], in0=cbs.t[:], in1=ycv.t[:].rearrange("p m t -> p (m t)"), op=ALU.mult), reads=[cbs.b, ycv.b], writes=[zT.b])
        S.barrier()
        with contextlib.ExitStack() as st2:
            sb2 = lambda name, shape, dt: self.sb(st2, name, shape, dt)
            wsl = [[sb2(f"wsl{i}_{k}", [128, 8, 128], BF16) for k in range(4)] for i in range(2)]
            sg = [[sb2(f"sg{i}_{k}", [128, 512], F32) for k in range(2)] for i in range(2)]
            m1 = [sb2(f"m1_{i}", [128, 512], F32) for i in range(2)]
            m2 = [sb2(f"m2_{i}", [128, 512], F32) for i in range(2)]
            it = 0
            for dt_ in range(8):
                ws = wsl[dt_ % 2]
                _wslab(self, ws[0], I["w_conv_out"], dt_ * 128, 128)
                _wslab(self, ws[1], I["w_in"], 6144 + dt_ * 128, 128)
                _wslab(self, ws[2], I["w_attn_out"], dt_ * 128, 128)
                _wslab(self, ws[3], I["w_in"], 7168 + dt_ * 128, 128)
                for tb in range(4):
                    pb = [self.bank[(it % 2) * 4 + k] for k in range(4)]
                    rhs_src = [zT, hT, self.outT, hT]
                    for k in range(4):
                        for kc in range(8):
                            S.op("pe", lambda e, k=k, kc=kc, tb=tb, ws=ws, pb=pb, rhs_src=rhs_src: e.matmul(pb[k].t[:], lhsT=ws[k].t[:, kc, :], rhs=rhs_src[k].t[:, kc, tb * 512:(tb + 1) * 512], start=(kc == 0), stop=(kc == 7)),
                                 reads=[ws[k].b, rhs_src[k].b], writes=[pb[k].b])
                    s0, s1 = sg[it % 2]
                    a1, a2 = m1[it % 2], m2[it % 2]
                    S.op("act", lambda e, s0=s0, pb=pb: e.activation(out=s0.t[:], in_=pb[1].t[:], func=AF.Sigmoid), reads=[pb[1].b], writes=[s0.b])
                    S.op("act", lambda e, s1=s1, pb=pb: e.activation(out=s1.t[:], in_=pb[3].t[:], func=AF.Sigmoid), reads=[pb[3].b], writes=[s1.b])
                    S.op("dve", lambda e, s0=s0, a1=a1, pb=pb: e.tensor_tensor(out=a1.t[:], in0=pb[0].t[:], in1=s0.t[:], op=ALU.mult), reads=[pb[0].b, s0.b], writes=[a1.b])
                    S.op("dve", lambda e, s1=s1, a2=a2, pb=pb: e.tensor_tensor(out=a2.t[:], in0=pb[2].t[:], in1=s1.t[:], op=ALU.mult), reads=[pb[2].b, s1.b], writes=[a2.b])
                    S.op("pool", lambda e, a1=a1, a2=a2, dt_=dt_, tb=tb: e.tensor_tensor(out=mergedT.t[:, dt_, tb * 512:(tb + 1) * 512], in0=a1.t[:], in1=a2.t[:], op=ALU.add), reads=[a1.b, a2.b], writes=[mergedT.b])
                    it += 1
        S.barrier()
        with contextlib.ExitStack() as st2:
            self.a_ptr = p0 + 97 * 1024
            wmix = self.sb(st2, "wmix", [128, 8, 1024], BF16)
            _wslab(self, wmix, I["w_mix_out"], 0, 1024)
            xres = self.xres
            S.dma("sp", lambda e: e.dma_start(out=xres.t[:], in_=I["x_own"].rearrange("(m p) d -> p m d", p=128)), xres.b, writes=[xres.b])
            it = 0
            for m in range(16):
                for dh in range(2):
                    bk = self.bank[it % 8]
                    it += 1
                    for kc in range(8):
                        S.op("pe", lambda e, bk=bk, kc=kc, m=m, dh=dh: e.matmul(bk.t[:], lhsT=mergedT.t[:, kc, m * 128:(m + 1) * 128], rhs=wmix.t[:, kc, dh * 512:(dh + 1) * 512], start=(kc == 0), stop=(kc == 7)),
                             reads=[mergedT.b, wmix.b], writes=[bk.b])
                    S.op("dve", lambda e, bk=bk, m=m, dh=dh: e.tensor_tensor(out=xres.t[:, m, dh * 512:(dh + 1) * 512], in0=bk.t[:], in1=xres.t[:, m, dh * 512:(dh + 1) * 512], op=ALU.add),
                         reads=[bk.b, xres.b], writes=[xres.b])


def _dump_res(self, name):
    S = self.S
    S.dma("sp", lambda e: e.dma_start(out=self.dbg[name].rearrange("(m p) d -> p m d", p=128), in_=self.xres.t[:]), self.xres.b, reads=[self.xres.b], is_output=True)


def _phase_cross(self, p0):
    S, I = self.S, self.I
    pA = self.pA
    xres = self.xres
    regB = p0 + 96 * 1024
    with contextlib.ExitStack() as st:
        self.a_ptr = pA
        sb = lambda name, shape, dt: self.sb(st, name, shape, dt)
        hcT = sb("hcT", [128, 8, 2048], BF16)
        wcq = sb("wcq", [128, 8, 1024], BF16)
        wco = sb("wco", [128, 8, 1024], BF16)
        kT = sb("kT", [128, 8, 256], BF16)
        vC = sb("vC", [128, 2, 1024], BF16)
        ones = sb("ones", [128, 128], BF16)
        qcT = sb("qcT", [128, 8, 512], BF16)
        assert self.a_ptr <= p0 + 32 * 1024, (self.a_ptr, p0)
        S.op("pool", lambda e: e.memset(ones.t[:], 1.0), writes=[ones.b])
        _wslab(self, wcq, I["w_cq"], 0, 1024)
        _wslab(self, wco, I["w_co"], 0, 1024)
        with contextlib.ExitStack() as st2:
            self.a_ptr = regB
            sb2 = lambda name, shape, dt: self.sb(st2, name, shape, dt)
            wckv = sb2("wckv", [128, 8, 2048], BF16)
            mT = sb2("mT", [128, 8, 256], BF16)
            xt = [sb2(f"xt{i}", [128, 1024], F32) for i in range(2)]
            xn = [sb2(f"xn{i}", [128, 1024], BF16) for i in range(2)]
            sqj = sb2("sqj", [128, 1024], BF16)
            ss = sb2("ss", [128, 4], F32)
            rstd = sb2("rstd", [128, 4], F32)
            _wslab(self, wckv, I["w_ckv"], 0, 2048)
            tiles = [(i, xt[i], xn[i]) for i in range(2)]
            self.norm_tiles(lambda tid: I["mem"][tid * 128:(tid + 1) * 128, :], tiles, xt, sqj, ss, rstd, xn)
            self.transpose_tiles(xn, mT, self.bank[0:2], 0, gsel=2, col0=0)
            for ct in range(8):
                bk = self.bank[2 + ct % 6]
                for kc in range(8):
                    S.op("pe", lambda e, bk=bk, kc=kc, ct=ct: e.matmul(bk.t[:, 0:256], lhsT=wckv.t[:, kc, ct * 128:(ct + 1) * 128], rhs=mT.t[:, kc, :], start=(kc == 0), stop=(kc == 7)),
                         reads=[wckv.b, mT.b], writes=[bk.b])
                S.op("act", lambda e, bk=bk, ct=ct: e.copy(out=kT.t[:, ct, :], in_=bk.t[:, 0:256]), reads=[bk.b], writes=[kT.b])
            for mt in range(2):
                for hh in range(2):
                    bk = self.bank[2 + (mt * 2 + hh) % 6]
                    for kc in range(8):
                        S.op("pe", lambda e, bk=bk, kc=kc, mt=mt, hh=hh: e.matmul(bk.t[:], lhsT=mT.t[:, kc, mt * 128:(mt + 1) * 128], rhs=wckv.t[:, kc, 1024 + hh * 512:1024 + (hh + 1) * 512], start=(kc == 0), stop=(kc == 7)),
                             reads=[wckv.b, mT.b], writes=[bk.b])
                    S.op("dve", lambda e, bk=bk, mt=mt, hh=hh: e.tensor_copy(out=vC.t[:, mt, hh * 512:(hh + 1) * 512], in_=bk.t[:]), reads=[bk.b], writes=[vC.b])
        S.barrier()
        with contextlib.ExitStack() as st2:
            self.a_ptr = regB
            sb2 = lambda name, shape, dt: self.sb(st2, name, shape, dt)
            xn = [sb2(f"xn{i}", [128, 1024], BF16) for i in range(4)]
            sqj = sb2("sqj", [128, 1024], BF16)
            ss = sb2("ss", [128, 4], F32)
            rstd = sb2("rstd", [128, 4], F32)
            P = [sb2(f"P{i}", [128, 2, 512], BF16) for i in range(2)]
            oT = sb2("oT", [128, 8, 512], BF16)
            R = [sb2(f"R{i}", [128, 512], F32) for i in range(2)]
            for bi in range(4):
                tiles = [(xres.t[:, bi * 4 + i, :], xres.b, xn[i]) for i in range(4)]
                _norm_res(self, tiles, sqj, ss, rstd)
                self.transpose_tiles(xn, hcT, self.bank[0:2], bi * 4, gsel=1, col0=bi * 512)
            it = 0
            for tb in range(4):
                for ct in range(8):
                    bk = self.bank[it % 8]
                    it += 1
                    for kc in range(8):
                        S.op("pe", lambda e, bk=bk, kc=kc, ct=ct, tb=tb: e.matmul(bk.t[:], lhsT=wcq.t[:, kc, ct * 128:(ct + 1) * 128], rhs=hcT.t[:, kc, tb * 512:(tb + 1) * 512], start=(kc == 0), stop=(kc == 7)),
                             reads=[wcq.b, hcT.b], writes=[bk.b])
                    if ct % 2 == 0:
                        S.op("act", lambda e, bk=bk, ct=ct: e.copy(out=qcT.t[:, ct, :], in_=bk.t[:]), reads=[bk.b], writes=[qcT.b])
                    else:
                        S.op("dve", lambda e, bk=bk, ct=ct: e.tensor_copy(out=qcT.t[:, ct, :], in_=bk.t[:]), reads=[bk.b], writes=[qcT.b])
                for hd in range(4):
                    Pt = P[hd % 2]
                    Rt = R[hd % 2]
                    for mt in range(2):
                        bk = self.bank[it % 8]
                        it += 1
                        for half in range(2):
                            S.op("pe", lambda e, bk=bk, hd=hd, half=half, mt=mt: e.matmul(bk.t[:], lhsT=kT.t[:, hd * 2 + half, mt * 128:(mt + 1) * 128], rhs=qcT.t[:, hd * 2 + half, :], start=(half == 0), stop=(half == 1)),
                                 reads=[kT.b, qcT.b], writes=[bk.b])
                        S.op("act", lambda e, bk=bk, Pt=Pt, mt=mt: e.activation(out=Pt.t[:, mt, :], in_=bk.t[:], func=AF.Exp, scale=1.0 / 16), reads=[bk.b], writes=[Pt.b])
                    bs = self.bank[it % 8]
                    it += 1
                    for mt in range(2):
                        S.op("pe", lambda e, bs=bs, Pt=Pt, mt=mt: e.matmul(bs.t[:], lhsT=ones.t[:], rhs=Pt.t[:, mt, :], start=(mt == 0), stop=(mt == 1)), reads=[ones.b, Pt.b], writes=[bs.b])
                    S.op("dve", lambda e, bs=bs, Rt=Rt: e.reciprocal(out=Rt.t[:], in_=bs.t[:]), reads=[bs.b], writes=[Rt.b])
                    for half in range(2):
                        bo = self.bank[it % 8]
                        it += 1
                        for mt in range(2):
                            S.op("pe", lambda e, bo=bo, Pt=Pt, mt=mt, hd=hd, half=half: e.matmul(bo.t[:], lhsT=vC.t[:, mt, hd * 256 + half * 128:hd * 256 + (half + 1) * 128], rhs=Pt.t[:, mt, :], start=(mt == 0), stop=(mt == 1)),
                                 reads=[vC.b, Pt.b], writes=[bo.b])
                        S.op("dve", lambda e, bo=bo, Rt=Rt, hd=hd, half=half: e.tensor_tensor(out=oT.t[:, hd * 2 + half, :], in0=bo.t[:], in1=Rt.t[:], op=ALU.mult), reads=[bo.b, Rt.b], writes=[oT.b])
                for i in range(4):
                    m = tb * 4 + i
                    for dh in range(2):
                        bk = self.bank[it % 8]
                        it += 1
                        for ct in range(8):
                            S.op("pe", lambda e, bk=bk, ct=ct, i=i, dh=dh: e.matmul(bk.t[:], lhsT=oT.t[:, ct, i * 128:(i + 1) * 128], rhs=wco.t[:, ct, dh * 512:(dh + 1) * 512], start=(ct == 0), stop=(ct == 7)),
                                 reads=[oT.b, wco.b], writes=[bk.b])
                        S.op("dve", lambda e, bk=bk, m=m, dh=dh: e.tensor_tensor(out=xres.t[:, m, dh * 512:(dh + 1) * 512], in0=bk.t[:], in1=xres.t[:, m, dh * 512:(dh + 1) * 512], op=ALU.add),
                             reads=[bk.b, xres.b], writes=[xres.b])


Builder.phase_mix = _phase_mix
Builder.phase_cross = _phase_cross
Builder.dump_res = _dump_res


def _phase_peer(self, p0):
    S, I = self.S, self.I
    xres = self.xres
    pA = self.pA
    X2_d = self.dscr("X2_d", [NT_OWN * 128, D], F32)
    B_X2 = Buf("X2_d")
    MAGIC = 12582912.0
    with contextlib.ExitStack() as st:
        self.a_ptr = p0 + 96 * 1024
        sb = lambda name, shape, dt: self.sb(st, name, shape, dt)
        with contextlib.ExitStack() as st2:
            sb2 = lambda name, shape, dt: self.sb(st2, name, shape, dt)
            xn = [sb2(f"xn{i}", [128, 1024], BF16) for i in range(4)]
            sqj = sb2("sqj", [128, 1024], BF16)
            ss = sb2("ss", [128, 4], F32)
            rstd = sb2("rstd", [128, 4], F32)
            self.a_ptr = pA
            hfT = self.sb(st, "hfT", [128, 8, 2048], BF16)
            for bi in range(4):
                tiles = [(xres.t[:, bi * 4 + i, :], xres.b, xn[i]) for i in range(4)]
                _norm_res(self, tiles, sqj, ss, rstd)
                self.transpose_tiles(xn, hfT, self.bank[0:2], bi * 4, gsel=3, col0=bi * 512)
            S.dma("sp", lambda e: e.dma_start(out=X2_d.rearrange("(m p) d -> p m d", p=128), in_=xres.t[:]), xres.b, reads=[xres.b], writes=[B_X2])
        S.barrier()
        self.a_ptr = pA + 32 * 1024
        RT = self.sb(st, "RT", [128, 3, 2048], F32)
        pR = self.a_ptr
        with contextlib.ExitStack() as st2:
            sb2 = lambda name, shape, dt: self.sb(st2, name, shape, dt)
            ub = [sb2(f"ub{i}", [128, 4, 1024], BF16) for i in range(2)]
            vb = [sb2(f"vb{i}", [128, 4, 1024], BF16) for i in range(2)]
            uts = [sb2(f"uts{i}", [128, 8, 128], BF16) for i in range(3)]
            for gq in range(32):
                u, v = ub[gq % 2], vb[gq % 2]
                S.dma("pool", lambda e, u=u, gq=gq: e.dma_start(out=u.t[:], in_=I["peer_u"][gq * 512:(gq + 1) * 512, :].rearrange("(i p) d -> p i d", p=128)), u.b, writes=[u.b])
                S.dma("pool", lambda e, v=v, gq=gq: e.dma_start(out=v.t[:], in_=I["peer_v"][gq * 512:(gq + 1) * 512, :].rearrange("(i p) d -> p i d", p=128)), v.b, writes=[v.b])
                S.dma("sp", lambda e, v=v, gq=gq: e.dma_start(out=self.Vb_d[gq * 512:(gq + 1) * 512, :].rearrange("(i p) d -> p i d", p=128), in_=v.t[:]), v.b, reads=[v.b], writes=[self.B_Vb])
                for i in range(4):
                    et = gq * 4 + i
                    bk = self.bank[et % 4]
                    pt = bk.t.bitcast(BF16)
                    ut = uts[et % 3]
                    for kc in range(8):
                        S.op("pe", lambda e, pt=pt, u=u, i=i, kc=kc: e.transpose(out=pt[:, kc * 128:(kc + 1) * 128], in_=u.t[:, i, kc * 128:(kc + 1) * 128], identity=self.ident_bf.t[:]),
                             reads=[u.b, self.ident_bf.b], writes=[bk.b])
                    if et % 2 == 0:
                        S.op("act", lambda e, pt=pt, ut=ut: e.copy(out=ut.t[:], in_=pt[:, :].rearrange("p (k t) -> p k t", k=8)), reads=[bk.b], writes=[ut.b])
                    else:
                        S.op("dve", lambda e, pt=pt, ut=ut: e.tensor_copy(out=ut.t[:], in_=pt[:, :].rearrange("p (k t) -> p k t", k=8)), reads=[bk.b], writes=[ut.b])
                    S.dma("act", lambda e, ut=ut, et=et: e.dma_start(out=self.UT_d[et], in_=ut.t[:]), ut.b, reads=[ut.b], writes=[self.B_UT])
        S.barrier()
        if self.peer_stop == "p0":
            with contextlib.ExitStack() as st2:
                tu = self.sb(st2, "tu", [128, 1024], BF16)
                tv = self.sb(st2, "tv", [128, 1024], BF16)
                tf = self.sb(st2, "tf", [128, 2, 1024], F32)
                S.dma("sp", lambda e: e.dma_start(out=tu.t[:], in_=self.UT_d[77].rearrange("p k e -> p (k e)")), tu.b, reads=[self.B_UT], writes=[tu.b])
                S.dma("sp", lambda e: e.dma_start(out=tv.t[:], in_=self.Vb_d[77 * 128:78 * 128, :]), tv.b, reads=[self.B_Vb], writes=[tv.b])
                S.op("dve", lambda e: e.tensor_copy(out=tf.t[:, 0, :], in_=tu.t[:]), reads=[tu.b], writes=[tf.b])
                S.op("dve", lambda e: e.tensor_copy(out=tf.t[:, 1, :], in_=tv.t[:]), reads=[tv.b], writes=[tf.b])
                S.dma("sp", lambda e: e.dma_start(out=self.out[0:128, :], in_=tf.t[:, 0, :]), tf.b, reads=[tf.b], is_output=True)
                S.dma("sp", lambda e: e.dma_start(out=self.out[128:256, :], in_=tf.t[:, 1, :]), tf.b, reads=[tf.b], is_output=True)
                S.dma("sp", lambda e: e.dma_start(out=self.out[256:384, :], in_=xres.t[:, 5, :]), xres.b, reads=[xres.b], is_output=True)
            return
        with contextlib.ExitStack() as st2:
            self.a_ptr = pR
            sb2 = lambda name, shape, dt: self.sb(st2, name, shape, dt)
            wpq = sb2("wpq", [128, 8, 2048], BF16)
            skT = sb2("skT", [128, 16, 128], BF16)
            qT = [sb2(f"qT{i}", [128, 16, 128], BF16) for i in range(2)]
            sc = sb2("sc", [128, 16, 128], F32)
            scr = sb2("scr", [128, 16, 128], F32)
            vals = sb2("vals", [128, 16, 16], F32)
            idx = sb2("idx", [128, 16, 16], U32)
            idxf = sb2("idxf", [128, 16, 16], F32)
            cand = sb2("cand", [128, 8, 256], F32)
            cscr = sb2("cscr", [128, 256], F32)
            ts = sb2("ts", [128, 8, 16], F32)
            tc_ = sb2("tc", [128, 8, 16], U32)
            tcf = sb2("tcf", [128, 8, 16], F32)
            af = sb2("af", [128, 8, 16], F32)
            bf = sb2("bf", [128, 8, 16], F32)
            oh = sb2("oh", [128, 8, 16, 16], F32)
            IJg = sb2("IJg", [128, 3, 128], F32)
            esum = sb2("esum", [128, 8], F32)
            _wslab(self, wpq, I["w_pq"], 0, 2048)
            S.dma("pool", lambda e: e.dma_start(out=skT.t[:], in_=I["skT"].rearrange("g d n -> d g n")), skT.b, writes=[skT.b])
            iota16 = self.iota_f.t[:, 0:16]
            for m in range(16):
                q = qT[m % 2]
                for gi in range(16):
                    bk = self.bank[gi % 4]
                    for kc in range(8):
                        S.op("pe", lambda e, bk=bk, kc=kc, gi=gi, m=m: e.matmul(bk.t[:, 0:128], lhsT=wpq.t[:, kc, gi * 128:(gi + 1) * 128], rhs=hfT.t[:, kc, m * 128:(m + 1) * 128], start=(kc == 0), stop=(kc == 7)),
                             reads=[wpq.b, hfT.b], writes=[bk.b])
                    S.op("act", lambda e, bk=bk, gi=gi, q=q: e.copy(out=q.t[:, gi, :], in_=bk.t[:, 0:128]), reads=[bk.b], writes=[q.b])
                for gi in range(16):
                    bk = self.bank[4 + gi // 4]
                    S.op("pe", lambda e, bk=bk, gi=gi, q=q: e.matmul(bk.t[:, (gi % 4) * 128:(gi % 4 + 1) * 128], lhsT=q.t[:, gi, :], rhs=skT.t[:, gi, :], start=True, stop=True),
                         reads=[q.b, skT.b], writes=[bk.b])
                for b4 in range(4):
                    bk = self.bank[4 + b4]
                    S.op("act", lambda e, bk=bk, b4=b4: e.copy(out=sc.t[:, b4 * 4:(b4 + 1) * 4, :], in_=bk.t[:].rearrange("p (g n) -> p g n", g=4)), reads=[bk.b], writes=[sc.b])
                if self.peer_stop == "p1sc" and m == 0:
                    S.dma("sp", lambda e: e.dma_start(out=self.out[0:256, :].rearrange("(p a) d -> p (a d)", p=128), in_=sc.t[:].rearrange("p g n -> p (g n)")), sc.b, reads=[sc.b], is_output=True)
                    qf = sb2("qf", [128, 16, 128], F32)
                    S.op("dve", lambda e: e.tensor_copy(out=qf.t[:], in_=q.t[:]), reads=[q.b], writes=[qf.b])
                    S.dma("sp", lambda e: e.dma_start(out=self.out[256:512, :].rearrange("(p a) d -> p (a d)", p=128), in_=qf.t[:].rearrange("p g n -> p (g n)")), qf.b, reads=[qf.b], is_output=True)
                    return
                for gi in range(16):
                    S.op("dve", lambda e, gi=gi: e.max(out=vals.t[:, gi, 0:8], in_=sc.t[:, gi, :]), reads=[sc.b], writes=[vals.b])
                    S.op("dve", lambda e, gi=gi: e.max_index(out=idx.t[:, gi, 0:8], in_max=vals.t[:, gi, 0:8], in_values=sc.t[:, gi, :]), reads=[sc.b, vals.b], writes=[idx.b])
                    S.op("dve", lambda e, gi=gi: e.match_replace(out=scr.t[:, gi, :], in_to_replace=vals.t[:, gi, 0:8], in_values=sc.t[:, gi, :], imm_value=-1e30), reads=[sc.b, vals.b], writes=[scr.b])
                    S.op("dve", lambda e, gi=gi: e.max(out=vals.t[:, gi, 8:16], in_=scr.t[:, gi, :]), reads=[scr.b], writes=[vals.b])
                    S.op("dve", lambda e, gi=gi: e.max_index(out=idx.t[:, gi, 8:16], in_max=vals.t[:, gi, 8:16], in_values=scr.t[:, gi, :]), reads=[scr.b, vals.b], writes=[idx.b])
                S.op("dve", lambda e: e.tensor_copy(out=idxf.t[:], in_=idx.t[:]), reads=[idx.b], writes=[idxf.b])
                v0 = _bcast_ap(vals.t, 0, [[32, 8], [1, 16], [0, 16]])
                v1 = _bcast_ap(vals.t, 16, [[32, 8], [0, 16], [1, 16]])
                S.op("dve", lambda e, v0=v0, v1=v1: e.tensor_tensor(out=cand.t[:].rearrange("p h (a b) -> p h a b", a=16), in0=v0, in1=v1, op=ALU.add), reads=[vals.b], writes=[cand.b])
                for h in range(8):
                    S.op("dve", lambda e, h=h: e.max(out=ts.t[:, h, 0:8], in_=cand.t[:, h, :]), reads=[cand.b], writes=[ts.b])
                    S.op("dve", lambda e, h=h: e.max_index(out=tc_.t[:, h, 0:8], in_max=ts.t[:, h, 0:8], in_values=cand.t[:, h, :]), reads=[cand.b, ts.b], writes=[tc_.b])
                    S.op("dve", lambda e, h=h: e.match_replace(out=cscr.t[:], in_to_replace=ts.t[:, h, 0:8], in_values=cand.t[:, h, :], imm_value=-1e30), reads=[cand.b, ts.b], writes=[cscr.b])
                    S.op("dve", lambda e, h=h: e.max(out=ts.t[:, h, 8:16], in_=cscr.t[:]), reads=[cscr.b], writes=[ts.b])
                    S.op("dve", lambda e, h=h: e.max_index(out=tc_.t[:, h, 8:16], in_max=ts.t[:, h, 8:16], in_values=cscr.t[:]), reads=[cscr.b, ts.b], writes=[tc_.b])
                S.op("dve", lambda e: e.tensor_copy(out=tcf.t[:], in_=tc_.t[:]), reads=[tc_.b], writes=[tcf.b])
                S.op("dve", lambda e: e.tensor_scalar(out=af.t[:], in0=tcf.t[:], scalar1=0.0625, scalar2=-0.46875, op0=ALU.mult, op1=ALU.add), reads=[tcf.b], writes=[af.b])
                S.op("dve", lambda e: e.tensor_scalar(out=af.t[:], in0=af.t[:], scalar1=MAGIC, scalar2=None, op0=ALU.add), reads=[af.b], writes=[af.b])
                S.op("dve", lambda e: e.tensor_scalar(out=af.t[:], in0=af.t[:], scalar1=-MAGIC, scalar2=None, op0=ALU.add), reads=[af.b], writes=[af.b])
                S.op("dve", lambda e: e.scalar_tensor_tensor(out=bf.t[:], in0=af.t[:], scalar=-16.0, in1=tcf.t[:], op0=ALU.mult, op1=ALU.add), reads=[af.b, tcf.b], writes=[bf.b])
                for which, sel in ((0, af), (1, bf)):
                    selb = _bcast_ap(sel.t, 0, [[16, 8], [1, 16], [0, 16]])
                    iob = _bcast_ap(self.iota_f.t, 0, [[0, 8], [0, 16], [1, 16]])
                    ixb = _bcast_ap(idxf.t, which * 16, [[32, 8], [0, 16], [1, 16]])
                    S.op("dve", lambda e, selb=selb, iob=iob: e.tensor_tensor(out=oh.t[:], in0=selb, in1=iob, op=ALU.is_equal), reads=[sel.b, self.iota_f.b], writes=[oh.b])
                    S.op("dve", lambda e, ixb=ixb: e.tensor_tensor(out=oh.t[:], in0=oh.t[:], in1=ixb, op=ALU.mult), reads=[oh.b, idxf.b], writes=[oh.b])
                    S.op("dve", lambda e, which=which: e.tensor_reduce(out=IJg.t[:, which, :], in_=oh.t[:].rearrange("p h k a -> p (h k) a"), axis=AX.X, op=ALU.add), reads=[oh.b], writes=[IJg.b])
                tmax = _bcast_ap(ts.t, 0, [[16, 8], [0, 16]])
                S.op("dve", lambda e, tmax=tmax: e.tensor_tensor(out=tcf.t[:], in0=ts.t[:], in1=tmax, op=ALU.subtract), reads=[ts.b], writes=[tcf.b])
                S.op("act", lambda e: e.activation(out=tcf.t[:], in_=tcf.t[:], func=AF.Exp), reads=[tcf.b], writes=[tcf.b])
                S.op("dve", lambda e: e.tensor_reduce(out=esum.t[:], in_=tcf.t[:], axis=AX.X, op=ALU.add), reads=[tcf.b], writes=[esum.b])
                S.op("dve", lambda e: e.reciprocal(out=esum.t[:], in_=esum.t[:]), reads=[esum.b], writes=[esum.b])
                esb = _bcast_ap(esum.t, 0, [[1, 8], [0, 16]])
                S.op("dve", lambda e, esb=esb: e.tensor_tensor(out=IJg.t[:, 2, :].rearrange("p (h k) -> p h k", h=8), in0=tcf.t[:], in1=esb, op=ALU.mult), reads=[tcf.b, esum.b], writes=[IJg.b])
                for w in range(3):
                    bk = self.bank[w]
                    S.op("pe", lambda e, bk=bk, w=w: e.transpose(out=bk.t[:, 0:128], in_=IJg.t[:, w, :], identity=self.ident_f.t[:]), reads=[IJg.b, self.ident_f.b], writes=[bk.b])
                    S.op("act", lambda e, bk=bk, w=w, m=m: e.copy(out=RT.t[:, w, m * 128:(m + 1) * 128], in_=bk.t[:, 0:128]), reads=[bk.b], writes=[RT.b])
        S.barrier()
        if self.peer_stop == "p1":
            S.dma("sp", lambda e: e.dma_start(out=self.out[0:768, :].rearrange("(p a) d -> p (a d)", p=128), in_=RT.t[:].rearrange("p w t -> p (w t)")), RT.b, reads=[RT.b], is_output=True)
            return
        with contextlib.ExitStack() as st2:
            self.a_ptr = pR
            sb2 = lambda name, shape, dt: self.sb(st2, name, shape, dt)
            TB = 256
            W = sb2("W", [128, 128, TB], BF16)
            ohj = [sb2(f"ohj{i}", [128, 128], BF16) for i in range(4)]
            ohi = [sb2(f"ohi{i}", [128, 128], BF16) for i in range(4)]
            utl = [sb2(f"utl{i}", [128, 8, 128], BF16) for i in range(3)]
            vtl = [sb2(f"vtl{i}", [128, 1024], BF16) for i in range(3)]
            ag = [sb2(f"ag{i}", [128, TB], F32) for i in range(2)]
            wa = [sb2(f"wa{i}", [128, TB], BF16) for i in range(2)]
            x2t = [sb2(f"x2t{i}", [128, 1024], F32) for i in range(2)]
            sqj = sb2("sqjf", [128, 1024], BF16)
            fs = sb2("fs", [128, 4], F32)
            frs = sb2("frs", [128, 4], F32)
            yo = [sb2(f"yo{i}", [128, 1024], F32) for i in range(2)]
            gfin = sb2("gfin", [128, 1024], F32)
            S.dma("sp", lambda e: e.dma_start(out=gfin.t[:], in_=I["final_g_rep"]), gfin.b, writes=[gfin.b])
            psO = self.bank[0:4]
            psA = self.bank[4:6]
            psW = self.bank[6]
            for blk in range(2048 // TB):
                t0 = blk * TB
                for tt in range(TB):
                    t = t0 + tt
                    oj, oi = ohj[tt % 4], ohi[tt % 4]
                    S.op("pool", lambda e, oj=oj, t=t: e.tensor_scalar(out=oj.t[:], in0=self.iota_f.t[:], scalar1=RT.t[:, 1, t:t + 1], scalar2=None, op0=ALU.is_equal), reads=[self.iota_f.b, RT.b], writes=[oj.b])
                    S.op("dve", lambda e, oi=oi, t=t: e.tensor_scalar(out=oi.t[:], in0=self.iota_f.t[:], scalar1=RT.t[:, 0, t:t + 1], scalar2=RT.t[:, 2, t:t + 1], op0=ALU.is_equal, op1=ALU.mult), reads=[self.iota_f.b, RT.b], writes=[oi.b])
                    S.op("pe", lambda e, oj=oj, oi=oi, tt=tt: e.matmul(psW.t[:, (tt % 4) * 128:(tt % 4 + 1) * 128], lhsT=oj.t[:], rhs=oi.t[:], start=True, stop=True), reads=[oj.b, oi.b], writes=[psW.b])
                    if tt % 4 == 3:
                        tb0 = tt - 3
                        wout = _bcast_ap(W.t, tb0, [[1, 4], [TB, 128]])
                        S.op("act", lambda e, wout=wout: e.copy(out=wout, in_=psW.t[:].rearrange("p (q i) -> p q i", q=4)), reads=[psW.b], writes=[W.b])
                for i in range(128):
                    ut, vt = utl[i % 3], vtl[i % 3]
                    S.dma("sp", lambda e, ut=ut, i=i: e.dma_start(out=ut.t[:], in_=self.UT_d[i]), ut.b, reads=[self.B_UT], writes=[ut.b])
                    S.dma("act", lambda e, vt=vt, i=i: e.dma_start(out=vt.t[:], in_=self.Vb_d[i * 128:(i + 1) * 128, :]), vt.b, reads=[self.B_Vb], writes=[vt.b])
                    pa = psA[i % 2]
                    for kc in range(8):
                        S.op("pe", lambda e, pa=pa, ut=ut, kc=kc, t0=t0: e.matmul(pa.t[:, 0:TB], lhsT=ut.t[:, kc, :], rhs=hfT.t[:, kc, t0:t0 + TB], start=(kc == 0), stop=(kc == 7)),
                             reads=[ut.b, hfT.b], writes=[pa.b])
                    a_, w_ = ag[i % 2], wa[i % 2]
                    S.op("act", lambda e, pa=pa, a_=a_: e.activation(out=a_.t[:], in_=pa.t[:, 0:TB], func=AF.Gelu), reads=[pa.b], writes=[a_.b])
                    S.op("pool", lambda e, a_=a_, w_=w_, i=i: e.tensor_tensor(out=w_.t[:], in0=a_.t[:], in1=W.t[:, i, :], op=ALU.mult), reads=[a_.b, W.b], writes=[w_.b])
                    for tl in range(TB // 128):
                        for dh in range(2):
                            po = psO[tl * 2 + dh]
                            S.op("pe", lambda e, po=po, w_=w_, vt=vt, tl=tl, dh=dh, i=i: e.matmul(po.t[:], lhsT=w_.t[:, tl * 128:(tl + 1) * 128], rhs=vt.t[:, dh * 512:(dh + 1) * 512], start=(i == 0), stop=(i == 127)),
                                 reads=[w_.b, vt.b], writes=[po.b])
                for tl in range(TB // 128):
                    m = (t0 // 128) + tl
                    xt_ = x2t[tl % 2]
                    yt = yo[tl % 2]
                    S.dma("sp", lambda e, xt_=xt_, m=m: e.dma_start(out=xt_.t[:], in_=X2_d[m * 128:(m + 1) * 128, :]), xt_.b, reads=[B_X2], writes=[xt_.b])
                    for dh in range(2):
                        po = psO[tl * 2 + dh]
                        S.op("dve", lambda e, po=po, xt_=xt_, dh=dh: e.tensor_tensor(out=xt_.t[:, dh * 512:(dh + 1) * 512], in0=po.t[:], in1=xt_.t[:, dh * 512:(dh + 1) * 512], op=ALU.add), reads=[po.b, xt_.b], writes=[xt_.b])
                    if "x3" in self.dumps:
                        S.dma("sp", lambda e, xt_=xt_, m=m: e.dma_start(out=self.dbg["x3"][m * 128:(m + 1) * 128, :], in_=xt_.t[:]), xt_.b, reads=[xt_.b], is_output=True)
                    S.op("act", lambda e, xt_=xt_: e.activation(out=sqj.t[:], in_=xt_.t[:], func=AF.Square, accum_out=fs.t[:, 0:1]), reads=[xt_.b], writes=[sqj.b, fs.b])
                    S.op("dve", lambda e: e.tensor_scalar(out=frs.t[:, 0:1], in0=fs.t[:, 0:1], scalar1=1.0 / D, scalar2=1e-6, op0=ALU.mult, op1=ALU.add), reads=[fs.b], writes=[frs.b])
                    S.op("pool", lambda e: e.tensor_tensor(out=frs.t[:, 0:1], in0=frs.t[:, 0:1], in1=self.mhalf.t[:, 0:1], op=ALU.pow), reads=[frs.b, self.mhalf.b], writes=[frs.b])
                    S.op("dve", lambda e, xt_=xt_, yt=yt: e.scalar_tensor_tensor(out=yt.t[:], in0=xt_.t[:], scalar=frs.t[:, 0:1], in1=gfin.t[:], op0=ALU.mult, op1=ALU.mult), reads=[xt_.b, frs.b, gfin.b], writes=[yt.b])
                    S.dma("sp", lambda e, yt=yt, m=m: e.dma_start(out=self.out[m * 128:(m + 1) * 128, :], in_=yt.t[:]), yt.b, reads=[yt.b], is_output=True)


def _phase_final(self):
    pass


Builder.phase_peer = _phase_peer
Builder.phase_final = _phase_final
```

```python
import contextlib
import math

import numpy as np
import ml_dtypes
import concourse.bass as bass
import concourse.mybir as mybir
from concourse.bass_utils import run_bass_kernel_spmd

F32 = mybir.dt.float32
BF16 = mybir.dt.bfloat16
U32 = mybir.dt.uint32
I32 = mybir.dt.int32
AF = mybir.ActivationFunctionType
ALU = mybir.AluOpType
AX = mybir.AxisListType

NCORE = 8
SEQ = 16384
D = 1024
NT_ALL = SEQ // 128
NT_OWN = 16
COMPUTE = ("pe", "act", "dve", "pool")
ALL_ENG = ("pe", "act", "dve", "pool", "sp")
SEM_ROT = 30000


class Buf:
    __slots__ = ("name", "writers", "readers", "dsem", "dcount")

    def __init__(self, name):
        self.name = name
        self.writers = []
        self.readers = []
        self.dsem = None
        self.dcount = 0


class Ins:
    __slots__ = ("eng", "fn", "deps", "is_dma", "sem", "semval", "needs_inc")

    def __init__(self, eng, fn, deps, is_dma):
        self.eng = eng
        self.fn = fn
        self.deps = deps
        self.is_dma = is_dma
        self.sem = None
        self.semval = 0
        self.needs_inc = False


class Sched:
    def __init__(self, nc, same_engine_sync=True):
        self.nc = nc
        self.ins = []
        self.streams = {e: [] for e in ALL_ENG}
        self.same_engine_sync = same_engine_sync
        self.dma_bufs = []
        self.out_tokens = []
        self.last_eng = {}
        self.last_dma = {}
        self.base_deps = frozenset()

    def barrier(self):
        self.base_deps = frozenset(list(self.last_eng.values()) + list(self.last_dma.values()))

    def _deps(self, reads, writes):
        deps = set(self.base_deps)
        for b in reads:
            deps.update(b.writers)
        for b in writes:
            deps.update(b.writers)
            deps.update(b.readers)
        last = {}
        for i in deps:
            it = self.ins[i]
            key = ("d", id(it.sem)) if it.is_dma else it.eng
            if key not in last or last[key] < i:
                last[key] = i
        return set(last.values())

    def _compress(self, lst):
        last = {}
        out = []
        for i in lst:
            it = self.ins[i]
            if it.is_dma:
                last[("d", id(it.sem))] = i
            else:
                last[it.eng] = i
        return list(last.values())

    def _commit(self, me, reads, writes):
        for b in writes:
            if b.readers:
                b.writers = [me]
                b.readers = []
            else:
                b.writers.append(me)
                if len(b.writers) > 32:
                    b.writers = self._compress(b.writers)
        for b in reads:
            if b in writes:
                continue
            b.readers.append(me)
            if len(b.readers) > 32:
                b.readers = self._compress(b.readers)

    def op(self, eng, fn, reads=(), writes=()):
        deps = self._deps(reads, writes)
        me = len(self.ins)
        it = Ins(eng, fn, deps, False)
        self.ins.append(it)
        self.streams[eng].append(it)
        self._commit(me, reads, writes)
        self.last_eng[eng] = me
        return me

    def dma(self, queue, fn, sem_buf, reads=(), writes=(), is_output=False):
        deps = self._deps(reads, writes)
        me = len(self.ins)
        it = Ins(queue, fn, deps, True)
        if sem_buf.dsem is None:
            sem_buf.dsem = "pending"
            self.dma_bufs.append(sem_buf)
        sem_buf.dcount += 16
        it.sem = sem_buf
        it.semval = sem_buf.dcount
        self.ins.append(it)
        self.streams[queue].append(it)
        self._commit(me, reads, writes)
        self.last_dma[id(sem_buf)] = me
        if is_output:
            self.out_tokens.append(me)
        return me

    def emit(self, final_engine="sp"):
        nc = self.nc
        ses = self.same_engine_sync
        for it in self.ins:
            for d in it.deps:
                dd = self.ins[d]
                if dd.is_dma:
                    continue
                if dd.eng == it.eng and not it.is_dma and (dd.eng == "pe" or not ses):
                    continue
                dd.needs_inc = True
        n_sems = {}
        for e in COMPUTE:
            c = 0
            for it in self.streams[e]:
                if it.is_dma:
                    continue
                if it.needs_inc:
                    c += 1
                    it.semval = c
            n_sems[e] = max(c - 1, 0) // SEM_ROT + 1
        with contextlib.ExitStack() as st:
            eng_sems = {e: [st.enter_context(nc.semaphore(f"s_{e}{k}")) for k in range(n_sems[e])] for e in COMPUTE}
            for i, b in enumerate(self.dma_bufs):
                b.dsem = st.enter_context(nc.semaphore(f"d{i}_{b.name}"))

            def token(it):
                if it.is_dma:
                    return (it.sem.dsem, it.semval, ("d", id(it.sem)))
                k = (it.semval - 1) // SEM_ROT
                return (eng_sems[it.eng][k], it.semval - k * SEM_ROT, (it.eng, k))

            def run(e, eng):
                known = {}
                for it in self.streams[e]:
                    need = {}
                    for d in it.deps:
                        dd = self.ins[d]
                        if not dd.is_dma and dd.eng == e and not it.is_dma and (e == "pe" or not ses):
                            continue
                        sem, val, key = token(dd)
                        if known.get(key, 0) >= val:
                            continue
                        if key not in need or need[key][1] < val:
                            need[key] = (sem, val)
                    for key, (sem, val) in need.items():
                        eng.wait_ge(sem, val)
                        known[key] = val
                    h = it.fn(eng)
                    if it.is_dma:
                        h.then_inc(it.sem.dsem, 16)
                    elif it.needs_inc:
                        k = (it.semval - 1) // SEM_ROT
                        h.then_inc(eng_sems[it.eng][k], 1)
                if e == final_engine:
                    need = {}
                    for d in self.out_tokens:
                        sem, val, key = token(self.ins[d])
                        if key not in need or need[key][1] < val:
                            need[key] = (sem, val)
                    for key, (sem, val) in need.items():
                        eng.wait_ge(sem, val)

            with nc.Block() as block:
                @block.sync
                def _(eng):
                    run("sp", eng)

                @block.tensor
                def _(eng):
                    run("pe", eng)

                @block.scalar
                def _(eng):
                    run("act", eng)

                @block.vector
                def _(eng):
                    run("dve", eng)

                @block.gpsimd
                def _(eng):
                    run("pool", eng)


class TT:
    __slots__ = ("t", "b")

    def __init__(self, t, b):
        self.t = t
        self.b = b


def _t5_bucket(n):
    n = np.maximum(n, 0)
    nf = np.maximum(n, 1).astype(np.float32)
    large = 16 + (np.log(nf / np.float32(16)) / np.float32(math.log(8)) * np.float32(16)).astype(np.int32)
    large = np.minimum(large, 31)
    return np.where(n < 16, n, large)


def _band_tables(c):
    ki = np.arange(128)[:, None]
    qi = np.arange(128)[None, :]
    bk = np.zeros((9, 128, 128), np.int64)
    mk = np.zeros((9, 128, 128), np.float32)
    for bi, b in enumerate(range(-1, 8)):
        rel = 128 * (c - b) + qi - ki
        bk[bi] = _t5_bucket(rel)
        mk[bi] = np.where(rel >= 0, 0.0, -30000.0)
    return bk, mk


class Builder:
    def __init__(self, stop_after=None, upto=None):
        self.stop_after = stop_after
        self.upto = upto
        self.nc = bass.Bass("TRN2", target_bir_lowering=False)
        self.S = Sched(self.nc)
        self.n = 0

    def din(self, name, shape, dt=F32):
        if self.upto and self.upto.startswith("peeronly") and name in ("x_all", "w_in", "w_conv_out", "w_attn_out", "w_mix_out", "w_cq", "w_ckv", "w_co", "mem"):
            shape = [1, 1]
        if not hasattr(self, "in_shapes"):
            self.in_shapes = {}
        self.in_shapes[name] = (tuple(shape), dt)
        return self.nc.dram_tensor(name, list(shape), dt, kind="ExternalInput").ap()

    def dout(self, name, shape, dt=F32):
        return self.nc.dram_tensor(name, list(shape), dt, kind="ExternalOutput").ap()

    def dscr(self, name, shape, dt):
        return self.nc.dram_tensor(name, list(shape), dt, kind="Internal").ap()

    def _arena_init(self):
        rem = self.nc.sbuf_bytes_remaining
        rem = rem() if callable(rem) else rem
        size = (int(rem) - 256) // 64 * 64
        beg, end = self.nc.bump_sbuf(size)
        self.a_beg = (int(beg) + 63) // 64 * 64
        self.a_end = int(end)
        self.a_ptr = self.a_beg
        self.a_marked = set()

    def _reset_ptr(self, mark, key):
        self.a_ptr = mark
        self.a_marked.discard(key)

    def sb(self, st, name, shape, dt, at=None):
        if id(st) not in self.a_marked:
            self.a_marked.add(id(st))
            st.callback(self._reset_ptr, self.a_ptr, id(st))
        self.n += 1
        nm = f"{name}_{self.n}"
        nbytes = int(np.prod(shape[1:])) * mybir.dt.size(dt)
        nbytes = (nbytes + 63) // 64 * 64
        if at is None:
            off = self.a_ptr
            self.a_ptr += nbytes
        else:
            off = at
        assert off + nbytes <= self.a_end, (name, off, nbytes, self.a_end)
        self.a_peak = max(getattr(self, "a_peak", 0), off + nbytes - self.a_beg)
        t = self.nc.alloc_sbuf_tensor_at(nm, list(shape), dt, offset=off)
        return TT(t, Buf(nm))

    def ps(self, st, name, shape, dt):
        self.n += 1
        nm = f"{name}_{self.n}"
        return TT(st.enter_context(self.nc.psum_tensor(nm, list(shape), dt)), Buf(nm))

    def build(self):
        nc, S = self.nc, self.S
        din, dout = self.din, self.dout
        I = {}
        I["x_all"] = din("x_all", [SEQ, D])
        I["x_own"] = din("x_own", [NT_OWN * 128, D])
        I["x_halo"] = din("x_halo", [32, D])
        I["mem"] = din("mem", [256, D])
        I["w_in"] = din("w_in", [D, 8192])
        I["gcols"] = din("gcols", [128, 4, 8])
        I["final_g_rep"] = din("final_g_rep", [128, D])
        I["conv_wT"] = din("conv_wT", [128, 8, 3])
        I["w_conv_out"] = din("w_conv_out", [D, D])
        I["lam_rep"] = din("lam_rep", [128, 4, 64])
        I["subln_rep"] = din("subln_rep", [128, 128])
        I["w_attn_out"] = din("w_attn_out", [D, D])
        I["w_mix_out"] = din("w_mix_out", [D, D])
        I["rb31_rep"] = din("rb31_rep", [128, 8])
        I["gbias"] = din("gbias", [8, 128, 9, 128])
        I["mband"] = din("mband", [128, 9, 128])
        I["w_cq"] = din("w_cq", [D, D])
        I["w_ckv"] = din("w_ckv", [D, 2048])
        I["w_co"] = din("w_co", [D, D])
        I["w_pq"] = din("w_pq", [D, 2048])
        I["skT"] = din("skT", [16, 128, 128])
        I["peer_u"] = din("peer_u", [SEQ, D])
        I["peer_v"] = din("peer_v", [SEQ, D])
        I["ident_bf"] = din("ident_bf", [128, 128], BF16)
        I["ident_f"] = din("ident_f", [128, 128])
        I["iota_f"] = din("iota_f", [128, 128])
        self.I = I
        self.out = dout("out", [NT_OWN * 128, D])
        self.dumps = set(self.stop_after.split(",")) if self.stop_after else set()
        self.dbg = {}
        if "attn" in self.dumps:
            self.dbg["attn"] = dout("dbg_attn", [128, 8 * NT_OWN * 128])
        for nm in ("x1", "x2", "x3"):
            if nm in self.dumps:
                self.dbg[nm] = dout("dbg_" + nm, [NT_OWN * 128, D])
        self.UT_d = self.dscr("UT_d", [128, 128, 8, 128], BF16)
        self.Vb_d = self.dscr("Vb_d", [SEQ, D], BF16)
        self.B_UT = Buf("UT_d")
        self.B_Vb = Buf("Vb_d")
        self.KT_d = self.dscr("KT_d", [8, 128, SEQ], BF16)
        self.V_d = self.dscr("V_d", [8, 128, NT_ALL, 128], BF16)
        self.B_KT = Buf("KT_d")
        self.B_V = Buf("V_d")

        with contextlib.ExitStack() as gst:
            self.gst = gst
            self._arena_init()
            self.bank = [self.ps(gst, f"bank{i}", [128, 512], F32) for i in range(8)]
            self.setup_consts()
            if self.upto and self.upto.startswith("peeronly"):
                with contextlib.ExitStack() as rst:
                    p0 = self.a_ptr
                    self.xres = self.sb(rst, "xres", [128, NT_OWN, D], F32, at=p0 + 32 * 1024)
                    S.dma("sp", lambda e: e.dma_start(out=self.xres.t[:], in_=I["x_own"].rearrange("(m p) d -> p m d", p=128)), self.xres.b, writes=[self.xres.b])
                    self.peer_stop = self.upto.split(":")[1] if ":" in self.upto else None
                    self.phase_peer(p0)
                S.emit()
                return nc
            self.peer_stop = None
            self.phase_kv()
            S.barrier()
            self.phase_q_attn()
            S.barrier()
            if "attn" in self.dumps:
                self.dump_attn()
            with contextlib.ExitStack() as rst:
                p0 = self.a_ptr
                self.xres = self.sb(rst, "xres", [128, NT_OWN, D], F32, at=p0 + 32 * 1024)
                self.phase_mix(p0)
                S.barrier()
                if "x1" in self.dumps:
                    self.dump_res("x1")
                if self.upto != "mix":
                    self.phase_cross(p0)
                    S.barrier()
                    if "x2" in self.dumps:
                        self.dump_res("x2")
                    if self.upto != "cross":
                        self.phase_peer(p0)
            S.emit()
        return nc

    def setup_consts(self):
        S, gst, I = self.S, self.gst, self.I
        sb = lambda name, shape, dt: self.sb(gst, name, shape, dt)
        self.ident_bf = sb("ident_bf", [128, 128], BF16)
        self.ident_f = sb("ident_f", [128, 128], F32)
        self.iota_f = sb("iota_f", [128, 128], F32)
        self.gcols = sb("gcols", [128, 4, 8], F32)
        self.lam = sb("lam", [128, 4], F32)
        self.subg = sb("subg", [128, 128], F32)
        self.rb31 = sb("rb31", [128, 8], F32)
        self.mhalf = sb("mhalf", [128, 4], F32)
        self.pA = self.a_ptr
        self.bias = sb("bias", [128, 8, 9, 128], BF16)
        self.outT = sb("outT", [128, 8, NT_OWN * 128], BF16)
        S.op("pool", lambda e: e.memset(self.mhalf.t[:], -0.5), writes=[self.mhalf.b])
        cset = Buf("consts")
        for tt, src in ((self.ident_bf, I["ident_bf"]), (self.ident_f, I["ident_f"]), (self.iota_f, I["iota_f"]),
                        (self.gcols, I["gcols"]), (self.subg, I["subln_rep"]), (self.rb31, I["rb31_rep"])):
            S.dma("sp", lambda e, tt=tt, src=src: e.dma_start(out=tt.t[:], in_=src), cset, writes=[tt.b])
        with contextlib.ExitStack() as st:
            lamin = self.sb(st, "lamin", [128, 4, 64], F32)
            prod = self.sb(st, "lamprod", [128, 2, 64], F32)
            red = self.sb(st, "lamred", [128, 2], F32)
            ex = self.sb(st, "lamex", [128, 2], F32)
            mb = self.sb(st, "mband", [128, 9, 128], F32)
            gb = [self.sb(st, f"gb{i}", [128, 9, 128], F32) for i in range(2)]
            S.dma("sp", lambda e: e.dma_start(out=lamin.t[:], in_=I["lam_rep"]), cset, writes=[lamin.b])
            S.dma("sp", lambda e: e.dma_start(out=mb.t[:], in_=I["mband"]), cset, writes=[mb.b])
            S.op("dve", lambda e: e.tensor_tensor(out=prod.t[:, 0, :], in0=lamin.t[:, 0, :], in1=lamin.t[:, 1, :], op=ALU.mult), reads=[lamin.b], writes=[prod.b])
            S.op("dve", lambda e: e.tensor_tensor(out=prod.t[:, 1, :], in0=lamin.t[:, 2, :], in1=lamin.t[:, 3, :], op=ALU.mult), reads=[lamin.b], writes=[prod.b])
            S.op("dve", lambda e: e.tensor_reduce(out=red.t[:], in_=prod.t[:], axis=AX.X, op=ALU.add), reads=[prod.b], writes=[red.b])
            S.op("act", lambda e: e.activation(out=ex.t[:], in_=red.t[:], func=AF.Exp), reads=[red.b], writes=[ex.b])
            S.op("dve", lambda e: e.tensor_tensor(out=self.lam.t[:, 0:1], in0=ex.t[:, 0:1], in1=ex.t[:, 1:2], op=ALU.subtract), reads=[ex.b], writes=[self.lam.b])
            S.op("dve", lambda e: e.tensor_scalar(out=self.lam.t[:, 0:1], in0=self.lam.t[:, 0:1], scalar1=0.2, scalar2=None, op0=ALU.add), reads=[self.lam.b], writes=[self.lam.b])
            S.op("dve", lambda e: e.tensor_scalar(out=self.lam.t[:, 1:2], in0=self.lam.t[:, 0:1], scalar1=-1.0, scalar2=None, op0=ALU.mult), reads=[self.lam.b], writes=[self.lam.b])
            S.op("dve", lambda e: e.tensor_scalar(out=self.subg.t[:], in0=self.subg.t[:], scalar1=0.8, scalar2=None, op0=ALU.mult), reads=[self.subg.b], writes=[self.subg.b])
            for h in range(8):
                g = gb[h % 2]
                S.dma("sp", lambda e, g=g, h=h: e.dma_start(out=g.t[:], in_=I["gbias"][h]), g.b, writes=[g.b])
                S.op("dve", lambda e, g=g, h=h: e.scalar_tensor_tensor(out=self.bias.t[:, h, :, :], in0=g.t[:], scalar=self.rb31.t[:, h:h + 1], in1=mb.t[:], op0=ALU.subtract, op1=ALU.add),
                     reads=[g.b, self.rb31.b, mb.b], writes=[self.bias.b])
            S.barrier()

    def load_weight(self, st_tiles, dst, col0, ncols, w_ap, gsel, kcs=range(8), queue="sp"):
        S = self.S
        for kc in kcs:
            stg = st_tiles[kc % len(st_tiles)]
            S.dma(queue, lambda e, stg=stg, kc=kc: e.dma_start(out=stg.t[:, 0:ncols], in_=w_ap[kc * 128:(kc + 1) * 128, col0:col0 + ncols]), stg.b, writes=[stg.b])
            if gsel is None:
                S.op("pool", lambda e, stg=stg, kc=kc: e.tensor_copy(out=dst.t[:, kc, 0:ncols], in_=stg.t[:, 0:ncols]), reads=[stg.b], writes=[dst.b])
            else:
                S.op("pool", lambda e, stg=stg, kc=kc: e.tensor_scalar(out=dst.t[:, kc, 0:ncols], in0=stg.t[:, 0:ncols], scalar1=self.gcols.t[:, gsel, kc:kc + 1], scalar2=None, op0=ALU.mult),
                     reads=[stg.b, self.gcols.b], writes=[dst.b])

    def norm_tiles(self, x_ap_fn, tiles, xt, sqj, ss, rstd, xn, eps=1e-6):
        S = self.S
        n = len(tiles)
        for i, (tid, xts, xns) in enumerate(tiles):
            S.dma("sp", lambda e, xts=xts, tid=tid: e.dma_start(out=xts.t[:], in_=x_ap_fn(tid)), xts.b, writes=[xts.b])
            S.op("act", lambda e, xts=xts, i=i: e.activation(out=sqj.t[:], in_=xts.t[:], func=AF.Square, accum_out=ss.t[:, i:i + 1]), reads=[xts.b], writes=[sqj.b, ss.b])
        S.op("dve", lambda e: e.tensor_scalar(out=rstd.t[:, 0:n], in0=ss.t[:, 0:n], scalar1=1.0 / D, scalar2=eps, op0=ALU.mult, op1=ALU.add), reads=[ss.b], writes=[rstd.b])
        S.op("pool", lambda e: e.tensor_tensor(out=rstd.t[:, 0:n], in0=rstd.t[:, 0:n], in1=self.mhalf.t[:, 0:n], op=ALU.pow), reads=[rstd.b, self.mhalf.b], writes=[rstd.b])
        for i, (tid, xts, xns) in enumerate(tiles):
            S.op("dve", lambda e, xts=xts, xns=xns, i=i: e.tensor_scalar(out=xns.t[:], in0=xts.t[:], scalar1=rstd.t[:, i:i + 1], scalar2=None, op0=ALU.mult), reads=[xts.b, rstd.b], writes=[xns.b])

    def transpose_tiles(self, tiles_xn, hT, tbanks, cnt0, gsel=None, col0=0, npart=128):
        S = self.S
        if gsel is not None or npart != 128:
            for i, xns in enumerate(tiles_xn):
                bk = tbanks[(cnt0 + i) % len(tbanks)]
                pt = bk.t.bitcast(BF16)
                for kc in range(8):
                    S.op("pe", lambda e, pt=pt, xns=xns, kc=kc: e.transpose(out=pt[:, kc * 128:kc * 128 + npart], in_=xns.t[0:npart, kc * 128:(kc + 1) * 128], identity=self.ident_bf.t[0:npart, 0:npart]),
                         reads=[xns.b, self.ident_bf.b], writes=[bk.b])
                for kc in range(8):
                    c0 = col0 + i * npart
                    if gsel is None:
                        S.op("act", lambda e, pt=pt, kc=kc, c0=c0: e.copy(out=hT.t[:, kc, c0:c0 + npart], in_=pt[:, kc * 128:kc * 128 + npart]), reads=[bk.b], writes=[hT.b])
                    else:
                        S.op("act", lambda e, pt=pt, kc=kc, c0=c0: e.activation(out=hT.t[:, kc, c0:c0 + npart], in_=pt[:, kc * 128:kc * 128 + npart], func=AF.Copy, scale=self.gcols.t[:, gsel, kc:kc + 1]),
                             reads=[bk.b, self.gcols.b], writes=[hT.b])
            return
        for i, xns in enumerate(tiles_xn):
            bk = tbanks[(cnt0 + i) % len(tbanks)]
            pt = bk.t.bitcast(BF16)
            for kc in range(8):
                S.op("pe", lambda e, pt=pt, xns=xns, kc=kc: e.transpose(out=pt[:, kc * 128:(kc + 1) * 128], in_=xns.t[:, kc * 128:(kc + 1) * 128], identity=self.ident_bf.t[:]),
                     reads=[xns.b, self.ident_bf.b], writes=[bk.b])
            S.op("act", lambda e, pt=pt, i=i: e.copy(out=hT.t[:, :, i * 128:(i + 1) * 128], in_=pt[:, :].rearrange("p (k t) -> p k t", k=8)), reads=[bk.b], writes=[hT.b])

    def phase_kv(self):
        S, I = self.S, self.I
        with contextlib.ExitStack() as st:
            sb = lambda name, shape, dt: self.sb(st, name, shape, dt)
            wk = sb("wk", [128, 8, 1024], BF16)
            wv = sb("wv", [128, 8, 1024], BF16)
            wst = [sb(f"wst{i}", [128, 1024], F32) for i in range(2)]
            self.load_weight(wst, wk, 4096, 1024, I["w_in"], 0)
            self.load_weight(wst, wv, 5120, 1024, I["w_in"], 0)
            xt = [sb(f"xt{i}", [128, 1024], F32) for i in range(8)]
            xn = [sb(f"xn{i}", [128, 1024], BF16) for i in range(12)]
            sqj = sb("sqj", [128, 1024], BF16)
            ss = [sb(f"ss{i}", [128, 4], F32) for i in range(3)]
            rstd = [sb(f"rstd{i}", [128, 4], F32) for i in range(3)]
            hT = [sb(f"hT{i}", [128, 8, 512], BF16) for i in range(2)]
            kst = [sb(f"kst{i}", [128, 8, 512], BF16) for i in range(2)]
            vst = [sb(f"vst{i}", [128, 4, 1024], BF16) for i in range(2)]
            tb = self.bank[0:2]
            mb = self.bank[2:8]
            NB = NT_ALL // 4
            x_all = I["x_all"]

            def A(bi):
                tiles = [(bi * 4 + i, xt[(bi * 4 + i) % 8], xn[(bi * 4 + i) % 12]) for i in range(4)]
                self.norm_tiles(lambda tid: x_all[tid * 128:(tid + 1) * 128, :], tiles, xt, sqj, ss[bi % 3], rstd[bi % 3], xn)

            def Bs(bi):
                self.transpose_tiles([xn[(bi * 4 + i) % 12] for i in range(4)], hT[bi % 2], tb, bi * 4)

            cnt = [0]

            def C(bi):
                h = hT[bi % 2]
                ks, vs = kst[bi % 2], vst[bi % 2]
                for j in range(8):
                    bk = mb[cnt[0] % 6]
                    cnt[0] += 1
                    for kc in range(8):
                        S.op("pe", lambda e, bk=bk, kc=kc, j=j: e.matmul(bk.t[:], lhsT=wk.t[:, kc, j * 128:(j + 1) * 128], rhs=h.t[:, kc, :], start=(kc == 0), stop=(kc == 7)),
                             reads=[wk.b, h.b], writes=[bk.b])
                    eng = "act" if j % 2 == 0 else "dve"
                    if eng == "act":
                        S.op("act", lambda e, bk=bk, j=j: e.copy(out=ks.t[:, j, :], in_=bk.t[:]), reads=[bk.b], writes=[ks.b])
                    else:
                        S.op("dve", lambda e, bk=bk, j=j: e.tensor_copy(out=ks.t[:, j, :], in_=bk.t[:]), reads=[bk.b], writes=[ks.b])
                S.dma("sp", lambda e: e.dma_start(out=self.KT_d[:, :, bi * 512:(bi + 1) * 512].rearrange("h p t -> p h t"), in_=ks.t[:]), ks.b, reads=[ks.b], writes=[self.B_KT])
                for i in range(4):
                    for hh in range(2):
                        bk = mb[cnt[0] % 6]
                        cnt[0] += 1
                        for kc in range(8):
                            S.op("pe", lambda e, bk=bk, kc=kc, i=i, hh=hh: e.matmul(bk.t[:], lhsT=h.t[:, kc, i * 128:(i + 1) * 128], rhs=wv.t[:, kc, hh * 512:(hh + 1) * 512], start=(kc == 0), stop=(kc == 7)),
                                 reads=[wv.b, h.b], writes=[bk.b])
                        if hh == 0:
                            S.op("act", lambda e, bk=bk, i=i, hh=hh: e.copy(out=vs.t[:, i, hh * 512:(hh + 1) * 512], in_=bk.t[:]), reads=[bk.b], writes=[vs.b])
                        else:
                            S.op("dve", lambda e, bk=bk, i=i, hh=hh: e.tensor_copy(out=vs.t[:, i, hh * 512:(hh + 1) * 512], in_=bk.t[:]), reads=[bk.b], writes=[vs.b])
                    S.dma("act", lambda e, i=i: e.dma_start(out=self.V_d[:, :, bi * 4 + i, :].rearrange("h p e -> p h e"), in_=vs.t[:, i, :].rearrange("p (h e) -> p h e", h=8)),
                          vs.b, reads=[vs.b], writes=[self.B_V])

            A(0)
            A(1)
            Bs(0)
            for bi in range(NB):
                if bi + 2 < NB:
                    A(bi + 2)
                if bi + 1 < NB:
                    Bs(bi + 1)
                C(bi)

    def phase_q_attn(self):
        S, I = self.S, self.I
        with contextlib.ExitStack() as st:
            sb = lambda name, shape, dt: self.sb(st, name, shape, dt)
            QT = sb("QT", [128, 8, NT_OWN * 128], BF16)
            with contextlib.ExitStack() as st2:
                sb2 = lambda name, shape, dt: self.sb(st2, name, shape, dt)
                wq = sb2("wq", [128, 8, 1024], BF16)
                wst = [sb2(f"wst{i}", [128, 1024], F32) for i in range(2)]
                self.load_weight(wst, wq, 3072, 1024, I["w_in"], 0)
                xt = [sb2(f"xt{i}", [128, 1024], F32) for i in range(4)]
                xn = [sb2(f"xn{i}", [128, 1024], BF16) for i in range(4)]
                sqj = sb2("sqj", [128, 1024], BF16)
                ss = sb2("ss", [128, 4], F32)
                rstd = sb2("rstd", [128, 4], F32)
                hT = sb2("hT", [128, 8, 512], BF16)
                x_own = I["x_own"]
                cnt = 0
                for bi in range(4):
                    tiles = [(bi * 4 + i, xt[i], xn[i]) for i in range(4)]
                    self.norm_tiles(lambda tid: x_own[tid * 128:(tid + 1) * 128, :], tiles, xt, sqj, ss, rstd, xn)
                    self.transpose_tiles(xn, hT, self.bank[0:2], bi * 4)
                    for j in range(8):
                        bk = self.bank[2 + cnt % 6]
                        cnt += 1
                        for kc in range(8):
                            S.op("pe", lambda e, bk=bk, kc=kc, j=j: e.matmul(bk.t[:], lhsT=wq.t[:, kc, j * 128:(j + 1) * 128], rhs=hT.t[:, kc, :], start=(kc == 0), stop=(kc == 7)),
                                 reads=[wq.b, hT.b], writes=[bk.b])
                        S.op("act", lambda e, bk=bk, j=j, bi=bi: e.activation(out=QT.t[:, j, bi * 512:(bi + 1) * 512], in_=bk.t[:], func=AF.Copy, scale=0.125), reads=[bk.b], writes=[QT.b])
                S.barrier()
            KT = [sb(f"KT{i}", [128, 4096], BF16) for i in range(4)]
            V1 = [sb(f"V1{i}", [128, 32, 130], BF16) for i in range(4)]
            E = [sb(f"E{i}", [128, 512], BF16) for i in range(3)]
            tmp0 = [sb(f"tmp0{i}", [128, 128], F32) for i in range(4)]
            oc = sb("oc", [128, 128], F32)
            on = sb("on", [128, 128], BF16)
            sqj = sb("sqj2", [128, 128], F32)
            sm = sb("sm", [128, 8], F32)
            for v in V1:
                S.op("pool", lambda e, v=v: e.memset(v.t[:, :, 128:130], 1.0), writes=[v.b])
            sbanks = self.bank[0:3]
            obanks = self.bank[3:7]
            tbank = self.bank[7]
            oacc = [(obanks[j], 0, obanks[j].b) for j in range(4)]
            scnt = 0
            for h in range(8):
                for q in range(4):
                    S.dma("sp", lambda e, q=q, h=h: e.dma_start(out=KT[q].t[:], in_=self.KT_d[h, :, q * 4096:(q + 1) * 4096]), KT[q].b, reads=[self.B_KT], writes=[KT[q].b])
                    S.dma("act", lambda e, q=q, h=h: e.dma_start(out=V1[q].t[:, :, 0:128], in_=self.V_d[h, :, q * 32:(q + 1) * 32, :]), V1[q].b, reads=[self.B_V], writes=[V1[q].b])
                for g in (3, 2, 1, 0):
                    for c in range(2):
                        nk = 32 * g + 32
                        for kt in range(nk):
                            jmin = max(0, -((-(kt - 32 * g - 7)) // 8))
                            band = {}
                            for j in range(jmin, 4):
                                b = kt - (32 * g + 8 * j)
                                if -1 <= b <= 7:
                                    band[j] = b
                            sbk = sbanks[scnt % 3]
                            Et = E[scnt % 3]
                            scnt += 1
                            kq, kl = kt // 32, kt % 32
                            lhs = KT[kq].t[c * 64:(c + 1) * 64, kl * 128:(kl + 1) * 128]
                            plain = [j for j in range(jmin, 4) if j not in band]
                            for j, b in band.items():
                                col = (j - jmin) * 128
                                S.op("pe", lambda e, sbk=sbk, col=col, b=b, h=h: e.matmul(sbk.t[:, col:col + 128], lhsT=self.ident_bf.t[:], rhs=self.bias.t[:, h, b + 1, :], start=True, stop=False),
                                     reads=[self.ident_bf.b, self.bias.b], writes=[sbk.b])
                                S.op("pe", lambda e, sbk=sbk, col=col, lhs=lhs, j=j, g=g, h=h, c=c: e.matmul(sbk.t[:, col:col + 128], lhsT=lhs, rhs=QT.t[c * 64:(c + 1) * 64, h, (4 * g + j) * 128:(4 * g + j + 1) * 128], start=False, stop=True),
                                     reads=[KT[kq].b, QT.b], writes=[sbk.b])
                            if plain:
                                j0 = plain[0]
                                assert plain == list(range(j0, 4))
                                col = (j0 - jmin) * 128
                                ncol = (4 - j0) * 128
                                S.op("pe", lambda e, sbk=sbk, col=col, ncol=ncol, lhs=lhs, j0=j0, g=g, h=h, c=c: e.matmul(sbk.t[:, col:col + ncol], lhsT=lhs, rhs=QT.t[c * 64:(c + 1) * 64, h, (4 * g + j0) * 128:(4 * g + 4) * 128], start=True, stop=True),
                                     reads=[KT[kq].b, QT.b], writes=[sbk.b])
                            nact = (4 - jmin) * 128
                            S.op("act", lambda e, sbk=sbk, Et=Et, nact=nact: e.activation(out=Et.t[:, 0:nact], in_=sbk.t[:, 0:nact], func=AF.Exp), reads=[sbk.b], writes=[Et.b])
                            for j in range(jmin, 4):
                                ob, oo, obuf = oacc[j]
                                col = (j - jmin) * 128
                                last = (kt == 32 * g + 8 * j + 7)
                                S.op("pe", lambda e, ob=ob, oo=oo, Et=Et, col=col, kq=kq, kl=kl, kt=kt, last=last, j=j: e.matmul(ob.t[:, oo:oo + 130], lhsT=Et.t[:, col:col + 128], rhs=V1[kq].t[:, kl, :], start=(kt == 0), stop=last),
                                     reads=[Et.b, V1[kq].b], writes=[obuf])
                        for j in range(4):
                            ob, oo, obuf = oacc[j]
                            m = 4 * g + j
                            if c == 0:
                                S.op("dve", lambda e, ob=ob, oo=oo, j=j: e.reciprocal(out=sm.t[:, j:j + 1], in_=ob.t[:, oo + 128:oo + 129]), reads=[obuf], writes=[sm.b])
                                S.op("dve", lambda e, ob=ob, oo=oo, j=j: e.tensor_scalar(out=tmp0[j].t[:], in0=ob.t[:, oo:oo + 128], scalar1=sm.t[:, j:j + 1], scalar2=None, op0=ALU.mult), reads=[obuf, sm.b], writes=[tmp0[j].b])
                            else:
                                S.op("dve", lambda e, ob=ob, oo=oo, j=j: e.reciprocal(out=sm.t[:, 4 + j:5 + j], in_=ob.t[:, oo + 128:oo + 129]), reads=[obuf], writes=[sm.b])
                                S.op("dve", lambda e, j=j: e.tensor_scalar(out=sm.t[:, 4 + j:5 + j], in0=sm.t[:, 4 + j:5 + j], scalar1=self.lam.t[:, 1:2], scalar2=None, op0=ALU.mult), reads=[sm.b, self.lam.b], writes=[sm.b])
                                S.op("dve", lambda e, ob=ob, oo=oo, j=j: e.scalar_tensor_tensor(out=oc.t[:], in0=ob.t[:, oo:oo + 128], scalar=sm.t[:, 4 + j:5 + j], in1=tmp0[j].t[:], op0=ALU.mult, op1=ALU.add),
                                     reads=[obuf, sm.b, tmp0[j].b], writes=[oc.b])
                                if self.stop_after == "attn_raw":
                                    pass
                                S.op("dve", lambda e: e.scalar_tensor_tensor(out=sqj.t[:], in0=oc.t[:], scalar=1.0, in1=oc.t[:], op0=ALU.mult, op1=ALU.mult, accum_out=sm.t[:, 0:1]), reads=[oc.b], writes=[sqj.b, sm.b])
                                S.op("dve", lambda e: e.tensor_scalar(out=sm.t[:, 0:1], in0=sm.t[:, 0:1], scalar1=1.0 / 128, scalar2=1e-5, op0=ALU.mult, op1=ALU.add), reads=[sm.b], writes=[sm.b])
                                S.op("pool", lambda e: e.tensor_tensor(out=sm.t[:, 0:1], in0=sm.t[:, 0:1], in1=self.mhalf.t[:, 0:1], op=ALU.pow), reads=[sm.b, self.mhalf.b], writes=[sm.b])
                                S.op("dve", lambda e: e.scalar_tensor_tensor(out=on.t[:], in0=oc.t[:], scalar=sm.t[:, 0:1], in1=self.subg.t[:], op0=ALU.mult, op1=ALU.mult),
                                     reads=[oc.b, sm.b, self.subg.b], writes=[on.b])
                                pt = tbank.t.bitcast(BF16)
                                S.op("pe", lambda e, pt=pt: e.transpose(out=pt[:, 0:128], in_=on.t[:], identity=self.ident_bf.t[:]), reads=[on.b, self.ident_bf.b], writes=[tbank.b])
                                S.op("act", lambda e, pt=pt, h=h, m=m: e.copy(out=self.outT.t[:, h, m * 128:(m + 1) * 128], in_=pt[:, 0:128]), reads=[tbank.b], writes=[self.outT.b])

    def dump_attn(self):
        S = self.S
        with contextlib.ExitStack() as st:
            f = self.sb(st, "dumpf", [128, 8, NT_OWN * 128], F32)
            S.op("dve", lambda e: e.tensor_copy(out=f.t[:], in_=self.outT.t[:]), reads=[self.outT.b], writes=[f.b])
            S.dma("sp", lambda e: e.dma_start(out=self.dbg["attn"], in_=f.t[:].rearrange("p h t -> p (h t)")), f.b, reads=[f.b], is_output=True)


def make_in_maps(inp):
    f32 = np.float32
    x = np.ascontiguousarray(inp["x"][0], dtype=f32)
    xt = x.reshape(16, 8, 128, D)
    common = {
        "x_all": x,
        "mem": np.ascontiguousarray(inp["mem"][0], dtype=f32),
        "w_in": np.ascontiguousarray(inp["w_in"][0], dtype=f32),
        "gcols": np.ascontiguousarray(np.stack([inp["norm_mix_g"][0], inp["norm_cross_g"][0], inp["norm_mem_g"][0], inp["norm_ffn_g"][0]], 0).reshape(4, 8, 128).transpose(2, 0, 1), dtype=f32),
        "final_g_rep": np.ascontiguousarray(np.broadcast_to(inp["final_g"][None, :], (128, D)), dtype=f32),
        "conv_wT": np.ascontiguousarray(inp["conv_w"][0].reshape(3, 8, 128).transpose(2, 1, 0), dtype=f32),
        "w_conv_out": np.ascontiguousarray(inp["w_conv_out"][0], dtype=f32),
        "lam_rep": np.ascontiguousarray(np.broadcast_to(np.stack([inp["lambda_q1"][0], inp["lambda_k1"][0], inp["lambda_q2"][0], inp["lambda_k2"][0]], 0)[None], (128, 4, 64)), dtype=f32),
        "subln_rep": np.ascontiguousarray(np.broadcast_to(inp["subln_g"][0][None, :], (128, 128)), dtype=f32),
        "w_attn_out": np.ascontiguousarray(inp["w_attn_out"][0], dtype=f32),
        "w_mix_out": np.ascontiguousarray(inp["w_mix_out"][0], dtype=f32),
        "rb31_rep": np.ascontiguousarray(np.broadcast_to(inp["rel_bias"][31][None, :], (128, 8)), dtype=f32),
        "w_cq": np.ascontiguousarray(inp["w_cq"][0], dtype=f32),
        "w_ckv": np.ascontiguousarray(inp["w_ckv"][0], dtype=f32),
        "w_co": np.ascontiguousarray(inp["w_co"][0], dtype=f32),
        "w_pq": np.ascontiguousarray(inp["w_pq"][0], dtype=f32),
        "skT": np.ascontiguousarray(inp["sub_keys"][0].transpose(1, 0, 3, 2).reshape(16, 128, 128), dtype=f32),
        "peer_u": np.ascontiguousarray(inp["peer_u"][0], dtype=f32),
        "peer_v": np.ascontiguousarray(inp["peer_v"][0], dtype=f32),
        "ident_bf": np.eye(128, dtype=f32).astype(ml_dtypes.bfloat16),
        "ident_f": np.eye(128, dtype=f32),
        "iota_f": np.ascontiguousarray(np.broadcast_to(np.arange(128, dtype=f32)[None, :], (128, 128))),
    }
    rel_bias = np.asarray(inp["rel_bias"], dtype=f32)
    maps = []
    for c in range(NCORE):
        m = dict(common)
        m["x_own"] = np.ascontiguousarray(xt[:, c].reshape(NT_OWN * 128, D))
        halo = np.zeros((16, 2, D), f32)
        for mm in range(16):
            t0 = (8 * mm + c) * 128
            if t0 >= 2:
                halo[mm] = x[t0 - 2:t0]
        m["x_halo"] = halo.reshape(32, D)
        bk, mk = _band_tables(c)
        gb = rel_bias[bk]
        m["gbias"] = np.ascontiguousarray(gb.transpose(3, 1, 0, 2), dtype=f32)
        m["mband"] = np.ascontiguousarray(mk.transpose(1, 0, 2), dtype=f32)
        maps.append(m)
    return maps


_CACHE = {}


def kernel(**inputs):
    stop = inputs.pop("_stop_after", None)
    upto = inputs.pop("_upto", None)
    key = (stop, upto)
    if key not in _CACHE:
        b = Builder(stop_after=stop, upto=upto)
        _CACHE[key] = (b.build(), b.in_shapes)
    nc, in_shapes = _CACHE[key]
    maps = make_in_maps(inputs)
    if upto and upto.startswith("peeronly"):
        m = maps[3]
        for nm, (shp, dt) in in_shapes.items():
            if tuple(m[nm].shape) != tuple(shp):
                m[nm] = np.zeros(shp, np.float32)
        res = run_bass_kernel_spmd(nc, [m], core_ids=[0])
        _CACHE["last_res"] = res
        return res
    res = run_bass_kernel_spmd(nc, maps, core_ids=list(range(NCORE)))
    if stop:
        _CACHE["last_res"] = res
    name = "out"
    out = np.zeros((16, 8, 128, D), np.float32)
    for c in range(NCORE):
        out[:, c] = np.asarray(res.results[c][name], dtype=np.float32).reshape(16, 128, D)
    return out.reshape(1, SEQ, D)


def _bcast_ap(t, offset, dims):
    base = t[:]
    return bass.AP(t, offset, [list(base.ap[0])] + [list(d) for d in dims])


def _norm_res(self, tiles, sqj, ss, rstd, eps=1e-6):
    S = self.S
    n = len(tiles)
    for i, (src, sbuf, xns) in enumerate(tiles):
        S.op("act", lambda e, src=src, i=i: e.activation(out=sqj.t[:], in_=src, func=AF.Square, accum_out=ss.t[:, i:i + 1]), reads=[sbuf], writes=[sqj.b, ss.b])
    S.op("dve", lambda e: e.tensor_scalar(out=rstd.t[:, 0:n], in0=ss.t[:, 0:n], scalar1=1.0 / D, scalar2=eps, op0=ALU.mult, op1=ALU.add), reads=[ss.b], writes=[rstd.b])
    S.op("pool", lambda e: e.tensor_tensor(out=rstd.t[:, 0:n], in0=rstd.t[:, 0:n], in1=self.mhalf.t[:, 0:n], op=ALU.pow), reads=[rstd.b, self.mhalf.b], writes=[rstd.b])
    for i, (src, sbuf, xns) in enumerate(tiles):
        S.op("dve", lambda e, src=src, xns=xns, i=i: e.tensor_scalar(out=xns.t[:], in0=src, scalar1=rstd.t[:, i:i + 1], scalar2=None, op0=ALU.mult), reads=[sbuf, rstd.b], writes=[xns.b])


def _wslab(self, dst, w_ap, col0, ncols, queue="pool"):
    self.S.dma(queue, lambda e: e.dma_start(out=dst.t[:, :, 0:ncols], in_=w_ap[:, col0:col0 + ncols].rearrange("(kc p) c -> p kc c", p=128)), dst.b, writes=[dst.b])


def _phase_mix(self, p0):
    S, I = self.S, self.I
    K = 1024
    with contextlib.ExitStack() as st:
        self.a_ptr = p0
        sb = lambda name, shape, dt, at=None: self.sb(st, name, shape, dt, at=at)
        mergedT = sb("mergedT", [128, 8, 2048], BF16)
        hT = sb("hT", [128, 8, 2048], BF16)
        hTh = sb("hTh", [128, 8, 32], BF16)
        zT = sb("zT", [128, 8, 2048], BF16)
        with contextlib.ExitStack() as st2:
            sb2 = lambda name, shape, dt: self.sb(st2, name, shape, dt)
            xt = [sb2(f"xt{i}", [128, 1024], F32) for i in range(4)]
            xn = [sb2(f"xn{i}", [128, 1024], BF16) for i in range(4)]
            sqj = sb2("sqj", [128, 1024], BF16)
            ss = sb2("ss", [128, 4], F32)
            rstd = sb2("rstd", [128, 4], F32)
            x_own = I["x_own"]
            for bi in range(4):
                tiles = [(bi * 4 + i, xt[i], xn[i]) for i in range(4)]
                self.norm_tiles(lambda tid: x_own[tid * 128:(tid + 1) * 128, :], tiles, xt, sqj, ss, rstd, xn)
                self.transpose_tiles(xn, hT, self.bank[0:2], bi * 4, gsel=0, col0=bi * 512)
            hx, hn = xt[0], xn[0]
            S.dma("sp", lambda e: e.dma_start(out=hx.t[0:32, :], in_=I["x_halo"]), hx.b, writes=[hx.b])
            S.op("act", lambda e: e.activation(out=sqj.t[0:32, :], in_=hx.t[0:32, :], func=AF.Square, accum_out=ss.t[0:32, 0:1]), reads=[hx.b], writes=[sqj.b, ss.b])
            S.op("dve", lambda e: e.tensor_scalar(out=rstd.t[0:32, 0:1], in0=ss.t[0:32, 0:1], scalar1=1.0 / D, scalar2=1e-6, op0=ALU.mult, op1=ALU.add), reads=[ss.b], writes=[rstd.b])
            S.op("pool", lambda e: e.tensor_tensor(out=rstd.t[0:32, 0:1], in0=rstd.t[0:32, 0:1], in1=self.mhalf.t[0:32, 0:1], op=ALU.pow), reads=[rstd.b, self.mhalf.b], writes=[rstd.b])
            S.op("dve", lambda e: e.tensor_scalar(out=hn.t[0:32, :], in0=hx.t[0:32, :], scalar1=rstd.t[0:32, 0:1], scalar2=None, op0=ALU.mult), reads=[hx.b, rstd.b], writes=[hn.b])
            self.transpose_tiles([hn], hTh, self.bank[0:2], 0, gsel=0, col0=0, npart=32)
        S.barrier()
        with contextlib.ExitStack() as st2:
            sb2 = lambda name, shape, dt: self.sb(st2, name, shape, dt)
            wc3 = [[sb2(f"wc3_{i}_{k}", [128, 8, 128], BF16) for k in range(3)] for i in range(2)]
            cwT = sb2("cwT", [128, 8, 3], F32)
            S.dma("sp", lambda e: e.dma_start(out=cwT.t[:], in_=I["conv_wT"]), cwT.b, writes=[cwT.b])
            U2 = sb2("U2", [128, 16, 130], F32)
            ycv = sb2("ycv", [128, 16, 128], F32)
            cbs = sb2("cbs", [128, 2048], F32)
            ccs = [sb2(f"ccs{i}", [128, 512], F32) for i in range(2)]
            cch_ = sb2("cch", [128, 32], F32)
            cnt = 0
            for cch in range(8):
                w3 = wc3[cch % 2]
                for k in range(3):
                    _wslab(self, w3[k], I["w_in"], k * 1024 + cch * 128, 128)
                for tb in range(4):
                    pb = [self.bank[(cnt + k) % 8] for k in range(3)]
                    cnt += 3
                    for k in range(3):
                        for kc in range(8):
                            S.op("pe", lambda e, k=k, kc=kc, tb=tb, w3=w3, pb=pb: e.matmul(pb[k].t[:], lhsT=w3[k].t[:, kc, :], rhs=hT.t[:, kc, tb * 512:(tb + 1) * 512], start=(kc == 0), stop=(kc == 7)),
                                 reads=[w3[k].b, hT.b], writes=[pb[k].b])
                    S.op("act", lambda e, tb=tb, pb=pb: e.copy(out=cbs.t[:, tb * 512:(tb + 1) * 512], in_=pb[0].t[:]), reads=[pb[0].b], writes=[cbs.b])
                    cs = ccs[tb % 2]
                    S.op("act", lambda e, cs=cs, pb=pb: e.copy(out=cs.t[:], in_=pb[1].t[:]), reads=[pb[1].b], writes=[cs.b])
                    S.op("dve", lambda e, cs=cs, pb=pb, tb=tb: e.tensor_tensor(out=U2.t[:, tb * 4:(tb + 1) * 4, 2:130], in0=pb[2].t[:].rearrange("p (m t) -> p m t", m=4), in1=cs.t[:].rearrange("p (m t) -> p m t", m=4), op=ALU.mult),
                         reads=[pb[2].b, cs.b], writes=[U2.b])
                pb = [self.bank[(cnt + k) % 8] for k in range(2)]
                cnt += 2
                for k in range(2):
                    for kc in range(8):
                        S.op("pe", lambda e, k=k, kc=kc, w3=w3, pb=pb: e.matmul(pb[k].t[:, 0:32], lhsT=w3[k + 1].t[:, kc, :], rhs=hTh.t[:, kc, :], start=(kc == 0), stop=(kc == 7)),
                             reads=[w3[k + 1].b, hTh.b], writes=[pb[k].b])
                S.op("act", lambda e, pb=pb: e.copy(out=cch_.t[:], in_=pb[0].t[:, 0:32]), reads=[pb[0].b], writes=[cch_.b])
                S.op("dve", lambda e, pb=pb: e.tensor_tensor(out=U2.t[:, :, 0:2], in0=pb[1].t[:, 0:32].rearrange("p (m t) -> p m t", m=16), in1=cch_.t[:].rearrange("p (m t) -> p m t", m=16), op=ALU.mult),
                     reads=[pb[1].b, cch_.b], writes=[U2.b])
                S.op("dve", lambda e, cch=cch: e.tensor_scalar(out=ycv.t[:], in0=U2.t[:, :, 2:130], scalar1=cwT.t[:, cch, 2:3], scalar2=None, op0=ALU.mult), reads=[U2.b, cwT.b], writes=[ycv.b])
                S.op("dve", lambda e, cch=cch: e.scalar_tensor_tensor(out=ycv.t[:], in0=U2.t[:, :, 1:129], scalar=cwT.t[:, cch, 1:2], in1=ycv.t[:], op0=ALU.mult, op1=ALU.add), reads=[U2.b, cwT.b, ycv.b], writes=[ycv.b])
                S.op("dve", lambda e, cch=cch: e.scalar_tensor_tensor(out=ycv.t[:], in0=U2.t[:, :, 0:128], scalar=cwT.t[:, cch, 0:1], in1=ycv.t[:], op0=ALU.mult, op1=ALU.add), reads=[U2.b, cwT.b, ycv.b], writes=[ycv.b])
                S.op("pool", lambda e, cch=cch: e.tensor_tensor(out=zT.t[:, cch, :], in0=cbs.t[:], in1=ycv.t[:].rearrange("p m t -> p (m t)"), op=ALU.mult), reads=[cbs.b, ycv.b], writes=[zT.b])
        S.barrier()
        with contextlib.ExitStack() as st2:
            sb2 = lambda name, shape, dt: self.sb(st2, name, shape, dt)
            wsl = [[sb2(f"wsl{i}_{k}", [128, 8, 128], BF16) for k in range(4)] for i in range(2)]
            sg = [[sb2(f"sg{i}_{k}", [128, 512], F32) for k in range(2)] for i in range(2)]
            m1 = [sb2(f"m1_{i}", [128, 512], F32) for i in range(2)]
            m2 = [sb2(f"m2_{i}", [128, 512], F32) for i in range(2)]
            it = 0
            for dt_ in range(8):
                ws = wsl[dt_ % 2]
                _wslab(self, ws[0], I["w_conv_out"], dt_ * 128, 128)
                _wslab(self, ws[1], I["w_in"], 6144 + dt_ * 128, 128)
                _wslab(self, ws[2], I["w_attn_out"], dt_ * 128, 128)
                _wslab(self, ws[3], I["w_in"], 7168 + dt_ * 128, 128)
                for tb in range(4):
                    pb = [self.bank[(it % 2) * 4 + k] for k in range(4)]
                    rhs_src = [zT, hT, self.outT, hT]
                    for k in range(4):
                        for kc in range(8):
                            S.op("pe", lambda e, k=k, kc=kc, tb=tb, ws=ws, pb=pb, rhs_src=rhs_src: e.matmul(pb[k].t[:], lhsT=ws[k].t[:, kc, :], rhs=rhs_src[k].t[:, kc, tb * 512:(tb + 1) * 512], start=(kc == 0), stop=(kc == 7)),
                                 reads=[ws[k].b, rhs_src[k].b], writes=[pb[k].b])
                    s0, s1 = sg[it % 2]
                    a1, a2 = m1[it % 2], m2[it % 2]
                    S.op("act", lambda e, s0=s0, pb=pb: e.activation(out=s0.t[:], in_=pb[1].t[:], func=AF.Sigmoid), reads=[pb[1].b], writes=[s0.b])
                    S.op("act", lambda e, s1=s1, pb=pb: e.activation(out=s1.t[:], in_=pb[3].t[:], func=AF.Sigmoid), reads=[pb[3].b], writes=[s1.b])
                    S.op("dve", lambda e, s0=s0, a1=a1, pb=pb: e.tensor_tensor(out=a1.t[:], in0=pb[0].t[:], in1=s0.t[:], op=ALU.mult), reads=[pb[0].b, s0.b], writes=[a1.b])
                    S.op("dve", lambda e, s1=s1, a2=a2, pb=pb: e.tensor_tensor(out=a2.t[:], in0=pb[2].t[:], in1=s1.t[:], op=ALU.mult), reads=[pb[2].b, s1.b], writes=[a2.b])
                    S.op("pool", lambda e, a1=a1, a2=a2, dt_=dt_, tb=tb: e.tensor_tensor(out=mergedT.t[:, dt_, tb * 512:(tb + 1) * 512], in0=a1.t[:], in1=a2.t[:], op=ALU.add), reads=[a1.b, a2.b], writes=[mergedT.b])
                    it += 1
        S.barrier()
        with contextlib.ExitStack() as st2:
            self.a_ptr = p0 + 97 * 1024
            wmix = self.sb(st2, "wmix", [128, 8, 1024], BF16)
            _wslab(self, wmix, I["w_mix_out"], 0, 1024)
            xres = self.xres
            S.dma("sp", lambda e: e.dma_start(out=xres.t[:], in_=I["x_own"].rearrange("(m p) d -> p m d", p=128)), xres.b, writes=[xres.b])
            it = 0
            for m in range(16):
                for dh in range(2):
                    bk = self.bank[it % 8]
                    it += 1
                    for kc in range(8):
                        S.op("pe", lambda e, bk=bk, kc=kc, m=m, dh=dh: e.matmul(bk.t[:], lhsT=mergedT.t[:, kc, m * 128:(m + 1) * 128], rhs=wmix.t[:, kc, dh * 512:(dh + 1) * 512], start=(kc == 0), stop=(kc == 7)),
                             reads=[mergedT.b, wmix.b], writes=[bk.b])
                    S.op("dve", lambda e, bk=bk, m=m, dh=dh: e.tensor_tensor(out=xres.t[:, m, dh * 512:(dh + 1) * 512], in0=bk.t[:], in1=xres.t[:, m, dh * 512:(dh + 1) * 512], op=ALU.add),
                         reads=[bk.b, xres.b], writes=[xres.b])


def _dump_res(self, name):
    S = self.S
    S.dma("sp", lambda e: e.dma_start(out=self.dbg[name].rearrange("(m p) d -> p m d", p=128), in_=self.xres.t[:]), self.xres.b, reads=[self.xres.b], is_output=True)


def _phase_cross(self, p0):
    S, I = self.S, self.I
    pA = self.pA
    xres = self.xres
    regB = p0 + 96 * 1024
    with contextlib.ExitStack() as st:
        self.a_ptr = pA
        sb = lambda name, shape, dt: self.sb(st, name, shape, dt)
        hcT = sb("hcT", [128, 8, 2048], BF16)
        wcq = sb("wcq", [128, 8, 1024], BF16)
        wco = sb("wco", [128, 8, 1024], BF16)
        kT = sb("kT", [128, 8, 256], BF16)
        vC = sb("vC", [128, 2, 1024], BF16)
        ones = sb("ones", [128, 128], BF16)
        qcT = sb("qcT", [128, 8, 512], BF16)
        assert self.a_ptr <= p0 + 32 * 1024, (self.a_ptr, p0)
        S.op("pool", lambda e: e.memset(ones.t[:], 1.0), writes=[ones.b])
        _wslab(self, wcq, I["w_cq"], 0, 1024)
        _wslab(self, wco, I["w_co"], 0, 1024)
        with contextlib.ExitStack() as st2:
            self.a_ptr = regB
            sb2 = lambda name, shape, dt: self.sb(st2, name, shape, dt)
            wckv = sb2("wckv", [128, 8, 2048], BF16)
            mT = sb2("mT", [128, 8, 256], BF16)
            xt = [sb2(f"xt{i}", [128, 1024], F32) for i in range(2)]
            xn = [sb2(f"xn{i}", [128, 1024], BF16) for i in range(2)]
            sqj = sb2("sqj", [128, 1024], BF16)
            ss = sb2("ss", [128, 4], F32)
            rstd = sb2("rstd", [128, 4], F32)
            _wslab(self, wckv, I["w_ckv"], 0, 2048)
            tiles = [(i, xt[i], xn[i]) for i in range(2)]
            self.norm_tiles(lambda tid: I["mem"][tid * 128:(tid + 1) * 128, :], tiles, xt, sqj, ss, rstd, xn)
            self.transpose_tiles(xn, mT, self.bank[0:2], 0, gsel=2, col0=0)
            for ct in range(8):
                bk = self.bank[2 + ct % 6]
                for kc in range(8):
                    S.op("pe", lambda e, bk=bk, kc=kc, ct=ct: e.matmul(bk.t[:, 0:256], lhsT=wckv.t[:, kc, ct * 128:(ct + 1) * 128], rhs=mT.t[:, kc, :], start=(kc == 0), stop=(kc == 7)),
                         reads=[wckv.b, mT.b], writes=[bk.b])
                S.op("act", lambda e, bk=bk, ct=ct: e.copy(out=kT.t[:, ct, :], in_=bk.t[:, 0:256]), reads=[bk.b], writes=[kT.b])
            for mt in range(2):
                for hh in range(2):
                    bk = self.bank[2 + (mt * 2 + hh) % 6]
                    for kc in range(8):
                        S.op("pe", lambda e, bk=bk, kc=kc, mt=mt, hh=hh: e.matmul(bk.t[:], lhsT=mT.t[:, kc, mt * 128:(mt + 1) * 128], rhs=wckv.t[:, kc, 1024 + hh * 512:1024 + (hh + 1) * 512], start=(kc == 0), stop=(kc == 7)),
                             reads=[wckv.b, mT.b], writes=[bk.b])
                    S.op("dve", lambda e, bk=bk, mt=mt, hh=hh: e.tensor_copy(out=vC.t[:, mt, hh * 512:(hh + 1) * 512], in_=bk.t[:]), reads=[bk.b], writes=[vC.b])
        S.barrier()
        with contextlib.ExitStack() as st2:
            self.a_ptr = regB
            sb2 = lambda name, shape, dt: self.sb(st2, name, shape, dt)
            xn = [sb2(f"xn{i}", [128, 1024], BF16) for i in range(4)]
            sqj = sb2("sqj", [128, 1024], BF16)
            ss = sb2("ss", [128, 4], F32)
            rstd = sb2("rstd", [128, 4], F32)
            P = [sb2(f"P{i}", [128, 2, 512], BF16) for i in range(2)]
            oT = sb2("oT", [128, 8, 512], BF16)
            R = [sb2(f"R{i}", [128, 512], F32) for i in range(2)]
            for bi in range(4):
                tiles = [(xres.t[:, bi * 4 + i, :], xres.b, xn[i]) for i in range(4)]
                _norm_res(self, tiles, sqj, ss, rstd)
                self.transpose_tiles(xn, hcT, self.bank[0:2], bi * 4, gsel=1, col0=bi * 512)
            it = 0
            for tb in range(4):
                for ct in range(8):
                    bk = self.bank[it % 8]
                    it += 1
                    for kc in range(8):
                        S.op("pe", lambda e, bk=bk, kc=kc, ct=ct, tb=tb: e.matmul(bk.t[:], lhsT=wcq.t[:, kc, ct * 128:(ct + 1) * 128], rhs=hcT.t[:, kc, tb * 512:(tb + 1) * 512], start=(kc == 0), stop=(kc == 7)),
                             reads=[wcq.b, hcT.b], writes=[bk.b])
                    if ct % 2 == 0:
                        S.op("act", lambda e, bk=bk, ct=ct: e.copy(out=qcT.t[:, ct, :], in_=bk.t[:]), reads=[bk.b], writes=[qcT.b])
                    else:
                        S.op("dve", lambda e, bk=bk, ct=ct: e.tensor_copy(out=qcT.t[:, ct, :], in_=bk.t[:]), reads=[bk.b], writes=[qcT.b])
                for hd in range(4):
                    Pt = P[hd % 2]
                    Rt = R[hd % 2]
                    for mt in range(2):
                        bk = self.bank[it % 8]
                        it += 1
                        for half in range(2):
                            S.op("pe", lambda e, bk=bk, hd=hd, half=half, mt=mt: e.matmul(bk.t[:], lhsT=kT.t[:, hd * 2 + half, mt * 128:(mt + 1) * 128], rhs=qcT.t[:, hd * 2 + half, :], start=(half == 0), stop=(half == 1)),
                                 reads=[kT.b, qcT.b], writes=[bk.b])
                        S.op("act", lambda e, bk=bk, Pt=Pt, mt=mt: e.activation(out=Pt.t[:, mt, :], in_=bk.t[:], func=AF.Exp, scale=1.0 / 16), reads=[bk.b], writes=[Pt.b])
                    bs = self.bank[it % 8]
                    it += 1
                    for mt in range(2):
                        S.op("pe", lambda e, bs=bs, Pt=Pt, mt=mt: e.matmul(bs.t[:], lhsT=ones.t[:], rhs=Pt.t[:, mt, :], start=(mt == 0), stop=(mt == 1)), reads=[ones.b, Pt.b], writes=[bs.b])
                    S.op("dve", lambda e, bs=bs, Rt=Rt: e.reciprocal(out=Rt.t[:], in_=bs.t[:]), reads=[bs.b], writes=[Rt.b])
                    for half in range(2):
                        bo = self.bank[it % 8]
                        it += 1
                        for mt in range(2):
                            S.op("pe", lambda e, bo=bo, Pt=Pt, mt=mt, hd=hd, half=half: e.matmul(bo.t[:], lhsT=vC.t[:, mt, hd * 256 + half * 128:hd * 256 + (half + 1) * 128], rhs=Pt.t[:, mt, :], start=(mt == 0), stop=(mt == 1)),
                                 reads=[vC.b, Pt.b], writes=[bo.b])
                        S.op("dve", lambda e, bo=bo, Rt=Rt, hd=hd, half=half: e.tensor_tensor(out=oT.t[:, hd * 2 + half, :], in0=bo.t[:], in1=Rt.t[:], op=ALU.mult), reads=[bo.b, Rt.b], writes=[oT.b])
                for i in range(4):
                    m = tb * 4 + i
                    for dh in range(2):
                        bk = self.bank[it % 8]
                        it += 1
                        for ct in range(8):
                            S.op("pe", lambda e, bk=bk, ct=ct, i=i, dh=dh: e.matmul(bk.t[:], lhsT=oT.t[:, ct, i * 128:(i + 1) * 128], rhs=wco.t[:, ct, dh * 512:(dh + 1) * 512], start=(ct == 0), stop=(ct == 7)),
                                 reads=[oT.b, wco.b], writes=[bk.b])
                        S.op("dve", lambda e, bk=bk, m=m, dh=dh: e.tensor_tensor(out=xres.t[:, m, dh * 512:(dh + 1) * 512], in0=bk.t[:], in1=xres.t[:, m, dh * 512:(dh + 1) * 512], op=ALU.add),
                             reads=[bk.b, xres.b], writes=[xres.b])


Builder.phase_mix = _phase_mix
Builder.phase_cross = _phase_cross
Builder.dump_res = _dump_res


def _phase_peer(self, p0):
    S, I = self.S, self.I
    xres = self.xres
    pA = self.pA
    X2_d = self.dscr("X2_d", [NT_OWN * 128, D], F32)
    B_X2 = Buf("X2_d")
    MAGIC = 12582912.0
    with contextlib.ExitStack() as st:
        self.a_ptr = p0 + 96 * 1024
        sb = lambda name, shape, dt: self.sb(st, name, shape, dt)
        with contextlib.ExitStack() as st2:
            sb2 = lambda name, shape, dt: self.sb(st2, name, shape, dt)
            xn = [sb2(f"xn{i}", [128, 1024], BF16) for i in range(4)]
            sqj = sb2("sqj", [128, 1024], BF16)
            ss = sb2("ss", [128, 4], F32)
            rstd = sb2("rstd", [128, 4], F32)
            self.a_ptr = pA
            hfT = self.sb(st, "hfT", [128, 8, 2048], BF16)
            for bi in range(4):
                tiles = [(xres.t[:, bi * 4 + i, :], xres.b, xn[i]) for i in range(4)]
                _norm_res(self, tiles, sqj, ss, rstd)
                self.transpose_tiles(xn, hfT, self.bank[0:2], bi * 4, gsel=3, col0=bi * 512)
            S.dma("sp", lambda e: e.dma_start(out=X2_d.rearrange("(m p) d -> p m d", p=128), in_=xres.t[:]), xres.b, reads=[xres.b], writes=[B_X2])
        S.barrier()
        self.a_ptr = pA + 32 * 1024
        RT = self.sb(st, "RT", [128, 3, 2048], F32)
        pR = self.a_ptr
        with contextlib.ExitStack() as st2:
            sb2 = lambda name, shape, dt: self.sb(st2, name, shape, dt)
            ub = [sb2(f"ub{i}", [128, 4, 1024], BF16) for i in range(2)]
            vb = [sb2(f"vb{i}", [128, 4, 1024], BF16) for i in range(2)]
            uts = [sb2(f"uts{i}", [128, 8, 128], BF16) for i in range(3)]
            for gq in range(32):
                u, v = ub[gq % 2], vb[gq % 2]
                S.dma("pool", lambda e, u=u, gq=gq: e.dma_start(out=u.t[:], in_=I["peer_u"][gq * 512:(gq + 1) * 512, :].rearrange("(i p) d -> p i d", p=128)), u.b, writes=[u.b])
                S.dma("pool", lambda e, v=v, gq=gq: e.dma_start(out=v.t[:], in_=I["peer_v"][gq * 512:(gq + 1) * 512, :].rearrange("(i p) d -> p i d", p=128)), v.b, writes=[v.b])
                S.dma("sp", lambda e, v=v, gq=gq: e.dma_start(out=self.Vb_d[gq * 512:(gq + 1) * 512, :].rearrange("(i p) d -> p i d", p=128), in_=v.t[:]), v.b, reads=[v.b], writes=[self.B_Vb])
                for i in range(4):
                    et = gq * 4 + i
                    bk = self.bank[et % 4]
                    pt = bk.t.bitcast(BF16)
                    ut = uts[et % 3]
                    for kc in range(8):
                        S.op("pe", lambda e, pt=pt, u=u, i=i, kc=kc: e.transpose(out=pt[:, kc * 128:(kc + 1) * 128], in_=u.t[:, i, kc * 128:(kc + 1) * 128], identity=self.ident_bf.t[:]),
                             reads=[u.b, self.ident_bf.b], writes=[bk.b])
                    if et % 2 == 0:
                        S.op("act", lambda e, pt=pt, ut=ut: e.copy(out=ut.t[:], in_=pt[:, :].rearrange("p (k t) -> p k t", k=8)), reads=[bk.b], writes=[ut.b])
                    else:
                        S.op("dve", lambda e, pt=pt, ut=ut: e.tensor_copy(out=ut.t[:], in_=pt[:, :].rearrange("p (k t) -> p k t", k=8)), reads=[bk.b], writes=[ut.b])
                    S.dma("act", lambda e, ut=ut, et=et: e.dma_start(out=self.UT_d[et], in_=ut.t[:]), ut.b, reads=[ut.b], writes=[self.B_UT])
        S.barrier()
        if self.peer_stop == "p0":
            with contextlib.ExitStack() as st2:
                tu = self.sb(st2, "tu", [128, 1024], BF16)
                tv = self.sb(st2, "tv", [128, 1024], BF16)
                tf = self.sb(st2, "tf", [128, 2, 1024], F32)
                S.dma("sp", lambda e: e.dma_start(out=tu.t[:], in_=self.UT_d[77].rearrange("p k e -> p (k e)")), tu.b, reads=[self.B_UT], writes=[tu.b])
                S.dma("sp", lambda e: e.dma_start(out=tv.t[:], in_=self.Vb_d[77 * 128:78 * 128, :]), tv.b, reads=[self.B_Vb], writes=[tv.b])
                S.op("dve", lambda e: e.tensor_copy(out=tf.t[:, 0, :], in_=tu.t[:]), reads=[tu.b], writes=[tf.b])
                S.op("dve", lambda e: e.tensor_copy(out=tf.t[:, 1, :], in_=tv.t[:]), reads=[tv.b], writes=[tf.b])
                S.dma("sp", lambda e: e.dma_start(out=self.out[0:128, :], in_=tf.t[:, 0, :]), tf.b, reads=[tf.b], is_output=True)
                S.dma("sp", lambda e: e.dma_start(out=self.out[128:256, :], in_=tf.t[:, 1, :]), tf.b, reads=[tf.b], is_output=True)
                S.dma("sp", lambda e: e.dma_start(out=self.out[256:384, :], in_=xres.t[:, 5, :]), xres.b, reads=[xres.b], is_output=True)
            return
        with contextlib.ExitStack() as st2:
            self.a_ptr = pR
            sb2 = lambda name, shape, dt: self.sb(st2, name, shape, dt)
            wpq = sb2("wpq", [128, 8, 2048], BF16)
            skT = sb2("skT", [128, 16, 128], BF16)
            qT = [sb2(f"qT{i}", [128, 16, 128], BF16) for i in range(2)]
            sc = sb2("sc", [128, 16, 128], F32)
            scr = sb2("scr", [128, 16, 128], F32)
            vals = sb2("vals", [128, 16, 16], F32)
            idx = sb2("idx", [128, 16, 16], U32)
            idxf = sb2("idxf", [128, 16, 16], F32)
            cand = sb2("cand", [128, 8, 256], F32)
            cscr = sb2("cscr", [128, 256], F32)
            ts = sb2("ts", [128, 8, 16], F32)
            tc_ = sb2("tc", [128, 8, 16], U32)
            tcf = sb2("tcf", [128, 8, 16], F32)
            af = sb2("af", [128, 8, 16], F32)
            bf = sb2("bf", [128, 8, 16], F32)
            oh = sb2("oh", [128, 8, 16, 16], F32)
            IJg = sb2("IJg", [128, 3, 128], F32)
            esum = sb2("esum", [128, 8], F32)
            _wslab(self, wpq, I["w_pq"], 0, 2048)
            S.dma("pool", lambda e: e.dma_start(out=skT.t[:], in_=I["skT"].rearrange("g d n -> d g n")), skT.b, writes=[skT.b])
            iota16 = self.iota_f.t[:, 0:16]
            for m in range(16):
                q = qT[m % 2]
                for gi in range(16):
                    bk = self.bank[gi % 4]
                    for kc in range(8):
                        S.op("pe", lambda e, bk=bk, kc=kc, gi=gi, m=m: e.matmul(bk.t[:, 0:128], lhsT=wpq.t[:, kc, gi * 128:(gi + 1) * 128], rhs=hfT.t[:, kc, m * 128:(m + 1) * 128], start=(kc == 0), stop=(kc == 7)),
                             reads=[wpq.b, hfT.b], writes=[bk.b])
                    S.op("act", lambda e, bk=bk, gi=gi, q=q: e.copy(out=q.t[:, gi, :], in_=bk.t[:, 0:128]), reads=[bk.b], writes=[q.b])
                for gi in range(16):
                    bk = self.bank[4 + gi // 4]
                    S.op("pe", lambda e, bk=bk, gi=gi, q=q: e.matmul(bk.t[:, (gi % 4) * 128:(gi % 4 + 1) * 128], lhsT=q.t[:, gi, :], rhs=skT.t[:, gi, :], start=True, stop=True),
                         reads=[q.b, skT.b], writes=[bk.b])
                for b4 in range(4):
                    bk = self.bank[4 + b4]
                    S.op("act", lambda e, bk=bk, b4=b4: e.copy(out=sc.t[:, b4 * 4:(b4 + 1) * 4, :], in_=bk.t[:].rearrange("p (g n) -> p g n", g=4)), reads=[bk.b], writes=[sc.b])
                if self.peer_stop == "p1sc" and m == 0:
                    S.dma("sp", lambda e: e.dma_start(out=self.out[0:256, :].rearrange("(p a) d -> p (a d)", p=128), in_=sc.t[:].rearrange("p g n -> p (g n)")), sc.b, reads=[sc.b], is_output=True)
                    qf = sb2("qf", [128, 16, 128], F32)
                    S.op("dve", lambda e: e.tensor_copy(out=qf.t[:], in_=q.t[:]), reads=[q.b], writes=[qf.b])
                    S.dma("sp", lambda e: e.dma_start(out=self.out[256:512, :].rearrange("(p a) d -> p (a d)", p=128), in_=qf.t[:].rearrange("p g n -> p (g n)")), qf.b, reads=[qf.b], is_output=True)
                    return
                for gi in range(16):
                    S.op("dve", lambda e, gi=gi: e.max(out=vals.t[:, gi, 0:8], in_=sc.t[:, gi, :]), reads=[sc.b], writes=[vals.b])
                    S.op("dve", lambda e, gi=gi: e.max_index(out=idx.t[:, gi, 0:8], in_max=vals.t[:, gi, 0:8], in_values=sc.t[:, gi, :]), reads=[sc.b, vals.b], writes=[idx.b])
                    S.op("dve", lambda e, gi=gi: e.match_replace(out=scr.t[:, gi, :], in_to_replace=vals.t[:, gi, 0:8], in_values=sc.t[:, gi, :], imm_value=-1e30), reads=[sc.b, vals.b], writes=[scr.b])
                    S.op("dve", lambda e, gi=gi: e.max(out=vals.t[:, gi, 8:16], in_=scr.t[:, gi, :]), reads=[scr.b], writes=[vals.b])
                    S.op("dve", lambda e, gi=gi: e.max_index(out=idx.t[:, gi, 8:16], in_max=vals.t[:, gi, 8:16], in_values=scr.t[:, gi, :]), reads=[scr.b, vals.b], writes=[idx.b])
                S.op("dve", lambda e: e.tensor_copy(out=idxf.t[:], in_=idx.t[:]), reads=[idx.b], writes=[idxf.b])
                v0 = _bcast_ap(vals.t, 0, [[32, 8], [1, 16], [0, 16]])
                v1 = _bcast_ap(vals.t, 16, [[32, 8], [0, 16], [1, 16]])
                S.op("dve", lambda e, v0=v0, v1=v1: e.tensor_tensor(out=cand.t[:].rearrange("p h (a b) -> p h a b", a=16), in0=v0, in1=v1, op=ALU.add), reads=[vals.b], writes=[cand.b])
                for h in range(8):
                    S.op("dve", lambda e, h=h: e.max(out=ts.t[:, h, 0:8], in_=cand.t[:, h, :]), reads=[cand.b], writes=[ts.b])
                    S.op("dve", lambda e, h=h: e.max_index(out=tc_.t[:, h, 0:8], in_max=ts.t[:, h, 0:8], in_values=cand.t[:, h, :]), reads=[cand.b, ts.b], writes=[tc_.b])
                    S.op("dve", lambda e, h=h: e.match_replace(out=cscr.t[:], in_to_replace=ts.t[:, h, 0:8], in_values=cand.t[:, h, :], imm_value=-1e30), reads=[cand.b, ts.b], writes=[cscr.b])
                    S.op("dve", lambda e, h=h: e.max(out=ts.t[:, h, 8:16], in_=cscr.t[:]), reads=[cscr.b], writes=[ts.b])
                    S.op("dve", lambda e, h=h: e.max_index(out=tc_.t[:, h, 8:16], in_max=ts.t[:, h, 8:16], in_values=cscr.t[:]), reads=[cscr.b, ts.b], writes=[tc_.b])
                S.op("dve", lambda e: e.tensor_copy(out=tcf.t[:], in_=tc_.t[:]), reads=[tc_.b], writes=[tcf.b])
                S.op("dve", lambda e: e.tensor_scalar(out=af.t[:], in0=tcf.t[:], scalar1=0.0625, scalar2=-0.46875, op0=ALU.mult, op1=ALU.add), reads=[tcf.b], writes=[af.b])
                S.op("dve", lambda e: e.tensor_scalar(out=af.t[:], in0=af.t[:], scalar1=MAGIC, scalar2=None, op0=ALU.add), reads=[af.b], writes=[af.b])
                S.op("dve", lambda e: e.tensor_scalar(out=af.t[:], in0=af.t[:], scalar1=-MAGIC, scalar2=None, op0=ALU.add), reads=[af.b], writes=[af.b])
                S.op("dve", lambda e: e.scalar_tensor_tensor(out=bf.t[:], in0=af.t[:], scalar=-16.0, in1=tcf.t[:], op0=ALU.mult, op1=ALU.add), reads=[af.b, tcf.b], writes=[bf.b])
                for which, sel in ((0, af), (1, bf)):
                    selb = _bcast_ap(sel.t, 0, [[16, 8], [1, 16], [0, 16]])
                    iob = _bcast_ap(self.iota_f.t, 0, [[0, 8], [0, 16], [1, 16]])
                    ixb = _bcast_ap(idxf.t, which * 16, [[32, 8], [0, 16], [1, 16]])
                    S.op("dve", lambda e, selb=selb, iob=iob: e.tensor_tensor(out=oh.t[:], in0=selb, in1=iob, op=ALU.is_equal), reads=[sel.b, self.iota_f.b], writes=[oh.b])
                    S.op("dve", lambda e, ixb=ixb: e.tensor_tensor(out=oh.t[:], in0=oh.t[:], in1=ixb, op=ALU.mult), reads=[oh.b, idxf.b], writes=[oh.b])
                    S.op("dve", lambda e, which=which: e.tensor_reduce(out=IJg.t[:, which, :], in_=oh.t[:].rearrange("p h k a -> p (h k) a"), axis=AX.X, op=ALU.add), reads=[oh.b], writes=[IJg.b])
                tmax = _bcast_ap(ts.t, 0, [[16, 8], [0, 16]])
                S.op("dve", lambda e, tmax=tmax: e.tensor_tensor(out=tcf.t[:], in0=ts.t[:], in1=tmax, op=ALU.subtract), reads=[ts.b], writes=[tcf.b])
                S.op("act", lambda e: e.activation(out=tcf.t[:], in_=tcf.t[:], func=AF.Exp), reads=[tcf.b], writes=[tcf.b])
                S.op("dve", lambda e: e.tensor_reduce(out=esum.t[:], in_=tcf.t[:], axis=AX.X, op=ALU.add), reads=[tcf.b], writes=[esum.b])
                S.op("dve", lambda e: e.reciprocal(out=esum.t[:], in_=esum.t[:]), reads=[esum.b], writes=[esum.b])
                esb = _bcast_ap(esum.t, 0, [[1, 8], [0, 16]])
                S.op("dve", lambda e, esb=esb: e.tensor_tensor(out=IJg.t[:, 2, :].rearrange("p (h k) -> p h k", h=8), in0=tcf.t[:], in1=esb, op=ALU.mult), reads=[tcf.b, esum.b], writes=[IJg.b])
                for w in range(3):
                    bk = self.bank[w]
                    S.op("pe", lambda e, bk=bk, w=w: e.transpose(out=bk.t[:, 0:128], in_=IJg.t[:, w, :], identity=self.ident_f.t[:]), reads=[IJg.b, self.ident_f.b], writes=[bk.b])
                    S.op("act", lambda e, bk=bk, w=w, m=m: e.copy(out=RT.t[:, w, m * 128:(m + 1) * 128], in_=bk.t[:, 0:128]), reads=[bk.b], writes=[RT.b])
        S.barrier()
        if self.peer_stop == "p1":
            S.dma("sp", lambda e: e.dma_start(out=self.out[0:768, :].rearrange("(p a) d -> p (a d)", p=128), in_=RT.t[:].rearrange("p w t -> p (w t)")), RT.b, reads=[RT.b], is_output=True)
            return
        with contextlib.ExitStack() as st2:
            self.a_ptr = pR
            sb2 = lambda name, shape, dt: self.sb(st2, name, shape, dt)
            TB = 256
            W = sb2("W", [128, 128, TB], BF16)
            ohj = [sb2(f"ohj{i}", [128, 16, 128], BF16) for i in range(2)]
            ohi = [sb2(f"ohi{i}", [128, 16, 128], BF16) for i in range(2)]
            utl = [sb2(f"utl{i}", [128, 8, 128], BF16) for i in range(3)]
            vtl = [sb2(f"vtl{i}", [128, 1024], BF16) for i in range(3)]
            ag = [sb2(f"ag{i}", [128, TB], F32) for i in range(2)]
            wa = [sb2(f"wa{i}", [128, TB], BF16) for i in range(2)]
            x2t = [sb2(f"x2t{i}", [128, 1024], F32) for i in range(2)]
            sqj = sb2("sqjf", [128, 1024], BF16)
            fs = sb2("fs", [128, 4], F32)
            frs = sb2("frs", [128, 4], F32)
            yo = [sb2(f"yo{i}", [128, 1024], F32) for i in range(2)]
            gfin = sb2("gfin", [128, 1024], F32)
            S.dma("sp", lambda e: e.dma_start(out=gfin.t[:], in_=I["final_g_rep"]), gfin.b, writes=[gfin.b])
            psO = self.bank[0:4]
            psA = self.bank[4:6]
            psW = self.bank[6]
            for blk in range(2048 // TB):
                t0 = blk * TB
                G = 16
                for tg in range(TB // G):
                    tq = t0 + tg * G
                    oj, oi = ohj[tg % 2], ohi[tg % 2]
                    iob = _bcast_ap(self.iota_f.t, 0, [[0, G], [1, 128]])
                    Ib = _bcast_ap(RT.t, 0 * 2048 + tq, [[1, G], [0, 128]])
                    Jb = _bcast_ap(RT.t, 1 * 2048 + tq, [[1, G], [0, 128]])
                    gb = _bcast_ap(RT.t, 2 * 2048 + tq, [[1, G], [0, 128]])
                    S.op("dve", lambda e, oj=oj, iob=iob, Jb=Jb: e.tensor_tensor(out=oj.t[:], in0=iob, in1=Jb, op=ALU.is_equal), reads=[self.iota_f.b, RT.b], writes=[oj.b])
                    S.op("dve", lambda e, oi=oi, iob=iob, Ib=Ib: e.tensor_tensor(out=oi.t[:], in0=iob, in1=Ib, op=ALU.is_equal), reads=[self.iota_f.b, RT.b], writes=[oi.b])
                    S.op("pool", lambda e, oi=oi, gb=gb: e.tensor_tensor(out=oi.t[:], in0=oi.t[:], in1=gb, op=ALU.mult), reads=[oi.b, RT.b], writes=[oi.b])
                    for k in range(G):
                        tt = tg * G + k
                        S.op("pe", lambda e, oj=oj, oi=oi, tt=tt, k=k: e.matmul(psW.t[:, (tt % 4) * 128:(tt % 4 + 1) * 128], lhsT=oj.t[:, k, :], rhs=oi.t[:, k, :], start=True, stop=True), reads=[oj.b, oi.b], writes=[psW.b])
                        if tt % 4 == 3:
                            tb0 = tt - 3
                            wout = _bcast_ap(W.t, tb0, [[1, 4], [TB, 128]])
                            S.op("act", lambda e, wout=wout: e.copy(out=wout, in_=psW.t[:].rearrange("p (q i) -> p q i", q=4)), reads=[psW.b], writes=[W.b])
                for i in range(128):
                    ut, vt = utl[i % 3], vtl[i % 3]
                    S.dma("sp", lambda e, ut=ut, i=i: e.dma_start(out=ut.t[:], in_=self.UT_d[i]), ut.b, reads=[self.B_UT], writes=[ut.b])
                    S.dma("act", lambda e, vt=vt, i=i: e.dma_start(out=vt.t[:], in_=self.Vb_d[i * 128:(i + 1) * 128, :]), vt.b, reads=[self.B_Vb], writes=[vt.b])
                    pa = psA[i % 2]
                    for kc in range(8):
                        S.op("pe", lambda e, pa=pa, ut=ut, kc=kc, t0=t0: e.matmul(pa.t[:, 0:TB], lhsT=ut.t[:, kc, :], rhs=hfT.t[:, kc, t0:t0 + TB], start=(kc == 0), stop=(kc == 7)),
                             reads=[ut.b, hfT.b], writes=[pa.b])
                    a_, w_ = ag[i % 2], wa[i % 2]
                    S.op("act", lambda e, pa=pa, a_=a_: e.activation(out=a_.t[:], in_=pa.t[:, 0:TB], func=AF.Gelu), reads=[pa.b], writes=[a_.b])
                    S.op("pool", lambda e, a_=a_, w_=w_, i=i: e.tensor_tensor(out=w_.t[:], in0=a_.t[:], in1=W.t[:, i, :], op=ALU.mult), reads=[a_.b, W.b], writes=[w_.b])
                    for tl in range(TB // 128):
                        for dh in range(2):
                            po = psO[tl * 2 + dh]
                            S.op("pe", lambda e, po=po, w_=w_, vt=vt, tl=tl, dh=dh, i=i: e.matmul(po.t[:], lhsT=w_.t[:, tl * 128:(tl + 1) * 128], rhs=vt.t[:, dh * 512:(dh + 1) * 512], start=(i == 0), stop=(i == 127)),
                                 reads=[w_.b, vt.b], writes=[po.b])
                for tl in range(TB // 128):
                    m = (t0 // 128) + tl
                    xt_ = x2t[tl % 2]
                    yt = yo[tl % 2]
                    S.dma("sp", lambda e, xt_=xt_, m=m: e.dma_start(out=xt_.t[:], in_=X2_d[m * 128:(m + 1) * 128, :]), xt_.b, reads=[B_X2], writes=[xt_.b])
                    for dh in range(2):
                        po = psO[tl * 2 + dh]
                        S.op("dve", lambda e, po=po, xt_=xt_, dh=dh: e.tensor_tensor(out=xt_.t[:, dh * 512:(dh + 1) * 512], in0=po.t[:], in1=xt_.t[:, dh * 512:(dh + 1) * 512], op=ALU.add), reads=[po.b, xt_.b], writes=[xt_.b])
                    if "x3" in self.dumps:
                        S.dma("sp", lambda e, xt_=xt_, m=m: e.dma_start(out=self.dbg["x3"][m * 128:(m + 1) * 128, :], in_=xt_.t[:]), xt_.b, reads=[xt_.b], is_output=True)
                    S.op("act", lambda e, xt_=xt_: e.activation(out=sqj.t[:], in_=xt_.t[:], func=AF.Square, accum_out=fs.t[:, 0:1]), reads=[xt_.b], writes=[sqj.b, fs.b])
                    S.op("dve", lambda e: e.tensor_scalar(out=frs.t[:, 0:1], in0=fs.t[:, 0:1], scalar1=1.0 / D, scalar2=1e-6, op0=ALU.mult, op1=ALU.add), reads=[fs.b], writes=[frs.b])
                    S.op("pool", lambda e: e.tensor_tensor(out=frs.t[:, 0:1], in0=frs.t[:, 0:1], in1=self.mhalf.t[:, 0:1], op=ALU.pow), reads=[frs.b, self.mhalf.b], writes=[frs.b])
                    S.op("dve", lambda e, xt_=xt_, yt=yt: e.scalar_tensor_tensor(out=yt.t[:], in0=xt_.t[:], scalar=frs.t[:, 0:1], in1=gfin.t[:], op0=ALU.mult, op1=ALU.mult), reads=[xt_.b, frs.b, gfin.b], writes=[yt.b])
                    S.dma("sp", lambda e, yt=yt, m=m: e.dma_start(out=self.out[m * 128:(m + 1) * 128, :], in_=yt.t[:]), yt.b, reads=[yt.b], is_output=True)


def _phase_final(self):
    pass


Builder.phase_peer = _phase_peer
Builder.phase_final = _phase_final
```

```python
import contextlib
import math

import numpy as np
import ml_dtypes
import concourse.bass as bass
import concourse.mybir as mybir
from concourse.bass_utils import run_bass_kernel_spmd

F32 = mybir.dt.float32
BF16 = mybir.dt.bfloat16
U32 = mybir.dt.uint32
I32 = mybir.dt.int32
AF = mybir.ActivationFunctionType
ALU = mybir.AluOpType
AX = mybir.AxisListType

NCORE = 8
SEQ = 16384
D = 1024
NT_ALL = SEQ // 128
NT_OWN = 16
COMPUTE = ("pe", "act", "dve", "pool")
ALL_ENG = ("pe", "act", "dve", "pool", "sp")
SEM_ROT = 30000


class Buf:
    __slots__ = ("name", "writers", "readers", "dsem", "dcount")

    def __init__(self, name):
        self.name = name
        self.writers = []
        self.readers = []
        self.dsem = None
        self.dcount = 0


class Ins:
    __slots__ = ("eng", "fn", "deps", "is_dma", "sem", "semval", "needs_inc")

    def __init__(self, eng, fn, deps, is_dma):
        self.eng = eng
        self.fn = fn
        self.deps = deps
        self.is_dma = is_dma
        self.sem = None
        self.semval = 0
        self.needs_inc = False


class Sched:
    def __init__(self, nc, same_engine_sync=True):
        self.nc = nc
        self.ins = []
        self.streams = {e: [] for e in ALL_ENG}
        self.same_engine_sync = same_engine_sync
        self.dma_bufs = []
        self.out_tokens = []
        self.last_eng = {}
        self.last_dma = {}
        self.base_deps = frozenset()

    def barrier(self):
        self.base_deps = frozenset(list(self.last_eng.values()) + list(self.last_dma.values()))

    def _deps(self, reads, writes):
        deps = set(self.base_deps)
        for b in reads:
            deps.update(b.writers)
        for b in writes:
            deps.update(b.writers)
            deps.update(b.readers)
        last = {}
        for i in deps:
            it = self.ins[i]
            key = ("d", id(it.sem)) if it.is_dma else it.eng
            if key not in last or last[key] < i:
                last[key] = i
        return set(last.values())

    def _compress(self, lst):
        last = {}
        out = []
        for i in lst:
            it = self.ins[i]
            if it.is_dma:
                last[("d", id(it.sem))] = i
            else:
                last[it.eng] = i
        return list(last.values())

    def _commit(self, me, reads, writes):
        for b in writes:
            if b.readers:
                b.writers = [me]
                b.readers = []
            else:
                b.writers.append(me)
                if len(b.writers) > 32:
                    b.writers = self._compress(b.writers)
        for b in reads:
            if b in writes:
                continue
            b.readers.append(me)
            if len(b.readers) > 32:
                b.readers = self._compress(b.readers)

    def op(self, eng, fn, reads=(), writes=()):
        deps = self._deps(reads, writes)
        me = len(self.ins)
        it = Ins(eng, fn, deps, False)
        self.ins.append(it)
        self.streams[eng].append(it)
        self._commit(me, reads, writes)
        self.last_eng[eng] = me
        return me

    def dma(self, queue, fn, sem_buf, reads=(), writes=(), is_output=False):
        deps = self._deps(reads, writes)
        me = len(self.ins)
        it = Ins(queue, fn, deps, True)
        if sem_buf.dsem is None:
            sem_buf.dsem = "pending"
            self.dma_bufs.append(sem_buf)
        sem_buf.dcount += 16
        it.sem = sem_buf
        it.semval = sem_buf.dcount
        self.ins.append(it)
        self.streams[queue].append(it)
        self._commit(me, reads, writes)
        self.last_dma[id(sem_buf)] = me
        if is_output:
            self.out_tokens.append(me)
        return me

    def emit(self, final_engine="sp"):
        nc = self.nc
        ses = self.same_engine_sync
        for it in self.ins:
            for d in it.deps:
                dd = self.ins[d]
                if dd.is_dma:
                    continue
                if dd.eng == it.eng and not it.is_dma and (dd.eng == "pe" or not ses):
                    continue
                dd.needs_inc = True
        n_sems = {}
        for e in COMPUTE:
            c = 0
            for it in self.streams[e]:
                if it.is_dma:
                    continue
                if it.needs_inc:
                    c += 1
                    it.semval = c
            n_sems[e] = max(c - 1, 0) // SEM_ROT + 1
        with contextlib.ExitStack() as st:
            eng_sems = {e: [st.enter_context(nc.semaphore(f"s_{e}{k}")) for k in range(n_sems[e])] for e in COMPUTE}
            for i, b in enumerate(self.dma_bufs):
                b.dsem = st.enter_context(nc.semaphore(f"d{i}_{b.name}"))

            def token(it):
                if it.is_dma:
                    return (it.sem.dsem, it.semval, ("d", id(it.sem)))
                k = (it.semval - 1) // SEM_ROT
                return (eng_sems[it.eng][k], it.semval - k * SEM_ROT, (it.eng, k))

            def run(e, eng):
                known = {}
                for it in self.streams[e]:
                    need = {}
                    for d in it.deps:
                        dd = self.ins[d]
                        if not dd.is_dma and dd.eng == e and not it.is_dma and (e == "pe" or not ses):
                            continue
                        sem, val, key = token(dd)
                        if known.get(key, 0) >= val:
                            continue
                        if key not in need or need[key][1] < val:
                            need[key] = (sem, val)
                    for key, (sem, val) in need.items():
                        eng.wait_ge(sem, val)
                        known[key] = val
                    h = it.fn(eng)
                    if it.is_dma:
                        h.then_inc(it.sem.dsem, 16)
                    elif it.needs_inc:
                        k = (it.semval - 1) // SEM_ROT
                        h.then_inc(eng_sems[it.eng][k], 1)
                if e == final_engine:
                    need = {}
                    for d in self.out_tokens:
                        sem, val, key = token(self.ins[d])
                        if key not in need or need[key][1] < val:
                            need[key] = (sem, val)
                    for key, (sem, val) in need.items():
                        eng.wait_ge(sem, val)

            with nc.Block() as block:
                @block.sync
                def _(eng):
                    run("sp", eng)

                @block.tensor
                def _(eng):
                    run("pe", eng)

                @block.scalar
                def _(eng):
                    run("act", eng)

                @block.vector
                def _(eng):
                    run("dve", eng)

                @block.gpsimd
                def _(eng):
                    run("pool", eng)


class TT:
    __slots__ = ("t", "b")

    def __init__(self, t, b):
        self.t = t
        self.b = b


def _t5_bucket(n):
    n = np.maximum(n, 0)
    nf = np.maximum(n, 1).astype(np.float32)
    large = 16 + (np.log(nf / np.float32(16)) / np.float32(math.log(8)) * np.float32(16)).astype(np.int32)
    large = np.minimum(large, 31)
    return np.where(n < 16, n, large)


def _band_tables(c):
    ki = np.arange(128)[:, None]
    qi = np.arange(128)[None, :]
    bk = np.zeros((9, 128, 128), np.int64)
    mk = np.zeros((9, 128, 128), np.float32)
    for bi, b in enumerate(range(-1, 8)):
        rel = 128 * (c - b) + qi - ki
        bk[bi] = _t5_bucket(rel)
        mk[bi] = np.where(rel >= 0, 0.0, -30000.0)
    return bk, mk


class Builder:
    def __init__(self, stop_after=None, upto=None):
        self.stop_after = stop_after
        self.upto = upto
        self.nc = bass.Bass("TRN2", target_bir_lowering=False)
        self.S = Sched(self.nc)
        self.n = 0

    def din(self, name, shape, dt=F32):
        if self.upto and self.upto.startswith("peeronly") and name in ("x_all", "w_in", "w_conv_out", "w_attn_out", "w_mix_out", "w_cq", "w_ckv", "w_co", "mem"):
            shape = [1, 1]
        if not hasattr(self, "in_shapes"):
            self.in_shapes = {}
        self.in_shapes[name] = (tuple(shape), dt)
        return self.nc.dram_tensor(name, list(shape), dt, kind="ExternalInput").ap()

    def dout(self, name, shape, dt=F32):
        return self.nc.dram_tensor(name, list(shape), dt, kind="ExternalOutput").ap()

    def dscr(self, name, shape, dt):
        return self.nc.dram_tensor(name, list(shape), dt, kind="Internal").ap()

    def _arena_init(self):
        rem = self.nc.sbuf_bytes_remaining
        rem = rem() if callable(rem) else rem
        size = (int(rem) - 256) // 64 * 64
        beg, end = self.nc.bump_sbuf(size)
        self.a_beg = (int(beg) + 63) // 64 * 64
        self.a_end = int(end)
        self.a_ptr = self.a_beg
        self.a_marked = set()

    def _reset_ptr(self, mark, key):
        self.a_ptr = mark
        self.a_marked.discard(key)

    def sb(self, st, name, shape, dt, at=None):
        if id(st) not in self.a_marked:
            self.a_marked.add(id(st))
            st.callback(self._reset_ptr, self.a_ptr, id(st))
        self.n += 1
        nm = f"{name}_{self.n}"
        nbytes = int(np.prod(shape[1:])) * mybir.dt.size(dt)
        nbytes = (nbytes + 63) // 64 * 64
        if at is None:
            off = self.a_ptr
            self.a_ptr += nbytes
        else:
            off = at
        assert off + nbytes <= self.a_end, (name, off, nbytes, self.a_end)
        self.a_peak = max(getattr(self, "a_peak", 0), off + nbytes - self.a_beg)
        t = self.nc.alloc_sbuf_tensor_at(nm, list(shape), dt, offset=off)
        return TT(t, Buf(nm))

    def ps(self, st, name, shape, dt):
        self.n += 1
        nm = f"{name}_{self.n}"
        return TT(st.enter_context(self.nc.psum_tensor(nm, list(shape), dt)), Buf(nm))

    def build(self):
        nc, S = self.nc, self.S
        din, dout = self.din, self.dout
        I = {}
        I["x_all"] = din("x_all", [SEQ, D])
        I["x_own"] = din("x_own", [NT_OWN * 128, D])
        I["x_halo"] = din("x_halo", [32, D])
        I["mem"] = din("mem", [256, D])
        I["w_in"] = din("w_in", [D, 8192])
        I["gcols"] = din("gcols", [128, 4, 8])
        I["final_g_rep"] = din("final_g_rep", [128, D])
        I["conv_wT"] = din("conv_wT", [128, 8, 3])
        I["w_conv_out"] = din("w_conv_out", [D, D])
        I["lam_rep"] = din("lam_rep", [128, 4, 64])
        I["subln_rep"] = din("subln_rep", [128, 128])
        I["w_attn_out"] = din("w_attn_out", [D, D])
        I["w_mix_out"] = din("w_mix_out", [D, D])
        I["rb31_rep"] = din("rb31_rep", [128, 8])
        I["gbias"] = din("gbias", [8, 128, 9, 128])
        I["mband"] = din("mband", [128, 9, 128])
        I["w_cq"] = din("w_cq", [D, D])
        I["w_ckv"] = din("w_ckv", [D, 2048])
        I["w_co"] = din("w_co", [D, D])
        I["w_pq"] = din("w_pq", [D, 2048])
        I["skT"] = din("skT", [16, 128, 128])
        I["peer_u"] = din("peer_u", [SEQ, D])
        I["peer_v"] = din("peer_v", [SEQ, D])
        I["ident_bf"] = din("ident_bf", [128, 128], BF16)
        I["ident_f"] = din("ident_f", [128, 128])
        I["iota_f"] = din("iota_f", [128, 128])
        self.I = I
        self.out = dout("out", [NT_OWN * 128, D])
        self.dumps = set(self.stop_after.split(",")) if self.stop_after else set()
        self.dbg = {}
        if "attn" in self.dumps:
            self.dbg["attn"] = dout("dbg_attn", [128, 8 * NT_OWN * 128])
        for nm in ("x1", "x2", "x3"):
            if nm in self.dumps:
                self.dbg[nm] = dout("dbg_" + nm, [NT_OWN * 128, D])
        self.UT_d = self.dscr("UT_d", [128, 128, 8, 128], BF16)
        self.Vb_d = self.dscr("Vb_d", [SEQ, D], BF16)
        self.B_UT = Buf("UT_d")
        self.B_Vb = Buf("Vb_d")
        self.KT_d = self.dscr("KT_d", [8, 128, SEQ], BF16)
        self.V_d = self.dscr("V_d", [8, 128, NT_ALL, 128], BF16)
        self.B_KT = Buf("KT_d")
        self.B_V = Buf("V_d")

        with contextlib.ExitStack() as gst:
            self.gst = gst
            self._arena_init()
            self.bank = [self.ps(gst, f"bank{i}", [128, 512], F32) for i in range(8)]
            self.setup_consts()
            if self.upto and self.upto.startswith("peeronly"):
                with contextlib.ExitStack() as rst:
                    p0 = self.a_ptr
                    self.xres = self.sb(rst, "xres", [128, NT_OWN, D], F32, at=p0 + 32 * 1024)
                    S.dma("sp", lambda e: e.dma_start(out=self.xres.t[:], in_=I["x_own"].rearrange("(m p) d -> p m d", p=128)), self.xres.b, writes=[self.xres.b])
                    self.peer_stop = self.upto.split(":")[1] if ":" in self.upto else None
                    self.phase_peer(p0)
                S.emit()
                return nc
            self.peer_stop = None
            self.phase_kv()
            S.barrier()
            self.phase_q_attn()
            S.barrier()
            if "attn" in self.dumps:
                self.dump_attn()
            with contextlib.ExitStack() as rst:
                p0 = self.a_ptr
                self.xres = self.sb(rst, "xres", [128, NT_OWN, D], F32, at=p0 + 32 * 1024)
                self.phase_mix(p0)
                S.barrier()
                if "x1" in self.dumps:
                    self.dump_res("x1")
                if self.upto != "mix":
                    self.phase_cross(p0)
                    S.barrier()
                    if "x2" in self.dumps:
                        self.dump_res("x2")
                    if self.upto != "cross":
                        self.phase_peer(p0)
            S.emit()
        return nc

    def setup_consts(self):
        S, gst, I = self.S, self.gst, self.I
        sb = lambda name, shape, dt: self.sb(gst, name, shape, dt)
        self.ident_bf = sb("ident_bf", [128, 128], BF16)
        self.ident_f = sb("ident_f", [128, 128], F32)
        self.iota_f = sb("iota_f", [128, 128], F32)
        self.gcols = sb("gcols", [128, 4, 8], F32)
        self.lam = sb("lam", [128, 4], F32)
        self.subg = sb("subg", [128, 128], F32)
        self.rb31 = sb("rb31", [128, 8], F32)
        self.mhalf = sb("mhalf", [128, 4], F32)
        self.pA = self.a_ptr
        self.bias = sb("bias", [128, 8, 9, 128], BF16)
        self.outT = sb("outT", [128, 8, NT_OWN * 128], BF16)
        S.op("pool", lambda e: e.memset(self.mhalf.t[:], -0.5), writes=[self.mhalf.b])
        cset = Buf("consts")
        for tt, src in ((self.ident_bf, I["ident_bf"]), (self.ident_f, I["ident_f"]), (self.iota_f, I["iota_f"]),
                        (self.gcols, I["gcols"]), (self.subg, I["subln_rep"]), (self.rb31, I["rb31_rep"])):
            S.dma("sp", lambda e, tt=tt, src=src: e.dma_start(out=tt.t[:], in_=src), cset, writes=[tt.b])
        with contextlib.ExitStack() as st:
            lamin = self.sb(st, "lamin", [128, 4, 64], F32)
            prod = self.sb(st, "lamprod", [128, 2, 64], F32)
            red = self.sb(st, "lamred", [128, 2], F32)
            ex = self.sb(st, "lamex", [128, 2], F32)
            mb = self.sb(st, "mband", [128, 9, 128], F32)
            gb = [self.sb(st, f"gb{i}", [128, 9, 128], F32) for i in range(2)]
            S.dma("sp", lambda e: e.dma_start(out=lamin.t[:], in_=I["lam_rep"]), cset, writes=[lamin.b])
            S.dma("sp", lambda e: e.dma_start(out=mb.t[:], in_=I["mband"]), cset, writes=[mb.b])
            S.op("dve", lambda e: e.tensor_tensor(out=prod.t[:, 0, :], in0=lamin.t[:, 0, :], in1=lamin.t[:, 1, :], op=ALU.mult), reads=[lamin.b], writes=[prod.b])
            S.op("dve", lambda e: e.tensor_tensor(out=prod.t[:, 1, :], in0=lamin.t[:, 2, :], in1=lamin.t[:, 3, :], op=ALU.mult), reads=[lamin.b], writes=[prod.b])
            S.op("dve", lambda e: e.tensor_reduce(out=red.t[:], in_=prod.t[:], axis=AX.X, op=ALU.add), reads=[prod.b], writes=[red.b])
            S.op("act", lambda e: e.activation(out=ex.t[:], in_=red.t[:], func=AF.Exp), reads=[red.b], writes=[ex.b])
            S.op("dve", lambda e: e.tensor_tensor(out=self.lam.t[:, 0:1], in0=ex.t[:, 0:1], in1=ex.t[:, 1:2], op=ALU.subtract), reads=[ex.b], writes=[self.lam.b])
            S.op("dve", lambda e: e.tensor_scalar(out=self.lam.t[:, 0:1], in0=self.lam.t[:, 0:1], scalar1=0.2, scalar2=None, op0=ALU.add), reads=[self.lam.b], writes=[self.lam.b])
            S.op("dve", lambda e: e.tensor_scalar(out=self.lam.t[:, 1:2], in0=self.lam.t[:, 0:1], scalar1=-1.0, scalar2=None, op0=ALU.mult), reads=[self.lam.b], writes=[self.lam.b])
            S.op("dve", lambda e: e.tensor_scalar(out=self.subg.t[:], in0=self.subg.t[:], scalar1=0.8, scalar2=None, op0=ALU.mult), reads=[self.subg.b], writes=[self.subg.b])
            for h in range(8):
                g = gb[h % 2]
                S.dma("sp", lambda e, g=g, h=h: e.dma_start(out=g.t[:], in_=I["gbias"][h]), g.b, writes=[g.b])
                S.op("dve", lambda e, g=g, h=h: e.scalar_tensor_tensor(out=self.bias.t[:, h, :, :], in0=g.t[:], scalar=self.rb31.t[:, h:h + 1], in1=mb.t[:], op0=ALU.subtract, op1=ALU.add),
                     reads=[g.b, self.rb31.b, mb.b], writes=[self.bias.b])
            S.barrier()

    def load_weight(self, st_tiles, dst, col0, ncols, w_ap, gsel, kcs=range(8), queue="sp"):
        S = self.S
        for kc in kcs:
            stg = st_tiles[kc % len(st_tiles)]
            S.dma(queue, lambda e, stg=stg, kc=kc: e.dma_start(out=stg.t[:, 0:ncols], in_=w_ap[kc * 128:(kc + 1) * 128, col0:col0 + ncols]), stg.b, writes=[stg.b])
            if gsel is None:
                S.op("pool", lambda e, stg=stg, kc=kc: e.tensor_copy(out=dst.t[:, kc, 0:ncols], in_=stg.t[:, 0:ncols]), reads=[stg.b], writes=[dst.b])
            else:
                S.op("pool", lambda e, stg=stg, kc=kc: e.tensor_scalar(out=dst.t[:, kc, 0:ncols], in0=stg.t[:, 0:ncols], scalar1=self.gcols.t[:, gsel, kc:kc + 1], scalar2=None, op0=ALU.mult),
                     reads=[stg.b, self.gcols.b], writes=[dst.b])

    def norm_tiles(self, x_ap_fn, tiles, xt, sqj, ss, rstd, xn, eps=1e-6):
        S = self.S
        n = len(tiles)
        for i, (tid, xts, xns) in enumerate(tiles):
            S.dma("sp", lambda e, xts=xts, tid=tid: e.dma_start(out=xts.t[:], in_=x_ap_fn(tid)), xts.b, writes=[xts.b])
            S.op("act", lambda e, xts=xts, i=i: e.activation(out=sqj.t[:], in_=xts.t[:], func=AF.Square, accum_out=ss.t[:, i:i + 1]), reads=[xts.b], writes=[sqj.b, ss.b])
        S.op("dve", lambda e: e.tensor_scalar(out=rstd.t[:, 0:n], in0=ss.t[:, 0:n], scalar1=1.0 / D, scalar2=eps, op0=ALU.mult, op1=ALU.add), reads=[ss.b], writes=[rstd.b])
        S.op("pool", lambda e: e.tensor_tensor(out=rstd.t[:, 0:n], in0=rstd.t[:, 0:n], in1=self.mhalf.t[:, 0:n], op=ALU.pow), reads=[rstd.b, self.mhalf.b], writes=[rstd.b])
        for i, (tid, xts, xns) in enumerate(tiles):
            S.op("dve", lambda e, xts=xts, xns=xns, i=i: e.tensor_scalar(out=xns.t[:], in0=xts.t[:], scalar1=rstd.t[:, i:i + 1], scalar2=None, op0=ALU.mult), reads=[xts.b, rstd.b], writes=[xns.b])

    def transpose_tiles(self, tiles_xn, hT, tbanks, cnt0, gsel=None, col0=0, npart=128):
        S = self.S
        if gsel is not None or npart != 128:
            for i, xns in enumerate(tiles_xn):
                bk = tbanks[(cnt0 + i) % len(tbanks)]
                pt = bk.t.bitcast(BF16)
                for kc in range(8):
                    S.op("pe", lambda e, pt=pt, xns=xns, kc=kc: e.transpose(out=pt[:, kc * 128:kc * 128 + npart], in_=xns.t[0:npart, kc * 128:(kc + 1) * 128], identity=self.ident_bf.t[0:npart, 0:npart]),
                         reads=[xns.b, self.ident_bf.b], writes=[bk.b])
                for kc in range(8):
                    c0 = col0 + i * npart
                    if gsel is None:
                        S.op("act", lambda e, pt=pt, kc=kc, c0=c0: e.copy(out=hT.t[:, kc, c0:c0 + npart], in_=pt[:, kc * 128:kc * 128 + npart]), reads=[bk.b], writes=[hT.b])
                    else:
                        S.op("act", lambda e, pt=pt, kc=kc, c0=c0: e.activation(out=hT.t[:, kc, c0:c0 + npart], in_=pt[:, kc * 128:kc * 128 + npart], func=AF.Copy, scale=self.gcols.t[:, gsel, kc:kc + 1]),
                             reads=[bk.b, self.gcols.b], writes=[hT.b])
            return
        for i, xns in enumerate(tiles_xn):
            bk = tbanks[(cnt0 + i) % len(tbanks)]
            pt = bk.t.bitcast(BF16)
            for kc in range(8):
                S.op("pe", lambda e, pt=pt, xns=xns, kc=kc: e.transpose(out=pt[:, kc * 128:(kc + 1) * 128], in_=xns.t[:, kc * 128:(kc + 1) * 128], identity=self.ident_bf.t[:]),
                     reads=[xns.b, self.ident_bf.b], writes=[bk.b])
            S.op("act", lambda e, pt=pt, i=i: e.copy(out=hT.t[:, :, i * 128:(i + 1) * 128], in_=pt[:, :].rearrange("p (k t) -> p k t", k=8)), reads=[bk.b], writes=[hT.b])

    def phase_kv(self):
        S, I = self.S, self.I
        with contextlib.ExitStack() as st:
            sb = lambda name, shape, dt: self.sb(st, name, shape, dt)
            wk = sb("wk", [128, 8, 1024], BF16)
            wv = sb("wv", [128, 8, 1024], BF16)
            wst = [sb(f"wst{i}", [128, 1024], F32) for i in range(2)]
            self.load_weight(wst, wk, 4096, 1024, I["w_in"], 0)
            self.load_weight(wst, wv, 5120, 1024, I["w_in"], 0)
            xt = [sb(f"xt{i}", [128, 1024], F32) for i in range(8)]
            xn = [sb(f"xn{i}", [128, 1024], BF16) for i in range(12)]
            sqj = sb("sqj", [128, 1024], BF16)
            ss = [sb(f"ss{i}", [128, 4], F32) for i in range(3)]
            rstd = [sb(f"rstd{i}", [128, 4], F32) for i in range(3)]
            hT = [sb(f"hT{i}", [128, 8, 512], BF16) for i in range(2)]
            kst = [sb(f"kst{i}", [128, 8, 512], BF16) for i in range(2)]
            vst = [sb(f"vst{i}", [128, 4, 1024], BF16) for i in range(2)]
            tb = self.bank[0:2]
            mb = self.bank[2:8]
            NB = NT_ALL // 4
            x_all = I["x_all"]

            def A(bi):
                tiles = [(bi * 4 + i, xt[(bi * 4 + i) % 8], xn[(bi * 4 + i) % 12]) for i in range(4)]
                self.norm_tiles(lambda tid: x_all[tid * 128:(tid + 1) * 128, :], tiles, xt, sqj, ss[bi % 3], rstd[bi % 3], xn)

            def Bs(bi):
                self.transpose_tiles([xn[(bi * 4 + i) % 12] for i in range(4)], hT[bi % 2], tb, bi * 4)

            cnt = [0]

            def C(bi):
                h = hT[bi % 2]
                ks, vs = kst[bi % 2], vst[bi % 2]
                for j in range(8):
                    bk = mb[cnt[0] % 6]
                    cnt[0] += 1
                    for kc in range(8):
                        S.op("pe", lambda e, bk=bk, kc=kc, j=j: e.matmul(bk.t[:], lhsT=wk.t[:, kc, j * 128:(j + 1) * 128], rhs=h.t[:, kc, :], start=(kc == 0), stop=(kc == 7)),
                             reads=[wk.b, h.b], writes=[bk.b])
                    eng = "act" if j % 2 == 0 else "dve"
                    if eng == "act":
                        S.op("act", lambda e, bk=bk, j=j: e.copy(out=ks.t[:, j, :], in_=bk.t[:]), reads=[bk.b], writes=[ks.b])
                    else:
                        S.op("dve", lambda e, bk=bk, j=j: e.tensor_copy(out=ks.t[:, j, :], in_=bk.t[:]), reads=[bk.b], writes=[ks.b])
                S.dma("sp", lambda e: e.dma_start(out=self.KT_d[:, :, bi * 512:(bi + 1) * 512].rearrange("h p t -> p h t"), in_=ks.t[:]), ks.b, reads=[ks.b], writes=[self.B_KT])
                for i in range(4):
                    for hh in range(2):
                        bk = mb[cnt[0] % 6]
                        cnt[0] += 1
                        for kc in range(8):
                            S.op("pe", lambda e, bk=bk, kc=kc, i=i, hh=hh: e.matmul(bk.t[:], lhsT=h.t[:, kc, i * 128:(i + 1) * 128], rhs=wv.t[:, kc, hh * 512:(hh + 1) * 512], start=(kc == 0), stop=(kc == 7)),
                                 reads=[wv.b, h.b], writes=[bk.b])
                        if hh == 0:
                            S.op("act", lambda e, bk=bk, i=i, hh=hh: e.copy(out=vs.t[:, i, hh * 512:(hh + 1) * 512], in_=bk.t[:]), reads=[bk.b], writes=[vs.b])
                        else:
                            S.op("dve", lambda e, bk=bk, i=i, hh=hh: e.tensor_copy(out=vs.t[:, i, hh * 512:(hh + 1) * 512], in_=bk.t[:]), reads=[bk.b], writes=[vs.b])
                    S.dma("act", lambda e, i=i: e.dma_start(out=self.V_d[:, :, bi * 4 + i, :].rearrange("h p e -> p h e"), in_=vs.t[:, i, :].rearrange("p (h e) -> p h e", h=8)),
                          vs.b, reads=[vs.b], writes=[self.B_V])

            A(0)
            A(1)
            Bs(0)
            for bi in range(NB):
                if bi + 2 < NB:
                    A(bi + 2)
                if bi + 1 < NB:
                    Bs(bi + 1)
                C(bi)

    def phase_q_attn(self):
        S, I = self.S, self.I
        with contextlib.ExitStack() as st:
            sb = lambda name, shape, dt: self.sb(st, name, shape, dt)
            QT = sb("QT", [128, 8, NT_OWN * 128], BF16)
            with contextlib.ExitStack() as st2:
                sb2 = lambda name, shape, dt: self.sb(st2, name, shape, dt)
                wq = sb2("wq", [128, 8, 1024], BF16)
                wst = [sb2(f"wst{i}", [128, 1024], F32) for i in range(2)]
                self.load_weight(wst, wq, 3072, 1024, I["w_in"], 0)
                xt = [sb2(f"xt{i}", [128, 1024], F32) for i in range(4)]
                xn = [sb2(f"xn{i}", [128, 1024], BF16) for i in range(4)]
                sqj = sb2("sqj", [128, 1024], BF16)
                ss = sb2("ss", [128, 4], F32)
                rstd = sb2("rstd", [128, 4], F32)
                hT = sb2("hT", [128, 8, 512], BF16)
                x_own = I["x_own"]
                cnt = 0
                for bi in range(4):
                    tiles = [(bi * 4 + i, xt[i], xn[i]) for i in range(4)]
                    self.norm_tiles(lambda tid: x_own[tid * 128:(tid + 1) * 128, :], tiles, xt, sqj, ss, rstd, xn)
                    self.transpose_tiles(xn, hT, self.bank[0:2], bi * 4)
                    for j in range(8):
                        bk = self.bank[2 + cnt % 6]
                        cnt += 1
                        for kc in range(8):
                            S.op("pe", lambda e, bk=bk, kc=kc, j=j: e.matmul(bk.t[:], lhsT=wq.t[:, kc, j * 128:(j + 1) * 128], rhs=hT.t[:, kc, :], start=(kc == 0), stop=(kc == 7)),
                                 reads=[wq.b, hT.b], writes=[bk.b])
                        S.op("act", lambda e, bk=bk, j=j, bi=bi: e.activation(out=QT.t[:, j, bi * 512:(bi + 1) * 512], in_=bk.t[:], func=AF.Copy, scale=0.125), reads=[bk.b], writes=[QT.b])
                S.barrier()
            KT = [sb(f"KT{i}", [128, 4096], BF16) for i in range(4)]
            V1 = [sb(f"V1{i}", [128, 32, 130], BF16) for i in range(4)]
            E = [sb(f"E{i}", [128, 512], BF16) for i in range(3)]
            tmp0 = [sb(f"tmp0{i}", [128, 128], F32) for i in range(4)]
            oc = sb("oc", [128, 128], F32)
            on = sb("on", [128, 128], BF16)
            sqj = sb("sqj2", [128, 128], F32)
            sm = sb("sm", [128, 8], F32)
            for v in V1:
                S.op("pool", lambda e, v=v: e.memset(v.t[:, :, 128:130], 1.0), writes=[v.b])
            sbanks = self.bank[0:3]
            obanks = self.bank[3:7]
            tbank = self.bank[7]
            oacc = [(obanks[j], 0, obanks[j].b) for j in range(4)]
            LA = 2
            steps = []
            for h in range(8):
                for g in (3, 2, 1, 0):
                    for c in range(2):
                        nk = 32 * g + 32
                        for kt in range(nk):
                            steps.append((h, g, c, kt, nk))

            def jmin_of(g, kt):
                return max(0, -((-(kt - 32 * g - 7)) // 8))

            def emit_qk(si):
                h, g, c, kt, nk = steps[si]
                if g == 3 and c == 0 and kt == 0:
                    for q in range(4):
                        S.dma("sp", lambda e, q=q, h=h: e.dma_start(out=KT[q].t[:], in_=self.KT_d[h, :, q * 4096:(q + 1) * 4096]), KT[q].b, reads=[self.B_KT], writes=[KT[q].b])
                jmin = jmin_of(g, kt)
                band = {}
                for j in range(jmin, 4):
                    b = kt - (32 * g + 8 * j)
                    if -1 <= b <= 7:
                        band[j] = b
                sbk = sbanks[si % 3]
                Et = E[si % 3]
                kq, kl = kt // 32, kt % 32
                lhs = KT[kq].t[c * 64:(c + 1) * 64, kl * 128:(kl + 1) * 128]
                plain = [j for j in range(jmin, 4) if j not in band]
                for j, b in band.items():
                    col = (j - jmin) * 128
                    S.op("pe", lambda e, sbk=sbk, col=col, b=b, h=h: e.matmul(sbk.t[:, col:col + 128], lhsT=self.ident_bf.t[:], rhs=self.bias.t[:, h, b + 1, :], start=True, stop=False),
                         reads=[self.ident_bf.b, self.bias.b], writes=[sbk.b])
                    S.op("pe", lambda e, sbk=sbk, col=col, lhs=lhs, j=j, g=g, h=h, c=c: e.matmul(sbk.t[:, col:col + 128], lhsT=lhs, rhs=QT.t[c * 64:(c + 1) * 64, h, (4 * g + j) * 128:(4 * g + j + 1) * 128], start=False, stop=True),
                         reads=[KT[kq].b, QT.b], writes=[sbk.b])
                if plain:
                    j0 = plain[0]
                    assert plain == list(range(j0, 4))
                    col = (j0 - jmin) * 128
                    ncol = (4 - j0) * 128
                    S.op("pe", lambda e, sbk=sbk, col=col, ncol=ncol, lhs=lhs, j0=j0, g=g, h=h, c=c: e.matmul(sbk.t[:, col:col + ncol], lhsT=lhs, rhs=QT.t[c * 64:(c + 1) * 64, h, (4 * g + j0) * 128:(4 * g + 4) * 128], start=True, stop=True),
                         reads=[KT[kq].b, QT.b], writes=[sbk.b])
                nact = (4 - jmin) * 128
                S.op("act", lambda e, sbk=sbk, Et=Et, nact=nact: e.activation(out=Et.t[:, 0:nact], in_=sbk.t[:, 0:nact], func=AF.Exp), reads=[sbk.b], writes=[Et.b])

            def emit_av(si):
                h, g, c, kt, nk = steps[si]
                if g == 3 and c == 0 and kt == 0:
                    for q in range(4):
                        S.dma("act", lambda e, q=q, h=h: e.dma_start(out=V1[q].t[:, :, 0:128], in_=self.V_d[h, :, q * 32:(q + 1) * 32, :]), V1[q].b, reads=[self.B_V], writes=[V1[q].b])
                jmin = jmin_of(g, kt)
                Et = E[si % 3]
                kq, kl = kt // 32, kt % 32
                for j in range(jmin, 4):
                    ob, oo, obuf = oacc[j]
                    col = (j - jmin) * 128
                    last = (kt == 32 * g + 8 * j + 7)
                    S.op("pe", lambda e, ob=ob, oo=oo, Et=Et, col=col, kq=kq, kl=kl, kt=kt, last=last: e.matmul(ob.t[:, oo:oo + 130], lhsT=Et.t[:, col:col + 128], rhs=V1[kq].t[:, kl, :], start=(kt == 0), stop=last),
                         reads=[Et.b, V1[kq].b], writes=[obuf])
                if kt != nk - 1:
                    return
                for j in range(4):
                    ob, oo, obuf = oacc[j]
                    m = 4 * g + j
                    if c == 0:
                        S.op("dve", lambda e, ob=ob, oo=oo, j=j: e.reciprocal(out=sm.t[:, j:j + 1], in_=ob.t[:, oo + 128:oo + 129]), reads=[obuf], writes=[sm.b])
                        S.op("dve", lambda e, ob=ob, oo=oo, j=j: e.tensor_scalar(out=tmp0[j].t[:], in0=ob.t[:, oo:oo + 128], scalar1=sm.t[:, j:j + 1], scalar2=None, op0=ALU.mult), reads=[obuf, sm.b], writes=[tmp0[j].b])
                    else:
                        S.op("dve", lambda e, ob=ob, oo=oo, j=j: e.reciprocal(out=sm.t[:, 4 + j:5 + j], in_=ob.t[:, oo + 128:oo + 129]), reads=[obuf], writes=[sm.b])
                        S.op("dve", lambda e, j=j: e.tensor_scalar(out=sm.t[:, 4 + j:5 + j], in0=sm.t[:, 4 + j:5 + j], scalar1=self.lam.t[:, 1:2], scalar2=None, op0=ALU.mult), reads=[sm.b, self.lam.b], writes=[sm.b])
                        S.op("dve", lambda e, ob=ob, oo=oo, j=j: e.scalar_tensor_tensor(out=oc.t[:], in0=ob.t[:, oo:oo + 128], scalar=sm.t[:, 4 + j:5 + j], in1=tmp0[j].t[:], op0=ALU.mult, op1=ALU.add),
                             reads=[obuf, sm.b, tmp0[j].b], writes=[oc.b])
                        S.op("dve", lambda e: e.scalar_tensor_tensor(out=sqj.t[:], in0=oc.t[:], scalar=1.0, in1=oc.t[:], op0=ALU.mult, op1=ALU.mult, accum_out=sm.t[:, 0:1]), reads=[oc.b], writes=[sqj.b, sm.b])
                        S.op("dve", lambda e: e.tensor_scalar(out=sm.t[:, 0:1], in0=sm.t[:, 0:1], scalar1=1.0 / 128, scalar2=1e-5, op0=ALU.mult, op1=ALU.add), reads=[sm.b], writes=[sm.b])
                        S.op("pool", lambda e: e.tensor_tensor(out=sm.t[:, 0:1], in0=sm.t[:, 0:1], in1=self.mhalf.t[:, 0:1], op=ALU.pow), reads=[sm.b, self.mhalf.b], writes=[sm.b])
                        S.op("dve", lambda e: e.scalar_tensor_tensor(out=on.t[:], in0=oc.t[:], scalar=sm.t[:, 0:1], in1=self.subg.t[:], op0=ALU.mult, op1=ALU.mult),
                             reads=[oc.b, sm.b, self.subg.b], writes=[on.b])
                        pt = tbank.t.bitcast(BF16)
                        S.op("pe", lambda e, pt=pt: e.transpose(out=pt[:, 0:128], in_=on.t[:], identity=self.ident_bf.t[:]), reads=[on.b, self.ident_bf.b], writes=[tbank.b])
                        S.op("act", lambda e, pt=pt, h=h, m=m: e.copy(out=self.outT.t[:, h, m * 128:(m + 1) * 128], in_=pt[:, 0:128]), reads=[tbank.b], writes=[self.outT.b])

            ns = len(steps)
            for idx in range(ns + LA):
                if idx < ns:
                    emit_qk(idx)
                if idx - LA >= 0:
                    emit_av(idx - LA)

    def dump_attn(self):
        S = self.S
        with contextlib.ExitStack() as st:
            f = self.sb(st, "dumpf", [128, 8, NT_OWN * 128], F32)
            S.op("dve", lambda e: e.tensor_copy(out=f.t[:], in_=self.outT.t[:]), reads=[self.outT.b], writes=[f.b])
            S.dma("sp", lambda e: e.dma_start(out=self.dbg["attn"], in_=f.t[:].rearrange("p h t -> p (h t)")), f.b, reads=[f.b], is_output=True)


def make_in_maps(inp):
    f32 = np.float32
    x = np.ascontiguousarray(inp["x"][0], dtype=f32)
    xt = x.reshape(16, 8, 128, D)
    common = {
        "x_all": x,
        "mem": np.ascontiguousarray(inp["mem"][0], dtype=f32),
        "w_in": np.ascontiguousarray(inp["w_in"][0], dtype=f32),
        "gcols": np.ascontiguousarray(np.stack([inp["norm_mix_g"][0], inp["norm_cross_g"][0], inp["norm_mem_g"][0], inp["norm_ffn_g"][0]], 0).reshape(4, 8, 128).transpose(2, 0, 1), dtype=f32),
        "final_g_rep": np.ascontiguousarray(np.broadcast_to(inp["final_g"][None, :], (128, D)), dtype=f32),
        "conv_wT": np.ascontiguousarray(inp["conv_w"][0].reshape(3, 8, 128).transpose(2, 1, 0), dtype=f32),
        "w_conv_out": np.ascontiguousarray(inp["w_conv_out"][0], dtype=f32),
        "lam_rep": np.ascontiguousarray(np.broadcast_to(np.stack([inp["lambda_q1"][0], inp["lambda_k1"][0], inp["lambda_q2"][0], inp["lambda_k2"][0]], 0)[None], (128, 4, 64)), dtype=f32),
        "subln_rep": np.ascontiguousarray(np.broadcast_to(inp["subln_g"][0][None, :], (128, 128)), dtype=f32),
        "w_attn_out": np.ascontiguousarray(inp["w_attn_out"][0], dtype=f32),
        "w_mix_out": np.ascontiguousarray(inp["w_mix_out"][0], dtype=f32),
        "rb31_rep": np.ascontiguousarray(np.broadcast_to(inp["rel_bias"][31][None, :], (128, 8)), dtype=f32),
        "w_cq": np.ascontiguousarray(inp["w_cq"][0], dtype=f32),
        "w_ckv": np.ascontiguousarray(inp["w_ckv"][0], dtype=f32),
        "w_co": np.ascontiguousarray(inp["w_co"][0], dtype=f32),
        "w_pq": np.ascontiguousarray(inp["w_pq"][0], dtype=f32),
        "skT": np.ascontiguousarray(inp["sub_keys"][0].transpose(1, 0, 3, 2).reshape(16, 128, 128), dtype=f32),
        "peer_u": np.ascontiguousarray(inp["peer_u"][0], dtype=f32),
        "peer_v": np.ascontiguousarray(inp["peer_v"][0], dtype=f32),
        "ident_bf": np.eye(128, dtype=f32).astype(ml_dtypes.bfloat16),
        "ident_f": np.eye(128, dtype=f32),
        "iota_f": np.ascontiguousarray(np.broadcast_to(np.arange(128, dtype=f32)[None, :], (128, 128))),
    }
    rel_bias = np.asarray(inp["rel_bias"], dtype=f32)
    maps = []
    for c in range(NCORE):
        m = dict(common)
        m["x_own"] = np.ascontiguousarray(xt[:, c].reshape(NT_OWN * 128, D))
        halo = np.zeros((16, 2, D), f32)
        for mm in range(16):
            t0 = (8 * mm + c) * 128
            if t0 >= 2:
                halo[mm] = x[t0 - 2:t0]
        m["x_halo"] = halo.reshape(32, D)
        bk, mk = _band_tables(c)
        gb = rel_bias[bk]
        m["gbias"] = np.ascontiguousarray(gb.transpose(3, 1, 0, 2), dtype=f32)
        m["mband"] = np.ascontiguousarray(mk.transpose(1, 0, 2), dtype=f32)
        maps.append(m)
    return maps


_CACHE = {}


def kernel(**inputs):
    stop = inputs.pop("_stop_after", None)
    upto = inputs.pop("_upto", None)
    key = (stop, upto)
    if key not in _CACHE:
        b = Builder(stop_after=stop, upto=upto)
        _CACHE[key] = (b.build(), b.in_shapes)
    nc, in_shapes = _CACHE[key]
    maps = make_in_maps(inputs)
    if upto and upto.startswith("peeronly"):
        m = maps[3]
        for nm, (shp, dt) in in_shapes.items():
            if tuple(m[nm].shape) != tuple(shp):
                m[nm] = np.zeros(shp, np.float32)
        res = run_bass_kernel_spmd(nc, [m], core_ids=[0])
        _CACHE["last_res"] = res
        return res
    res = run_bass_kernel_spmd(nc, maps, core_ids=list(range(NCORE)))
    if stop:
        _CACHE["last_res"] = res
    name = "out"
    out = np.zeros((16, 8, 128, D), np.float32)
    for c in range(NCORE):
        out[:, c] = np.asarray(res.results[c][name], dtype=np.float32).reshape(16, 128, D)
    return out.reshape(1, SEQ, D)


def _bcast_ap(t, offset, dims):
    base = t[:]
    return bass.AP(t, offset, [list(base.ap[0])] + [list(d) for d in dims])


def _norm_res(self, tiles, sqj, ss, rstd, eps=1e-6):
    S = self.S
    n = len(tiles)
    for i, (src, sbuf, xns) in enumerate(tiles):
        S.op("act", lambda e, src=src, i=i: e.activation(out=sqj.t[:], in_=src, func=AF.Square, accum_out=ss.t[:, i:i + 1]), reads=[sbuf], writes=[sqj.b, ss.b])
    S.op("dve", lambda e: e.tensor_scalar(out=rstd.t[:, 0:n], in0=ss.t[:, 0:n], scalar1=1.0 / D, scalar2=eps, op0=ALU.mult, op1=ALU.add), reads=[ss.b], writes=[rstd.b])
    S.op("pool", lambda e: e.tensor_tensor(out=rstd.t[:, 0:n], in0=rstd.t[:, 0:n], in1=self.mhalf.t[:, 0:n], op=ALU.pow), reads=[rstd.b, self.mhalf.b], writes=[rstd.b])
    for i, (src, sbuf, xns) in enumerate(tiles):
        S.op("dve", lambda e, src=src, xns=xns, i=i: e.tensor_scalar(out=xns.t[:], in0=src, scalar1=rstd.t[:, i:i + 1], scalar2=None, op0=ALU.mult), reads=[sbuf, rstd.b], writes=[xns.b])


def _wslab(self, dst, w_ap, col0, ncols, queue="pool"):
    self.S.dma(queue, lambda e: e.dma_start(out=dst.t[:, :, 0:ncols], in_=w_ap[:, col0:col0 + ncols].rearrange("(kc p) c -> p kc c", p=128)), dst.b, writes=[dst.b])


def _phase_mix(self, p0):
    S, I = self.S, self.I
    K = 1024
    with contextlib.ExitStack() as st:
        self.a_ptr = p0
        sb = lambda name, shape, dt, at=None: self.sb(st, name, shape, dt, at=at)
        mergedT = sb("mergedT", [128, 8, 2048], BF16)
        hT = sb("hT", [128, 8, 2048], BF16)
        hTh = sb("hTh", [128, 8, 32], BF16)
        zT = sb("zT", [128, 8, 2048], BF16)
        with contextlib.ExitStack() as st2:
            sb2 = lambda name, shape, dt: self.sb(st2, name, shape, dt)
            xt = [sb2(f"xt{i}", [128, 1024], F32) for i in range(4)]
            xn = [sb2(f"xn{i}", [128, 1024], BF16) for i in range(4)]
            sqj = sb2("sqj", [128, 1024], BF16)
            ss = sb2("ss", [128, 4], F32)
            rstd = sb2("rstd", [128, 4], F32)
            x_own = I["x_own"]
            for bi in range(4):
                tiles = [(bi * 4 + i, xt[i], xn[i]) for i in range(4)]
                self.norm_tiles(lambda tid: x_own[tid * 128:(tid + 1) * 128, :], tiles, xt, sqj, ss, rstd, xn)
                self.transpose_tiles(xn, hT, self.bank[0:2], bi * 4, gsel=0, col0=bi * 512)
            hx, hn = xt[0], xn[0]
            S.dma("sp", lambda e: e.dma_start(out=hx.t[0:32, :], in_=I["x_halo"]), hx.b, writes=[hx.b])
            S.op("act", lambda e: e.activation(out=sqj.t[0:32, :], in_=hx.t[0:32, :], func=AF.Square, accum_out=ss.t[0:32, 0:1]), reads=[hx.b], writes=[sqj.b, ss.b])
            S.op("dve", lambda e: e.tensor_scalar(out=rstd.t[0:32, 0:1], in0=ss.t[0:32, 0:1], scalar1=1.0 / D, scalar2=1e-6, op0=ALU.mult, op1=ALU.add), reads=[ss.b], writes=[rstd.b])
            S.op("pool", lambda e: e.tensor_tensor(out=rstd.t[0:32, 0:1], in0=rstd.t[0:32, 0:1], in1=self.mhalf.t[0:32, 0:1], op=ALU.pow), reads=[rstd.b, self.mhalf.b], writes=[rstd.b])
            S.op("dve", lambda e: e.tensor_scalar(out=hn.t[0:32, :], in0=hx.t[0:32, :], scalar1=rstd.t[0:32, 0:1], scalar2=None, op0=ALU.mult), reads=[hx.b, rstd.b], writes=[hn.b])
            self.transpose_tiles([hn], hTh, self.bank[0:2], 0, gsel=0, col0=0, npart=32)
        S.barrier()
        with contextlib.ExitStack() as st2:
            sb2 = lambda name, shape, dt: self.sb(st2, name, shape, dt)
            wc3 = [[sb2(f"wc3_{i}_{k}", [128, 8, 128], BF16) for k in range(3)] for i in range(2)]
            cwT = sb2("cwT", [128, 8, 3], F32)
            S.dma("sp", lambda e: e.dma_start(out=cwT.t[:], in_=I["conv_wT"]), cwT.b, writes=[cwT.b])
            U2 = sb2("U2", [128, 16, 130], F32)
            ycv = sb2("ycv", [128, 16, 128], F32)
            cbs = sb2("cbs", [128, 2048], F32)
            ccs = [sb2(f"ccs{i}", [128, 512], F32) for i in range(2)]
            cch_ = sb2("cch", [128, 32], F32)
            cnt = 0
            for cch in range(8):
                w3 = wc3[cch % 2]
                for k in range(3):
                    _wslab(self, w3[k], I["w_in"], k * 1024 + cch * 128, 128)
                for tb in range(4):
                    pb = [self.bank[(cnt + k) % 8] for k in range(3)]
                    cnt += 3
                    for k in range(3):
                        for kc in range(8):
                            S.op("pe", lambda e, k=k, kc=kc, tb=tb, w3=w3, pb=pb: e.matmul(pb[k].t[:], lhsT=w3[k].t[:, kc, :], rhs=hT.t[:, kc, tb * 512:(tb + 1) * 512], start=(kc == 0), stop=(kc == 7)),
                                 reads=[w3[k].b, hT.b], writes=[pb[k].b])
                    S.op("act", lambda e, tb=tb, pb=pb: e.copy(out=cbs.t[:, tb * 512:(tb + 1) * 512], in_=pb[0].t[:]), reads=[pb[0].b], writes=[cbs.b])
                    cs = ccs[tb % 2]
                    S.op("act", lambda e, cs=cs, pb=pb: e.copy(out=cs.t[:], in_=pb[1].t[:]), reads=[pb[1].b], writes=[cs.b])
                    S.op("dve", lambda e, cs=cs, pb=pb, tb=tb: e.tensor_tensor(out=U2.t[:, tb * 4:(tb + 1) * 4, 2:130], in0=pb[2].t[:].rearrange("p (m t) -> p m t", m=4), in1=cs.t[:].rearrange("p (m t) -> p m t", m=4), op=ALU.mult),
                         reads=[pb[2].b, cs.b], writes=[U2.b])
                pb = [self.bank[(cnt + k) % 8] for k in range(2)]
                cnt += 2
                for k in range(2):
                    for kc in range(8):
                        S.op("pe", lambda e, k=k, kc=kc, w3=w3, pb=pb: e.matmul(pb[k].t[:, 0:32], lhsT=w3[k + 1].t[:, kc, :], rhs=hTh.t[:, kc, :], start=(kc == 0), stop=(kc == 7)),
                             reads=[w3[k + 1].b, hTh.b], writes=[pb[k].b])
                S.op("act", lambda e, pb=pb: e.copy(out=cch_.t[:], in_=pb[0].t[:, 0:32]), reads=[pb[0].b], writes=[cch_.b])
                S.op("dve", lambda e, pb=pb: e.tensor_tensor(out=U2.t[:, :, 0:2], in0=pb[1].t[:, 0:32].rearrange("p (m t) -> p m t", m=16), in1=cch_.t[:].rearrange("p (m t) -> p m t", m=16), op=ALU.mult),
                     reads=[pb[1].b, cch_.b], writes=[U2.b])
                S.op("dve", lambda e, cch=cch: e.tensor_scalar(out=ycv.t[:], in0=U2.t[:, :, 2:130], scalar1=cwT.t[:, cch, 2:3], scalar2=None, op0=ALU.mult), reads=[U2.b, cwT.b], writes=[ycv.b])
                S.op("dve", lambda e, cch=cch: e.scalar_tensor_tensor(out=ycv.t[:], in0=U2.t[:, :, 1:129], scalar=cwT.t[:, cch, 1:2], in1=ycv.t[:], op0=ALU.mult, op1=ALU.add), reads=[U2.b, cwT.b, ycv.b], writes=[ycv.b])
                S.op("dve", lambda e, cch=cch: e.scalar_tensor_tensor(out=ycv.t[:], in0=U2.t[:, :, 0:128], scalar=cwT.t[:, cch, 0:1], in1=ycv.t[:], op0=ALU.mult, op1=ALU.add), reads=[U2.b, cwT.b, ycv.b], writes=[ycv.b])
                S.op("pool", lambda e, cch=cch: e.tensor_tensor(out=zT.t[:, cch, :], in0=cbs.t[:], in1=ycv.t[:].rearrange("p m t -> p (m t)"), op=ALU.mult), reads=[cbs.b, ycv.b], writes=[zT.b])
        S.barrier()
        with contextlib.ExitStack() as st2:
            sb2 = lambda name, shape, dt: self.sb(st2, name, shape, dt)
            wsl = [[sb2(f"wsl{i}_{k}", [128, 8, 128], BF16) for k in range(4)] for i in range(2)]
            sg = [[sb2(f"sg{i}_{k}", [128, 512], F32) for k in range(2)] for i in range(2)]
            m1 = [sb2(f"m1_{i}", [128, 512], F32) for i in range(2)]
            m2 = [sb2(f"m2_{i}", [128, 512], F32) for i in range(2)]
            it = 0
            for dt_ in range(8):
                ws = wsl[dt_ % 2]
                _wslab(self, ws[0], I["w_conv_out"], dt_ * 128, 128)
                _wslab(self, ws[1], I["w_in"], 6144 + dt_ * 128, 128)
                _wslab(self, ws[2], I["w_attn_out"], dt_ * 128, 128)
                _wslab(self, ws[3], I["w_in"], 7168 + dt_ * 128, 128)
                for tb in range(4):
                    pb = [self.bank[(it % 2) * 4 + k] for k in range(4)]
                    rhs_src = [zT, hT, self.outT, hT]
                    for k in range(4):
                        for kc in range(8):
                            S.op("pe", lambda e, k=k, kc=kc, tb=tb, ws=ws, pb=pb, rhs_src=rhs_src: e.matmul(pb[k].t[:], lhsT=ws[k].t[:, kc, :], rhs=rhs_src[k].t[:, kc, tb * 512:(tb + 1) * 512], start=(kc == 0), stop=(kc == 7)),
                                 reads=[ws[k].b, rhs_src[k].b], writes=[pb[k].b])
                    s0, s1 = sg[it % 2]
                    a1, a2 = m1[it % 2], m2[it % 2]
                    S.op("act", lambda e, s0=s0, pb=pb: e.activation(out=s0.t[:], in_=pb[1].t[:], func=AF.Sigmoid), reads=[pb[1].b], writes=[s0.b])
                    S.op("act", lambda e, s1=s1, pb=pb: e.activation(out=s1.t[:], in_=pb[3].t[:], func=AF.Sigmoid), reads=[pb[3].b], writes=[s1.b])
                    S.op("dve", lambda e, s0=s0, a1=a1, pb=pb: e.tensor_tensor(out=a1.t[:], in0=pb[0].t[:], in1=s0.t[:], op=ALU.mult), reads=[pb[0].b, s0.b], writes=[a1.b])
                    S.op("dve", lambda e, s1=s1, a2=a2, pb=pb: e.tensor_tensor(out=a2.t[:], in0=pb[2].t[:], in1=s1.t[:], op=ALU.mult), reads=[pb[2].b, s1.b], writes=[a2.b])
                    S.op("pool", lambda e, a1=a1, a2=a2, dt_=dt_, tb=tb: e.tensor_tensor(out=mergedT.t[:, dt_, tb * 512:(tb + 1) * 512], in0=a1.t[:], in1=a2.t[:], op=ALU.add), reads=[a1.b, a2.b], writes=[mergedT.b])
                    it += 1
        S.barrier()
        with contextlib.ExitStack() as st2:
            self.a_ptr = p0 + 97 * 1024
            wmix = self.sb(st2, "wmix", [128, 8, 1024], BF16)
            _wslab(self, wmix, I["w_mix_out"], 0, 1024)
            xres = self.xres
            S.dma("sp", lambda e: e.dma_start(out=xres.t[:], in_=I["x_own"].rearrange("(m p) d -> p m d", p=128)), xres.b, writes=[xres.b])
            it = 0
            for m in range(16):
                for dh in range(2):
                    bk = self.bank[it % 8]
                    it += 1
                    for kc in range(8):
                        S.op("pe", lambda e, bk=bk, kc=kc, m=m, dh=dh: e.matmul(bk.t[:], lhsT=mergedT.t[:, kc, m * 128:(m + 1) * 128], rhs=wmix.t[:, kc, dh * 512:(dh + 1) * 512], start=(kc == 0), stop=(kc == 7)),
                             reads=[mergedT.b, wmix.b], writes=[bk.b])
                    S.op("dve", lambda e, bk=bk, m=m, dh=dh: e.tensor_tensor(out=xres.t[:, m, dh * 512:(dh + 1) * 512], in0=bk.t[:], in1=xres.t[:, m, dh * 512:(dh + 1) * 512], op=ALU.add),
                         reads=[bk.b, xres.b], writes=[xres.b])


def _dump_res(self, name):
    S = self.S
    S.dma("sp", lambda e: e.dma_start(out=self.dbg[name].rearrange("(m p) d -> p m d", p=128), in_=self.xres.t[:]), self.xres.b, reads=[self.xres.b], is_output=True)


def _phase_cross(self, p0):
    S, I = self.S, self.I
    pA = self.pA
    xres = self.xres
    regB = p0 + 96 * 1024
    with contextlib.ExitStack() as st:
        self.a_ptr = pA
        sb = lambda name, shape, dt: self.sb(st, name, shape, dt)
        hcT = sb("hcT", [128, 8, 2048], BF16)
        wcq = sb("wcq", [128, 8, 1024], BF16)
        wco = sb("wco", [128, 8, 1024], BF16)
        kT = sb("kT", [128, 8, 256], BF16)
        vC = sb("vC", [128, 2, 1024], BF16)
        ones = sb("ones", [128, 128], BF16)
        qcT = sb("qcT", [128, 8, 512], BF16)
        assert self.a_ptr <= p0 + 32 * 1024, (self.a_ptr, p0)
        S.op("pool", lambda e: e.memset(ones.t[:], 1.0), writes=[ones.b])
        _wslab(self, wcq, I["w_cq"], 0, 1024)
        _wslab(self, wco, I["w_co"], 0, 1024)
        with contextlib.ExitStack() as st2:
            self.a_ptr = regB
            sb2 = lambda name, shape, dt: self.sb(st2, name, shape, dt)
            wckv = sb2("wckv", [128, 8, 2048], BF16)
            mT = sb2("mT", [128, 8, 256], BF16)
            xt = [sb2(f"xt{i}", [128, 1024], F32) for i in range(2)]
            xn = [sb2(f"xn{i}", [128, 1024], BF16) for i in range(2)]
            sqj = sb2("sqj", [128, 1024], BF16)
            ss = sb2("ss", [128, 4], F32)
            rstd = sb2("rstd", [128, 4], F32)
            _wslab(self, wckv, I["w_ckv"], 0, 2048)
            tiles = [(i, xt[i], xn[i]) for i in range(2)]
            self.norm_tiles(lambda tid: I["mem"][tid * 128:(tid + 1) * 128, :], tiles, xt, sqj, ss, rstd, xn)
            self.transpose_tiles(xn, mT, self.bank[0:2], 0, gsel=2, col0=0)
            for ct in range(8):
                bk = self.bank[2 + ct % 6]
                for kc in range(8):
                    S.op("pe", lambda e, bk=bk, kc=kc, ct=ct: e.matmul(bk.t[:, 0:256], lhsT=wckv.t[:, kc, ct * 128:(ct + 1) * 128], rhs=mT.t[:, kc, :], start=(kc == 0), stop=(kc == 7)),
                         reads=[wckv.b, mT.b], writes=[bk.b])
                S.op("act", lambda e, bk=bk, ct=ct: e.copy(out=kT.t[:, ct, :], in_=bk.t[:, 0:256]), reads=[bk.b], writes=[kT.b])
            for mt in range(2):
                for hh in range(2):
                    bk = self.bank[2 + (mt * 2 + hh) % 6]
                    for kc in range(8):
                        S.op("pe", lambda e, bk=bk, kc=kc, mt=mt, hh=hh: e.matmul(bk.t[:], lhsT=mT.t[:, kc, mt * 128:(mt + 1) * 128], rhs=wckv.t[:, kc, 1024 + hh * 512:1024 + (hh + 1) * 512], start=(kc == 0), stop=(kc == 7)),
                             reads=[wckv.b, mT.b], writes=[bk.b])
                    S.op("dve", lambda e, bk=bk, mt=mt, hh=hh: e.tensor_copy(out=vC.t[:, mt, hh * 512:(hh + 1) * 512], in_=bk.t[:]), reads=[bk.b], writes=[vC.b])
        S.barrier()
        with contextlib.ExitStack() as st2:
            self.a_ptr = regB
            sb2 = lambda name, shape, dt: self.sb(st2, name, shape, dt)
            xn = [sb2(f"xn{i}", [128, 1024], BF16) for i in range(4)]
            sqj = sb2("sqj", [128, 1024], BF16)
            ss = sb2("ss", [128, 4], F32)
            rstd = sb2("rstd", [128, 4], F32)
            P = [sb2(f"P{i}", [128, 2, 512], BF16) for i in range(2)]
            oT = sb2("oT", [128, 8, 512], BF16)
            R = [sb2(f"R{i}", [128, 512], F32) for i in range(2)]
            for bi in range(4):
                tiles = [(xres.t[:, bi * 4 + i, :], xres.b, xn[i]) for i in range(4)]
                _norm_res(self, tiles, sqj, ss, rstd)
                self.transpose_tiles(xn, hcT, self.bank[0:2], bi * 4, gsel=1, col0=bi * 512)
            it = 0
            for tb in range(4):
                for ct in range(8):
                    bk = self.bank[it % 8]
                    it += 1
                    for kc in range(8):
                        S.op("pe", lambda e, bk=bk, kc=kc, ct=ct, tb=tb: e.matmul(bk.t[:], lhsT=wcq.t[:, kc, ct * 128:(ct + 1) * 128], rhs=hcT.t[:, kc, tb * 512:(tb + 1) * 512], start=(kc == 0), stop=(kc == 7)),
                             reads=[wcq.b, hcT.b], writes=[bk.b])
                    if ct % 2 == 0:
                        S.op("act", lambda e, bk=bk, ct=ct: e.copy(out=qcT.t[:, ct, :], in_=bk.t[:]), reads=[bk.b], writes=[qcT.b])
                    else:
                        S.op("dve", lambda e, bk=bk, ct=ct: e.tensor_copy(out=qcT.t[:, ct, :], in_=bk.t[:]), reads=[bk.b], writes=[qcT.b])
                for hd in range(4):
                    Pt = P[hd % 2]
                    Rt = R[hd % 2]
                    for mt in range(2):
                        bk = self.bank[it % 8]
                        it += 1
                        for half in range(2):
                            S.op("pe", lambda e, bk=bk, hd=hd, half=half, mt=mt: e.matmul(bk.t[:], lhsT=kT.t[:, hd * 2 + half, mt * 128:(mt + 1) * 128], rhs=qcT.t[:, hd * 2 + half, :], start=(half == 0), stop=(half == 1)),
                                 reads=[kT.b, qcT.b], writes=[bk.b])
                        S.op("act", lambda e, bk=bk, Pt=Pt, mt=mt: e.activation(out=Pt.t[:, mt, :], in_=bk.t[:], func=AF.Exp, scale=1.0 / 16), reads=[bk.b], writes=[Pt.b])
                    bs = self.bank[it % 8]
                    it += 1
                    for mt in range(2):
                        S.op("pe", lambda e, bs=bs, Pt=Pt, mt=mt: e.matmul(bs.t[:], lhsT=ones.t[:], rhs=Pt.t[:, mt, :], start=(mt == 0), stop=(mt == 1)), reads=[ones.b, Pt.b], writes=[bs.b])
                    S.op("dve", lambda e, bs=bs, Rt=Rt: e.reciprocal(out=Rt.t[:], in_=bs.t[:]), reads=[bs.b], writes=[Rt.b])
                    for half in range(2):
                        bo = self.bank[it % 8]
                        it += 1
                        for mt in range(2):
                            S.op("pe", lambda e, bo=bo, Pt=Pt, mt=mt, hd=hd, half=half: e.matmul(bo.t[:], lhsT=vC.t[:, mt, hd * 256 + half * 128:hd * 256 + (half + 1) * 128], rhs=Pt.t[:, mt, :], start=(mt == 0), stop=(mt == 1)),
                                 reads=[vC.b, Pt.b], writes=[bo.b])
                        S.op("dve", lambda e, bo=bo, Rt=Rt, hd=hd, half=half: e.tensor_tensor(out=oT.t[:, hd * 2 + half, :], in0=bo.t[:], in1=Rt.t[:], op=ALU.mult), reads=[bo.b, Rt.b], writes=[oT.b])
                for i in range(4):
                    m = tb * 4 + i
                    for dh in range(2):
                        bk = self.bank[it % 8]
                        it += 1
                        for ct in range(8):
                            S.op("pe", lambda e, bk=bk, ct=ct, i=i, dh=dh: e.matmul(bk.t[:], lhsT=oT.t[:, ct, i * 128:(i + 1) * 128], rhs=wco.t[:, ct, dh * 512:(dh + 1) * 512], start=(ct == 0), stop=(ct == 7)),
                                 reads=[oT.b, wco.b], writes=[bk.b])
                        S.op("dve", lambda e, bk=bk, m=m, dh=dh: e.tensor_tensor(out=xres.t[:, m, dh * 512:(dh + 1) * 512], in0=bk.t[:], in1=xres.t[:, m, dh * 512:(dh + 1) * 512], op=ALU.add),
                             reads=[bk.b, xres.b], writes=[xres.b])


Builder.phase_mix = _phase_mix
Builder.phase_cross = _phase_cross
Builder.dump_res = _dump_res


def _phase_peer(self, p0):
    S, I = self.S, self.I
    xres = self.xres
    pA = self.pA
    X2_d = self.dscr("X2_d", [NT_OWN * 128, D], F32)
    B_X2 = Buf("X2_d")
    MAGIC = 12582912.0
    with contextlib.ExitStack() as st:
        self.a_ptr = p0 + 96 * 1024
        sb = lambda name, shape, dt: self.sb(st, name, shape, dt)
        with contextlib.ExitStack() as st2:
            sb2 = lambda name, shape, dt: self.sb(st2, name, shape, dt)
            xn = [sb2(f"xn{i}", [128, 1024], BF16) for i in range(4)]
            sqj = sb2("sqj", [128, 1024], BF16)
            ss = sb2("ss", [128, 4], F32)
            rstd = sb2("rstd", [128, 4], F32)
            self.a_ptr = pA
            hfT = self.sb(st, "hfT", [128, 8, 2048], BF16)
            for bi in range(4):
                tiles = [(xres.t[:, bi * 4 + i, :], xres.b, xn[i]) for i in range(4)]
                _norm_res(self, tiles, sqj, ss, rstd)
                self.transpose_tiles(xn, hfT, self.bank[0:2], bi * 4, gsel=3, col0=bi * 512)
            S.dma("sp", lambda e: e.dma_start(out=X2_d.rearrange("(m p) d -> p m d", p=128), in_=xres.t[:]), xres.b, reads=[xres.b], writes=[B_X2])
        S.barrier()
        self.a_ptr = pA + 32 * 1024
        RT = self.sb(st, "RT", [128, 3, 2048], F32)
        pR = self.a_ptr
        with contextlib.ExitStack() as st2:
            sb2 = lambda name, shape, dt: self.sb(st2, name, shape, dt)
            ub = [sb2(f"ub{i}", [128, 4, 1024], BF16) for i in range(2)]
            vb = [sb2(f"vb{i}", [128, 4, 1024], BF16) for i in range(2)]
            uts = [sb2(f"uts{i}", [128, 8, 128], BF16) for i in range(3)]
            for gq in range(32):
                u, v = ub[gq % 2], vb[gq % 2]
                S.dma("pool", lambda e, u=u, gq=gq: e.dma_start(out=u.t[:], in_=I["peer_u"][gq * 512:(gq + 1) * 512, :].rearrange("(i p) d -> p i d", p=128)), u.b, writes=[u.b])
                S.dma("pool", lambda e, v=v, gq=gq: e.dma_start(out=v.t[:], in_=I["peer_v"][gq * 512:(gq + 1) * 512, :].rearrange("(i p) d -> p i d", p=128)), v.b, writes=[v.b])
                S.dma("sp", lambda e, v=v, gq=gq: e.dma_start(out=self.Vb_d[gq * 512:(gq + 1) * 512, :].rearrange("(i p) d -> p i d", p=128), in_=v.t[:]), v.b, reads=[v.b], writes=[self.B_Vb])
                for i in range(4):
                    et = gq * 4 + i
                    bk = self.bank[et % 4]
                    pt = bk.t.bitcast(BF16)
                    ut = uts[et % 3]
                    for kc in range(8):
                        S.op("pe", lambda e, pt=pt, u=u, i=i, kc=kc: e.transpose(out=pt[:, kc * 128:(kc + 1) * 128], in_=u.t[:, i, kc * 128:(kc + 1) * 128], identity=self.ident_bf.t[:]),
                             reads=[u.b, self.ident_bf.b], writes=[bk.b])
                    if et % 2 == 0:
                        S.op("act", lambda e, pt=pt, ut=ut: e.copy(out=ut.t[:], in_=pt[:, :].rearrange("p (k t) -> p k t", k=8)), reads=[bk.b], writes=[ut.b])
                    else:
                        S.op("dve", lambda e, pt=pt, ut=ut: e.tensor_copy(out=ut.t[:], in_=pt[:, :].rearrange("p (k t) -> p k t", k=8)), reads=[bk.b], writes=[ut.b])
                    S.dma("act", lambda e, ut=ut, et=et: e.dma_start(out=self.UT_d[et], in_=ut.t[:]), ut.b, reads=[ut.b], writes=[self.B_UT])
        S.barrier()
        if self.peer_stop == "p0":
            with contextlib.ExitStack() as st2:
                tu = self.sb(st2, "tu", [128, 1024], BF16)
                tv = self.sb(st2, "tv", [128, 1024], BF16)
                tf = self.sb(st2, "tf", [128, 2, 1024], F32)
                S.dma("sp", lambda e: e.dma_start(out=tu.t[:], in_=self.UT_d[77].rearrange("p k e -> p (k e)")), tu.b, reads=[self.B_UT], writes=[tu.b])
                S.dma("sp", lambda e: e.dma_start(out=tv.t[:], in_=self.Vb_d[77 * 128:78 * 128, :]), tv.b, reads=[self.B_Vb], writes=[tv.b])
                S.op("dve", lambda e: e.tensor_copy(out=tf.t[:, 0, :], in_=tu.t[:]), reads=[tu.b], writes=[tf.b])
                S.op("dve", lambda e: e.tensor_copy(out=tf.t[:, 1, :], in_=tv.t[:]), reads=[tv.b], writes=[tf.b])
                S.dma("sp", lambda e: e.dma_start(out=self.out[0:128, :], in_=tf.t[:, 0, :]), tf.b, reads=[tf.b], is_output=True)
                S.dma("sp", lambda e: e.dma_start(out=self.out[128:256, :], in_=tf.t[:, 1, :]), tf.b, reads=[tf.b], is_output=True)
                S.dma("sp", lambda e: e.dma_start(out=self.out[256:384, :], in_=xres.t[:, 5, :]), xres.b, reads=[xres.b], is_output=True)
            return
        with contextlib.ExitStack() as st2:
            self.a_ptr = pR
            sb2 = lambda name, shape, dt: self.sb(st2, name, shape, dt)
            wpq = sb2("wpq", [128, 8, 2048], BF16)
            skT = sb2("skT", [128, 16, 128], BF16)
            qT = [sb2(f"qT{i}", [128, 16, 128], BF16) for i in range(2)]
            sc = sb2("sc", [128, 16, 128], F32)
            scr = sb2("scr", [128, 16, 128], F32)
            vals = sb2("vals", [128, 16, 16], F32)
            idx = sb2("idx", [128, 16, 16], U32)
            idxf = sb2("idxf", [128, 16, 16], F32)
            cand = sb2("cand", [128, 8, 256], F32)
            cscr = sb2("cscr", [128, 256], F32)
            ts = sb2("ts", [128, 8, 16], F32)
            tc_ = sb2("tc", [128, 8, 16], U32)
            tcf = sb2("tcf", [128, 8, 16], F32)
            af = sb2("af", [128, 8, 16], F32)
            bf = sb2("bf", [128, 8, 16], F32)
            oh = sb2("oh", [128, 8, 16, 16], F32)
            IJg = sb2("IJg", [128, 3, 128], F32)
            esum = sb2("esum", [128, 8], F32)
            _wslab(self, wpq, I["w_pq"], 0, 2048)
            S.dma("pool", lambda e: e.dma_start(out=skT.t[:], in_=I["skT"].rearrange("g d n -> d g n")), skT.b, writes=[skT.b])
            iota16 = self.iota_f.t[:, 0:16]
            for m in range(16):
                q = qT[m % 2]
                for gi in range(16):
                    bk = self.bank[gi % 4]
                    for kc in range(8):
                        S.op("pe", lambda e, bk=bk, kc=kc, gi=gi, m=m: e.matmul(bk.t[:, 0:128], lhsT=wpq.t[:, kc, gi * 128:(gi + 1) * 128], rhs=hfT.t[:, kc, m * 128:(m + 1) * 128], start=(kc == 0), stop=(kc == 7)),
                             reads=[wpq.b, hfT.b], writes=[bk.b])
                    S.op("act", lambda e, bk=bk, gi=gi, q=q: e.copy(out=q.t[:, gi, :], in_=bk.t[:, 0:128]), reads=[bk.b], writes=[q.b])
                for gi in range(16):
                    bk = self.bank[4 + gi // 4]
                    S.op("pe", lambda e, bk=bk, gi=gi, q=q: e.matmul(bk.t[:, (gi % 4) * 128:(gi % 4 + 1) * 128], lhsT=q.t[:, gi, :], rhs=skT.t[:, gi, :], start=True, stop=True),
                         reads=[q.b, skT.b], writes=[bk.b])
                for b4 in range(4):
                    bk = self.bank[4 + b4]
                    S.op("act", lambda e, bk=bk, b4=b4: e.copy(out=sc.t[:, b4 * 4:(b4 + 1) * 4, :], in_=bk.t[:].rearrange("p (g n) -> p g n", g=4)), reads=[bk.b], writes=[sc.b])
                if self.peer_stop == "p1sc" and m == 0:
                    S.dma("sp", lambda e: e.dma_start(out=self.out[0:256, :].rearrange("(p a) d -> p (a d)", p=128), in_=sc.t[:].rearrange("p g n -> p (g n)")), sc.b, reads=[sc.b], is_output=True)
                    qf = sb2("qf", [128, 16, 128], F32)
                    S.op("dve", lambda e: e.tensor_copy(out=qf.t[:], in_=q.t[:]), reads=[q.b], writes=[qf.b])
                    S.dma("sp", lambda e: e.dma_start(out=self.out[256:512, :].rearrange("(p a) d -> p (a d)", p=128), in_=qf.t[:].rearrange("p g n -> p (g n)")), qf.b, reads=[qf.b], is_output=True)
                    return
                for gi in range(16):
                    S.op("dve", lambda e, gi=gi: e.max(out=vals.t[:, gi, 0:8], in_=sc.t[:, gi, :]), reads=[sc.b], writes=[vals.b])
                    S.op("dve", lambda e, gi=gi: e.max_index(out=idx.t[:, gi, 0:8], in_max=vals.t[:, gi, 0:8], in_values=sc.t[:, gi, :]), reads=[sc.b, vals.b], writes=[idx.b])
                    S.op("dve", lambda e, gi=gi: e.match_replace(out=scr.t[:, gi, :], in_to_replace=vals.t[:, gi, 0:8], in_values=sc.t[:, gi, :], imm_value=-1e30), reads=[sc.b, vals.b], writes=[scr.b])
                    S.op("dve", lambda e, gi=gi: e.max(out=vals.t[:, gi, 8:16], in_=scr.t[:, gi, :]), reads=[scr.b], writes=[vals.b])
                    S.op("dve", lambda e, gi=gi: e.max_index(out=idx.t[:, gi, 8:16], in_max=vals.t[:, gi, 8:16], in_values=scr.t[:, gi, :]), reads=[scr.b, vals.b], writes=[idx.b])
                S.op("dve", lambda e: e.tensor_copy(out=idxf.t[:], in_=idx.t[:]), reads=[idx.b], writes=[idxf.b])
                v0 = _bcast_ap(vals.t, 0, [[32, 8], [1, 16], [0, 16]])
                v1 = _bcast_ap(vals.t, 16, [[32, 8], [0, 16], [1, 16]])
                S.op("dve", lambda e, v0=v0, v1=v1: e.tensor_tensor(out=cand.t[:].rearrange("p h (a b) -> p h a b", a=16), in0=v0, in1=v1, op=ALU.add), reads=[vals.b], writes=[cand.b])
                for h in range(8):
                    S.op("dve", lambda e, h=h: e.max(out=ts.t[:, h, 0:8], in_=cand.t[:, h, :]), reads=[cand.b], writes=[ts.b])
                    S.op("dve", lambda e, h=h: e.max_index(out=tc_.t[:, h, 0:8], in_max=ts.t[:, h, 0:8], in_values=cand.t[:, h, :]), reads=[cand.b, ts.b], writes=[tc_.b])
                    S.op("dve", lambda e, h=h: e.match_replace(out=cscr.t[:], in_to_replace=ts.t[:, h, 0:8], in_values=cand.t[:, h, :], imm_value=-1e30), reads=[cand.b, ts.b], writes=[cscr.b])
                    S.op("dve", lambda e, h=h: e.max(out=ts.t[:, h, 8:16], in_=cscr.t[:]), reads=[cscr.b], writes=[ts.b])
                    S.op("dve", lambda e, h=h: e.max_index(out=tc_.t[:, h, 8:16], in_max=ts.t[:, h, 8:16], in_values=cscr.t[:]), reads=[cscr.b, ts.b], writes=[tc_.b])
                S.op("dve", lambda e: e.tensor_copy(out=tcf.t[:], in_=tc_.t[:]), reads=[tc_.b], writes=[tcf.b])
                S.op("dve", lambda e: e.tensor_scalar(out=af.t[:], in0=tcf.t[:], scalar1=0.0625, scalar2=-0.46875, op0=ALU.mult, op1=ALU.add), reads=[tcf.b], writes=[af.b])
                S.op("dve", lambda e: e.tensor_scalar(out=af.t[:], in0=af.t[:], scalar1=MAGIC, scalar2=None, op0=ALU.add), reads=[af.b], writes=[af.b])
                S.op("dve", lambda e: e.tensor_scalar(out=af.t[:], in0=af.t[:], scalar1=-MAGIC, scalar2=None, op0=ALU.add), reads=[af.b], writes=[af.b])
                S.op("dve", lambda e: e.scalar_tensor_tensor(out=bf.t[:], in0=af.t[:], scalar=-16.0, in1=tcf.t[:], op0=ALU.mult, op1=ALU.add), reads=[af.b, tcf.b], writes=[bf.b])
                for which, sel in ((0, af), (1, bf)):
                    selb = _bcast_ap(sel.t, 0, [[16, 8], [1, 16], [0, 16]])
                    iob = _bcast_ap(self.iota_f.t, 0, [[0, 8], [0, 16], [1, 16]])
                    ixb = _bcast_ap(idxf.t, which * 16, [[32, 8], [0, 16], [1, 16]])
                    S.op("dve", lambda e, selb=selb, iob=iob: e.tensor_tensor(out=oh.t[:], in0=selb, in1=iob, op=ALU.is_equal), reads=[sel.b, self.iota_f.b], writes=[oh.b])
                    S.op("dve", lambda e, ixb=ixb: e.tensor_tensor(out=oh.t[:], in0=oh.t[:], in1=ixb, op=ALU.mult), reads=[oh.b, idxf.b], writes=[oh.b])
                    S.op("dve", lambda e, which=which: e.tensor_reduce(out=IJg.t[:, which, :], in_=oh.t[:].rearrange("p h k a -> p (h k) a"), axis=AX.X, op=ALU.add), reads=[oh.b], writes=[IJg.b])
                tmax = _bcast_ap(ts.t, 0, [[16, 8], [0, 16]])
                S.op("dve", lambda e, tmax=tmax: e.tensor_tensor(out=tcf.t[:], in0=ts.t[:], in1=tmax, op=ALU.subtract), reads=[ts.b], writes=[tcf.b])
                S.op("act", lambda e: e.activation(out=tcf.t[:], in_=tcf.t[:], func=AF.Exp), reads=[tcf.b], writes=[tcf.b])
                S.op("dve", lambda e: e.tensor_reduce(out=esum.t[:], in_=tcf.t[:], axis=AX.X, op=ALU.add), reads=[tcf.b], writes=[esum.b])
                S.op("dve", lambda e: e.reciprocal(out=esum.t[:], in_=esum.t[:]), reads=[esum.b], writes=[esum.b])
                esb = _bcast_ap(esum.t, 0, [[1, 8], [0, 16]])
                S.op("dve", lambda e, esb=esb: e.tensor_tensor(out=IJg.t[:, 2, :].rearrange("p (h k) -> p h k", h=8), in0=tcf.t[:], in1=esb, op=ALU.mult), reads=[tcf.b, esum.b], writes=[IJg.b])
                for w in range(3):
                    bk = self.bank[w]
                    S.op("pe", lambda e, bk=bk, w=w: e.transpose(out=bk.t[:, 0:128], in_=IJg.t[:, w, :], identity=self.ident_f.t[:]), reads=[IJg.b, self.ident_f.b], writes=[bk.b])
                    S.op("act", lambda e, bk=bk, w=w, m=m: e.copy(out=RT.t[:, w, m * 128:(m + 1) * 128], in_=bk.t[:, 0:128]), reads=[bk.b], writes=[RT.b])
        S.barrier()
        if self.peer_stop == "p1":
            S.dma("sp", lambda e: e.dma_start(out=self.out[0:768, :].rearrange("(p a) d -> p (a d)", p=128), in_=RT.t[:].rearrange("p w t -> p (w t)")), RT.b, reads=[RT.b], is_output=True)
            return
        with contextlib.ExitStack() as st2:
            self.a_ptr = pR
            sb2 = lambda name, shape, dt: self.sb(st2, name, shape, dt)
            TB = 256
            W = sb2("W", [128, 128, TB], BF16)
            ohj = [sb2(f"ohj{i}", [128, 16, 128], BF16) for i in range(2)]
            ohi = [sb2(f"ohi{i}", [128, 16, 128], BF16) for i in range(2)]
            utl = [sb2(f"utl{i}", [128, 8, 128], BF16) for i in range(4)]
            vtl = [sb2(f"vtl{i}", [128, 1024], BF16) for i in range(5)]
            ag = [sb2(f"ag{i}", [128, TB], F32) for i in range(4)]
            wa = [sb2(f"wa{i}", [128, TB], BF16) for i in range(4)]
            x2t = [sb2(f"x2t{i}", [128, 1024], F32) for i in range(2)]
            sqj = sb2("sqjf", [128, 1024], BF16)
            fs = sb2("fs", [128, 4], F32)
            frs = sb2("frs", [128, 4], F32)
            yo = [sb2(f"yo{i}", [128, 1024], F32) for i in range(2)]
            gfin = sb2("gfin", [128, 1024], F32)
            S.dma("sp", lambda e: e.dma_start(out=gfin.t[:], in_=I["final_g_rep"]), gfin.b, writes=[gfin.b])
            psO = self.bank[0:4]
            psA = self.bank[4:8]
            psW = self.bank[6]
            for blk in range(2048 // TB):
                t0 = blk * TB
                G = 16
                for tg in range(TB // G):
                    tq = t0 + tg * G
                    oj, oi = ohj[tg % 2], ohi[tg % 2]
                    iob = _bcast_ap(self.iota_f.t, 0, [[0, G], [1, 128]])
                    Ib = _bcast_ap(RT.t, 0 * 2048 + tq, [[1, G], [0, 128]])
                    Jb = _bcast_ap(RT.t, 1 * 2048 + tq, [[1, G], [0, 128]])
                    gb = _bcast_ap(RT.t, 2 * 2048 + tq, [[1, G], [0, 128]])
                    S.op("dve", lambda e, oj=oj, iob=iob, Jb=Jb: e.tensor_tensor(out=oj.t[:], in0=iob, in1=Jb, op=ALU.is_equal), reads=[self.iota_f.b, RT.b], writes=[oj.b])
                    S.op("dve", lambda e, oi=oi, iob=iob, Ib=Ib: e.tensor_tensor(out=oi.t[:], in0=iob, in1=Ib, op=ALU.is_equal), reads=[self.iota_f.b, RT.b], writes=[oi.b])
                    S.op("pool", lambda e, oi=oi, gb=gb: e.tensor_tensor(out=oi.t[:], in0=oi.t[:], in1=gb, op=ALU.mult), reads=[oi.b, RT.b], writes=[oi.b])
                    for k in range(G):
                        tt = tg * G + k
                        S.op("pe", lambda e, oj=oj, oi=oi, tt=tt, k=k: e.matmul(psW.t[:, (tt % 4) * 128:(tt % 4 + 1) * 128], lhsT=oj.t[:, k, :], rhs=oi.t[:, k, :], start=True, stop=True), reads=[oj.b, oi.b], writes=[psW.b])
                        if tt % 4 == 3:
                            tb0 = tt - 3
                            wout = _bcast_ap(W.t, tb0, [[1, 4], [TB, 128]])
                            S.op("act", lambda e, wout=wout: e.copy(out=wout, in_=psW.t[:].rearrange("p (q i) -> p q i", q=4)), reads=[psW.b], writes=[W.b])
                LA = 2

                def emit_A(i, t0=t0):
                    ut, vt = utl[i % 4], vtl[i % 5]
                    S.dma("sp", lambda e, ut=ut, i=i: e.dma_start(out=ut.t[:], in_=self.UT_d[i]), ut.b, reads=[self.B_UT], writes=[ut.b])
                    S.dma("act", lambda e, vt=vt, i=i: e.dma_start(out=vt.t[:], in_=self.Vb_d[i * 128:(i + 1) * 128, :]), vt.b, reads=[self.B_Vb], writes=[vt.b])
                    pa = psA[i % 4]
                    for kc in range(8):
                        S.op("pe", lambda e, pa=pa, ut=ut, kc=kc, t0=t0: e.matmul(pa.t[:, 0:TB], lhsT=ut.t[:, kc, :], rhs=hfT.t[:, kc, t0:t0 + TB], start=(kc == 0), stop=(kc == 7)),
                             reads=[ut.b, hfT.b], writes=[pa.b])
                    a_, w_ = ag[i % 4], wa[i % 4]
                    S.op("act", lambda e, pa=pa, a_=a_: e.activation(out=a_.t[:], in_=pa.t[:, 0:TB], func=AF.Gelu), reads=[pa.b], writes=[a_.b])
                    S.op("pool", lambda e, a_=a_, w_=w_, i=i: e.tensor_tensor(out=w_.t[:], in0=a_.t[:], in1=W.t[:, i, :], op=ALU.mult), reads=[a_.b, W.b], writes=[w_.b])

                def emit_out(i):
                    vt = vtl[i % 5]
                    w_ = wa[i % 4]
                    for tl in range(TB // 128):
                        for dh in range(2):
                            po = psO[tl * 2 + dh]
                            S.op("pe", lambda e, po=po, w_=w_, vt=vt, tl=tl, dh=dh, i=i: e.matmul(po.t[:], lhsT=w_.t[:, tl * 128:(tl + 1) * 128], rhs=vt.t[:, dh * 512:(dh + 1) * 512], start=(i == 0), stop=(i == 127)),
                                 reads=[w_.b, vt.b], writes=[po.b])

                for step_i in range(128 + LA):
                    if step_i < 128:
                        emit_A(step_i)
                    if step_i >= LA:
                        emit_out(step_i - LA)
                for tl in range(TB // 128):
                    m = (t0 // 128) + tl
                    xt_ = x2t[tl % 2]
                    yt = yo[tl % 2]
                    S.dma("sp", lambda e, xt_=xt_, m=m: e.dma_start(out=xt_.t[:], in_=X2_d[m * 128:(m + 1) * 128, :]), xt_.b, reads=[B_X2], writes=[xt_.b])
                    for dh in range(2):
                        po = psO[tl * 2 + dh]
                        S.op("dve", lambda e, po=po, xt_=xt_, dh=dh: e.tensor_tensor(out=xt_.t[:, dh * 512:(dh + 1) * 512], in0=po.t[:], in1=xt_.t[:, dh * 512:(dh + 1) * 512], op=ALU.add), reads=[po.b, xt_.b], writes=[xt_.b])
                    if "x3" in self.dumps:
                        S.dma("sp", lambda e, xt_=xt_, m=m: e.dma_start(out=self.dbg["x3"][m * 128:(m + 1) * 128, :], in_=xt_.t[:]), xt_.b, reads=[xt_.b], is_output=True)
                    S.op("act", lambda e, xt_=xt_: e.activation(out=sqj.t[:], in_=xt_.t[:], func=AF.Square, accum_out=fs.t[:, 0:1]), reads=[xt_.b], writes=[sqj.b, fs.b])
                    S.op("dve", lambda e: e.tensor_scalar(out=frs.t[:, 0:1], in0=fs.t[:, 0:1], scalar1=1.0 / D, scalar2=1e-6, op0=ALU.mult, op1=ALU.add), reads=[fs.b], writes=[frs.b])
                    S.op("pool", lambda e: e.tensor_tensor(out=frs.t[:, 0:1], in0=frs.t[:, 0:1], in1=self.mhalf.t[:, 0:1], op=ALU.pow), reads=[frs.b, self.mhalf.b], writes=[frs.b])
                    S.op("dve", lambda e, xt_=xt_, yt=yt: e.scalar_tensor_tensor(out=yt.t[:], in0=xt_.t[:], scalar=frs.t[:, 0:1], in1=gfin.t[:], op0=ALU.mult, op1=ALU.mult), reads=[xt_.b, frs.b, gfin.b], writes=[yt.b])
                    S.dma("sp", lambda e, yt=yt, m=m: e.dma_start(out=self.out[m * 128:(m + 1) * 128, :], in_=yt.t[:]), yt.b, reads=[yt.b], is_output=True)


def _phase_final(self):
    pass


Builder.phase_peer = _phase_peer
Builder.phase_final = _phase_final
```

```python
import contextlib
import math

import numpy as np
import ml_dtypes
import concourse.bass as bass
import concourse.mybir as mybir
from concourse.bass_utils import run_bass_kernel_spmd

F32 = mybir.dt.float32
BF16 = mybir.dt.bfloat16
U32 = mybir.dt.uint32
I32 = mybir.dt.int32
AF = mybir.ActivationFunctionType
ALU = mybir.AluOpType
AX = mybir.AxisListType

NCORE = 8
SEQ = 16384
D = 1024
NT_ALL = SEQ // 128
NT_OWN = 16
COMPUTE = ("pe", "act", "dve", "pool")
ALL_ENG = ("pe", "act", "dve", "pool", "sp")
SEM_ROT = 30000


class Buf:
    __slots__ = ("name", "writers", "readers", "dsem", "dcount")

    def __init__(self, name):
        self.name = name
        self.writers = []
        self.readers = []
        self.dsem = None
        self.dcount = 0


class Ins:
    __slots__ = ("eng", "fn", "deps", "is_dma", "sem", "semval", "needs_inc")

    def __init__(self, eng, fn, deps, is_dma):
        self.eng = eng
        self.fn = fn
        self.deps = deps
        self.is_dma = is_dma
        self.sem = None
        self.semval = 0
        self.needs_inc = False


class Sched:
    def __init__(self, nc, same_engine_sync=True):
        self.nc = nc
        self.ins = []
        self.streams = {e: [] for e in ALL_ENG}
        self.same_engine_sync = same_engine_sync
        self.dma_bufs = []
        self.out_tokens = []
        self.last_eng = {}
        self.last_dma = {}
        self.base_deps = frozenset()

    def barrier(self):
        self.base_deps = frozenset(list(self.last_eng.values()) + list(self.last_dma.values()))

    def _deps(self, reads, writes):
        deps = set(self.base_deps)
        for b in reads:
            deps.update(b.writers)
        for b in writes:
            deps.update(b.writers)
            deps.update(b.readers)
        last = {}
        for i in deps:
            it = self.ins[i]
            key = ("d", id(it.sem)) if it.is_dma else it.eng
            if key not in last or last[key] < i:
                last[key] = i
        return set(last.values())

    def _compress(self, lst):
        last = {}
        out = []
        for i in lst:
            it = self.ins[i]
            if it.is_dma:
                last[("d", id(it.sem))] = i
            else:
                last[it.eng] = i
        return list(last.values())

    def _commit(self, me, reads, writes):
        for b in writes:
            if b.readers:
                b.writers = [me]
                b.readers = []
            else:
                b.writers.append(me)
                if len(b.writers) > 32:
                    b.writers = self._compress(b.writers)
        for b in reads:
            if b in writes:
                continue
            b.readers.append(me)
            if len(b.readers) > 32:
                b.readers = self._compress(b.readers)

    def op(self, eng, fn, reads=(), writes=()):
        deps = self._deps(reads, writes)
        me = len(self.ins)
        it = Ins(eng, fn, deps, False)
        self.ins.append(it)
        self.streams[eng].append(it)
        self._commit(me, reads, writes)
        self.last_eng[eng] = me
        return me

    def dma(self, queue, fn, sem_buf, reads=(), writes=(), is_output=False):
        deps = self._deps(reads, writes)
        me = len(self.ins)
        it = Ins(queue, fn, deps, True)
        if sem_buf.dsem is None:
            sem_buf.dsem = "pending"
            self.dma_bufs.append(sem_buf)
        sem_buf.dcount += 16
        it.sem = sem_buf
        it.semval = sem_buf.dcount
        self.ins.append(it)
        self.streams[queue].append(it)
        self._commit(me, reads, writes)
        self.last_dma[id(sem_buf)] = me
        if is_output:
            self.out_tokens.append(me)
        return me

    def emit(self, final_engine="sp"):
        nc = self.nc
        ses = self.same_engine_sync
        for it in self.ins:
            for d in it.deps:
                dd = self.ins[d]
                if dd.is_dma:
                    continue
                if dd.eng == it.eng and not it.is_dma and (dd.eng == "pe" or not ses):
                    continue
                dd.needs_inc = True
        n_sems = {}
        for e in COMPUTE:
            c = 0
            for it in self.streams[e]:
                if it.is_dma:
                    continue
                if it.needs_inc:
                    c += 1
                    it.semval = c
            n_sems[e] = max(c - 1, 0) // SEM_ROT + 1
        with contextlib.ExitStack() as st:
            eng_sems = {e: [st.enter_context(nc.semaphore(f"s_{e}{k}")) for k in range(n_sems[e])] for e in COMPUTE}
            for i, b in enumerate(self.dma_bufs):
                b.dsem = st.enter_context(nc.semaphore(f"d{i}_{b.name}"))

            def token(it):
                if it.is_dma:
                    return (it.sem.dsem, it.semval, ("d", id(it.sem)))
                k = (it.semval - 1) // SEM_ROT
                return (eng_sems[it.eng][k], it.semval - k * SEM_ROT, (it.eng, k))

            def run(e, eng):
                known = {}
                for it in self.streams[e]:
                    need = {}
                    for d in it.deps:
                        dd = self.ins[d]
                        if not dd.is_dma and dd.eng == e and not it.is_dma and (e == "pe" or not ses):
                            continue
                        sem, val, key = token(dd)
                        if known.get(key, 0) >= val:
                            continue
                        if key not in need or need[key][1] < val:
                            need[key] = (sem, val)
                    for key, (sem, val) in need.items():
                        eng.wait_ge(sem, val)
                        known[key] = val
                    h = it.fn(eng)
                    if it.is_dma:
                        h.then_inc(it.sem.dsem, 16)
                    elif it.needs_inc:
                        k = (it.semval - 1) // SEM_ROT
                        h.then_inc(eng_sems[it.eng][k], 1)
                if e == final_engine:
                    need = {}
                    for d in self.out_tokens:
                        sem, val, key = token(self.ins[d])
                        if key not in need or need[key][1] < val:
                            need[key] = (sem, val)
                    for key, (sem, val) in need.items():
                        eng.wait_ge(sem, val)

            with nc.Block() as block:
                @block.sync
                def _(eng):
                    run("sp", eng)

                @block.tensor
                def _(eng):
                    run("pe", eng)

                @block.scalar
                def _(eng):
                    run("act", eng)

                @block.vector
                def _(eng):
                    run("dve", eng)

                @block.gpsimd
                def _(eng):
                    run("pool", eng)


class TT:
    __slots__ = ("t", "b")

    def __init__(self, t, b):
        self.t = t
        self.b = b


def _t5_bucket(n):
    n = np.maximum(n, 0)
    nf = np.maximum(n, 1).astype(np.float32)
    large = 16 + (np.log(nf / np.float32(16)) / np.float32(math.log(8)) * np.float32(16)).astype(np.int32)
    large = np.minimum(large, 31)
    return np.where(n < 16, n, large)


def _band_tables(c):
    ki = np.arange(128)[:, None]
    qi = np.arange(128)[None, :]
    bk = np.zeros((9, 128, 128), np.int64)
    mk = np.zeros((9, 128, 128), np.float32)
    for bi, b in enumerate(range(-1, 8)):
        rel = 128 * (c - b) + qi - ki
        bk[bi] = _t5_bucket(rel)
        mk[bi] = np.where(rel >= 0, 0.0, -30000.0)
    return bk, mk


class Builder:
    def __init__(self, stop_after=None, upto=None):
        self.stop_after = stop_after
        self.upto = upto
        self.nc = bass.Bass("TRN2", target_bir_lowering=False)
        self.S = Sched(self.nc)
        self.n = 0

    def din(self, name, shape, dt=F32):
        if self.upto and self.upto.startswith("peeronly") and name in ("x_all", "w_in", "w_conv_out", "w_attn_out", "w_mix_out", "w_cq", "w_ckv", "w_co", "mem"):
            shape = [1, 1]
        if not hasattr(self, "in_shapes"):
            self.in_shapes = {}
        self.in_shapes[name] = (tuple(shape), dt)
        return self.nc.dram_tensor(name, list(shape), dt, kind="ExternalInput").ap()

    def dout(self, name, shape, dt=F32):
        return self.nc.dram_tensor(name, list(shape), dt, kind="ExternalOutput").ap()

    def dscr(self, name, shape, dt):
        return self.nc.dram_tensor(name, list(shape), dt, kind="Internal").ap()

    def _arena_init(self):
        rem = self.nc.sbuf_bytes_remaining
        rem = rem() if callable(rem) else rem
        size = (int(rem) - 256) // 64 * 64
        beg, end = self.nc.bump_sbuf(size)
        self.a_beg = (int(beg) + 63) // 64 * 64
        self.a_end = int(end)
        self.a_ptr = self.a_beg
        self.a_marked = set()

    def _reset_ptr(self, mark, key):
        self.a_ptr = mark
        self.a_marked.discard(key)

    def sb(self, st, name, shape, dt, at=None):
        if id(st) not in self.a_marked:
            self.a_marked.add(id(st))
            st.callback(self._reset_ptr, self.a_ptr, id(st))
        self.n += 1
        nm = f"{name}_{self.n}"
        nbytes = int(np.prod(shape[1:])) * mybir.dt.size(dt)
        nbytes = (nbytes + 63) // 64 * 64
        if at is None:
            off = self.a_ptr
            self.a_ptr += nbytes
        else:
            off = at
        assert off + nbytes <= self.a_end, (name, off, nbytes, self.a_end)
        self.a_peak = max(getattr(self, "a_peak", 0), off + nbytes - self.a_beg)
        t = self.nc.alloc_sbuf_tensor_at(nm, list(shape), dt, offset=off)
        return TT(t, Buf(nm))

    def ps(self, st, name, shape, dt):
        self.n += 1
        nm = f"{name}_{self.n}"
        return TT(st.enter_context(self.nc.psum_tensor(nm, list(shape), dt)), Buf(nm))

    def build(self):
        nc, S = self.nc, self.S
        din, dout = self.din, self.dout
        I = {}
        I["x_all"] = din("x_all", [SEQ, D])
        I["x_own"] = din("x_own", [NT_OWN * 128, D])
        I["x_halo"] = din("x_halo", [32, D])
        I["mem"] = din("mem", [256, D])
        I["w_in"] = din("w_in", [D, 8192])
        I["gcols"] = din("gcols", [128, 4, 8])
        I["final_g_rep"] = din("final_g_rep", [128, D])
        I["conv_wT"] = din("conv_wT", [128, 8, 3])
        I["w_conv_out"] = din("w_conv_out", [D, D])
        I["lam_rep"] = din("lam_rep", [128, 4, 64])
        I["subln_rep"] = din("subln_rep", [128, 128])
        I["w_attn_out"] = din("w_attn_out", [D, D])
        I["w_mix_out"] = din("w_mix_out", [D, D])
        I["rb31_rep"] = din("rb31_rep", [128, 8])
        I["gbias"] = din("gbias", [8, 128, 9, 128])
        I["mband"] = din("mband", [128, 9, 128])
        I["w_cq"] = din("w_cq", [D, D])
        I["w_ckv"] = din("w_ckv", [D, 2048])
        I["w_co"] = din("w_co", [D, D])
        I["w_pq"] = din("w_pq", [D, 2048])
        I["skT"] = din("skT", [16, 128, 128])
        I["peer_u"] = din("peer_u", [SEQ, D])
        I["peer_v"] = din("peer_v", [SEQ, D])
        I["ident_bf"] = din("ident_bf", [128, 128], BF16)
        I["ident_f"] = din("ident_f", [128, 128])
        I["iota_f"] = din("iota_f", [128, 128])
        self.I = I
        self.out = dout("out", [NT_OWN * 128, D])
        self.dumps = set(self.stop_after.split(",")) if self.stop_after else set()
        self.dbg = {}
        if "attn" in self.dumps:
            self.dbg["attn"] = dout("dbg_attn", [128, 8 * NT_OWN * 128])
        for nm in ("x1", "x2", "x3"):
            if nm in self.dumps:
                self.dbg[nm] = dout("dbg_" + nm, [NT_OWN * 128, D])
        self.UT_d = self.dscr("UT_d", [128, 128, 8, 128], BF16)
        self.Vb_d = self.dscr("Vb_d", [SEQ, D], BF16)
        self.B_UT = Buf("UT_d")
        self.B_Vb = Buf("Vb_d")
        self.KT_d = self.dscr("KT_d", [8, 128, SEQ], BF16)
        self.V_d = self.dscr("V_d", [8, 128, NT_ALL, 128], BF16)
        self.B_KT = Buf("KT_d")
        self.B_V = Buf("V_d")

        with contextlib.ExitStack() as gst:
            self.gst = gst
            self._arena_init()
            self.bank = [self.ps(gst, f"bank{i}", [128, 512], F32) for i in range(8)]
            self.setup_consts()
            if self.upto and self.upto.startswith("peeronly"):
                with contextlib.ExitStack() as rst:
                    p0 = self.a_ptr
                    self.xres = self.sb(rst, "xres", [128, NT_OWN, D], F32, at=p0 + 32 * 1024)
                    S.dma("sp", lambda e: e.dma_start(out=self.xres.t[:], in_=I["x_own"].rearrange("(m p) d -> p m d", p=128)), self.xres.b, writes=[self.xres.b])
                    self.peer_stop = self.upto.split(":")[1] if ":" in self.upto else None
                    self.phase_peer(p0)
                S.emit()
                return nc
            self.peer_stop = None
            self.phase_kv()
            S.barrier()
            self.phase_q_attn()
            S.barrier()
            if "attn" in self.dumps:
                self.dump_attn()
            with contextlib.ExitStack() as rst:
                p0 = self.a_ptr
                self.xres = self.sb(rst, "xres", [128, NT_OWN, D], F32, at=p0 + 32 * 1024)
                self.phase_mix(p0)
                S.barrier()
                if "x1" in self.dumps:
                    self.dump_res("x1")
                if self.upto != "mix":
                    self.phase_cross(p0)
                    S.barrier()
                    if "x2" in self.dumps:
                        self.dump_res("x2")
                    if self.upto != "cross":
                        self.phase_peer(p0)
            S.emit()
        return nc

    def setup_consts(self):
        S, gst, I = self.S, self.gst, self.I
        sb = lambda name, shape, dt: self.sb(gst, name, shape, dt)
        self.ident_bf = sb("ident_bf", [128, 128], BF16)
        self.ident_f = sb("ident_f", [128, 128], F32)
        self.iota_f = sb("iota_f", [128, 128], F32)
        self.gcols = sb("gcols", [128, 4, 8], F32)
        self.lam = sb("lam", [128, 4], F32)
        self.subg = sb("subg", [128, 128], F32)
        self.rb31 = sb("rb31", [128, 8], F32)
        self.mhalf = sb("mhalf", [128, 4], F32)
        self.pA = self.a_ptr
        self.bias = sb("bias", [128, 8, 9, 128], BF16)
        self.outT = sb("outT", [128, 8, NT_OWN * 128], BF16)
        S.op("pool", lambda e: e.memset(self.mhalf.t[:], -0.5), writes=[self.mhalf.b])
        cset = Buf("consts")
        for tt, src in ((self.ident_bf, I["ident_bf"]), (self.ident_f, I["ident_f"]), (self.iota_f, I["iota_f"]),
                        (self.gcols, I["gcols"]), (self.subg, I["subln_rep"]), (self.rb31, I["rb31_rep"])):
            S.dma("sp", lambda e, tt=tt, src=src: e.dma_start(out=tt.t[:], in_=src), cset, writes=[tt.b])
        with contextlib.ExitStack() as st:
            lamin = self.sb(st, "lamin", [128, 4, 64], F32)
            prod = self.sb(st, "lamprod", [128, 2, 64], F32)
            red = self.sb(st, "lamred", [128, 2], F32)
            ex = self.sb(st, "lamex", [128, 2], F32)
            mb = self.sb(st, "mband", [128, 9, 128], F32)
            gb = [self.sb(st, f"gb{i}", [128, 9, 128], F32) for i in range(2)]
            S.dma("sp", lambda e: e.dma_start(out=lamin.t[:], in_=I["lam_rep"]), cset, writes=[lamin.b])
            S.dma("sp", lambda e: e.dma_start(out=mb.t[:], in_=I["mband"]), cset, writes=[mb.b])
            S.op("dve", lambda e: e.tensor_tensor(out=prod.t[:, 0, :], in0=lamin.t[:, 0, :], in1=lamin.t[:, 1, :], op=ALU.mult), reads=[lamin.b], writes=[prod.b])
            S.op("dve", lambda e: e.tensor_tensor(out=prod.t[:, 1, :], in0=lamin.t[:, 2, :], in1=lamin.t[:, 3, :], op=ALU.mult), reads=[lamin.b], writes=[prod.b])
            S.op("dve", lambda e: e.tensor_reduce(out=red.t[:], in_=prod.t[:], axis=AX.X, op=ALU.add), reads=[prod.b], writes=[red.b])
            S.op("act", lambda e: e.activation(out=ex.t[:], in_=red.t[:], func=AF.Exp), reads=[red.b], writes=[ex.b])
            S.op("dve", lambda e: e.tensor_tensor(out=self.lam.t[:, 0:1], in0=ex.t[:, 0:1], in1=ex.t[:, 1:2], op=ALU.subtract), reads=[ex.b], writes=[self.lam.b])
            S.op("dve", lambda e: e.tensor_scalar(out=self.lam.t[:, 0:1], in0=self.lam.t[:, 0:1], scalar1=0.2, scalar2=None, op0=ALU.add), reads=[self.lam.b], writes=[self.lam.b])
            S.op("dve", lambda e: e.tensor_scalar(out=self.lam.t[:, 1:2], in0=self.lam.t[:, 0:1], scalar1=-1.0, scalar2=None, op0=ALU.mult), reads=[self.lam.b], writes=[self.lam.b])
            S.op("dve", lambda e: e.tensor_scalar(out=self.subg.t[:], in0=self.subg.t[:], scalar1=0.8, scalar2=None, op0=ALU.mult), reads=[self.subg.b], writes=[self.subg.b])
            for h in range(8):
                g = gb[h % 2]
                S.dma("sp", lambda e, g=g, h=h: e.dma_start(out=g.t[:], in_=I["gbias"][h]), g.b, writes=[g.b])
                S.op("dve", lambda e, g=g, h=h: e.scalar_tensor_tensor(out=self.bias.t[:, h, :, :], in0=g.t[:], scalar=self.rb31.t[:, h:h + 1], in1=mb.t[:], op0=ALU.subtract, op1=ALU.add),
                     reads=[g.b, self.rb31.b, mb.b], writes=[self.bias.b])
            S.barrier()

    def load_weight(self, st_tiles, dst, col0, ncols, w_ap, gsel, kcs=range(8), queue="sp"):
        S = self.S
        for kc in kcs:
            stg = st_tiles[kc % len(st_tiles)]
            S.dma(queue, lambda e, stg=stg, kc=kc: e.dma_start(out=stg.t[:, 0:ncols], in_=w_ap[kc * 128:(kc + 1) * 128, col0:col0 + ncols]), stg.b, writes=[stg.b])
            if gsel is None:
                S.op("pool", lambda e, stg=stg, kc=kc: e.tensor_copy(out=dst.t[:, kc, 0:ncols], in_=stg.t[:, 0:ncols]), reads=[stg.b], writes=[dst.b])
            else:
                S.op("pool", lambda e, stg=stg, kc=kc: e.tensor_scalar(out=dst.t[:, kc, 0:ncols], in0=stg.t[:, 0:ncols], scalar1=self.gcols.t[:, gsel, kc:kc + 1], scalar2=None, op0=ALU.mult),
                     reads=[stg.b, self.gcols.b], writes=[dst.b])

    def norm_tiles(self, x_ap_fn, tiles, xt, sqj, ss, rstd, xn, eps=1e-6):
        S = self.S
        n = len(tiles)
        for i, (tid, xts, xns) in enumerate(tiles):
            S.dma("sp", lambda e, xts=xts, tid=tid: e.dma_start(out=xts.t[:], in_=x_ap_fn(tid)), xts.b, writes=[xts.b])
            S.op("act", lambda e, xts=xts, i=i: e.activation(out=sqj.t[:], in_=xts.t[:], func=AF.Square, accum_out=ss.t[:, i:i + 1]), reads=[xts.b], writes=[sqj.b, ss.b])
        S.op("dve", lambda e: e.tensor_scalar(out=rstd.t[:, 0:n], in0=ss.t[:, 0:n], scalar1=1.0 / D, scalar2=eps, op0=ALU.mult, op1=ALU.add), reads=[ss.b], writes=[rstd.b])
        S.op("pool", lambda e: e.tensor_tensor(out=rstd.t[:, 0:n], in0=rstd.t[:, 0:n], in1=self.mhalf.t[:, 0:n], op=ALU.pow), reads=[rstd.b, self.mhalf.b], writes=[rstd.b])
        for i, (tid, xts, xns) in enumerate(tiles):
            S.op("dve", lambda e, xts=xts, xns=xns, i=i: e.tensor_scalar(out=xns.t[:], in0=xts.t[:], scalar1=rstd.t[:, i:i + 1], scalar2=None, op0=ALU.mult), reads=[xts.b, rstd.b], writes=[xns.b])

    def transpose_tiles(self, tiles_xn, hT, tbanks, cnt0, gsel=None, col0=0, npart=128):
        S = self.S
        if gsel is not None or npart != 128:
            for i, xns in enumerate(tiles_xn):
                bk = tbanks[(cnt0 + i) % len(tbanks)]
                pt = bk.t.bitcast(BF16)
                for kc in range(8):
                    S.op("pe", lambda e, pt=pt, xns=xns, kc=kc: e.transpose(out=pt[:, kc * 128:kc * 128 + npart], in_=xns.t[0:npart, kc * 128:(kc + 1) * 128], identity=self.ident_bf.t[0:npart, 0:npart]),
                         reads=[xns.b, self.ident_bf.b], writes=[bk.b])
                for kc in range(8):
                    c0 = col0 + i * npart
                    if gsel is None:
                        S.op("act", lambda e, pt=pt, kc=kc, c0=c0: e.copy(out=hT.t[:, kc, c0:c0 + npart], in_=pt[:, kc * 128:kc * 128 + npart]), reads=[bk.b], writes=[hT.b])
                    else:
                        S.op("act", lambda e, pt=pt, kc=kc, c0=c0: e.activation(out=hT.t[:, kc, c0:c0 + npart], in_=pt[:, kc * 128:kc * 128 + npart], func=AF.Copy, scale=self.gcols.t[:, gsel, kc:kc + 1]),
                             reads=[bk.b, self.gcols.b], writes=[hT.b])
            return
        for i, xns in enumerate(tiles_xn):
            bk = tbanks[(cnt0 + i) % len(tbanks)]
            pt = bk.t.bitcast(BF16)
            for kc in range(8):
                S.op("pe", lambda e, pt=pt, xns=xns, kc=kc: e.transpose(out=pt[:, kc * 128:(kc + 1) * 128], in_=xns.t[:, kc * 128:(kc + 1) * 128], identity=self.ident_bf.t[:]),
                     reads=[xns.b, self.ident_bf.b], writes=[bk.b])
            S.op("act", lambda e, pt=pt, i=i: e.copy(out=hT.t[:, :, i * 128:(i + 1) * 128], in_=pt[:, :].rearrange("p (k t) -> p k t", k=8)), reads=[bk.b], writes=[hT.b])

    def phase_kv(self):
        S, I = self.S, self.I
        with contextlib.ExitStack() as st:
            sb = lambda name, shape, dt: self.sb(st, name, shape, dt)
            wk = sb("wk", [128, 8, 1024], BF16)
            wv = sb("wv", [128, 8, 1024], BF16)
            wst = [sb(f"wst{i}", [128, 1024], F32) for i in range(2)]
            self.load_weight(wst, wk, 4096, 1024, I["w_in"], 0)
            self.load_weight(wst, wv, 5120, 1024, I["w_in"], 0)
            xt = [sb(f"xt{i}", [128, 1024], F32) for i in range(8)]
            xn = [sb(f"xn{i}", [128, 1024], BF16) for i in range(12)]
            sqj = sb("sqj", [128, 1024], BF16)
            ss = [sb(f"ss{i}", [128, 4], F32) for i in range(3)]
            rstd = [sb(f"rstd{i}", [128, 4], F32) for i in range(3)]
            hT = [sb(f"hT{i}", [128, 8, 512], BF16) for i in range(2)]
            kst = [sb(f"kst{i}", [128, 8, 512], BF16) for i in range(2)]
            vst = [sb(f"vst{i}", [128, 4, 1024], BF16) for i in range(2)]
            tb = self.bank[0:2]
            mb = self.bank[2:8]
            NB = NT_ALL // 4
            x_all = I["x_all"]

            def A(bi):
                tiles = [(bi * 4 + i, xt[(bi * 4 + i) % 8], xn[(bi * 4 + i) % 12]) for i in range(4)]
                self.norm_tiles(lambda tid: x_all[tid * 128:(tid + 1) * 128, :], tiles, xt, sqj, ss[bi % 3], rstd[bi % 3], xn)

            def Bs(bi):
                self.transpose_tiles([xn[(bi * 4 + i) % 12] for i in range(4)], hT[bi % 2], tb, bi * 4)

            cnt = [0]

            def C(bi):
                h = hT[bi % 2]
                ks, vs = kst[bi % 2], vst[bi % 2]
                for j in range(8):
                    bk = mb[cnt[0] % 6]
                    cnt[0] += 1
                    for kc in range(8):
                        S.op("pe", lambda e, bk=bk, kc=kc, j=j: e.matmul(bk.t[:], lhsT=wk.t[:, kc, j * 128:(j + 1) * 128], rhs=h.t[:, kc, :], start=(kc == 0), stop=(kc == 7)),
                             reads=[wk.b, h.b], writes=[bk.b])
                    eng = "act" if j % 2 == 0 else "dve"
                    if eng == "act":
                        S.op("act", lambda e, bk=bk, j=j: e.copy(out=ks.t[:, j, :], in_=bk.t[:]), reads=[bk.b], writes=[ks.b])
                    else:
                        S.op("dve", lambda e, bk=bk, j=j: e.tensor_copy(out=ks.t[:, j, :], in_=bk.t[:]), reads=[bk.b], writes=[ks.b])
                S.dma("sp", lambda e: e.dma_start(out=self.KT_d[:, :, bi * 512:(bi + 1) * 512].rearrange("h p t -> p h t"), in_=ks.t[:]), ks.b, reads=[ks.b], writes=[self.B_KT])
                for i in range(4):
                    for hh in range(2):
                        bk = mb[cnt[0] % 6]
                        cnt[0] += 1
                        for kc in range(8):
                            S.op("pe", lambda e, bk=bk, kc=kc, i=i, hh=hh: e.matmul(bk.t[:], lhsT=h.t[:, kc, i * 128:(i + 1) * 128], rhs=wv.t[:, kc, hh * 512:(hh + 1) * 512], start=(kc == 0), stop=(kc == 7)),
                                 reads=[wv.b, h.b], writes=[bk.b])
                        if hh == 0:
                            S.op("act", lambda e, bk=bk, i=i, hh=hh: e.copy(out=vs.t[:, i, hh * 512:(hh + 1) * 512], in_=bk.t[:]), reads=[bk.b], writes=[vs.b])
                        else:
                            S.op("dve", lambda e, bk=bk, i=i, hh=hh: e.tensor_copy(out=vs.t[:, i, hh * 512:(hh + 1) * 512], in_=bk.t[:]), reads=[bk.b], writes=[vs.b])
                    S.dma("act", lambda e, i=i: e.dma_start(out=self.V_d[:, :, bi * 4 + i, :].rearrange("h p e -> p h e"), in_=vs.t[:, i, :].rearrange("p (h e) -> p h e", h=8)),
                          vs.b, reads=[vs.b], writes=[self.B_V])

            A(0)
            A(1)
            Bs(0)
            for bi in range(NB):
                if bi + 2 < NB:
                    A(bi + 2)
                if bi + 1 < NB:
                    Bs(bi + 1)
                C(bi)

    def phase_q_attn(self):
        S, I = self.S, self.I
        with contextlib.ExitStack() as st:
            sb = lambda name, shape, dt: self.sb(st, name, shape, dt)
            QT = sb("QT", [128, 8, NT_OWN * 128], BF16)
            with contextlib.ExitStack() as st2:
                sb2 = lambda name, shape, dt: self.sb(st2, name, shape, dt)
                wq = sb2("wq", [128, 8, 1024], BF16)
                wst = [sb2(f"wst{i}", [128, 1024], F32) for i in range(2)]
                self.load_weight(wst, wq, 3072, 1024, I["w_in"], 0)
                xt = [sb2(f"xt{i}", [128, 1024], F32) for i in range(4)]
                xn = [sb2(f"xn{i}", [128, 1024], BF16) for i in range(4)]
                sqj = sb2("sqj", [128, 1024], BF16)
                ss = sb2("ss", [128, 4], F32)
                rstd = sb2("rstd", [128, 4], F32)
                hT = sb2("hT", [128, 8, 512], BF16)
                x_own = I["x_own"]
                cnt = 0
                for bi in range(4):
                    tiles = [(bi * 4 + i, xt[i], xn[i]) for i in range(4)]
                    self.norm_tiles(lambda tid: x_own[tid * 128:(tid + 1) * 128, :], tiles, xt, sqj, ss, rstd, xn)
                    self.transpose_tiles(xn, hT, self.bank[0:2], bi * 4)
                    for j in range(8):
                        bk = self.bank[2 + cnt % 6]
                        cnt += 1
                        for kc in range(8):
                            S.op("pe", lambda e, bk=bk, kc=kc, j=j: e.matmul(bk.t[:], lhsT=wq.t[:, kc, j * 128:(j + 1) * 128], rhs=hT.t[:, kc, :], start=(kc == 0), stop=(kc == 7)),
                                 reads=[wq.b, hT.b], writes=[bk.b])
                        S.op("act", lambda e, bk=bk, j=j, bi=bi: e.activation(out=QT.t[:, j, bi * 512:(bi + 1) * 512], in_=bk.t[:], func=AF.Copy, scale=0.125), reads=[bk.b], writes=[QT.b])
                S.barrier()
            KT = [sb(f"KT{i}", [128, 4096], BF16) for i in range(4)]
            V1 = [sb(f"V1{i}", [128, 32, 130], BF16) for i in range(4)]
            E = [sb(f"E{i}", [128, 512], BF16) for i in range(3)]
            tmp0 = [sb(f"tmp0{i}", [128, 128], F32) for i in range(4)]
            oc = sb("oc", [128, 128], F32)
            on = sb("on", [128, 128], BF16)
            sqj = sb("sqj2", [128, 128], F32)
            sm = sb("sm", [128, 8], F32)
            for v in V1:
                S.op("pool", lambda e, v=v: e.memset(v.t[:, :, 128:130], 1.0), writes=[v.b])
            sbanks = self.bank[0:3]
            obanks = self.bank[3:7]
            tbank = self.bank[7]
            oacc = [(obanks[j], 0, obanks[j].b) for j in range(4)]
            LA = 2
            steps = []
            for h in range(8):
                for g in (3, 2, 1, 0):
                    for c in range(2):
                        nk = 32 * g + 32
                        for kt in range(nk):
                            steps.append((h, g, c, kt, nk))

            def jmin_of(g, kt):
                return max(0, -((-(kt - 32 * g - 7)) // 8))

            def emit_qk(si):
                h, g, c, kt, nk = steps[si]
                if g == 3 and c == 0 and kt == 0:
                    for q in range(4):
                        S.dma("sp", lambda e, q=q, h=h: e.dma_start(out=KT[q].t[:], in_=self.KT_d[h, :, q * 4096:(q + 1) * 4096]), KT[q].b, reads=[self.B_KT], writes=[KT[q].b])
                jmin = jmin_of(g, kt)
                band = {}
                for j in range(jmin, 4):
                    b = kt - (32 * g + 8 * j)
                    if -1 <= b <= 7:
                        band[j] = b
                sbk = sbanks[si % 3]
                Et = E[si % 3]
                kq, kl = kt // 32, kt % 32
                lhs = KT[kq].t[c * 64:(c + 1) * 64, kl * 128:(kl + 1) * 128]
                ncol = (4 - jmin) * 128
                S.op("pe", lambda e, sbk=sbk, ncol=ncol, lhs=lhs, jmin=jmin, g=g, h=h, c=c: e.matmul(sbk.t[:, 0:ncol], lhsT=lhs, rhs=QT.t[c * 64:(c + 1) * 64, h, (4 * g + jmin) * 128:(4 * g + 4) * 128], start=True, stop=True),
                     reads=[KT[kq].b, QT.b], writes=[sbk.b])
                for j, b in band.items():
                    col = (j - jmin) * 128
                    S.op("dve", lambda e, sbk=sbk, col=col, b=b, h=h: e.tensor_tensor(out=sbk.t[:, col:col + 128], in0=sbk.t[:, col:col + 128], in1=self.bias.t[:, h, b + 1, :], op=ALU.add),
                         reads=[sbk.b, self.bias.b], writes=[sbk.b])
                nact = (4 - jmin) * 128
                S.op("act", lambda e, sbk=sbk, Et=Et, nact=nact: e.activation(out=Et.t[:, 0:nact], in_=sbk.t[:, 0:nact], func=AF.Exp), reads=[sbk.b], writes=[Et.b])

            def emit_av(si):
                h, g, c, kt, nk = steps[si]
                if g == 3 and c == 0 and kt == 0:
                    for q in range(4):
                        S.dma("act", lambda e, q=q, h=h: e.dma_start(out=V1[q].t[:, :, 0:128], in_=self.V_d[h, :, q * 32:(q + 1) * 32, :]), V1[q].b, reads=[self.B_V], writes=[V1[q].b])
                jmin = jmin_of(g, kt)
                Et = E[si % 3]
                kq, kl = kt // 32, kt % 32
                for j in range(jmin, 4):
                    ob, oo, obuf = oacc[j]
                    col = (j - jmin) * 128
                    last = (kt == 32 * g + 8 * j + 7)
                    S.op("pe", lambda e, ob=ob, oo=oo, Et=Et, col=col, kq=kq, kl=kl, kt=kt, last=last: e.matmul(ob.t[:, oo:oo + 130], lhsT=Et.t[:, col:col + 128], rhs=V1[kq].t[:, kl, :], start=(kt == 0), stop=last),
                         reads=[Et.b, V1[kq].b], writes=[obuf])
                if kt != nk - 1:
                    return
                for j in range(4):
                    ob, oo, obuf = oacc[j]
                    m = 4 * g + j
                    if c == 0:
                        S.op("dve", lambda e, ob=ob, oo=oo, j=j: e.reciprocal(out=sm.t[:, j:j + 1], in_=ob.t[:, oo + 128:oo + 129]), reads=[obuf], writes=[sm.b])
                        S.op("dve", lambda e, ob=ob, oo=oo, j=j: e.tensor_scalar(out=tmp0[j].t[:], in0=ob.t[:, oo:oo + 128], scalar1=sm.t[:, j:j + 1], scalar2=None, op0=ALU.mult), reads=[obuf, sm.b], writes=[tmp0[j].b])
                    else:
                        S.op("dve", lambda e, ob=ob, oo=oo, j=j: e.reciprocal(out=sm.t[:, 4 + j:5 + j], in_=ob.t[:, oo + 128:oo + 129]), reads=[obuf], writes=[sm.b])
                        S.op("dve", lambda e, j=j: e.tensor_scalar(out=sm.t[:, 4 + j:5 + j], in0=sm.t[:, 4 + j:5 + j], scalar1=self.lam.t[:, 1:2], scalar2=None, op0=ALU.mult), reads=[sm.b, self.lam.b], writes=[sm.b])
                        S.op("dve", lambda e, ob=ob, oo=oo, j=j: e.scalar_tensor_tensor(out=oc.t[:], in0=ob.t[:, oo:oo + 128], scalar=sm.t[:, 4 + j:5 + j], in1=tmp0[j].t[:], op0=ALU.mult, op1=ALU.add),
                             reads=[obuf, sm.b, tmp0[j].b], writes=[oc.b])
                        S.op("dve", lambda e: e.scalar_tensor_tensor(out=sqj.t[:], in0=oc.t[:], scalar=1.0, in1=oc.t[:], op0=ALU.mult, op1=ALU.mult, accum_out=sm.t[:, 0:1]), reads=[oc.b], writes=[sqj.b, sm.b])
                        S.op("dve", lambda e: e.tensor_scalar(out=sm.t[:, 0:1], in0=sm.t[:, 0:1], scalar1=1.0 / 128, scalar2=1e-5, op0=ALU.mult, op1=ALU.add), reads=[sm.b], writes=[sm.b])
                        S.op("pool", lambda e: e.tensor_tensor(out=sm.t[:, 0:1], in0=sm.t[:, 0:1], in1=self.mhalf.t[:, 0:1], op=ALU.pow), reads=[sm.b, self.mhalf.b], writes=[sm.b])
                        S.op("dve", lambda e: e.scalar_tensor_tensor(out=on.t[:], in0=oc.t[:], scalar=sm.t[:, 0:1], in1=self.subg.t[:], op0=ALU.mult, op1=ALU.mult),
                             reads=[oc.b, sm.b, self.subg.b], writes=[on.b])
                        pt = tbank.t.bitcast(BF16)
                        S.op("pe", lambda e, pt=pt: e.transpose(out=pt[:, 0:128], in_=on.t[:], identity=self.ident_bf.t[:]), reads=[on.b, self.ident_bf.b], writes=[tbank.b])
                        S.op("act", lambda e, pt=pt, h=h, m=m: e.copy(out=self.outT.t[:, h, m * 128:(m + 1) * 128], in_=pt[:, 0:128]), reads=[tbank.b], writes=[self.outT.b])

            ns = len(steps)
            for idx in range(ns + LA):
                if idx < ns:
                    emit_qk(idx)
                if idx - LA >= 0:
                    emit_av(idx - LA)

    def dump_attn(self):
        S = self.S
        with contextlib.ExitStack() as st:
            f = self.sb(st, "dumpf", [128, 8, NT_OWN * 128], F32)
            S.op("dve", lambda e: e.tensor_copy(out=f.t[:], in_=self.outT.t[:]), reads=[self.outT.b], writes=[f.b])
            S.dma("sp", lambda e: e.dma_start(out=self.dbg["attn"], in_=f.t[:].rearrange("p h t -> p (h t)")), f.b, reads=[f.b], is_output=True)


def make_in_maps(inp):
    f32 = np.float32
    x = np.ascontiguousarray(inp["x"][0], dtype=f32)
    xt = x.reshape(16, 8, 128, D)
    common = {
        "x_all": x,
        "mem": np.ascontiguousarray(inp["mem"][0], dtype=f32),
        "w_in": np.ascontiguousarray(inp["w_in"][0], dtype=f32),
        "gcols": np.ascontiguousarray(np.stack([inp["norm_mix_g"][0], inp["norm_cross_g"][0], inp["norm_mem_g"][0], inp["norm_ffn_g"][0]], 0).reshape(4, 8, 128).transpose(2, 0, 1), dtype=f32),
        "final_g_rep": np.ascontiguousarray(np.broadcast_to(inp["final_g"][None, :], (128, D)), dtype=f32),
        "conv_wT": np.ascontiguousarray(inp["conv_w"][0].reshape(3, 8, 128).transpose(2, 1, 0), dtype=f32),
        "w_conv_out": np.ascontiguousarray(inp["w_conv_out"][0], dtype=f32),
        "lam_rep": np.ascontiguousarray(np.broadcast_to(np.stack([inp["lambda_q1"][0], inp["lambda_k1"][0], inp["lambda_q2"][0], inp["lambda_k2"][0]], 0)[None], (128, 4, 64)), dtype=f32),
        "subln_rep": np.ascontiguousarray(np.broadcast_to(inp["subln_g"][0][None, :], (128, 128)), dtype=f32),
        "w_attn_out": np.ascontiguousarray(inp["w_attn_out"][0], dtype=f32),
        "w_mix_out": np.ascontiguousarray(inp["w_mix_out"][0], dtype=f32),
        "rb31_rep": np.ascontiguousarray(np.broadcast_to(inp["rel_bias"][31][None, :], (128, 8)), dtype=f32),
        "w_cq": np.ascontiguousarray(inp["w_cq"][0], dtype=f32),
        "w_ckv": np.ascontiguousarray(inp["w_ckv"][0], dtype=f32),
        "w_co": np.ascontiguousarray(inp["w_co"][0], dtype=f32),
        "w_pq": np.ascontiguousarray(inp["w_pq"][0], dtype=f32),
        "skT": np.ascontiguousarray(inp["sub_keys"][0].transpose(1, 0, 3, 2).reshape(16, 128, 128), dtype=f32),
        "peer_u": np.ascontiguousarray(inp["peer_u"][0], dtype=f32),
        "peer_v": np.ascontiguousarray(inp["peer_v"][0], dtype=f32),
        "ident_bf": np.eye(128, dtype=f32).astype(ml_dtypes.bfloat16),
        "ident_f": np.eye(128, dtype=f32),
        "iota_f": np.ascontiguousarray(np.broadcast_to(np.arange(128, dtype=f32)[None, :], (128, 128))),
    }
    rel_bias = np.asarray(inp["rel_bias"], dtype=f32)
    maps = []
    for c in range(NCORE):
        m = dict(common)
        m["x_own"] = np.ascontiguousarray(xt[:, c].reshape(NT_OWN * 128, D))
        halo = np.zeros((16, 2, D), f32)
        for mm in range(16):
            t0 = (8 * mm + c) * 128
            if t0 >= 2:
                halo[mm] = x[t0 - 2:t0]
        m["x_halo"] = halo.reshape(32, D)
        bk, mk = _band_tables(c)
        gb = rel_bias[bk]
        m["gbias"] = np.ascontiguousarray(gb.transpose(3, 1, 0, 2), dtype=f32)
        m["mband"] = np.ascontiguousarray(mk.transpose(1, 0, 2), dtype=f32)
        maps.append(m)
    return maps


_CACHE = {}


def kernel(**inputs):
    stop = inputs.pop("_stop_after", None)
    upto = inputs.pop("_upto", None)
    key = (stop, upto)
    if key not in _CACHE:
        b = Builder(stop_after=stop, upto=upto)
        _CACHE[key] = (b.build(), b.in_shapes)
    nc, in_shapes = _CACHE[key]
    maps = make_in_maps(inputs)
    if upto and upto.startswith("peeronly"):
        m = maps[3]
        for nm, (shp, dt) in in_shapes.items():
            if tuple(m[nm].shape) != tuple(shp):
                m[nm] = np.zeros(shp, np.float32)
        res = run_bass_kernel_spmd(nc, [m], core_ids=[0])
        _CACHE["last_res"] = res
        return res
    res = run_bass_kernel_spmd(nc, maps, core_ids=list(range(NCORE)))
    if stop:
        _CACHE["last_res"] = res
    name = "out"
    out = np.zeros((16, 8, 128, D), np.float32)
    for c in range(NCORE):
        out[:, c] = np.asarray(res.results[c][name], dtype=np.float32).reshape(16, 128, D)
    return out.reshape(1, SEQ, D)


def _bcast_ap(t, offset, dims):
    base = t[:]
    return bass.AP(t, offset, [list(base.ap[0])] + [list(d) for d in dims])


def _norm_res(self, tiles, sqj, ss, rstd, eps=1e-6):
    S = self.S
    n = len(tiles)
    for i, (src, sbuf, xns) in enumerate(tiles):
        S.op("act", lambda e, src=src, i=i: e.activation(out=sqj.t[:], in_=src, func=AF.Square, accum_out=ss.t[:, i:i + 1]), reads=[sbuf], writes=[sqj.b, ss.b])
    S.op("dve", lambda e: e.tensor_scalar(out=rstd.t[:, 0:n], in0=ss.t[:, 0:n], scalar1=1.0 / D, scalar2=eps, op0=ALU.mult, op1=ALU.add), reads=[ss.b], writes=[rstd.b])
    S.op("pool", lambda e: e.tensor_tensor(out=rstd.t[:, 0:n], in0=rstd.t[:, 0:n], in1=self.mhalf.t[:, 0:n], op=ALU.pow), reads=[rstd.b, self.mhalf.b], writes=[rstd.b])
    for i, (src, sbuf, xns) in enumerate(tiles):
        S.op("dve", lambda e, src=src, xns=xns, i=i: e.tensor_scalar(out=xns.t[:], in0=src, scalar1=rstd.t[:, i:i + 1], scalar2=None, op0=ALU.mult), reads=[sbuf, rstd.b], writes=[xns.b])


def _wslab(self, dst, w_ap, col0, ncols, queue="pool"):
    self.S.dma(queue, lambda e: e.dma_start(out=dst.t[:, :, 0:ncols], in_=w_ap[:, col0:col0 + ncols].rearrange("(kc p) c -> p kc c", p=128)), dst.b, writes=[dst.b])


def _phase_mix(self, p0):
    S, I = self.S, self.I
    K = 1024
    with contextlib.ExitStack() as st:
        self.a_ptr = p0
        sb = lambda name, shape, dt, at=None: self.sb(st, name, shape, dt, at=at)
        mergedT = sb("mergedT", [128, 8, 2048], BF16)
        hT = sb("hT", [128, 8, 2048], BF16)
        hTh = sb("hTh", [128, 8, 32], BF16)
        zT = sb("zT", [128, 8, 2048], BF16)
        with contextlib.ExitStack() as st2:
            sb2 = lambda name, shape, dt: self.sb(st2, name, shape, dt)
            xt = [sb2(f"xt{i}", [128, 1024], F32) for i in range(4)]
            xn = [sb2(f"xn{i}", [128, 1024], BF16) for i in range(4)]
            sqj = sb2("sqj", [128, 1024], BF16)
            ss = sb2("ss", [128, 4], F32)
            rstd = sb2("rstd", [128, 4], F32)
            x_own = I["x_own"]
            for bi in range(4):
                tiles = [(bi * 4 + i, xt[i], xn[i]) for i in range(4)]
                self.norm_tiles(lambda tid: x_own[tid * 128:(tid + 1) * 128, :], tiles, xt, sqj, ss, rstd, xn)
                self.transpose_tiles(xn, hT, self.bank[0:2], bi * 4, gsel=0, col0=bi * 512)
            hx, hn = xt[0], xn[0]
            S.dma("sp", lambda e: e.dma_start(out=hx.t[0:32, :], in_=I["x_halo"]), hx.b, writes=[hx.b])
            S.op("act", lambda e: e.activation(out=sqj.t[0:32, :], in_=hx.t[0:32, :], func=AF.Square, accum_out=ss.t[0:32, 0:1]), reads=[hx.b], writes=[sqj.b, ss.b])
            S.op("dve", lambda e: e.tensor_scalar(out=rstd.t[0:32, 0:1], in0=ss.t[0:32, 0:1], scalar1=1.0 / D, scalar2=1e-6, op0=ALU.mult, op1=ALU.add), reads=[ss.b], writes=[rstd.b])
            S.op("pool", lambda e: e.tensor_tensor(out=rstd.t[0:32, 0:1], in0=rstd.t[0:32, 0:1], in1=self.mhalf.t[0:32, 0:1], op=ALU.pow), reads=[rstd.b, self.mhalf.b], writes=[rstd.b])
            S.op("dve", lambda e: e.tensor_scalar(out=hn.t[0:32, :], in0=hx.t[0:32, :], scalar1=rstd.t[0:32, 0:1], scalar2=None, op0=ALU.mult), reads=[hx.b, rstd.b], writes=[hn.b])
            self.transpose_tiles([hn], hTh, self.bank[0:2], 0, gsel=0, col0=0, npart=32)
        S.barrier()
        with contextlib.ExitStack() as st2:
            sb2 = lambda name, shape, dt: self.sb(st2, name, shape, dt)
            wc3 = [[sb2(f"wc3_{i}_{k}", [128, 8, 128], BF16) for k in range(3)] for i in range(2)]
            cwT = sb2("cwT", [128, 8, 3], F32)
            S.dma("sp", lambda e: e.dma_start(out=cwT.t[:], in_=I["conv_wT"]), cwT.b, writes=[cwT.b])
            U2 = sb2("U2", [128, 16, 130], F32)
            ycv = sb2("ycv", [128, 16, 128], F32)
            cbs = sb2("cbs", [128, 2048], F32)
            ccs = [sb2(f"ccs{i}", [128, 512], F32) for i in range(2)]
            cch_ = sb2("cch", [128, 32], F32)
            cnt = 0
            for cch in range(8):
                w3 = wc3[cch % 2]
                for k in range(3):
                    _wslab(self, w3[k], I["w_in"], k * 1024 + cch * 128, 128)
                for tb in range(4):
                    pb = [self.bank[(cnt + k) % 8] for k in range(3)]
                    cnt += 3
                    for k in range(3):
                        for kc in range(8):
                            S.op("pe", lambda e, k=k, kc=kc, tb=tb, w3=w3, pb=pb: e.matmul(pb[k].t[:], lhsT=w3[k].t[:, kc, :], rhs=hT.t[:, kc, tb * 512:(tb + 1) * 512], start=(kc == 0), stop=(kc == 7)),
                                 reads=[w3[k].b, hT.b], writes=[pb[k].b])
                    S.op("act", lambda e, tb=tb, pb=pb: e.copy(out=cbs.t[:, tb * 512:(tb + 1) * 512], in_=pb[0].t[:]), reads=[pb[0].b], writes=[cbs.b])
                    cs = ccs[tb % 2]
                    S.op("act", lambda e, cs=cs, pb=pb: e.copy(out=cs.t[:], in_=pb[1].t[:]), reads=[pb[1].b], writes=[cs.b])
                    S.op("dve", lambda e, cs=cs, pb=pb, tb=tb: e.tensor_tensor(out=U2.t[:, tb * 4:(tb + 1) * 4, 2:130], in0=pb[2].t[:].rearrange("p (m t) -> p m t", m=4), in1=cs.t[:].rearrange("p (m t) -> p m t", m=4), op=ALU.mult),
                         reads=[pb[2].b, cs.b], writes=[U2.b])
                pb = [self.bank[(cnt + k) % 8] for k in range(2)]
                cnt += 2
                for k in range(2):
                    for kc in range(8):
                        S.op("pe", lambda e, k=k, kc=kc, w3=w3, pb=pb: e.matmul(pb[k].t[:, 0:32], lhsT=w3[k + 1].t[:, kc, :], rhs=hTh.t[:, kc, :], start=(kc == 0), stop=(kc == 7)),
                             reads=[w3[k + 1].b, hTh.b], writes=[pb[k].b])
                S.op("act", lambda e, pb=pb: e.copy(out=cch_.t[:], in_=pb[0].t[:, 0:32]), reads=[pb[0].b], writes=[cch_.b])
                S.op("dve", lambda e, pb=pb: e.tensor_tensor(out=U2.t[:, :, 0:2], in0=pb[1].t[:, 0:32].rearrange("p (m t) -> p m t", m=16), in1=cch_.t[:].rearrange("p (m t) -> p m t", m=16), op=ALU.mult),
                     reads=[pb[1].b, cch_.b], writes=[U2.b])
                S.op("dve", lambda e, cch=cch: e.tensor_scalar(out=ycv.t[:], in0=U2.t[:, :, 2:130], scalar1=cwT.t[:, cch, 2:3], scalar2=None, op0=ALU.mult), reads=[U2.b, cwT.b], writes=[ycv.b])
                S.op("dve", lambda e, cch=cch: e.scalar_tensor_tensor(out=ycv.t[:], in0=U2.t[:, :, 1:129], scalar=cwT.t[:, cch, 1:2], in1=ycv.t[:], op0=ALU.mult, op1=ALU.add), reads=[U2.b, cwT.b, ycv.b], writes=[ycv.b])
                S.op("dve", lambda e, cch=cch: e.scalar_tensor_tensor(out=ycv.t[:], in0=U2.t[:, :, 0:128], scalar=cwT.t[:, cch, 0:1], in1=ycv.t[:], op0=ALU.mult, op1=ALU.add), reads=[U2.b, cwT.b, ycv.b], writes=[ycv.b])
                S.op("pool", lambda e, cch=cch: e.tensor_tensor(out=zT.t[:, cch, :], in0=cbs.t[:], in1=ycv.t[:].rearrange("p m t -> p (m t)"), op=ALU.mult), reads=[cbs.b, ycv.b], writes=[zT.b])
        S.barrier()
        with contextlib.ExitStack() as st2:
            sb2 = lambda name, shape, dt: self.sb(st2, name, shape, dt)
            wsl = [[sb2(f"wsl{i}_{k}", [128, 8, 128], BF16) for k in range(4)] for i in range(2)]
            sg = [[sb2(f"sg{i}_{k}", [128, 512], F32) for k in range(2)] for i in range(2)]
            m1 = [sb2(f"m1_{i}", [128, 512], F32) for i in range(2)]
            m2 = [sb2(f"m2_{i}", [128, 512], F32) for i in range(2)]
            it = 0
            for dt_ in range(8):
                ws = wsl[dt_ % 2]
                _wslab(self, ws[0], I["w_conv_out"], dt_ * 128, 128)
                _wslab(self, ws[1], I["w_in"], 6144 + dt_ * 128, 128)
                _wslab(self, ws[2], I["w_attn_out"], dt_ * 128, 128)
                _wslab(self, ws[3], I["w_in"], 7168 + dt_ * 128, 128)
                for tb in range(4):
                    pb = [self.bank[(it % 2) * 4 + k] for k in range(4)]
                    rhs_src = [zT, hT, self.outT, hT]
                    for k in range(4):
                        for kc in range(8):
                            S.op("pe", lambda e, k=k, kc=kc, tb=tb, ws=ws, pb=pb, rhs_src=rhs_src: e.matmul(pb[k].t[:], lhsT=ws[k].t[:, kc, :], rhs=rhs_src[k].t[:, kc, tb * 512:(tb + 1) * 512], start=(kc == 0), stop=(kc == 7)),
                                 reads=[ws[k].b, rhs_src[k].b], writes=[pb[k].b])
                    s0, s1 = sg[it % 2]
                    a1, a2 = m1[it % 2], m2[it % 2]
                    S.op("act", lambda e, s0=s0, pb=pb: e.activation(out=s0.t[:], in_=pb[1].t[:], func=AF.Sigmoid), reads=[pb[1].b], writes=[s0.b])
                    S.op("act", lambda e, s1=s1, pb=pb: e.activation(out=s1.t[:], in_=pb[3].t[:], func=AF.Sigmoid), reads=[pb[3].b], writes=[s1.b])
                    S.op("dve", lambda e, s0=s0, a1=a1, pb=pb: e.tensor_tensor(out=a1.t[:], in0=pb[0].t[:], in1=s0.t[:], op=ALU.mult), reads=[pb[0].b, s0.b], writes=[a1.b])
                    S.op("dve", lambda e, s1=s1, a2=a2, pb=pb: e.tensor_tensor(out=a2.t[:], in0=pb[2].t[:], in1=s1.t[:], op=ALU.mult), reads=[pb[2].b, s1.b], writes=[a2.b])
                    S.op("pool", lambda e, a1=a1, a2=a2, dt_=dt_, tb=tb: e.tensor_tensor(out=mergedT.t[:, dt_, tb * 512:(tb + 1) * 512], in0=a1.t[:], in1=a2.t[:], op=ALU.add), reads=[a1.b, a2.b], writes=[mergedT.b])
                    it += 1
        S.barrier()
        with contextlib.ExitStack() as st2:
            self.a_ptr = p0 + 97 * 1024
            wmix = self.sb(st2, "wmix", [128, 8, 1024], BF16)
            _wslab(self, wmix, I["w_mix_out"], 0, 1024)
            xres = self.xres
            S.dma("sp", lambda e: e.dma_start(out=xres.t[:], in_=I["x_own"].rearrange("(m p) d -> p m d", p=128)), xres.b, writes=[xres.b])
            it = 0
            for m in range(16):
                for dh in range(2):
                    bk = self.bank[it % 8]
                    it += 1
                    for kc in range(8):
                        S.op("pe", lambda e, bk=bk, kc=kc, m=m, dh=dh: e.matmul(bk.t[:], lhsT=mergedT.t[:, kc, m * 128:(m + 1) * 128], rhs=wmix.t[:, kc, dh * 512:(dh + 1) * 512], start=(kc == 0), stop=(kc == 7)),
                             reads=[mergedT.b, wmix.b], writes=[bk.b])
                    S.op("dve", lambda e, bk=bk, m=m, dh=dh: e.tensor_tensor(out=xres.t[:, m, dh * 512:(dh + 1) * 512], in0=bk.t[:], in1=xres.t[:, m, dh * 512:(dh + 1) * 512], op=ALU.add),
                         reads=[bk.b, xres.b], writes=[xres.b])


def _dump_res(self, name):
    S = self.S
    S.dma("sp", lambda e: e.dma_start(out=self.dbg[name].rearrange("(m p) d -> p m d", p=128), in_=self.xres.t[:]), self.xres.b, reads=[self.xres.b], is_output=True)


def _phase_cross(self, p0):
    S, I = self.S, self.I
    pA = self.pA
    xres = self.xres
    regB = p0 + 96 * 1024
    with contextlib.ExitStack() as st:
        self.a_ptr = pA
        sb = lambda name, shape, dt: self.sb(st, name, shape, dt)
        hcT = sb("hcT", [128, 8, 2048], BF16)
        wcq = sb("wcq", [128, 8, 1024], BF16)
        wco = sb("wco", [128, 8, 1024], BF16)
        kT = sb("kT", [128, 8, 256], BF16)
        vC = sb("vC", [128, 2, 1024], BF16)
        ones = sb("ones", [128, 128], BF16)
        qcT = sb("qcT", [128, 8, 512], BF16)
        assert self.a_ptr <= p0 + 32 * 1024, (self.a_ptr, p0)
        S.op("pool", lambda e: e.memset(ones.t[:], 1.0), writes=[ones.b])
        _wslab(self, wcq, I["w_cq"], 0, 1024)
        _wslab(self, wco, I["w_co"], 0, 1024)
        with contextlib.ExitStack() as st2:
            self.a_ptr = regB
            sb2 = lambda name, shape, dt: self.sb(st2, name, shape, dt)
            wckv = sb2("wckv", [128, 8, 2048], BF16)
            mT = sb2("mT", [128, 8, 256], BF16)
            xt = [sb2(f"xt{i}", [128, 1024], F32) for i in range(2)]
            xn = [sb2(f"xn{i}", [128, 1024], BF16) for i in range(2)]
            sqj = sb2("sqj", [128, 1024], BF16)
            ss = sb2("ss", [128, 4], F32)
            rstd = sb2("rstd", [128, 4], F32)
            _wslab(self, wckv, I["w_ckv"], 0, 2048)
            tiles = [(i, xt[i], xn[i]) for i in range(2)]
            self.norm_tiles(lambda tid: I["mem"][tid * 128:(tid + 1) * 128, :], tiles, xt, sqj, ss, rstd, xn)
            self.transpose_tiles(xn, mT, self.bank[0:2], 0, gsel=2, col0=0)
            for ct in range(8):
                bk = self.bank[2 + ct % 6]
                for kc in range(8):
                    S.op("pe", lambda e, bk=bk, kc=kc, ct=ct: e.matmul(bk.t[:, 0:256], lhsT=wckv.t[:, kc, ct * 128:(ct + 1) * 128], rhs=mT.t[:, kc, :], start=(kc == 0), stop=(kc == 7)),
                         reads=[wckv.b, mT.b], writes=[bk.b])
                S.op("act", lambda e, bk=bk, ct=ct: e.copy(out=kT.t[:, ct, :], in_=bk.t[:, 0:256]), reads=[bk.b], writes=[kT.b])
            for mt in range(2):
                for hh in range(2):
                    bk = self.bank[2 + (mt * 2 + hh) % 6]
                    for kc in range(8):
                        S.op("pe", lambda e, bk=bk, kc=kc, mt=mt, hh=hh: e.matmul(bk.t[:], lhsT=mT.t[:, kc, mt * 128:(mt + 1) * 128], rhs=wckv.t[:, kc, 1024 + hh * 512:1024 + (hh + 1) * 512], start=(kc == 0), stop=(kc == 7)),
                             reads=[wckv.b, mT.b], writes=[bk.b])
                    S.op("dve", lambda e, bk=bk, mt=mt, hh=hh: e.tensor_copy(out=vC.t[:, mt, hh * 512:(hh + 1) * 512], in_=bk.t[:]), reads=[bk.b], writes=[vC.b])
        S.barrier()
        with contextlib.ExitStack() as st2:
            self.a_ptr = regB
            sb2 = lambda name, shape, dt: self.sb(st2, name, shape, dt)
            xn = [sb2(f"xn{i}", [128, 1024], BF16) for i in range(4)]
            sqj = sb2("sqj", [128, 1024], BF16)
            ss = sb2("ss", [128, 4], F32)
            rstd = sb2("rstd", [128, 4], F32)
            P = [sb2(f"P{i}", [128, 2, 512], BF16) for i in range(2)]
            oT = sb2("oT", [128, 8, 512], BF16)
            R = [sb2(f"R{i}", [128, 512], F32) for i in range(2)]
            for bi in range(4):
                tiles = [(xres.t[:, bi * 4 + i, :], xres.b, xn[i]) for i in range(4)]
                _norm_res(self, tiles, sqj, ss, rstd)
                self.transpose_tiles(xn, hcT, self.bank[0:2], bi * 4, gsel=1, col0=bi * 512)
            it = 0
            for tb in range(4):
                for ct in range(8):
                    bk = self.bank[it % 8]
                    it += 1
                    for kc in range(8):
                        S.op("pe", lambda e, bk=bk, kc=kc, ct=ct, tb=tb: e.matmul(bk.t[:], lhsT=wcq.t[:, kc, ct * 128:(ct + 1) * 128], rhs=hcT.t[:, kc, tb * 512:(tb + 1) * 512], start=(kc == 0), stop=(kc == 7)),
                             reads=[wcq.b, hcT.b], writes=[bk.b])
                    if ct % 2 == 0:
                        S.op("act", lambda e, bk=bk, ct=ct: e.copy(out=qcT.t[:, ct, :], in_=bk.t[:]), reads=[bk.b], writes=[qcT.b])
                    else:
                        S.op("dve", lambda e, bk=bk, ct=ct: e.tensor_copy(out=qcT.t[:, ct, :], in_=bk.t[:]), reads=[bk.b], writes=[qcT.b])
                for hd in range(4):
                    Pt = P[hd % 2]
                    Rt = R[hd % 2]
                    for mt in range(2):
                        bk = self.bank[it % 8]
                        it += 1
                        for half in range(2):
                            S.op("pe", lambda e, bk=bk, hd=hd, half=half, mt=mt: e.matmul(bk.t[:], lhsT=kT.t[:, hd * 2 + half, mt * 128:(mt + 1) * 128], rhs=qcT.t[:, hd * 2 + half, :], start=(half == 0), stop=(half == 1)),
                                 reads=[kT.b, qcT.b], writes=[bk.b])
                        S.op("act", lambda e, bk=bk, Pt=Pt, mt=mt: e.activation(out=Pt.t[:, mt, :], in_=bk.t[:], func=AF.Exp, scale=1.0 / 16), reads=[bk.b], writes=[Pt.b])
                    bs = self.bank[it % 8]
                    it += 1
                    for mt in range(2):
                        S.op("pe", lambda e, bs=bs, Pt=Pt, mt=mt: e.matmul(bs.t[:], lhsT=ones.t[:], rhs=Pt.t[:, mt, :], start=(mt == 0), stop=(mt == 1)), reads=[ones.b, Pt.b], writes=[bs.b])
                    S.op("dve", lambda e, bs=bs, Rt=Rt: e.reciprocal(out=Rt.t[:], in_=bs.t[:]), reads=[bs.b], writes=[Rt.b])
                    for half in range(2):
                        bo = self.bank[it % 8]
                        it += 1
                        for mt in range(2):
                            S.op("pe", lambda e, bo=bo, Pt=Pt, mt=mt, hd=hd, half=half: e.matmul(bo.t[:], lhsT=vC.t[:, mt, hd * 256 + half * 128:hd * 256 + (half + 1) * 128], rhs=Pt.t[:, mt, :], start=(mt == 0), stop=(mt == 1)),
                                 reads=[vC.b, Pt.b], writes=[bo.b])
                        S.op("dve", lambda e, bo=bo, Rt=Rt, hd=hd, half=half: e.tensor_tensor(out=oT.t[:, hd * 2 + half, :], in0=bo.t[:], in1=Rt.t[:], op=ALU.mult), reads=[bo.b, Rt.b], writes=[oT.b])
                for i in range(4):
                    m = tb * 4 + i
                    for dh in range(2):
                        bk = self.bank[it % 8]
                        it += 1
                        for ct in range(8):
                            S.op("pe", lambda e, bk=bk, ct=ct, i=i, dh=dh: e.matmul(bk.t[:], lhsT=oT.t[:, ct, i * 128:(i + 1) * 128], rhs=wco.t[:, ct, dh * 512:(dh + 1) * 512], start=(ct == 0), stop=(ct == 7)),
                                 reads=[oT.b, wco.b], writes=[bk.b])
                        S.op("dve", lambda e, bk=bk, m=m, dh=dh: e.tensor_tensor(out=xres.t[:, m, dh * 512:(dh + 1) * 512], in0=bk.t[:], in1=xres.t[:, m, dh * 512:(dh + 1) * 512], op=ALU.add),
                             reads=[bk.b, xres.b], writes=[xres.b])


Builder.phase_mix = _phase_mix
Builder.phase_cross = _phase_cross
Builder.dump_res = _dump_res


def _phase_peer(self, p0):
    S, I = self.S, self.I
    xres = self.xres
    pA = self.pA
    X2_d = self.dscr("X2_d", [NT_OWN * 128, D], F32)
    B_X2 = Buf("X2_d")
    MAGIC = 12582912.0
    with contextlib.ExitStack() as st:
        self.a_ptr = p0 + 96 * 1024
        sb = lambda name, shape, dt: self.sb(st, name, shape, dt)
        with contextlib.ExitStack() as st2:
            sb2 = lambda name, shape, dt: self.sb(st2, name, shape, dt)
            xn = [sb2(f"xn{i}", [128, 1024], BF16) for i in range(4)]
            sqj = sb2("sqj", [128, 1024], BF16)
            ss = sb2("ss", [128, 4], F32)
            rstd = sb2("rstd", [128, 4], F32)
            self.a_ptr = pA
            hfT = self.sb(st, "hfT", [128, 8, 2048], BF16)
            for bi in range(4):
                tiles = [(xres.t[:, bi * 4 + i, :], xres.b, xn[i]) for i in range(4)]
                _norm_res(self, tiles, sqj, ss, rstd)
                self.transpose_tiles(xn, hfT, self.bank[0:2], bi * 4, gsel=3, col0=bi * 512)
            S.dma("sp", lambda e: e.dma_start(out=X2_d.rearrange("(m p) d -> p m d", p=128), in_=xres.t[:]), xres.b, reads=[xres.b], writes=[B_X2])
        S.barrier()
        self.a_ptr = pA + 32 * 1024
        RT = self.sb(st, "RT", [128, 3, 2048], F32)
        pR = self.a_ptr
        with contextlib.ExitStack() as st2:
            sb2 = lambda name, shape, dt: self.sb(st2, name, shape, dt)
            ub = [sb2(f"ub{i}", [128, 4, 1024], BF16) for i in range(2)]
            vb = [sb2(f"vb{i}", [128, 4, 1024], BF16) for i in range(2)]
            uts = [sb2(f"uts{i}", [128, 8, 128], BF16) for i in range(3)]
            for gq in range(32):
                u, v = ub[gq % 2], vb[gq % 2]
                S.dma("pool", lambda e, u=u, gq=gq: e.dma_start(out=u.t[:], in_=I["peer_u"][gq * 512:(gq + 1) * 512, :].rearrange("(i p) d -> p i d", p=128)), u.b, writes=[u.b])
                S.dma("pool", lambda e, v=v, gq=gq: e.dma_start(out=v.t[:], in_=I["peer_v"][gq * 512:(gq + 1) * 512, :].rearrange("(i p) d -> p i d", p=128)), v.b, writes=[v.b])
                S.dma("sp", lambda e, v=v, gq=gq: e.dma_start(out=self.Vb_d[gq * 512:(gq + 1) * 512, :].rearrange("(i p) d -> p i d", p=128), in_=v.t[:]), v.b, reads=[v.b], writes=[self.B_Vb])
                for i in range(4):
                    et = gq * 4 + i
                    bk = self.bank[et % 4]
                    pt = bk.t.bitcast(BF16)
                    ut = uts[et % 3]
                    for kc in range(8):
                        S.op("pe", lambda e, pt=pt, u=u, i=i, kc=kc: e.transpose(out=pt[:, kc * 128:(kc + 1) * 128], in_=u.t[:, i, kc * 128:(kc + 1) * 128], identity=self.ident_bf.t[:]),
                             reads=[u.b, self.ident_bf.b], writes=[bk.b])
                    if et % 2 == 0:
                        S.op("act", lambda e, pt=pt, ut=ut: e.copy(out=ut.t[:], in_=pt[:, :].rearrange("p (k t) -> p k t", k=8)), reads=[bk.b], writes=[ut.b])
                    else:
                        S.op("dve", lambda e, pt=pt, ut=ut: e.tensor_copy(out=ut.t[:], in_=pt[:, :].rearrange("p (k t) -> p k t", k=8)), reads=[bk.b], writes=[ut.b])
                    S.dma("act", lambda e, ut=ut, et=et: e.dma_start(out=self.UT_d[et], in_=ut.t[:]), ut.b, reads=[ut.b], writes=[self.B_UT])
        S.barrier()
        if self.peer_stop == "p0":
            with contextlib.ExitStack() as st2:
                tu = self.sb(st2, "tu", [128, 1024], BF16)
                tv = self.sb(st2, "tv", [128, 1024], BF16)
                tf = self.sb(st2, "tf", [128, 2, 1024], F32)
                S.dma("sp", lambda e: e.dma_start(out=tu.t[:], in_=self.UT_d[77].rearrange("p k e -> p (k e)")), tu.b, reads=[self.B_UT], writes=[tu.b])
                S.dma("sp", lambda e: e.dma_start(out=tv.t[:], in_=self.Vb_d[77 * 128:78 * 128, :]), tv.b, reads=[self.B_Vb], writes=[tv.b])
                S.op("dve", lambda e: e.tensor_copy(out=tf.t[:, 0, :], in_=tu.t[:]), reads=[tu.b], writes=[tf.b])
                S.op("dve", lambda e: e.tensor_copy(out=tf.t[:, 1, :], in_=tv.t[:]), reads=[tv.b], writes=[tf.b])
                S.dma("sp", lambda e: e.dma_start(out=self.out[0:128, :], in_=tf.t[:, 0, :]), tf.b, reads=[tf.b], is_output=True)
                S.dma("sp", lambda e: e.dma_start(out=self.out[128:256, :], in_=tf.t[:, 1, :]), tf.b, reads=[tf.b], is_output=True)
                S.dma("sp", lambda e: e.dma_start(out=self.out[256:384, :], in_=xres.t[:, 5, :]), xres.b, reads=[xres.b], is_output=True)
            return
        with contextlib.ExitStack() as st2:
            self.a_ptr = pR
            sb2 = lambda name, shape, dt: self.sb(st2, name, shape, dt)
            wpq = sb2("wpq", [128, 8, 2048], BF16)
            skT = sb2("skT", [128, 16, 128], BF16)
            qT = [sb2(f"qT{i}", [128, 16, 128], BF16) for i in range(2)]
            sc = sb2("sc", [128, 16, 128], F32)
            scr = sb2("scr", [128, 16, 128], F32)
            vals = sb2("vals", [128, 16, 16], F32)
            idx = sb2("idx", [128, 16, 16], U32)
            idxf = sb2("idxf", [128, 16, 16], F32)
            cand = sb2("cand", [128, 8, 256], F32)
            cscr = sb2("cscr", [128, 256], F32)
            ts = sb2("ts", [128, 8, 16], F32)
            tc_ = sb2("tc", [128, 8, 16], U32)
            tcf = sb2("tcf", [128, 8, 16], F32)
            af = sb2("af", [128, 8, 16], F32)
            bf = sb2("bf", [128, 8, 16], F32)
            oh = sb2("oh", [128, 8, 16, 16], F32)
            IJg = sb2("IJg", [128, 3, 128], F32)
            esum = sb2("esum", [128, 8], F32)
            _wslab(self, wpq, I["w_pq"], 0, 2048)
            S.dma("pool", lambda e: e.dma_start(out=skT.t[:], in_=I["skT"].rearrange("g d n -> d g n")), skT.b, writes=[skT.b])
            iota16 = self.iota_f.t[:, 0:16]
            for m in range(16):
                q = qT[m % 2]
                for gi in range(16):
                    bk = self.bank[gi % 4]
                    for kc in range(8):
                        S.op("pe", lambda e, bk=bk, kc=kc, gi=gi, m=m: e.matmul(bk.t[:, 0:128], lhsT=wpq.t[:, kc, gi * 128:(gi + 1) * 128], rhs=hfT.t[:, kc, m * 128:(m + 1) * 128], start=(kc == 0), stop=(kc == 7)),
                             reads=[wpq.b, hfT.b], writes=[bk.b])
                    S.op("act", lambda e, bk=bk, gi=gi, q=q: e.copy(out=q.t[:, gi, :], in_=bk.t[:, 0:128]), reads=[bk.b], writes=[q.b])
                for gi in range(16):
                    bk = self.bank[4 + gi // 4]
                    S.op("pe", lambda e, bk=bk, gi=gi, q=q: e.matmul(bk.t[:, (gi % 4) * 128:(gi % 4 + 1) * 128], lhsT=q.t[:, gi, :], rhs=skT.t[:, gi, :], start=True, stop=True),
                         reads=[q.b, skT.b], writes=[bk.b])
                for b4 in range(4):
                    bk = self.bank[4 + b4]
                    S.op("act", lambda e, bk=bk, b4=b4: e.copy(out=sc.t[:, b4 * 4:(b4 + 1) * 4, :], in_=bk.t[:].rearrange("p (g n) -> p g n", g=4)), reads=[bk.b], writes=[sc.b])
                if self.peer_stop == "p1sc" and m == 0:
                    S.dma("sp", lambda e: e.dma_start(out=self.out[0:256, :].rearrange("(p a) d -> p (a d)", p=128), in_=sc.t[:].rearrange("p g n -> p (g n)")), sc.b, reads=[sc.b], is_output=True)
                    qf = sb2("qf", [128, 16, 128], F32)
                    S.op("dve", lambda e: e.tensor_copy(out=qf.t[:], in_=q.t[:]), reads=[q.b], writes=[qf.b])
                    S.dma("sp", lambda e: e.dma_start(out=self.out[256:512, :].rearrange("(p a) d -> p (a d)", p=128), in_=qf.t[:].rearrange("p g n -> p (g n)")), qf.b, reads=[qf.b], is_output=True)
                    return
                for gi in range(16):
                    S.op("dve", lambda e, gi=gi: e.max(out=vals.t[:, gi, 0:8], in_=sc.t[:, gi, :]), reads=[sc.b], writes=[vals.b])
                    S.op("dve", lambda e, gi=gi: e.max_index(out=idx.t[:, gi, 0:8], in_max=vals.t[:, gi, 0:8], in_values=sc.t[:, gi, :]), reads=[sc.b, vals.b], writes=[idx.b])
                    S.op("dve", lambda e, gi=gi: e.match_replace(out=scr.t[:, gi, :], in_to_replace=vals.t[:, gi, 0:8], in_values=sc.t[:, gi, :], imm_value=-1e30), reads=[sc.b, vals.b], writes=[scr.b])
                    S.op("dve", lambda e, gi=gi: e.max(out=vals.t[:, gi, 8:16], in_=scr.t[:, gi, :]), reads=[scr.b], writes=[vals.b])
                    S.op("dve", lambda e, gi=gi: e.max_index(out=idx.t[:, gi, 8:16], in_max=vals.t[:, gi, 8:16], in_values=scr.t[:, gi, :]), reads=[scr.b, vals.b], writes=[idx.b])
                S.op("dve", lambda e: e.tensor_copy(out=idxf.t[:], in_=idx.t[:]), reads=[idx.b], writes=[idxf.b])
                v0 = _bcast_ap(vals.t, 0, [[32, 8], [1, 16], [0, 16]])
                v1 = _bcast_ap(vals.t, 16, [[32, 8], [0, 16], [1, 16]])
                S.op("dve", lambda e, v0=v0, v1=v1: e.tensor_tensor(out=cand.t[:].rearrange("p h (a b) -> p h a b", a=16), in0=v0, in1=v1, op=ALU.add), reads=[vals.b], writes=[cand.b])
                for h in range(8):
                    S.op("dve", lambda e, h=h: e.max(out=ts.t[:, h, 0:8], in_=cand.t[:, h, :]), reads=[cand.b], writes=[ts.b])
                    S.op("dve", lambda e, h=h: e.max_index(out=tc_.t[:, h, 0:8], in_max=ts.t[:, h, 0:8], in_values=cand.t[:, h, :]), reads=[cand.b, ts.b], writes=[tc_.b])
                    S.op("dve", lambda e, h=h: e.match_replace(out=cscr.t[:], in_to_replace=ts.t[:, h, 0:8], in_values=cand.t[:, h, :], imm_value=-1e30), reads=[cand.b, ts.b], writes=[cscr.b])
                    S.op("dve", lambda e, h=h: e.max(out=ts.t[:, h, 8:16], in_=cscr.t[:]), reads=[cscr.b], writes=[ts.b])
                    S.op("dve", lambda e, h=h: e.max_index(out=tc_.t[:, h, 8:16], in_max=ts.t[:, h, 8:16], in_values=cscr.t[:]), reads=[cscr.b, ts.b], writes=[tc_.b])
                S.op("dve", lambda e: e.tensor_copy(out=tcf.t[:], in_=tc_.t[:]), reads=[tc_.b], writes=[tcf.b])
                S.op("dve", lambda e: e.tensor_scalar(out=af.t[:], in0=tcf.t[:], scalar1=0.0625, scalar2=-0.46875, op0=ALU.mult, op1=ALU.add), reads=[tcf.b], writes=[af.b])
                S.op("dve", lambda e: e.tensor_scalar(out=af.t[:], in0=af.t[:], scalar1=MAGIC, scalar2=None, op0=ALU.add), reads=[af.b], writes=[af.b])
                S.op("dve", lambda e: e.tensor_scalar(out=af.t[:], in0=af.t[:], scalar1=-MAGIC, scalar2=None, op0=ALU.add), reads=[af.b], writes=[af.b])
                S.op("dve", lambda e: e.scalar_tensor_tensor(out=bf.t[:], in0=af.t[:], scalar=-16.0, in1=tcf.t[:], op0=ALU.mult, op1=ALU.add), reads=[af.b, tcf.b], writes=[bf.b])
                for which, sel in ((0, af), (1, bf)):
                    selb = _bcast_ap(sel.t, 0, [[16, 8], [1, 16], [0, 16]])
                    iob = _bcast_ap(self.iota_f.t, 0, [[0, 8], [0, 16], [1, 16]])
                    ixb = _bcast_ap(idxf.t, which * 16, [[32, 8], [0, 16], [1, 16]])
                    S.op("dve", lambda e, selb=selb, iob=iob: e.tensor_tensor(out=oh.t[:], in0=selb, in1=iob, op=ALU.is_equal), reads=[sel.b, self.iota_f.b], writes=[oh.b])
                    S.op("dve", lambda e, ixb=ixb: e.tensor_tensor(out=oh.t[:], in0=oh.t[:], in1=ixb, op=ALU.mult), reads=[oh.b, idxf.b], writes=[oh.b])
                    S.op("dve", lambda e, which=which: e.tensor_reduce(out=IJg.t[:, which, :], in_=oh.t[:].rearrange("p h k a -> p (h k) a"), axis=AX.X, op=ALU.add), reads=[oh.b], writes=[IJg.b])
                tmax = _bcast_ap(ts.t, 0, [[16, 8], [0, 16]])
                S.op("dve", lambda e, tmax=tmax: e.tensor_tensor(out=tcf.t[:], in0=ts.t[:], in1=tmax, op=ALU.subtract), reads=[ts.b], writes=[tcf.b])
                S.op("act", lambda e: e.activation(out=tcf.t[:], in_=tcf.t[:], func=AF.Exp), reads=[tcf.b], writes=[tcf.b])
                S.op("dve", lambda e: e.tensor_reduce(out=esum.t[:], in_=tcf.t[:], axis=AX.X, op=ALU.add), reads=[tcf.b], writes=[esum.b])
                S.op("dve", lambda e: e.reciprocal(out=esum.t[:], in_=esum.t[:]), reads=[esum.b], writes=[esum.b])
                esb = _bcast_ap(esum.t, 0, [[1, 8], [0, 16]])
                S.op("dve", lambda e, esb=esb: e.tensor_tensor(out=IJg.t[:, 2, :].rearrange("p (h k) -> p h k", h=8), in0=tcf.t[:], in1=esb, op=ALU.mult), reads=[tcf.b, esum.b], writes=[IJg.b])
                for w in range(3):
                    bk = self.bank[w]
                    S.op("pe", lambda e, bk=bk, w=w: e.transpose(out=bk.t[:, 0:128], in_=IJg.t[:, w, :], identity=self.ident_f.t[:]), reads=[IJg.b, self.ident_f.b], writes=[bk.b])
                    S.op("act", lambda e, bk=bk, w=w, m=m: e.copy(out=RT.t[:, w, m * 128:(m + 1) * 128], in_=bk.t[:, 0:128]), reads=[bk.b], writes=[RT.b])
        S.barrier()
        if self.peer_stop == "p1":
            S.dma("sp", lambda e: e.dma_start(out=self.out[0:768, :].rearrange("(p a) d -> p (a d)", p=128), in_=RT.t[:].rearrange("p w t -> p (w t)")), RT.b, reads=[RT.b], is_output=True)
            return
        with contextlib.ExitStack() as st2:
            self.a_ptr = pR
            sb2 = lambda name, shape, dt: self.sb(st2, name, shape, dt)
            TB = 256
            W = sb2("W", [128, TB, 128], BF16)
            ohj = [sb2(f"ohj{i}", [128, 16, 128], BF16) for i in range(2)]
            ohi = [sb2(f"ohi{i}", [128, 16, 128], BF16) for i in range(2)]
            utl = [sb2(f"utl{i}", [128, 8, 128], BF16) for i in range(4)]
            vtl = [sb2(f"vtl{i}", [128, 1024], BF16) for i in range(5)]
            ag = [sb2(f"ag{i}", [128, TB], F32) for i in range(4)]
            wa = [sb2(f"wa{i}", [128, TB], BF16) for i in range(4)]
            x2t = [sb2(f"x2t{i}", [128, 1024], F32) for i in range(2)]
            sqj = sb2("sqjf", [128, 1024], BF16)
            fs = sb2("fs", [128, 4], F32)
            frs = sb2("frs", [128, 4], F32)
            yo = [sb2(f"yo{i}", [128, 1024], F32) for i in range(2)]
            gfin = sb2("gfin", [128, 1024], F32)
            S.dma("sp", lambda e: e.dma_start(out=gfin.t[:], in_=I["final_g_rep"]), gfin.b, writes=[gfin.b])
            psO = self.bank[0:4]
            psA = self.bank[4:8]
            psW = self.bank[6]
            for blk in range(2048 // TB):
                t0 = blk * TB
                G = 16
                for tg in range(TB // G):
                    tq = t0 + tg * G
                    oj, oi = ohj[tg % 2], ohi[tg % 2]
                    iob = _bcast_ap(self.iota_f.t, 0, [[0, G], [1, 128]])
                    Ib = _bcast_ap(RT.t, 0 * 2048 + tq, [[1, G], [0, 128]])
                    Jb = _bcast_ap(RT.t, 1 * 2048 + tq, [[1, G], [0, 128]])
                    gb = _bcast_ap(RT.t, 2 * 2048 + tq, [[1, G], [0, 128]])
                    S.op("dve", lambda e, oj=oj, iob=iob, Jb=Jb: e.tensor_tensor(out=oj.t[:], in0=iob, in1=Jb, op=ALU.is_equal), reads=[self.iota_f.b, RT.b], writes=[oj.b])
                    S.op("dve", lambda e, oi=oi, iob=iob, Ib=Ib: e.tensor_tensor(out=oi.t[:], in0=iob, in1=Ib, op=ALU.is_equal), reads=[self.iota_f.b, RT.b], writes=[oi.b])
                    S.op("pool", lambda e, oi=oi, gb=gb: e.tensor_tensor(out=oi.t[:], in0=oi.t[:], in1=gb, op=ALU.mult), reads=[oi.b, RT.b], writes=[oi.b])
                    for k in range(G):
                        tt = tg * G + k
                        S.op("pe", lambda e, oj=oj, oi=oi, tt=tt, k=k: e.matmul(psW.t[:, (tt % 4) * 128:(tt % 4 + 1) * 128], lhsT=oj.t[:, k, :], rhs=oi.t[:, k, :], start=True, stop=True), reads=[oj.b, oi.b], writes=[psW.b])
                        if tt % 4 == 3:
                            tb0 = tt - 3
                            S.op("act", lambda e, tb0=tb0: e.copy(out=W.t[:, tb0:tb0 + 4, :], in_=psW.t[:].rearrange("p (q i) -> p q i", q=4)), reads=[psW.b], writes=[W.b])
                LA = 2

                def emit_A(i, t0=t0):
                    ut, vt = utl[i % 4], vtl[i % 5]
                    S.dma("sp", lambda e, ut=ut, i=i: e.dma_start(out=ut.t[:], in_=self.UT_d[i]), ut.b, reads=[self.B_UT], writes=[ut.b])
                    S.dma("act", lambda e, vt=vt, i=i: e.dma_start(out=vt.t[:], in_=self.Vb_d[i * 128:(i + 1) * 128, :]), vt.b, reads=[self.B_Vb], writes=[vt.b])
                    pa = psA[i % 4]
                    for kc in range(8):
                        S.op("pe", lambda e, pa=pa, ut=ut, kc=kc, t0=t0: e.matmul(pa.t[:, 0:TB], lhsT=ut.t[:, kc, :], rhs=hfT.t[:, kc, t0:t0 + TB], start=(kc == 0), stop=(kc == 7)),
                             reads=[ut.b, hfT.b], writes=[pa.b])
                    a_, w_ = ag[i % 4], wa[i % 4]
                    S.op("act", lambda e, pa=pa, a_=a_: e.activation(out=a_.t[:], in_=pa.t[:, 0:TB], func=AF.Gelu), reads=[pa.b], writes=[a_.b])
                    S.op("pool", lambda e, a_=a_, w_=w_, i=i: e.tensor_tensor(out=w_.t[:], in0=a_.t[:], in1=W.t[:, :, i], op=ALU.mult), reads=[a_.b, W.b], writes=[w_.b])

                def emit_out(i):
                    vt = vtl[i % 5]
                    w_ = wa[i % 4]
                    for tl in range(TB // 128):
                        for dh in range(2):
                            po = psO[tl * 2 + dh]
                            S.op("pe", lambda e, po=po, w_=w_, vt=vt, tl=tl, dh=dh, i=i: e.matmul(po.t[:], lhsT=w_.t[:, tl * 128:(tl + 1) * 128], rhs=vt.t[:, dh * 512:(dh + 1) * 512], start=(i == 0), stop=(i == 127)),
                                 reads=[w_.b, vt.b], writes=[po.b])

                for step_i in range(128 + LA):
                    if step_i < 128:
                        emit_A(step_i)
                    if step_i >= LA:
                        emit_out(step_i - LA)
                for tl in range(TB // 128):
                    m = (t0 // 128) + tl
                    xt_ = x2t[tl % 2]
                    yt = yo[tl % 2]
                    S.dma("sp", lambda e, xt_=xt_, m=m: e.dma_start(out=xt_.t[:], in_=X2_d[m * 128:(m + 1) * 128, :]), xt_.b, reads=[B_X2], writes=[xt_.b])
                    for dh in range(2):
                        po = psO[tl * 2 + dh]
                        S.op("dve", lambda e, po=po, xt_=xt_, dh=dh: e.tensor_tensor(out=xt_.t[:, dh * 512:(dh + 1) * 512], in0=po.t[:], in1=xt_.t[:, dh * 512:(dh + 1) * 512], op=ALU.add), reads=[po.b, xt_.b], writes=[xt_.b])
                    if "x3" in self.dumps:
                        S.dma("sp", lambda e, xt_=xt_, m=m: e.dma_start(out=self.dbg["x3"][m * 128:(m + 1) * 128, :], in_=xt_.t[:]), xt_.b, reads=[xt_.b], is_output=True)
                    S.op("act", lambda e, xt_=xt_: e.activation(out=sqj.t[:], in_=xt_.t[:], func=AF.Square, accum_out=fs.t[:, 0:1]), reads=[xt_.b], writes=[sqj.b, fs.b])
                    S.op("dve", lambda e: e.tensor_scalar(out=frs.t[:, 0:1], in0=fs.t[:, 0:1], scalar1=1.0 / D, scalar2=1e-6, op0=ALU.mult, op1=ALU.add), reads=[fs.b], writes=[frs.b])
                    S.op("pool", lambda e: e.tensor_tensor(out=frs.t[:, 0:1], in0=frs.t[:, 0:1], in1=self.mhalf.t[:, 0:1], op=ALU.pow), reads=[frs.b, self.mhalf.b], writes=[frs.b])
                    S.op("dve", lambda e, xt_=xt_, yt=yt: e.scalar_tensor_tensor(out=yt.t[:], in0=xt_.t[:], scalar=frs.t[:, 0:1], in1=gfin.t[:], op0=ALU.mult, op1=ALU.mult), reads=[xt_.b, frs.b, gfin.b], writes=[yt.b])
                    S.dma("sp", lambda e, yt=yt, m=m: e.dma_start(out=self.out[m * 128:(m + 1) * 128, :], in_=yt.t[:]), yt.b, reads=[yt.b], is_output=True)


def _phase_final(self):
    pass


Builder.phase_peer = _phase_peer
Builder.phase_final = _phase_final
```

```python
import contextlib
import math

import numpy as np
import ml_dtypes
import concourse.bass as bass
import concourse.mybir as mybir
from concourse.bass_utils import run_bass_kernel_spmd

F32 = mybir.dt.float32
BF16 = mybir.dt.bfloat16
U32 = mybir.dt.uint32
I32 = mybir.dt.int32
AF = mybir.ActivationFunctionType
ALU = mybir.AluOpType
AX = mybir.AxisListType

NCORE = 8
SEQ = 16384
D = 1024
NT_ALL = SEQ // 128
NT_OWN = 16
COMPUTE = ("pe", "act", "dve", "pool")
ALL_ENG = ("pe", "act", "dve", "pool", "sp")
SEM_ROT = 30000


class Buf:
    __slots__ = ("name", "writers", "readers", "dsem", "dcount")

    def __init__(self, name):
        self.name = name
        self.writers = []
        self.readers = []
        self.dsem = None
        self.dcount = 0


class Ins:
    __slots__ = ("eng", "fn", "deps", "is_dma", "sem", "semval", "needs_inc")

    def __init__(self, eng, fn, deps, is_dma):
        self.eng = eng
        self.fn = fn
        self.deps = deps
        self.is_dma = is_dma
        self.sem = None
        self.semval = 0
        self.needs_inc = False


class Sched:
    def __init__(self, nc, same_engine_sync=True):
        self.nc = nc
        self.ins = []
        self.streams = {e: [] for e in ALL_ENG}
        self.same_engine_sync = same_engine_sync
        self.dma_bufs = []
        self.out_tokens = []
        self.last_eng = {}
        self.last_dma = {}
        self.base_deps = frozenset()

    def barrier(self):
        self.base_deps = frozenset(list(self.last_eng.values()) + list(self.last_dma.values()))

    def _deps(self, reads, writes):
        deps = set(self.base_deps)
        for b in reads:
            deps.update(b.writers)
        for b in writes:
            deps.update(b.writers)
            deps.update(b.readers)
        last = {}
        for i in deps:
            it = self.ins[i]
            key = ("d", id(it.sem)) if it.is_dma else it.eng
            if key not in last or last[key] < i:
                last[key] = i
        return set(last.values())

    def _compress(self, lst):
        last = {}
        out = []
        for i in lst:
            it = self.ins[i]
            if it.is_dma:
                last[("d", id(it.sem))] = i
            else:
                last[it.eng] = i
        return list(last.values())

    def _commit(self, me, reads, writes):
        for b in writes:
            if b.readers:
                b.writers = [me]
                b.readers = []
            else:
                b.writers.append(me)
                if len(b.writers) > 32:
                    b.writers = self._compress(b.writers)
        for b in reads:
            if b in writes:
                continue
            b.readers.append(me)
            if len(b.readers) > 32:
                b.readers = self._compress(b.readers)

    def op(self, eng, fn, reads=(), writes=()):
        deps = self._deps(reads, writes)
        me = len(self.ins)
        it = Ins(eng, fn, deps, False)
        self.ins.append(it)
        self.streams[eng].append(it)
        self._commit(me, reads, writes)
        self.last_eng[eng] = me
        return me

    def dma(self, queue, fn, sem_buf, reads=(), writes=(), is_output=False):
        deps = self._deps(reads, writes)
        me = len(self.ins)
        it = Ins(queue, fn, deps, True)
        if sem_buf.dsem is None:
            sem_buf.dsem = "pending"
            self.dma_bufs.append(sem_buf)
        sem_buf.dcount += 16
        it.sem = sem_buf
        it.semval = sem_buf.dcount
        self.ins.append(it)
        self.streams[queue].append(it)
        self._commit(me, reads, writes)
        self.last_dma[id(sem_buf)] = me
        if is_output:
            self.out_tokens.append(me)
        return me

    def emit(self, final_engine="sp"):
        nc = self.nc
        ses = self.same_engine_sync
        for it in self.ins:
            for d in it.deps:
                dd = self.ins[d]
                if dd.is_dma:
                    continue
                if dd.eng == it.eng and not it.is_dma and (dd.eng == "pe" or not ses):
                    continue
                dd.needs_inc = True
        n_sems = {}
        for e in COMPUTE:
            c = 0
            for it in self.streams[e]:
                if it.is_dma:
                    continue
                if it.needs_inc:
                    c += 1
                    it.semval = c
            n_sems[e] = max(c - 1, 0) // SEM_ROT + 1
        with contextlib.ExitStack() as st:
            eng_sems = {e: [st.enter_context(nc.semaphore(f"s_{e}{k}")) for k in range(n_sems[e])] for e in COMPUTE}
            for i, b in enumerate(self.dma_bufs):
                b.dsem = st.enter_context(nc.semaphore(f"d{i}_{b.name}"))

            def token(it):
                if it.is_dma:
                    return (it.sem.dsem, it.semval, ("d", id(it.sem)))
                k = (it.semval - 1) // SEM_ROT
                return (eng_sems[it.eng][k], it.semval - k * SEM_ROT, (it.eng, k))

            def run(e, eng):
                known = {}
                for it in self.streams[e]:
                    need = {}
                    for d in it.deps:
                        dd = self.ins[d]
                        if not dd.is_dma and dd.eng == e and not it.is_dma and (e == "pe" or not ses):
                            continue
                        sem, val, key = token(dd)
                        if known.get(key, 0) >= val:
                            continue
                        if key not in need or need[key][1] < val:
                            need[key] = (sem, val)
                    for key, (sem, val) in need.items():
                        eng.wait_ge(sem, val)
                        known[key] = val
                    h = it.fn(eng)
                    if it.is_dma:
                        h.then_inc(it.sem.dsem, 16)
                    elif it.needs_inc:
                        k = (it.semval - 1) // SEM_ROT
                        h.then_inc(eng_sems[it.eng][k], 1)
                if e == final_engine:
                    need = {}
                    for d in self.out_tokens:
                        sem, val, key = token(self.ins[d])
                        if key not in need or need[key][1] < val:
                            need[key] = (sem, val)
                    for key, (sem, val) in need.items():
                        eng.wait_ge(sem, val)

            with nc.Block() as block:
                @block.sync
                def _(eng):
                    run("sp", eng)

                @block.tensor
                def _(eng):
                    run("pe", eng)

                @block.scalar
                def _(eng):
                    run("act", eng)

                @block.vector
                def _(eng):
                    run("dve", eng)

                @block.gpsimd
                def _(eng):
                    run("pool", eng)


class TT:
    __slots__ = ("t", "b")

    def __init__(self, t, b):
        self.t = t
        self.b = b


def _t5_bucket(n):
    n = np.maximum(n, 0)
    nf = np.maximum(n, 1).astype(np.float32)
    large = 16 + (np.log(nf / np.float32(16)) / np.float32(math.log(8)) * np.float32(16)).astype(np.int32)
    large = np.minimum(large, 31)
    return np.where(n < 16, n, large)


def _band_tables(c):
    ki = np.arange(128)[:, None]
    qi = np.arange(128)[None, :]
    bk = np.zeros((9, 128, 128), np.int64)
    mk = np.zeros((9, 128, 128), np.float32)
    for bi, b in enumerate(range(-1, 8)):
        rel = 128 * (c - b) + qi - ki
        bk[bi] = _t5_bucket(rel)
        mk[bi] = np.where(rel >= 0, 0.0, -30000.0)
    return bk, mk


class Builder:
    def __init__(self, stop_after=None, upto=None):
        self.stop_after = stop_after
        self.upto = upto
        self.nc = bass.Bass("TRN2", target_bir_lowering=False)
        self.S = Sched(self.nc)
        self.n = 0

    def din(self, name, shape, dt=F32):
        if self.upto and self.upto.startswith("peeronly") and name in ("x_all", "w_in", "w_conv_out", "w_attn_out", "w_mix_out", "w_cq", "w_ckv", "w_co", "mem"):
            shape = [1, 1]
        if not hasattr(self, "in_shapes"):
            self.in_shapes = {}
        self.in_shapes[name] = (tuple(shape), dt)
        return self.nc.dram_tensor(name, list(shape), dt, kind="ExternalInput").ap()

    def dout(self, name, shape, dt=F32):
        return self.nc.dram_tensor(name, list(shape), dt, kind="ExternalOutput").ap()

    def dscr(self, name, shape, dt):
        return self.nc.dram_tensor(name, list(shape), dt, kind="Internal").ap()

    def _arena_init(self):
        rem = self.nc.sbuf_bytes_remaining
        rem = rem() if callable(rem) else rem
        size = (int(rem) - 256) // 64 * 64
        beg, end = self.nc.bump_sbuf(size)
        self.a_beg = (int(beg) + 63) // 64 * 64
        self.a_end = int(end)
        self.a_ptr = self.a_beg
        self.a_marked = set()

    def _reset_ptr(self, mark, key):
        self.a_ptr = mark
        self.a_marked.discard(key)

    def sb(self, st, name, shape, dt, at=None):
        if id(st) not in self.a_marked:
            self.a_marked.add(id(st))
            st.callback(self._reset_ptr, self.a_ptr, id(st))
        self.n += 1
        nm = f"{name}_{self.n}"
        nbytes = int(np.prod(shape[1:])) * mybir.dt.size(dt)
        nbytes = (nbytes + 63) // 64 * 64
        if at is None:
            off = self.a_ptr
            self.a_ptr += nbytes
        else:
            off = at
        assert off + nbytes <= self.a_end, (name, off, nbytes, self.a_end)
        self.a_peak = max(getattr(self, "a_peak", 0), off + nbytes - self.a_beg)
        t = self.nc.alloc_sbuf_tensor_at(nm, list(shape), dt, offset=off)
        return TT(t, Buf(nm))

    def ps(self, st, name, shape, dt):
        self.n += 1
        nm = f"{name}_{self.n}"
        return TT(st.enter_context(self.nc.psum_tensor(nm, list(shape), dt)), Buf(nm))

    def build(self):
        nc, S = self.nc, self.S
        din, dout = self.din, self.dout
        I = {}
        I["x_all"] = din("x_all", [SEQ, D])
        I["x_own"] = din("x_own", [NT_OWN * 128, D])
        I["x_halo"] = din("x_halo", [32, D])
        I["mem"] = din("mem", [256, D])
        I["w_in"] = din("w_in", [D, 8192])
        I["gcols"] = din("gcols", [128, 4, 8])
        I["final_g_rep"] = din("final_g_rep", [128, D])
        I["conv_wT"] = din("conv_wT", [128, 8, 3])
        I["w_conv_out"] = din("w_conv_out", [D, D])
        I["lam_rep"] = din("lam_rep", [128, 4, 64])
        I["subln_rep"] = din("subln_rep", [128, 128])
        I["w_attn_out"] = din("w_attn_out", [D, D])
        I["w_mix_out"] = din("w_mix_out", [D, D])
        I["rb31_rep"] = din("rb31_rep", [128, 8])
        I["gbias"] = din("gbias", [8, 128, 9, 128])
        I["mband"] = din("mband", [128, 9, 128])
        I["w_cq"] = din("w_cq", [D, D])
        I["w_ckv"] = din("w_ckv", [D, 2048])
        I["w_co"] = din("w_co", [D, D])
        I["w_pq"] = din("w_pq", [D, 2048])
        I["skT"] = din("skT", [16, 128, 128])
        I["peer_u"] = din("peer_u", [SEQ, D])
        I["peer_v"] = din("peer_v", [SEQ, D])
        I["ident_bf"] = din("ident_bf", [128, 128], BF16)
        I["ident_f"] = din("ident_f", [128, 128])
        I["iota_f"] = din("iota_f", [128, 128])
        self.I = I
        self.out = dout("out", [NT_OWN * 128, D])
        self.dumps = set(self.stop_after.split(",")) if self.stop_after else set()
        self.dbg = {}
        if "attn" in self.dumps:
            self.dbg["attn"] = dout("dbg_attn", [128, 8 * NT_OWN * 128])
        for nm in ("x1", "x2", "x3"):
            if nm in self.dumps:
                self.dbg[nm] = dout("dbg_" + nm, [NT_OWN * 128, D])
        self.UT_d = self.dscr("UT_d", [128, 128, 8, 128], BF16)
        self.Vb_d = self.dscr("Vb_d", [SEQ, D], BF16)
        self.B_UT = Buf("UT_d")
        self.B_Vb = Buf("Vb_d")
        self.KT_d = self.dscr("KT_d", [8, 128, SEQ], BF16)
        self.V_d = self.dscr("V_d", [8, 128, NT_ALL, 128], BF16)
        self.B_KT = Buf("KT_d")
        self.B_V = Buf("V_d")

        with contextlib.ExitStack() as gst:
            self.gst = gst
            self._arena_init()
            self.bank = [self.ps(gst, f"bank{i}", [128, 512], F32) for i in range(8)]
            self.setup_consts()
            if self.upto and self.upto.startswith("peeronly"):
                with contextlib.ExitStack() as rst:
                    p0 = self.a_ptr
                    self.xres = self.sb(rst, "xres", [128, NT_OWN, D], F32, at=p0 + 32 * 1024)
                    S.dma("sp", lambda e: e.dma_start(out=self.xres.t[:], in_=I["x_own"].rearrange("(m p) d -> p m d", p=128)), self.xres.b, writes=[self.xres.b])
                    self.peer_stop = self.upto.split(":")[1] if ":" in self.upto else None
                    self.phase_peer(p0)
                S.emit()
                return nc
            self.peer_stop = None
            self.phase_kv()
            S.barrier()
            self.phase_q_attn()
            S.barrier()
            if "attn" in self.dumps:
                self.dump_attn()
            with contextlib.ExitStack() as rst:
                p0 = self.a_ptr
                self.xres = self.sb(rst, "xres", [128, NT_OWN, D], F32, at=p0 + 32 * 1024)
                self.phase_mix(p0)
                S.barrier()
                if "x1" in self.dumps:
                    self.dump_res("x1")
                if self.upto != "mix":
                    self.phase_cross(p0)
                    S.barrier()
                    if "x2" in self.dumps:
                        self.dump_res("x2")
                    if self.upto != "cross":
                        self.phase_peer(p0)
            S.emit()
        return nc

    def setup_consts(self):
        S, gst, I = self.S, self.gst, self.I
        sb = lambda name, shape, dt: self.sb(gst, name, shape, dt)
        self.ident_bf = sb("ident_bf", [128, 128], BF16)
        self.ident_f = sb("ident_f", [128, 128], F32)
        self.iota_f = sb("iota_f", [128, 128], F32)
        self.gcols = sb("gcols", [128, 4, 8], F32)
        self.lam = sb("lam", [128, 4], F32)
        self.subg = sb("subg", [128, 128], F32)
        self.rb31 = sb("rb31", [128, 8], F32)
        self.mhalf = sb("mhalf", [128, 4], F32)
        self.pA = self.a_ptr
        self.bias = sb("bias", [128, 8, 9, 128], BF16)
        self.outT = sb("outT", [128, 8, NT_OWN * 128], BF16)
        S.op("pool", lambda e: e.memset(self.mhalf.t[:], -0.5), writes=[self.mhalf.b])
        cset = Buf("consts")
        for tt, src in ((self.ident_bf, I["ident_bf"]), (self.ident_f, I["ident_f"]), (self.iota_f, I["iota_f"]),
                        (self.gcols, I["gcols"]), (self.subg, I["subln_rep"]), (self.rb31, I["rb31_rep"])):
            S.dma("sp", lambda e, tt=tt, src=src: e.dma_start(out=tt.t[:], in_=src), cset, writes=[tt.b])
        with contextlib.ExitStack() as st:
            lamin = self.sb(st, "lamin", [128, 4, 64], F32)
            prod = self.sb(st, "lamprod", [128, 2, 64], F32)
            red = self.sb(st, "lamred", [128, 2], F32)
            ex = self.sb(st, "lamex", [128, 2], F32)
            mb = self.sb(st, "mband", [128, 9, 128], F32)
            gb = [self.sb(st, f"gb{i}", [128, 9, 128], F32) for i in range(2)]
            S.dma("sp", lambda e: e.dma_start(out=lamin.t[:], in_=I["lam_rep"]), cset, writes=[lamin.b])
            S.dma("sp", lambda e: e.dma_start(out=mb.t[:], in_=I["mband"]), cset, writes=[mb.b])
            S.op("dve", lambda e: e.tensor_tensor(out=prod.t[:, 0, :], in0=lamin.t[:, 0, :], in1=lamin.t[:, 1, :], op=ALU.mult), reads=[lamin.b], writes=[prod.b])
            S.op("dve", lambda e: e.tensor_tensor(out=prod.t[:, 1, :], in0=lamin.t[:, 2, :], in1=lamin.t[:, 3, :], op=ALU.mult), reads=[lamin.b], writes=[prod.b])
            S.op("dve", lambda e: e.tensor_reduce(out=red.t[:], in_=prod.t[:], axis=AX.X, op=ALU.add), reads=[prod.b], writes=[red.b])
            S.op("act", lambda e: e.activation(out=ex.t[:], in_=red.t[:], func=AF.Exp), reads=[red.b], writes=[ex.b])
            S.op("dve", lambda e: e.tensor_tensor(out=self.lam.t[:, 0:1], in0=ex.t[:, 0:1], in1=ex.t[:, 1:2], op=ALU.subtract), reads=[ex.b], writes=[self.lam.b])
            S.op("dve", lambda e: e.tensor_scalar(out=self.lam.t[:, 0:1], in0=self.lam.t[:, 0:1], scalar1=0.2, scalar2=None, op0=ALU.add), reads=[self.lam.b], writes=[self.lam.b])
            S.op("dve", lambda e: e.tensor_scalar(out=self.lam.t[:, 1:2], in0=self.lam.t[:, 0:1], scalar1=-1.0, scalar2=None, op0=ALU.mult), reads=[self.lam.b], writes=[self.lam.b])
            S.op("dve", lambda e: e.tensor_scalar(out=self.subg.t[:], in0=self.subg.t[:], scalar1=0.8, scalar2=None, op0=ALU.mult), reads=[self.subg.b], writes=[self.subg.b])
            for h in range(8):
                g = gb[h % 2]
                S.dma("sp", lambda e, g=g, h=h: e.dma_start(out=g.t[:], in_=I["gbias"][h]), g.b, writes=[g.b])
                S.op("dve", lambda e, g=g, h=h: e.scalar_tensor_tensor(out=self.bias.t[:, h, :, :], in0=g.t[:], scalar=self.rb31.t[:, h:h + 1], in1=mb.t[:], op0=ALU.subtract, op1=ALU.add),
                     reads=[g.b, self.rb31.b, mb.b], writes=[self.bias.b])
            S.barrier()

    def load_weight(self, st_tiles, dst, col0, ncols, w_ap, gsel, kcs=range(8), queue="sp"):
        S = self.S
        for kc in kcs:
            stg = st_tiles[kc % len(st_tiles)]
            S.dma(queue, lambda e, stg=stg, kc=kc: e.dma_start(out=stg.t[:, 0:ncols], in_=w_ap[kc * 128:(kc + 1) * 128, col0:col0 + ncols]), stg.b, writes=[stg.b])
            if gsel is None:
                S.op("pool", lambda e, stg=stg, kc=kc: e.tensor_copy(out=dst.t[:, kc, 0:ncols], in_=stg.t[:, 0:ncols]), reads=[stg.b], writes=[dst.b])
            else:
                S.op("pool", lambda e, stg=stg, kc=kc: e.tensor_scalar(out=dst.t[:, kc, 0:ncols], in0=stg.t[:, 0:ncols], scalar1=self.gcols.t[:, gsel, kc:kc + 1], scalar2=None, op0=ALU.mult),
                     reads=[stg.b, self.gcols.b], writes=[dst.b])

    def norm_tiles(self, x_ap_fn, tiles, xt, sqj, ss, rstd, xn, eps=1e-6):
        S = self.S
        n = len(tiles)
        for i, (tid, xts, xns) in enumerate(tiles):
            S.dma("sp", lambda e, xts=xts, tid=tid: e.dma_start(out=xts.t[:], in_=x_ap_fn(tid)), xts.b, writes=[xts.b])
            S.op("act", lambda e, xts=xts, i=i: e.activation(out=sqj.t[:], in_=xts.t[:], func=AF.Square, accum_out=ss.t[:, i:i + 1]), reads=[xts.b], writes=[sqj.b, ss.b])
        S.op("dve", lambda e: e.tensor_scalar(out=rstd.t[:, 0:n], in0=ss.t[:, 0:n], scalar1=1.0 / D, scalar2=eps, op0=ALU.mult, op1=ALU.add), reads=[ss.b], writes=[rstd.b])
        S.op("pool", lambda e: e.tensor_tensor(out=rstd.t[:, 0:n], in0=rstd.t[:, 0:n], in1=self.mhalf.t[:, 0:n], op=ALU.pow), reads=[rstd.b, self.mhalf.b], writes=[rstd.b])
        for i, (tid, xts, xns) in enumerate(tiles):
            S.op("dve", lambda e, xts=xts, xns=xns, i=i: e.tensor_scalar(out=xns.t[:], in0=xts.t[:], scalar1=rstd.t[:, i:i + 1], scalar2=None, op0=ALU.mult), reads=[xts.b, rstd.b], writes=[xns.b])

    def transpose_tiles(self, tiles_xn, hT, tbanks, cnt0, gsel=None, col0=0, npart=128):
        S = self.S
        if gsel is not None or npart != 128:
            for i, xns in enumerate(tiles_xn):
                bk = tbanks[(cnt0 + i) % len(tbanks)]
                pt = bk.t.bitcast(BF16)
                for kc in range(8):
                    S.op("pe", lambda e, pt=pt, xns=xns, kc=kc: e.transpose(out=pt[:, kc * 128:kc * 128 + npart], in_=xns.t[0:npart, kc * 128:(kc + 1) * 128], identity=self.ident_bf.t[0:npart, 0:npart]),
                         reads=[xns.b, self.ident_bf.b], writes=[bk.b])
                for kc in range(8):
                    c0 = col0 + i * npart
                    if gsel is None:
                        S.op("act", lambda e, pt=pt, kc=kc, c0=c0: e.copy(out=hT.t[:, kc, c0:c0 + npart], in_=pt[:, kc * 128:kc * 128 + npart]), reads=[bk.b], writes=[hT.b])
                    else:
                        S.op("act", lambda e, pt=pt, kc=kc, c0=c0: e.activation(out=hT.t[:, kc, c0:c0 + npart], in_=pt[:, kc * 128:kc * 128 + npart], func=AF.Copy, scale=self.gcols.t[:, gsel, kc:kc + 1]),
                             reads=[bk.b, self.gcols.b], writes=[hT.b])
            return
        for i, xns in enumerate(tiles_xn):
            bk = tbanks[(cnt0 + i) % len(tbanks)]
            pt = bk.t.bitcast(BF16)
            for kc in range(8):
                S.op("pe", lambda e, pt=pt, xns=xns, kc=kc: e.transpose(out=pt[:, kc * 128:(kc + 1) * 128], in_=xns.t[:, kc * 128:(kc + 1) * 128], identity=self.ident_bf.t[:]),
                     reads=[xns.b, self.ident_bf.b], writes=[bk.b])
            S.op("act", lambda e, pt=pt, i=i: e.copy(out=hT.t[:, :, i * 128:(i + 1) * 128], in_=pt[:, :].rearrange("p (k t) -> p k t", k=8)), reads=[bk.b], writes=[hT.b])

    def phase_kv(self):
        S, I = self.S, self.I
        with contextlib.ExitStack() as st:
            sb = lambda name, shape, dt: self.sb(st, name, shape, dt)
            wk = sb("wk", [128, 8, 1024], BF16)
            wv = sb("wv", [128, 8, 1024], BF16)
            wst = [sb(f"wst{i}", [128, 1024], F32) for i in range(2)]
            self.load_weight(wst, wk, 4096, 1024, I["w_in"], 0)
            self.load_weight(wst, wv, 5120, 1024, I["w_in"], 0)
            xt = [sb(f"xt{i}", [128, 1024], F32) for i in range(8)]
            xn = [sb(f"xn{i}", [128, 1024], BF16) for i in range(12)]
            sqj = sb("sqj", [128, 1024], BF16)
            ss = [sb(f"ss{i}", [128, 4], F32) for i in range(3)]
            rstd = [sb(f"rstd{i}", [128, 4], F32) for i in range(3)]
            hT = [sb(f"hT{i}", [128, 8, 512], BF16) for i in range(2)]
            kst = [sb(f"kst{i}", [128, 8, 512], BF16) for i in range(2)]
            vst = [sb(f"vst{i}", [128, 4, 1024], BF16) for i in range(2)]
            tb = self.bank[0:2]
            mb = self.bank[2:8]
            NB = NT_ALL // 4
            x_all = I["x_all"]

            def A(bi):
                tiles = [(bi * 4 + i, xt[(bi * 4 + i) % 8], xn[(bi * 4 + i) % 12]) for i in range(4)]
                self.norm_tiles(lambda tid: x_all[tid * 128:(tid + 1) * 128, :], tiles, xt, sqj, ss[bi % 3], rstd[bi % 3], xn)

            def Bs(bi):
                self.transpose_tiles([xn[(bi * 4 + i) % 12] for i in range(4)], hT[bi % 2], tb, bi * 4)

            cnt = [0]

            def C(bi):
                h = hT[bi % 2]
                ks, vs = kst[bi % 2], vst[bi % 2]
                for j in range(8):
                    bk = mb[cnt[0] % 6]
                    cnt[0] += 1
                    for kc in range(8):
                        S.op("pe", lambda e, bk=bk, kc=kc, j=j: e.matmul(bk.t[:], lhsT=wk.t[:, kc, j * 128:(j + 1) * 128], rhs=h.t[:, kc, :], start=(kc == 0), stop=(kc == 7)),
                             reads=[wk.b, h.b], writes=[bk.b])
                    eng = "act" if j % 2 == 0 else "dve"
                    if eng == "act":
                        S.op("act", lambda e, bk=bk, j=j: e.copy(out=ks.t[:, j, :], in_=bk.t[:]), reads=[bk.b], writes=[ks.b])
                    else:
                        S.op("dve", lambda e, bk=bk, j=j: e.tensor_copy(out=ks.t[:, j, :], in_=bk.t[:]), reads=[bk.b], writes=[ks.b])
                S.dma("sp", lambda e: e.dma_start(out=self.KT_d[:, :, bi * 512:(bi + 1) * 512].rearrange("h p t -> p h t"), in_=ks.t[:]), ks.b, reads=[ks.b], writes=[self.B_KT])
                for i in range(4):
                    for hh in range(2):
                        bk = mb[cnt[0] % 6]
                        cnt[0] += 1
                        for kc in range(8):
                            S.op("pe", lambda e, bk=bk, kc=kc, i=i, hh=hh: e.matmul(bk.t[:], lhsT=h.t[:, kc, i * 128:(i + 1) * 128], rhs=wv.t[:, kc, hh * 512:(hh + 1) * 512], start=(kc == 0), stop=(kc == 7)),
                                 reads=[wv.b, h.b], writes=[bk.b])
                        if hh == 0:
                            S.op("act", lambda e, bk=bk, i=i, hh=hh: e.copy(out=vs.t[:, i, hh * 512:(hh + 1) * 512], in_=bk.t[:]), reads=[bk.b], writes=[vs.b])
                        else:
                            S.op("dve", lambda e, bk=bk, i=i, hh=hh: e.tensor_copy(out=vs.t[:, i, hh * 512:(hh + 1) * 512], in_=bk.t[:]), reads=[bk.b], writes=[vs.b])
                    S.dma("act", lambda e, i=i: e.dma_start(out=self.V_d[:, :, bi * 4 + i, :].rearrange("h p e -> p h e"), in_=vs.t[:, i, :].rearrange("p (h e) -> p h e", h=8)),
                          vs.b, reads=[vs.b], writes=[self.B_V])

            A(0)
            A(1)
            Bs(0)
            for bi in range(NB):
                if bi + 2 < NB:
                    A(bi + 2)
                if bi + 1 < NB:
                    Bs(bi + 1)
                C(bi)

    def phase_q_attn(self):
        S, I = self.S, self.I
        with contextlib.ExitStack() as st:
            sb = lambda name, shape, dt: self.sb(st, name, shape, dt)
            QTc = [sb(f"QT{c}", [128, 8, NT_OWN * 128], BF16) for c in range(2)]
            for c in range(2):
                S.op("pool", lambda e, c=c: e.memset(QTc[c].t[:], 0.0), writes=[QTc[c].b])
            with contextlib.ExitStack() as st2:
                sb2 = lambda name, shape, dt: self.sb(st2, name, shape, dt)
                wq = sb2("wq", [128, 8, 1024], BF16)
                wst = [sb2(f"wst{i}", [128, 1024], F32) for i in range(2)]
                self.load_weight(wst, wq, 3072, 1024, I["w_in"], 0)
                xt = [sb2(f"xt{i}", [128, 1024], F32) for i in range(4)]
                xn = [sb2(f"xn{i}", [128, 1024], BF16) for i in range(4)]
                sqj = sb2("sqj", [128, 1024], BF16)
                ss = sb2("ss", [128, 4], F32)
                rstd = sb2("rstd", [128, 4], F32)
                hT = sb2("hT", [128, 8, 512], BF16)
                x_own = I["x_own"]
                cnt = 0
                for bi in range(4):
                    tiles = [(bi * 4 + i, xt[i], xn[i]) for i in range(4)]
                    self.norm_tiles(lambda tid: x_own[tid * 128:(tid + 1) * 128, :], tiles, xt, sqj, ss, rstd, xn)
                    self.transpose_tiles(xn, hT, self.bank[0:2], bi * 4)
                    for j in range(8):
                        bk = self.bank[2 + cnt % 6]
                        cnt += 1
                        for kc in range(8):
                            S.op("pe", lambda e, bk=bk, kc=kc, j=j: e.matmul(bk.t[:], lhsT=wq.t[:, kc, j * 128:(j + 1) * 128], rhs=hT.t[:, kc, :], start=(kc == 0), stop=(kc == 7)),
                                 reads=[wq.b, hT.b], writes=[bk.b])
                        S.op("act", lambda e, bk=bk, j=j, bi=bi: e.activation(out=QTc[0].t[0:64, j, bi * 512:(bi + 1) * 512], in_=bk.t[0:64, :], func=AF.Copy, scale=0.125), reads=[bk.b], writes=[QTc[0].b])
                        S.op("dve", lambda e, bk=bk, j=j, bi=bi: e.tensor_scalar(out=QTc[1].t[64:128, j, bi * 512:(bi + 1) * 512], in0=bk.t[64:128, :], scalar1=0.125, scalar2=None, op0=ALU.mult), reads=[bk.b], writes=[QTc[1].b])
                S.barrier()
            KT = [sb(f"KT{i}", [128, 4096], BF16) for i in range(4)]
            V1 = [sb(f"V1{i}", [128, 32, 130], BF16) for i in range(4)]
            E = [sb(f"E{i}", [128, 512], BF16) for i in range(3)]
            tmp0 = [sb(f"tmp0{i}", [128, 128], F32) for i in range(4)]
            oc = sb("oc", [128, 128], F32)
            on = sb("on", [128, 128], BF16)
            sqj = sb("sqj2", [128, 128], F32)
            sm = sb("sm", [128, 8], F32)
            for v in V1:
                S.op("pool", lambda e, v=v: e.memset(v.t[:, :, 128:130], 1.0), writes=[v.b])
            sbanks = self.bank[0:3]
            obanks = self.bank[3:7]
            tbank = self.bank[7]
            oacc = [(obanks[j], 0, obanks[j].b) for j in range(4)]
            LA = 2
            steps = []
            for h in range(8):
                for g in (3, 2, 1, 0):
                    for c in range(2):
                        nk = 32 * g + 32
                        for kt in range(nk):
                            steps.append((h, g, c, kt, nk))

            def jmin_of(g, kt):
                return max(0, -((-(kt - 32 * g - 7)) // 8))

            def emit_qk(si):
                h, g, c, kt, nk = steps[si]
                if g == 3 and c == 0 and kt == 0:
                    for q in range(4):
                        S.dma("sp", lambda e, q=q, h=h: e.dma_start(out=KT[q].t[:], in_=self.KT_d[h, :, q * 4096:(q + 1) * 4096]), KT[q].b, reads=[self.B_KT], writes=[KT[q].b])
                jmin = jmin_of(g, kt)
                band = {}
                for j in range(jmin, 4):
                    b = kt - (32 * g + 8 * j)
                    if -1 <= b <= 7:
                        band[j] = b
                sbk = sbanks[si % 3]
                Et = E[si % 3]
                kq, kl = kt // 32, kt % 32
                lhs = KT[kq].t[:, kl * 128:(kl + 1) * 128]
                ncol = (4 - jmin) * 128
                S.op("pe", lambda e, sbk=sbk, ncol=ncol, lhs=lhs, jmin=jmin, g=g, h=h, c=c: e.matmul(sbk.t[:, 0:ncol], lhsT=lhs, rhs=QTc[c].t[:, h, (4 * g + jmin) * 128:(4 * g + 4) * 128], start=True, stop=True),
                     reads=[KT[kq].b, QTc[c].b], writes=[sbk.b])
                for j, b in band.items():
                    col = (j - jmin) * 128
                    S.op("dve", lambda e, sbk=sbk, col=col, b=b, h=h: e.tensor_tensor(out=sbk.t[:, col:col + 128], in0=sbk.t[:, col:col + 128], in1=self.bias.t[:, h, b + 1, :], op=ALU.add),
                         reads=[sbk.b, self.bias.b], writes=[sbk.b])
                nact = (4 - jmin) * 128
                S.op("act", lambda e, sbk=sbk, Et=Et, nact=nact: e.activation(out=Et.t[:, 0:nact], in_=sbk.t[:, 0:nact], func=AF.Exp), reads=[sbk.b], writes=[Et.b])

            def emit_av(si):
                h, g, c, kt, nk = steps[si]
                if g == 3 and c == 0 and kt == 0:
                    for q in range(4):
                        S.dma("act", lambda e, q=q, h=h: e.dma_start(out=V1[q].t[:, :, 0:128], in_=self.V_d[h, :, q * 32:(q + 1) * 32, :]), V1[q].b, reads=[self.B_V], writes=[V1[q].b])
                jmin = jmin_of(g, kt)
                Et = E[si % 3]
                kq, kl = kt // 32, kt % 32
                for j in range(jmin, 4):
                    ob, oo, obuf = oacc[j]
                    col = (j - jmin) * 128
                    last = (kt == 32 * g + 8 * j + 7)
                    S.op("pe", lambda e, ob=ob, oo=oo, Et=Et, col=col, kq=kq, kl=kl, kt=kt, last=last: e.matmul(ob.t[:, oo:oo + 130], lhsT=Et.t[:, col:col + 128], rhs=V1[kq].t[:, kl, :], start=(kt == 0), stop=last),
                         reads=[Et.b, V1[kq].b], writes=[obuf])
                if kt != nk - 1:
                    return
                for j in range(4):
                    ob, oo, obuf = oacc[j]
                    m = 4 * g + j
                    if c == 0:
                        S.op("dve", lambda e, ob=ob, oo=oo, j=j: e.reciprocal(out=sm.t[:, j:j + 1], in_=ob.t[:, oo + 128:oo + 129]), reads=[obuf], writes=[sm.b])
                        S.op("dve", lambda e, ob=ob, oo=oo, j=j: e.tensor_scalar(out=tmp0[j].t[:], in0=ob.t[:, oo:oo + 128], scalar1=sm.t[:, j:j + 1], scalar2=None, op0=ALU.mult), reads=[obuf, sm.b], writes=[tmp0[j].b])
                    else:
                        S.op("dve", lambda e, ob=ob, oo=oo, j=j: e.reciprocal(out=sm.t[:, 4 + j:5 + j], in_=ob.t[:, oo + 128:oo + 129]), reads=[obuf], writes=[sm.b])
                        S.op("dve", lambda e, j=j: e.tensor_scalar(out=sm.t[:, 4 + j:5 + j], in0=sm.t[:, 4 + j:5 + j], scalar1=self.lam.t[:, 1:2], scalar2=None, op0=ALU.mult), reads=[sm.b, self.lam.b], writes=[sm.b])
                        S.op("dve", lambda e, ob=ob, oo=oo, j=j: e.scalar_tensor_tensor(out=oc.t[:], in0=ob.t[:, oo:oo + 128], scalar=sm.t[:, 4 + j:5 + j], in1=tmp0[j].t[:], op0=ALU.mult, op1=ALU.add),
                             reads=[obuf, sm.b, tmp0[j].b], writes=[oc.b])
                        S.op("dve", lambda e: e.scalar_tensor_tensor(out=sqj.t[:], in0=oc.t[:], scalar=1.0, in1=oc.t[:], op0=ALU.mult, op1=ALU.mult, accum_out=sm.t[:, 0:1]), reads=[oc.b], writes=[sqj.b, sm.b])
                        S.op("dve", lambda e: e.tensor_scalar(out=sm.t[:, 0:1], in0=sm.t[:, 0:1], scalar1=1.0 / 128, scalar2=1e-5, op0=ALU.mult, op1=ALU.add), reads=[sm.b], writes=[sm.b])
                        S.op("pool", lambda e: e.tensor_tensor(out=sm.t[:, 0:1], in0=sm.t[:, 0:1], in1=self.mhalf.t[:, 0:1], op=ALU.pow), reads=[sm.b, self.mhalf.b], writes=[sm.b])
                        S.op("dve", lambda e: e.scalar_tensor_tensor(out=on.t[:], in0=oc.t[:], scalar=sm.t[:, 0:1], in1=self.subg.t[:], op0=ALU.mult, op1=ALU.mult),
                             reads=[oc.b, sm.b, self.subg.b], writes=[on.b])
                        pt = tbank.t.bitcast(BF16)
                        S.op("pe", lambda e, pt=pt: e.transpose(out=pt[:, 0:128], in_=on.t[:], identity=self.ident_bf.t[:]), reads=[on.b, self.ident_bf.b], writes=[tbank.b])
                        S.op("act", lambda e, pt=pt, h=h, m=m: e.copy(out=self.outT.t[:, h, m * 128:(m + 1) * 128], in_=pt[:, 0:128]), reads=[tbank.b], writes=[self.outT.b])

            ns = len(steps)
            for idx in range(ns + LA):
                if idx < ns:
                    emit_qk(idx)
                if idx - LA >= 0:
                    emit_av(idx - LA)

    def dump_attn(self):
        S = self.S
        with contextlib.ExitStack() as st:
            f = self.sb(st, "dumpf", [128, 8, NT_OWN * 128], F32)
            S.op("dve", lambda e: e.tensor_copy(out=f.t[:], in_=self.outT.t[:]), reads=[self.outT.b], writes=[f.b])
            S.dma("sp", lambda e: e.dma_start(out=self.dbg["attn"], in_=f.t[:].rearrange("p h t -> p (h t)")), f.b, reads=[f.b], is_output=True)


def make_in_maps(inp):
    f32 = np.float32
    x = np.ascontiguousarray(inp["x"][0], dtype=f32)
    xt = x.reshape(16, 8, 128, D)
    common = {
        "x_all": x,
        "mem": np.ascontiguousarray(inp["mem"][0], dtype=f32),
        "w_in": np.ascontiguousarray(inp["w_in"][0], dtype=f32),
        "gcols": np.ascontiguousarray(np.stack([inp["norm_mix_g"][0], inp["norm_cross_g"][0], inp["norm_mem_g"][0], inp["norm_ffn_g"][0]], 0).reshape(4, 8, 128).transpose(2, 0, 1), dtype=f32),
        "final_g_rep": np.ascontiguousarray(np.broadcast_to(inp["final_g"][None, :], (128, D)), dtype=f32),
        "conv_wT": np.ascontiguousarray(inp["conv_w"][0].reshape(3, 8, 128).transpose(2, 1, 0), dtype=f32),
        "w_conv_out": np.ascontiguousarray(inp["w_conv_out"][0], dtype=f32),
        "lam_rep": np.ascontiguousarray(np.broadcast_to(np.stack([inp["lambda_q1"][0], inp["lambda_k1"][0], inp["lambda_q2"][0], inp["lambda_k2"][0]], 0)[None], (128, 4, 64)), dtype=f32),
        "subln_rep": np.ascontiguousarray(np.broadcast_to(inp["subln_g"][0][None, :], (128, 128)), dtype=f32),
        "w_attn_out": np.ascontiguousarray(inp["w_attn_out"][0], dtype=f32),
        "w_mix_out": np.ascontiguousarray(inp["w_mix_out"][0], dtype=f32),
        "rb31_rep": np.ascontiguousarray(np.broadcast_to(inp["rel_bias"][31][None, :], (128, 8)), dtype=f32),
        "w_cq": np.ascontiguousarray(inp["w_cq"][0], dtype=f32),
        "w_ckv": np.ascontiguousarray(inp["w_ckv"][0], dtype=f32),
        "w_co": np.ascontiguousarray(inp["w_co"][0], dtype=f32),
        "w_pq": np.ascontiguousarray(inp["w_pq"][0], dtype=f32),
        "skT": np.ascontiguousarray(inp["sub_keys"][0].transpose(1, 0, 3, 2).reshape(16, 128, 128), dtype=f32),
        "peer_u": np.ascontiguousarray(inp["peer_u"][0], dtype=f32),
        "peer_v": np.ascontiguousarray(inp["peer_v"][0], dtype=f32),
        "ident_bf": np.eye(128, dtype=f32).astype(ml_dtypes.bfloat16),
        "ident_f": np.eye(128, dtype=f32),
        "iota_f": np.ascontiguousarray(np.broadcast_to(np.arange(128, dtype=f32)[None, :], (128, 128))),
    }
    rel_bias = np.asarray(inp["rel_bias"], dtype=f32)
    maps = []
    for c in range(NCORE):
        m = dict(common)
        m["x_own"] = np.ascontiguousarray(xt[:, c].reshape(NT_OWN * 128, D))
        halo = np.zeros((16, 2, D), f32)
        for mm in range(16):
            t0 = (8 * mm + c) * 128
            if t0 >= 2:
                halo[mm] = x[t0 - 2:t0]
        m["x_halo"] = halo.reshape(32, D)
        bk, mk = _band_tables(c)
        gb = rel_bias[bk]
        m["gbias"] = np.ascontiguousarray(gb.transpose(3, 1, 0, 2), dtype=f32)
        m["mband"] = np.ascontiguousarray(mk.transpose(1, 0, 2), dtype=f32)
        maps.append(m)
    return maps


_CACHE = {}


def kernel(**inputs):
    stop = inputs.pop("_stop_after", None)
    upto = inputs.pop("_upto", None)
    key = (stop, upto)
    if key not in _CACHE:
        b = Builder(stop_after=stop, upto=upto)
        _CACHE[key] = (b.build(), b.in_shapes)
    nc, in_shapes = _CACHE[key]
    maps = make_in_maps(inputs)
    if upto and upto.startswith("peeronly"):
        m = maps[3]
        for nm, (shp, dt) in in_shapes.items():
            if tuple(m[nm].shape) != tuple(shp):
                m[nm] = np.zeros(shp, np.float32)
        res = run_bass_kernel_spmd(nc, [m], core_ids=[0])
        _CACHE["last_res"] = res
        return res
    res = run_bass_kernel_spmd(nc, maps, core_ids=list(range(NCORE)))
    if stop:
        _CACHE["last_res"] = res
    name = "out"
    out = np.zeros((16, 8, 128, D), np.float32)
    for c in range(NCORE):
        out[:, c] = np.asarray(res.results[c][name], dtype=np.float32).reshape(16, 128, D)
    return out.reshape(1, SEQ, D)


def _bcast_ap(t, offset, dims):
    base = t[:]
    return bass.AP(t, offset, [list(base.ap[0])] + [list(d) for d in dims])


def _norm_res(self, tiles, sqj, ss, rstd, eps=1e-6):
    S = self.S
    n = len(tiles)
    for i, (src, sbuf, xns) in enumerate(tiles):
        S.op("act", lambda e, src=src, i=i: e.activation(out=sqj.t[:], in_=src, func=AF.Square, accum_out=ss.t[:, i:i + 1]), reads=[sbuf], writes=[sqj.b, ss.b])
    S.op("dve", lambda e: e.tensor_scalar(out=rstd.t[:, 0:n], in0=ss.t[:, 0:n], scalar1=1.0 / D, scalar2=eps, op0=ALU.mult, op1=ALU.add), reads=[ss.b], writes=[rstd.b])
    S.op("pool", lambda e: e.tensor_tensor(out=rstd.t[:, 0:n], in0=rstd.t[:, 0:n], in1=self.mhalf.t[:, 0:n], op=ALU.pow), reads=[rstd.b, self.mhalf.b], writes=[rstd.b])
    for i, (src, sbuf, xns) in enumerate(tiles):
        S.op("dve", lambda e, src=src, xns=xns, i=i: e.tensor_scalar(out=xns.t[:], in0=src, scalar1=rstd.t[:, i:i + 1], scalar2=None, op0=ALU.mult), reads=[sbuf, rstd.b], writes=[xns.b])


def _wslab(self, dst, w_ap, col0, ncols, queue="pool"):
    self.S.dma(queue, lambda e: e.dma_start(out=dst.t[:, :, 0:ncols], in_=w_ap[:, col0:col0 + ncols].rearrange("(kc p) c -> p kc c", p=128)), dst.b, writes=[dst.b])


def _phase_mix(self, p0):
    S, I = self.S, self.I
    K = 1024
    with contextlib.ExitStack() as st:
        self.a_ptr = p0
        sb = lambda name, shape, dt, at=None: self.sb(st, name, shape, dt, at=at)
        mergedT = sb("mergedT", [128, 8, 2048], BF16)
        hT = sb("hT", [128, 8, 2048], BF16)
        hTh = sb("hTh", [128, 8, 32], BF16)
        zT = sb("zT", [128, 8, 2048], BF16)
        with contextlib.ExitStack() as st2:
            sb2 = lambda name, shape, dt: self.sb(st2, name, shape, dt)
            xt = [sb2(f"xt{i}", [128, 1024], F32) for i in range(4)]
            xn = [sb2(f"xn{i}", [128, 1024], BF16) for i in range(4)]
            sqj = sb2("sqj", [128, 1024], BF16)
            ss = sb2("ss", [128, 4], F32)
            rstd = sb2("rstd", [128, 4], F32)
            x_own = I["x_own"]
            for bi in range(4):
                tiles = [(bi * 4 + i, xt[i], xn[i]) for i in range(4)]
                self.norm_tiles(lambda tid: x_own[tid * 128:(tid + 1) * 128, :], tiles, xt, sqj, ss, rstd, xn)
                self.transpose_tiles(xn, hT, self.bank[0:2], bi * 4, gsel=0, col0=bi * 512)
            hx, hn = xt[0], xn[0]
            S.dma("sp", lambda e: e.dma_start(out=hx.t[0:32, :], in_=I["x_halo"]), hx.b, writes=[hx.b])
            S.op("act", lambda e: e.activation(out=sqj.t[0:32, :], in_=hx.t[0:32, :], func=AF.Square, accum_out=ss.t[0:32, 0:1]), reads=[hx.b], writes=[sqj.b, ss.b])
            S.op("dve", lambda e: e.tensor_scalar(out=rstd.t[0:32, 0:1], in0=ss.t[0:32, 0:1], scalar1=1.0 / D, scalar2=1e-6, op0=ALU.mult, op1=ALU.add), reads=[ss.b], writes=[rstd.b])
            S.op("pool", lambda e: e.tensor_tensor(out=rstd.t[0:32, 0:1], in0=rstd.t[0:32, 0:1], in1=self.mhalf.t[0:32, 0:1], op=ALU.pow), reads=[rstd.b, self.mhalf.b], writes=[rstd.b])
            S.op("dve", lambda e: e.tensor_scalar(out=hn.t[0:32, :], in0=hx.t[0:32, :], scalar1=rstd.t[0:32, 0:1], scalar2=None, op0=ALU.mult), reads=[hx.b, rstd.b], writes=[hn.b])
            self.transpose_tiles([hn], hTh, self.bank[0:2], 0, gsel=0, col0=0, npart=32)
        S.barrier()
        with contextlib.ExitStack() as st2:
            sb2 = lambda name, shape, dt: self.sb(st2, name, shape, dt)
            wc3 = [[sb2(f"wc3_{i}_{k}", [128, 8, 128], BF16) for k in range(3)] for i in range(2)]
            cwT = sb2("cwT", [128, 8, 3], F32)
            S.dma("sp", lambda e: e.dma_start(out=cwT.t[:], in_=I["conv_wT"]), cwT.b, writes=[cwT.b])
            U2 = sb2("U2", [128, 16, 130], F32)
            ycv = sb2("ycv", [128, 16, 128], F32)
            cbs = sb2("cbs", [128, 2048], F32)
            ccs = [sb2(f"ccs{i}", [128, 512], F32) for i in range(2)]
            cch_ = sb2("cch", [128, 32], F32)
            cnt = 0
            for cch in range(8):
                w3 = wc3[cch % 2]
                for k in range(3):
                    _wslab(self, w3[k], I["w_in"], k * 1024 + cch * 128, 128)
                for tb in range(4):
                    pb = [self.bank[(cnt + k) % 8] for k in range(3)]
                    cnt += 3
                    for k in range(3):
                        for kc in range(8):
                            S.op("pe", lambda e, k=k, kc=kc, tb=tb, w3=w3, pb=pb: e.matmul(pb[k].t[:], lhsT=w3[k].t[:, kc, :], rhs=hT.t[:, kc, tb * 512:(tb + 1) * 512], start=(kc == 0), stop=(kc == 7)),
                                 reads=[w3[k].b, hT.b], writes=[pb[k].b])
                    S.op("act", lambda e, tb=tb, pb=pb: e.copy(out=cbs.t[:, tb * 512:(tb + 1) * 512], in_=pb[0].t[:]), reads=[pb[0].b], writes=[cbs.b])
                    cs = ccs[tb % 2]
                    S.op("act", lambda e, cs=cs, pb=pb: e.copy(out=cs.t[:], in_=pb[1].t[:]), reads=[pb[1].b], writes=[cs.b])
                    S.op("dve", lambda e, cs=cs, pb=pb, tb=tb: e.tensor_tensor(out=U2.t[:, tb * 4:(tb + 1) * 4, 2:130], in0=pb[2].t[:].rearrange("p (m t) -> p m t", m=4), in1=cs.t[:].rearrange("p (m t) -> p m t", m=4), op=ALU.mult),
                         reads=[pb[2].b, cs.b], writes=[U2.b])
                pb = [self.bank[(cnt + k) % 8] for k in range(2)]
                cnt += 2
                for k in range(2):
                    for kc in range(8):
                        S.op("pe", lambda e, k=k, kc=kc, w3=w3, pb=pb: e.matmul(pb[k].t[:, 0:32], lhsT=w3[k + 1].t[:, kc, :], rhs=hTh.t[:, kc, :], start=(kc == 0), stop=(kc == 7)),
                             reads=[w3[k + 1].b, hTh.b], writes=[pb[k].b])
                S.op("act", lambda e, pb=pb: e.copy(out=cch_.t[:], in_=pb[0].t[:, 0:32]), reads=[pb[0].b], writes=[cch_.b])
                S.op("dve", lambda e, pb=pb: e.tensor_tensor(out=U2.t[:, :, 0:2], in0=pb[1].t[:, 0:32].rearrange("p (m t) -> p m t", m=16), in1=cch_.t[:].rearrange("p (m t) -> p m t", m=16), op=ALU.mult),
                     reads=[pb[1].b, cch_.b], writes=[U2.b])
                S.op("dve", lambda e, cch=cch: e.tensor_scalar(out=ycv.t[:], in0=U2.t[:, :, 2:130], scalar1=cwT.t[:, cch, 2:3], scalar2=None, op0=ALU.mult), reads=[U2.b, cwT.b], writes=[ycv.b])
                S.op("dve", lambda e, cch=cch: e.scalar_tensor_tensor(out=ycv.t[:], in0=U2.t[:, :, 1:129], scalar=cwT.t[:, cch, 1:2], in1=ycv.t[:], op0=ALU.mult, op1=ALU.add), reads=[U2.b, cwT.b, ycv.b], writes=[ycv.b])
                S.op("dve", lambda e, cch=cch: e.scalar_tensor_tensor(out=ycv.t[:], in0=U2.t[:, :, 0:128], scalar=cwT.t[:, cch, 0:1], in1=ycv.t[:], op0=ALU.mult, op1=ALU.add), reads=[U2.b, cwT.b, ycv.b], writes=[ycv.b])
                S.op("pool", lambda e, cch=cch: e.tensor_tensor(out=zT.t[:, cch, :], in0=cbs.t[:], in1=ycv.t[:].rearrange("p m t -> p (m t)"), op=ALU.mult), reads=[cbs.b, ycv.b], writes=[zT.b])
        S.barrier()
        with contextlib.ExitStack() as st2:
            sb2 = lambda name, shape, dt: self.sb(st2, name, shape, dt)
            wsl = [[sb2(f"wsl{i}_{k}", [128, 8, 128], BF16) for k in range(4)] for i in range(2)]
            sg = [[sb2(f"sg{i}_{k}", [128, 512], F32) for k in range(2)] for i in range(2)]
            m1 = [sb2(f"m1_{i}", [128, 512], F32) for i in range(2)]
            m2 = [sb2(f"m2_{i}", [128, 512], F32) for i in range(2)]
            it = 0
            for dt_ in range(8):
                ws = wsl[dt_ % 2]
                _wslab(self, ws[0], I["w_conv_out"], dt_ * 128, 128)
                _wslab(self, ws[1], I["w_in"], 6144 + dt_ * 128, 128)
                _wslab(self, ws[2], I["w_attn_out"], dt_ * 128, 128)
                _wslab(self, ws[3], I["w_in"], 7168 + dt_ * 128, 128)
                for tb in range(4):
                    pb = [self.bank[(it % 2) * 4 + k] for k in range(4)]
                    rhs_src = [zT, hT, self.outT, hT]
                    for k in range(4):
                        for kc in range(8):
                            S.op("pe", lambda e, k=k, kc=kc, tb=tb, ws=ws, pb=pb, rhs_src=rhs_src: e.matmul(pb[k].t[:], lhsT=ws[k].t[:, kc, :], rhs=rhs_src[k].t[:, kc, tb * 512:(tb + 1) * 512], start=(kc == 0), stop=(kc == 7)),
                                 reads=[ws[k].b, rhs_src[k].b], writes=[pb[k].b])
                    s0, s1 = sg[it % 2]
                    a1, a2 = m1[it % 2], m2[it % 2]
                    S.op("act", lambda e, s0=s0, pb=pb: e.activation(out=s0.t[:], in_=pb[1].t[:], func=AF.Sigmoid), reads=[pb[1].b], writes=[s0.b])
                    S.op("act", lambda e, s1=s1, pb=pb: e.activation(out=s1.t[:], in_=pb[3].t[:], func=AF.Sigmoid), reads=[pb[3].b], writes=[s1.b])
                    S.op("dve", lambda e, s0=s0, a1=a1, pb=pb: e.tensor_tensor(out=a1.t[:], in0=pb[0].t[:], in1=s0.t[:], op=ALU.mult), reads=[pb[0].b, s0.b], writes=[a1.b])
                    S.op("dve", lambda e, s1=s1, a2=a2, pb=pb: e.tensor_tensor(out=a2.t[:], in0=pb[2].t[:], in1=s1.t[:], op=ALU.mult), reads=[pb[2].b, s1.b], writes=[a2.b])
                    S.op("pool", lambda e, a1=a1, a2=a2, dt_=dt_, tb=tb: e.tensor_tensor(out=mergedT.t[:, dt_, tb * 512:(tb + 1) * 512], in0=a1.t[:], in1=a2.t[:], op=ALU.add), reads=[a1.b, a2.b], writes=[mergedT.b])
                    it += 1
        S.barrier()
        with contextlib.ExitStack() as st2:
            self.a_ptr = p0 + 97 * 1024
            wmix = self.sb(st2, "wmix", [128, 8, 1024], BF16)
            _wslab(self, wmix, I["w_mix_out"], 0, 1024)
            xres = self.xres
            S.dma("sp", lambda e: e.dma_start(out=xres.t[:], in_=I["x_own"].rearrange("(m p) d -> p m d", p=128)), xres.b, writes=[xres.b])
            it = 0
            for m in range(16):
                for dh in range(2):
                    bk = self.bank[it % 8]
                    it += 1
                    for kc in range(8):
                        S.op("pe", lambda e, bk=bk, kc=kc, m=m, dh=dh: e.matmul(bk.t[:], lhsT=mergedT.t[:, kc, m * 128:(m + 1) * 128], rhs=wmix.t[:, kc, dh * 512:(dh + 1) * 512], start=(kc == 0), stop=(kc == 7)),
                             reads=[mergedT.b, wmix.b], writes=[bk.b])
                    S.op("dve", lambda e, bk=bk, m=m, dh=dh: e.tensor_tensor(out=xres.t[:, m, dh * 512:(dh + 1) * 512], in0=bk.t[:], in1=xres.t[:, m, dh * 512:(dh + 1) * 512], op=ALU.add),
                         reads=[bk.b, xres.b], writes=[xres.b])


def _dump_res(self, name):
    S = self.S
    S.dma("sp", lambda e: e.dma_start(out=self.dbg[name].rearrange("(m p) d -> p m d", p=128), in_=self.xres.t[:]), self.xres.b, reads=[self.xres.b], is_output=True)


def _phase_cross(self, p0):
    S, I = self.S, self.I
    pA = self.pA
    xres = self.xres
    regB = p0 + 96 * 1024
    with contextlib.ExitStack() as st:
        self.a_ptr = pA
        sb = lambda name, shape, dt: self.sb(st, name, shape, dt)
        hcT = sb("hcT", [128, 8, 2048], BF16)
        wcq = sb("wcq", [128, 8, 1024], BF16)
        wco = sb("wco", [128, 8, 1024], BF16)
        kT = sb("kT", [128, 8, 256], BF16)
        vC = sb("vC", [128, 2, 1024], BF16)
        ones = sb("ones", [128, 128], BF16)
        qcT = sb("qcT", [128, 8, 512], BF16)
        assert self.a_ptr <= p0 + 32 * 1024, (self.a_ptr, p0)
        S.op("pool", lambda e: e.memset(ones.t[:], 1.0), writes=[ones.b])
        _wslab(self, wcq, I["w_cq"], 0, 1024)
        _wslab(self, wco, I["w_co"], 0, 1024)
        with contextlib.ExitStack() as st2:
            self.a_ptr = regB
            sb2 = lambda name, shape, dt: self.sb(st2, name, shape, dt)
            wckv = sb2("wckv", [128, 8, 2048], BF16)
            mT = sb2("mT", [128, 8, 256], BF16)
            xt = [sb2(f"xt{i}", [128, 1024], F32) for i in range(2)]
            xn = [sb2(f"xn{i}", [128, 1024], BF16) for i in range(2)]
            sqj = sb2("sqj", [128, 1024], BF16)
            ss = sb2("ss", [128, 4], F32)
            rstd = sb2("rstd", [128, 4], F32)
            _wslab(self, wckv, I["w_ckv"], 0, 2048)
            tiles = [(i, xt[i], xn[i]) for i in range(2)]
            self.norm_tiles(lambda tid: I["mem"][tid * 128:(tid + 1) * 128, :], tiles, xt, sqj, ss, rstd, xn)
            self.transpose_tiles(xn, mT, self.bank[0:2], 0, gsel=2, col0=0)
            for ct in range(8):
                bk = self.bank[2 + ct % 6]
                for kc in range(8):
                    S.op("pe", lambda e, bk=bk, kc=kc, ct=ct: e.matmul(bk.t[:, 0:256], lhsT=wckv.t[:, kc, ct * 128:(ct + 1) * 128], rhs=mT.t[:, kc, :], start=(kc == 0), stop=(kc == 7)),
                         reads=[wckv.b, mT.b], writes=[bk.b])
                S.op("act", lambda e, bk=bk, ct=ct: e.copy(out=kT.t[:, ct, :], in_=bk.t[:, 0:256]), reads=[bk.b], writes=[kT.b])
            for mt in range(2):
                for hh in range(2):
                    bk = self.bank[2 + (mt * 2 + hh) % 6]
                    for kc in range(8):
                        S.op("pe", lambda e, bk=bk, kc=kc, mt=mt, hh=hh: e.matmul(bk.t[:], lhsT=mT.t[:, kc, mt * 128:(mt + 1) * 128], rhs=wckv.t[:, kc, 1024 + hh * 512:1024 + (hh + 1) * 512], start=(kc == 0), stop=(kc == 7)),
                             reads=[wckv.b, mT.b], writes=[bk.b])
                    S.op("dve", lambda e, bk=bk, mt=mt, hh=hh: e.tensor_copy(out=vC.t[:, mt, hh * 512:(hh + 1) * 512], in_=bk.t[:]), reads=[bk.b], writes=[vC.b])
        S.barrier()
        with contextlib.ExitStack() as st2:
            self.a_ptr = regB
            sb2 = lambda name, shape, dt: self.sb(st2, name, shape, dt)
            xn = [sb2(f"xn{i}", [128, 1024], BF16) for i in range(4)]
            sqj = sb2("sqj", [128, 1024], BF16)
            ss = sb2("ss", [128, 4], F32)
            rstd = sb2("rstd", [128, 4], F32)
            P = [sb2(f"P{i}", [128, 2, 512], BF16) for i in range(2)]
            oT = sb2("oT", [128, 8, 512], BF16)
            R = [sb2(f"R{i}", [128, 512], F32) for i in range(2)]
            for bi in range(4):
                tiles = [(xres.t[:, bi * 4 + i, :], xres.b, xn[i]) for i in range(4)]
                _norm_res(self, tiles, sqj, ss, rstd)
                self.transpose_tiles(xn, hcT, self.bank[0:2], bi * 4, gsel=1, col0=bi * 512)
            it = 0
            for tb in range(4):
                for ct in range(8):
                    bk = self.bank[it % 8]
                    it += 1
                    for kc in range(8):
                        S.op("pe", lambda e, bk=bk, kc=kc, ct=ct, tb=tb: e.matmul(bk.t[:], lhsT=wcq.t[:, kc, ct * 128:(ct + 1) * 128], rhs=hcT.t[:, kc, tb * 512:(tb + 1) * 512], start=(kc == 0), stop=(kc == 7)),
                             reads=[wcq.b, hcT.b], writes=[bk.b])
                    if ct % 2 == 0:
                        S.op("act", lambda e, bk=bk, ct=ct: e.copy(out=qcT.t[:, ct, :], in_=bk.t[:]), reads=[bk.b], writes=[qcT.b])
                    else:
                        S.op("dve", lambda e, bk=bk, ct=ct: e.tensor_copy(out=qcT.t[:, ct, :], in_=bk.t[:]), reads=[bk.b], writes=[qcT.b])
                for hd in range(4):
                    Pt = P[hd % 2]
                    Rt = R[hd % 2]
                    for mt in range(2):
                        bk = self.bank[it % 8]
                        it += 1
                        for half in range(2):
                            S.op("pe", lambda e, bk=bk, hd=hd, half=half, mt=mt: e.matmul(bk.t[:], lhsT=kT.t[:, hd * 2 + half, mt * 128:(mt + 1) * 128], rhs=qcT.t[:, hd * 2 + half, :], start=(half == 0), stop=(half == 1)),
                                 reads=[kT.b, qcT.b], writes=[bk.b])
                        S.op("act", lambda e, bk=bk, Pt=Pt, mt=mt: e.activation(out=Pt.t[:, mt, :], in_=bk.t[:], func=AF.Exp, scale=1.0 / 16), reads=[bk.b], writes=[Pt.b])
                    bs = self.bank[it % 8]
                    it += 1
                    for mt in range(2):
                        S.op("pe", lambda e, bs=bs, Pt=Pt, mt=mt: e.matmul(bs.t[:], lhsT=ones.t[:], rhs=Pt.t[:, mt, :], start=(mt == 0), stop=(mt == 1)), reads=[ones.b, Pt.b], writes=[bs.b])
                    S.op("dve", lambda e, bs=bs, Rt=Rt: e.reciprocal(out=Rt.t[:], in_=bs.t[:]), reads=[bs.b], writes=[Rt.b])
                    for half in range(2):
                        bo = self.bank[it % 8]
                        it += 1
                        for mt in range(2):
                            S.op("pe", lambda e, bo=bo, Pt=Pt, mt=mt, hd=hd, half=half: e.matmul(bo.t[:], lhsT=vC.t[:, mt, hd * 256 + half * 128:hd * 256 + (half + 1) * 128], rhs=Pt.t[:, mt, :], start=(mt == 0), stop=(mt == 1)),
                                 reads=[vC.b, Pt.b], writes=[bo.b])
                        S.op("dve", lambda e, bo=bo, Rt=Rt, hd=hd, half=half: e.tensor_tensor(out=oT.t[:, hd * 2 + half, :], in0=bo.t[:], in1=Rt.t[:], op=ALU.mult), reads=[bo.b, Rt.b], writes=[oT.b])
                for i in range(4):
                    m = tb * 4 + i
                    for dh in range(2):
                        bk = self.bank[it % 8]
                        it += 1
                        for ct in range(8):
                            S.op("pe", lambda e, bk=bk, ct=ct, i=i, dh=dh: e.matmul(bk.t[:], lhsT=oT.t[:, ct, i * 128:(i + 1) * 128], rhs=wco.t[:, ct, dh * 512:(dh + 1) * 512], start=(ct == 0), stop=(ct == 7)),
                                 reads=[oT.b, wco.b], writes=[bk.b])
                        S.op("dve", lambda e, bk=bk, m=m, dh=dh: e.tensor_tensor(out=xres.t[:, m, dh * 512:(dh + 1) * 512], in0=bk.t[:], in1=xres.t[:, m, dh * 512:(dh + 1) * 512], op=ALU.add),
                             reads=[bk.b, xres.b], writes=[xres.b])


Builder.phase_mix = _phase_mix
Builder.phase_cross = _phase_cross
Builder.dump_res = _dump_res


def _phase_peer(self, p0):
    S, I = self.S, self.I
    xres = self.xres
    pA = self.pA
    X2_d = self.dscr("X2_d", [NT_OWN * 128, D], F32)
    B_X2 = Buf("X2_d")
    MAGIC = 12582912.0
    with contextlib.ExitStack() as st:
        self.a_ptr = p0 + 96 * 1024
        sb = lambda name, shape, dt: self.sb(st, name, shape, dt)
        with contextlib.ExitStack() as st2:
            sb2 = lambda name, shape, dt: self.sb(st2, name, shape, dt)
            xn = [sb2(f"xn{i}", [128, 1024], BF16) for i in range(4)]
            sqj = sb2("sqj", [128, 1024], BF16)
            ss = sb2("ss", [128, 4], F32)
            rstd = sb2("rstd", [128, 4], F32)
            self.a_ptr = pA
            hfT = self.sb(st, "hfT", [128, 8, 2048], BF16)
            for bi in range(4):
                tiles = [(xres.t[:, bi * 4 + i, :], xres.b, xn[i]) for i in range(4)]
                _norm_res(self, tiles, sqj, ss, rstd)
                self.transpose_tiles(xn, hfT, self.bank[0:2], bi * 4, gsel=3, col0=bi * 512)
            S.dma("sp", lambda e: e.dma_start(out=X2_d.rearrange("(m p) d -> p m d", p=128), in_=xres.t[:]), xres.b, reads=[xres.b], writes=[B_X2])
        S.barrier()
        self.a_ptr = pA + 32 * 1024
        RT = self.sb(st, "RT", [128, 3, 2048], F32)
        pR = self.a_ptr
        with contextlib.ExitStack() as st2:
            sb2 = lambda name, shape, dt: self.sb(st2, name, shape, dt)
            ub = [sb2(f"ub{i}", [128, 4, 1024], BF16) for i in range(2)]
            vb = [sb2(f"vb{i}", [128, 4, 1024], BF16) for i in range(2)]
            uts = [sb2(f"uts{i}", [128, 8, 128], BF16) for i in range(3)]
            for gq in range(32):
                u, v = ub[gq % 2], vb[gq % 2]
                S.dma("pool", lambda e, u=u, gq=gq: e.dma_start(out=u.t[:], in_=I["peer_u"][gq * 512:(gq + 1) * 512, :].rearrange("(i p) d -> p i d", p=128)), u.b, writes=[u.b])
                S.dma("pool", lambda e, v=v, gq=gq: e.dma_start(out=v.t[:], in_=I["peer_v"][gq * 512:(gq + 1) * 512, :].rearrange("(i p) d -> p i d", p=128)), v.b, writes=[v.b])
                S.dma("sp", lambda e, v=v, gq=gq: e.dma_start(out=self.Vb_d[gq * 512:(gq + 1) * 512, :].rearrange("(i p) d -> p i d", p=128), in_=v.t[:]), v.b, reads=[v.b], writes=[self.B_Vb])
                for i in range(4):
                    et = gq * 4 + i
                    bk = self.bank[et % 4]
                    pt = bk.t.bitcast(BF16)
                    ut = uts[et % 3]
                    for kc in range(8):
                        S.op("pe", lambda e, pt=pt, u=u, i=i, kc=kc: e.transpose(out=pt[:, kc * 128:(kc + 1) * 128], in_=u.t[:, i, kc * 128:(kc + 1) * 128], identity=self.ident_bf.t[:]),
                             reads=[u.b, self.ident_bf.b], writes=[bk.b])
                    if et % 2 == 0:
                        S.op("act", lambda e, pt=pt, ut=ut: e.copy(out=ut.t[:], in_=pt[:, :].rearrange("p (k t) -> p k t", k=8)), reads=[bk.b], writes=[ut.b])
                    else:
                        S.op("dve", lambda e, pt=pt, ut=ut: e.tensor_copy(out=ut.t[:], in_=pt[:, :].rearrange("p (k t) -> p k t", k=8)), reads=[bk.b], writes=[ut.b])
                    S.dma("act", lambda e, ut=ut, et=et: e.dma_start(out=self.UT_d[et], in_=ut.t[:]), ut.b, reads=[ut.b], writes=[self.B_UT])
        S.barrier()
        if self.peer_stop == "p0":
            with contextlib.ExitStack() as st2:
                tu = self.sb(st2, "tu", [128, 1024], BF16)
                tv = self.sb(st2, "tv", [128, 1024], BF16)
                tf = self.sb(st2, "tf", [128, 2, 1024], F32)
                S.dma("sp", lambda e: e.dma_start(out=tu.t[:], in_=self.UT_d[77].rearrange("p k e -> p (k e)")), tu.b, reads=[self.B_UT], writes=[tu.b])
                S.dma("sp", lambda e: e.dma_start(out=tv.t[:], in_=self.Vb_d[77 * 128:78 * 128, :]), tv.b, reads=[self.B_Vb], writes=[tv.b])
                S.op("dve", lambda e: e.tensor_copy(out=tf.t[:, 0, :], in_=tu.t[:]), reads=[tu.b], writes=[tf.b])
                S.op("dve", lambda e: e.tensor_copy(out=tf.t[:, 1, :], in_=tv.t[:]), reads=[tv.b], writes=[tf.b])
                S.dma("sp", lambda e: e.dma_start(out=self.out[0:128, :], in_=tf.t[:, 0, :]), tf.b, reads=[tf.b], is_output=True)
                S.dma("sp", lambda e: e.dma_start(out=self.out[128:256, :], in_=tf.t[:, 1, :]), tf.b, reads=[tf.b], is_output=True)
                S.dma("sp", lambda e: e.dma_start(out=self.out[256:384, :], in_=xres.t[:, 5, :]), xres.b, reads=[xres.b], is_output=True)
            return
        with contextlib.ExitStack() as st2:
            self.a_ptr = pR
            sb2 = lambda name, shape, dt: self.sb(st2, name, shape, dt)
            wpq = sb2("wpq", [128, 8, 2048], BF16)
            skT = sb2("skT", [128, 16, 128], BF16)
            qT = [sb2(f"qT{i}", [128, 16, 128], BF16) for i in range(2)]
            sc = sb2("sc", [128, 16, 128], F32)
            scr = sb2("scr", [128, 16, 128], F32)
            vals = sb2("vals", [128, 16, 16], F32)
            idx = sb2("idx", [128, 16, 16], U32)
            idxf = sb2("idxf", [128, 16, 16], F32)
            cand = sb2("cand", [128, 8, 256], F32)
            cscr = sb2("cscr", [128, 256], F32)
            ts = sb2("ts", [128, 8, 16], F32)
            tc_ = sb2("tc", [128, 8, 16], U32)
            tcf = sb2("tcf", [128, 8, 16], F32)
            af = sb2("af", [128, 8, 16], F32)
            bf = sb2("bf", [128, 8, 16], F32)
            oh = sb2("oh", [128, 8, 16, 16], F32)
            IJg = sb2("IJg", [128, 3, 128], F32)
            esum = sb2("esum", [128, 8], F32)
            _wslab(self, wpq, I["w_pq"], 0, 2048)
            S.dma("pool", lambda e: e.dma_start(out=skT.t[:], in_=I["skT"].rearrange("g d n -> d g n")), skT.b, writes=[skT.b])
            iota16 = self.iota_f.t[:, 0:16]
            for m in range(16):
                q = qT[m % 2]
                for gi in range(16):
                    bk = self.bank[gi % 4]
                    for kc in range(8):
                        S.op("pe", lambda e, bk=bk, kc=kc, gi=gi, m=m: e.matmul(bk.t[:, 0:128], lhsT=wpq.t[:, kc, gi * 128:(gi + 1) * 128], rhs=hfT.t[:, kc, m * 128:(m + 1) * 128], start=(kc == 0), stop=(kc == 7)),
                             reads=[wpq.b, hfT.b], writes=[bk.b])
                    S.op("act", lambda e, bk=bk, gi=gi, q=q: e.copy(out=q.t[:, gi, :], in_=bk.t[:, 0:128]), reads=[bk.b], writes=[q.b])
                for gi in range(16):
                    bk = self.bank[4 + gi // 4]
                    S.op("pe", lambda e, bk=bk, gi=gi, q=q: e.matmul(bk.t[:, (gi % 4) * 128:(gi % 4 + 1) * 128], lhsT=q.t[:, gi, :], rhs=skT.t[:, gi, :], start=True, stop=True),
                         reads=[q.b, skT.b], writes=[bk.b])
                for b4 in range(4):
                    bk = self.bank[4 + b4]
                    S.op("act", lambda e, bk=bk, b4=b4: e.copy(out=sc.t[:, b4 * 4:(b4 + 1) * 4, :], in_=bk.t[:].rearrange("p (g n) -> p g n", g=4)), reads=[bk.b], writes=[sc.b])
                if self.peer_stop == "p1sc" and m == 0:
                    S.dma("sp", lambda e: e.dma_start(out=self.out[0:256, :].rearrange("(p a) d -> p (a d)", p=128), in_=sc.t[:].rearrange("p g n -> p (g n)")), sc.b, reads=[sc.b], is_output=True)
                    qf = sb2("qf", [128, 16, 128], F32)
                    S.op("dve", lambda e: e.tensor_copy(out=qf.t[:], in_=q.t[:]), reads=[q.b], writes=[qf.b])
                    S.dma("sp", lambda e: e.dma_start(out=self.out[256:512, :].rearrange("(p a) d -> p (a d)", p=128), in_=qf.t[:].rearrange("p g n -> p (g n)")), qf.b, reads=[qf.b], is_output=True)
                    return
                for gi in range(16):
                    S.op("dve", lambda e, gi=gi: e.max(out=vals.t[:, gi, 0:8], in_=sc.t[:, gi, :]), reads=[sc.b], writes=[vals.b])
                    S.op("dve", lambda e, gi=gi: e.max_index(out=idx.t[:, gi, 0:8], in_max=vals.t[:, gi, 0:8], in_values=sc.t[:, gi, :]), reads=[sc.b, vals.b], writes=[idx.b])
                    S.op("dve", lambda e, gi=gi: e.match_replace(out=scr.t[:, gi, :], in_to_replace=vals.t[:, gi, 0:8], in_values=sc.t[:, gi, :], imm_value=-1e30), reads=[sc.b, vals.b], writes=[scr.b])
                    S.op("dve", lambda e, gi=gi: e.max(out=vals.t[:, gi, 8:16], in_=scr.t[:, gi, :]), reads=[scr.b], writes=[vals.b])
                    S.op("dve", lambda e, gi=gi: e.max_index(out=idx.t[:, gi, 8:16], in_max=vals.t[:, gi, 8:16], in_values=scr.t[:, gi, :]), reads=[scr.b, vals.b], writes=[idx.b])
                S.op("dve", lambda e: e.tensor_copy(out=idxf.t[:], in_=idx.t[:]), reads=[idx.b], writes=[idxf.b])
                v0 = _bcast_ap(vals.t, 0, [[32, 8], [1, 16], [0, 16]])
                v1 = _bcast_ap(vals.t, 16, [[32, 8], [0, 16], [1, 16]])
                S.op("dve", lambda e, v0=v0, v1=v1: e.tensor_tensor(out=cand.t[:].rearrange("p h (a b) -> p h a b", a=16), in0=v0, in1=v1, op=ALU.add), reads=[vals.b], writes=[cand.b])
                for h in range(8):
                    S.op("dve", lambda e, h=h: e.max(out=ts.t[:, h, 0:8], in_=cand.t[:, h, :]), reads=[cand.b], writes=[ts.b])
                    S.op("dve", lambda e, h=h: e.max_index(out=tc_.t[:, h, 0:8], in_max=ts.t[:, h, 0:8], in_values=cand.t[:, h, :]), reads=[cand.b, ts.b], writes=[tc_.b])
                    S.op("dve", lambda e, h=h: e.match_replace(out=cscr.t[:], in_to_replace=ts.t[:, h, 0:8], in_values=cand.t[:, h, :], imm_value=-1e30), reads=[cand.b, ts.b], writes=[cscr.b])
                    S.op("dve", lambda e, h=h: e.max(out=ts.t[:, h, 8:16], in_=cscr.t[:]), reads=[cscr.b], writes=[ts.b])
                    S.op("dve", lambda e, h=h: e.max_index(out=tc_.t[:, h, 8:16], in_max=ts.t[:, h, 8:16], in_values=cscr.t[:]), reads=[cscr.b, ts.b], writes=[tc_.b])
                S.op("dve", lambda e: e.tensor_copy(out=tcf.t[:], in_=tc_.t[:]), reads=[tc_.b], writes=[tcf.b])
                S.op("dve", lambda e: e.tensor_scalar(out=af.t[:], in0=tcf.t[:], scalar1=0.0625, scalar2=-0.46875, op0=ALU.mult, op1=ALU.add), reads=[tcf.b], writes=[af.b])
                S.op("dve", lambda e: e.tensor_scalar(out=af.t[:], in0=af.t[:], scalar1=MAGIC, scalar2=None, op0=ALU.add), reads=[af.b], writes=[af.b])
                S.op("dve", lambda e: e.tensor_scalar(out=af.t[:], in0=af.t[:], scalar1=-MAGIC, scalar2=None, op0=ALU.add), reads=[af.b], writes=[af.b])
                S.op("dve", lambda e: e.scalar_tensor_tensor(out=bf.t[:], in0=af.t[:], scalar=-16.0, in1=tcf.t[:], op0=ALU.mult, op1=ALU.add), reads=[af.b, tcf.b], writes=[bf.b])
                for which, sel in ((0, af), (1, bf)):
                    selb = _bcast_ap(sel.t, 0, [[16, 8], [1, 16], [0, 16]])
                    iob = _bcast_ap(self.iota_f.t, 0, [[0, 8], [0, 16], [1, 16]])
                    ixb = _bcast_ap(idxf.t, which * 16, [[32, 8], [0, 16], [1, 16]])
                    S.op("dve", lambda e, selb=selb, iob=iob: e.tensor_tensor(out=oh.t[:], in0=selb, in1=iob, op=ALU.is_equal), reads=[sel.b, self.iota_f.b], writes=[oh.b])
                    S.op("dve", lambda e, ixb=ixb: e.tensor_tensor(out=oh.t[:], in0=oh.t[:], in1=ixb, op=ALU.mult), reads=[oh.b, idxf.b], writes=[oh.b])
                    S.op("dve", lambda e, which=which: e.tensor_reduce(out=IJg.t[:, which, :], in_=oh.t[:].rearrange("p h k a -> p (h k) a"), axis=AX.X, op=ALU.add), reads=[oh.b], writes=[IJg.b])
                tmax = _bcast_ap(ts.t, 0, [[16, 8], [0, 16]])
                S.op("dve", lambda e, tmax=tmax: e.tensor_tensor(out=tcf.t[:], in0=ts.t[:], in1=tmax, op=ALU.subtract), reads=[ts.b], writes=[tcf.b])
                S.op("act", lambda e: e.activation(out=tcf.t[:], in_=tcf.t[:], func=AF.Exp), reads=[tcf.b], writes=[tcf.b])
                S.op("dve", lambda e: e.tensor_reduce(out=esum.t[:], in_=tcf.t[:], axis=AX.X, op=ALU.add), reads=[tcf.b], writes=[esum.b])
                S.op("dve", lambda e: e.reciprocal(out=esum.t[:], in_=esum.t[:]), reads=[esum.b], writes=[esum.b])
                esb = _bcast_ap(esum.t, 0, [[1, 8], [0, 16]])
                S.op("dve", lambda e, esb=esb: e.tensor_tensor(out=IJg.t[:, 2, :].rearrange("p (h k) -> p h k", h=8), in0=tcf.t[:], in1=esb, op=ALU.mult), reads=[tcf.b, esum.b], writes=[IJg.b])
                for w in range(3):
                    bk = self.bank[w]
                    S.op("pe", lambda e, bk=bk, w=w: e.transpose(out=bk.t[:, 0:128], in_=IJg.t[:, w, :], identity=self.ident_f.t[:]), reads=[IJg.b, self.ident_f.b], writes=[bk.b])
                    S.op("act", lambda e, bk=bk, w=w, m=m: e.copy(out=RT.t[:, w, m * 128:(m + 1) * 128], in_=bk.t[:, 0:128]), reads=[bk.b], writes=[RT.b])
        S.barrier()
        if self.peer_stop == "p1":
            S.dma("sp", lambda e: e.dma_start(out=self.out[0:768, :].rearrange("(p a) d -> p (a d)", p=128), in_=RT.t[:].rearrange("p w t -> p (w t)")), RT.b, reads=[RT.b], is_output=True)
            return
        with contextlib.ExitStack() as st2:
            self.a_ptr = pR
            sb2 = lambda name, shape, dt: self.sb(st2, name, shape, dt)
            TB = 256
            W = sb2("W", [128, TB, 128], BF16)
            ohj = [sb2(f"ohj{i}", [128, 16, 128], BF16) for i in range(2)]
            ohi = [sb2(f"ohi{i}", [128, 16, 128], BF16) for i in range(2)]
            utl = [sb2(f"utl{i}", [128, 8, 128], BF16) for i in range(4)]
            vtl = [sb2(f"vtl{i}", [128, 1024], BF16) for i in range(5)]
            ag = [sb2(f"ag{i}", [128, TB], F32) for i in range(4)]
            wa = [sb2(f"wa{i}", [128, TB], BF16) for i in range(4)]
            x2t = [sb2(f"x2t{i}", [128, 1024], F32) for i in range(2)]
            sqj = sb2("sqjf", [128, 1024], BF16)
            fs = sb2("fs", [128, 4], F32)
            frs = sb2("frs", [128, 4], F32)
            yo = [sb2(f"yo{i}", [128, 1024], F32) for i in range(2)]
            gfin = sb2("gfin", [128, 1024], F32)
            S.dma("sp", lambda e: e.dma_start(out=gfin.t[:], in_=I["final_g_rep"]), gfin.b, writes=[gfin.b])
            psO = self.bank[0:4]
            psA = self.bank[4:8]
            psW = self.bank[6]
            for blk in range(2048 // TB):
                t0 = blk * TB
                G = 16
                for tg in range(TB // G):
                    tq = t0 + tg * G
                    oj, oi = ohj[tg % 2], ohi[tg % 2]
                    iob = _bcast_ap(self.iota_f.t, 0, [[0, G], [1, 128]])
                    Ib = _bcast_ap(RT.t, 0 * 2048 + tq, [[1, G], [0, 128]])
                    Jb = _bcast_ap(RT.t, 1 * 2048 + tq, [[1, G], [0, 128]])
                    gb = _bcast_ap(RT.t, 2 * 2048 + tq, [[1, G], [0, 128]])
                    S.op("dve", lambda e, oj=oj, iob=iob, Jb=Jb: e.tensor_tensor(out=oj.t[:], in0=iob, in1=Jb, op=ALU.is_equal), reads=[self.iota_f.b, RT.b], writes=[oj.b])
                    S.op("dve", lambda e, oi=oi, iob=iob, Ib=Ib: e.tensor_tensor(out=oi.t[:], in0=iob, in1=Ib, op=ALU.is_equal), reads=[self.iota_f.b, RT.b], writes=[oi.b])
                    S.op("pool", lambda e, oi=oi, gb=gb: e.tensor_tensor(out=oi.t[:], in0=oi.t[:], in1=gb, op=ALU.mult), reads=[oi.b, RT.b], writes=[oi.b])
                    for k in range(G):
                        tt = tg * G + k
                        S.op("pe", lambda e, oj=oj, oi=oi, tt=tt, k=k: e.matmul(psW.t[:, (tt % 4) * 128:(tt % 4 + 1) * 128], lhsT=oj.t[:, k, :], rhs=oi.t[:, k, :], start=True, stop=True), reads=[oj.b, oi.b], writes=[psW.b])
                        if tt % 4 == 3:
                            tb0 = tt - 3
                            S.op("act", lambda e, tb0=tb0: e.copy(out=W.t[:, tb0:tb0 + 4, :], in_=psW.t[:].rearrange("p (q i) -> p q i", q=4)), reads=[psW.b], writes=[W.b])
                LA = 2

                def emit_A(i, t0=t0):
                    ut, vt = utl[i % 4], vtl[i % 5]
                    S.dma("sp", lambda e, ut=ut, i=i: e.dma_start(out=ut.t[:], in_=self.UT_d[i]), ut.b, reads=[self.B_UT], writes=[ut.b])
                    S.dma("act", lambda e, vt=vt, i=i: e.dma_start(out=vt.t[:], in_=self.Vb_d[i * 128:(i + 1) * 128, :]), vt.b, reads=[self.B_Vb], writes=[vt.b])
                    pa = psA[i % 4]
                    for kc in range(8):
                        S.op("pe", lambda e, pa=pa, ut=ut, kc=kc, t0=t0: e.matmul(pa.t[:, 0:TB], lhsT=ut.t[:, kc, :], rhs=hfT.t[:, kc, t0:t0 + TB], start=(kc == 0), stop=(kc == 7)),
                             reads=[ut.b, hfT.b], writes=[pa.b])
                    a_, w_ = ag[i % 4], wa[i % 4]
                    S.op("act", lambda e, pa=pa, a_=a_: e.activation(out=a_.t[:], in_=pa.t[:, 0:TB], func=AF.Gelu), reads=[pa.b], writes=[a_.b])
                    S.op("pool", lambda e, a_=a_, w_=w_, i=i: e.tensor_tensor(out=w_.t[:], in0=a_.t[:], in1=W.t[:, :, i], op=ALU.mult), reads=[a_.b, W.b], writes=[w_.b])

                def emit_out(i):
                    vt = vtl[i % 5]
                    w_ = wa[i % 4]
                    for tl in range(TB // 128):
                        for dh in range(2):
                            po = psO[tl * 2 + dh]
                            S.op("pe", lambda e, po=po, w_=w_, vt=vt, tl=tl, dh=dh, i=i: e.matmul(po.t[:], lhsT=w_.t[:, tl * 128:(tl + 1) * 128], rhs=vt.t[:, dh * 512:(dh + 1) * 512], start=(i == 0), stop=(i == 127)),
                                 reads=[w_.b, vt.b], writes=[po.b])

                for step_i in range(128 + LA):
                    if step_i < 128:
                        emit_A(step_i)
                    if step_i >= LA:
                        emit_out(step_i - LA)
                for tl in range(TB // 128):
                    m = (t0 // 128) + tl
                    xt_ = x2t[tl % 2]
                    yt = yo[tl % 2]
                    S.dma("sp", lambda e, xt_=xt_, m=m: e.dma_start(out=xt_.t[:], in_=X2_d[m * 128:(m + 1) * 128, :]), xt_.b, reads=[B_X2], writes=[xt_.b])
                    for dh in range(2):
                        po = psO[tl * 2 + dh]
                        S.op("dve", lambda e, po=po, xt_=xt_, dh=dh: e.tensor_tensor(out=xt_.t[:, dh * 512:(dh + 1) * 512], in0=po.t[:], in1=xt_.t[:, dh * 512:(dh + 1) * 512], op=ALU.add), reads=[po.b, xt_.b], writes=[xt_.b])
                    if "x3" in self.dumps:
                        S.dma("sp", lambda e, xt_=xt_, m=m: e.dma_start(out=self.dbg["x3"][m * 128:(m + 1) * 128, :], in_=xt_.t[:]), xt_.b, reads=[xt_.b], is_output=True)
                    S.op("act", lambda e, xt_=xt_: e.activation(out=sqj.t[:], in_=xt_.t[:], func=AF.Square, accum_out=fs.t[:, 0:1]), reads=[xt_.b], writes=[sqj.b, fs.b])
                    S.op("dve", lambda e: e.tensor_scalar(out=frs.t[:, 0:1], in0=fs.t[:, 0:1], scalar1=1.0 / D, scalar2=1e-6, op0=ALU.mult, op1=ALU.add), reads=[fs.b], writes=[frs.b])
                    S.op("pool", lambda e: e.tensor_tensor(out=frs.t[:, 0:1], in0=frs.t[:, 0:1], in1=self.mhalf.t[:, 0:1], op=ALU.pow), reads=[frs.b, self.mhalf.b], writes=[frs.b])
                    S.op("dve", lambda e, xt_=xt_, yt=yt: e.scalar_tensor_tensor(out=yt.t[:], in0=xt_.t[:], scalar=frs.t[:, 0:1], in1=gfin.t[:], op0=ALU.mult, op1=ALU.mult), reads=[xt_.b, frs.b, gfin.b], writes=[yt.b])
                    S.dma("sp", lambda e, yt=yt, m=m: e.dma_start(out=self.out[m * 128:(m + 1) * 128, :], in_=yt.t[:]), yt.b, reads=[yt.b], is_output=True)


def _phase_final(self):
    pass


Builder.phase_peer = _phase_peer
Builder.phase_final = _phase_final
```
